# Optimizing a Trainium2 kernel written in Bass

```python
import math
import jax
import jax.numpy as jnp
from jax import lax
import numpy as np

D_MODEL = 1024
BATCH = 8
SEQ = 4096
DEPTH = 2

CTX_LEN = 256
GRID_W = 64

N_BRANCH = 4
N_MOD = 9
FF_DIM = 2816
EPS = 1e-6
ROPE_BASE = 10000.0
NEG_INF = -1e30
BRANCH_DIM = 512

MLA_HEADS = 8
MLA_NOPE = 64
MLA_ROPE = 32
MLA_V = 64
MLA_Q_RANK = 256
MLA_KV_RANK = 128

RWKV_HEADS = 8
RWKV_HEAD = 64
RWKV_DIM = RWKV_HEADS * RWKV_HEAD
DECAY_LORA = 64
AAA_LORA = 64
GATE_LORA = 128
RWKV_LN_EPS = 64e-5

HYENA_DIM = 512
HYENA_ORDER = 2
HYENA_EMB = 33
HYENA_BANDS = (HYENA_EMB - 1) // 2
HYENA_FW = 64
HYENA_TARGET = 1e-2
HYENA_FAST = 0.3
HYENA_SLOW = 1.5
SHORT_CONV = 3

SWA_HEADS = 8
SWA_KV_HEADS = 2
SWA_HEAD = 64
SWA_GROUP = SWA_HEADS // SWA_KV_HEADS
WINDOW = 128
BLOCK = 128

GATE_COLS = N_BRANCH * D_MODEL
MLA_COLS = MLA_Q_RANK + MLA_KV_RANK + MLA_ROPE
RWKV_COLS = 3 * RWKV_DIM + DECAY_LORA + AAA_LORA + GATE_LORA
HYENA_COLS = 3 * HYENA_DIM
SWA_COLS = (SWA_HEADS + 2 * SWA_KV_HEADS) * SWA_HEAD
IN_COLS = GATE_COLS + MLA_COLS + RWKV_COLS + HYENA_COLS + SWA_COLS
IN_SPLITS = (GATE_COLS, GATE_COLS + MLA_COLS, GATE_COLS + MLA_COLS + RWKV_COLS,
             GATE_COLS + MLA_COLS + RWKV_COLS + HYENA_COLS)
RWKV_SPLITS = (RWKV_DIM, 2 * RWKV_DIM, 3 * RWKV_DIM, 3 * RWKV_DIM + DECAY_LORA,
               3 * RWKV_DIM + DECAY_LORA + AAA_LORA)

kernel_name = "hybrid_mla_rwkv7_hyena_swa_diffusion_block"


def rmsnorm(x, g):
    xf = x.astype(jnp.float32)
    y = xf * lax.rsqrt(jnp.mean(xf * xf, axis=-1, keepdims=True) + EPS)
    return (y * g.astype(jnp.float32)).astype(x.dtype)


def modulate(x, shift, scale):
    return x * (1 + scale) + shift


def swiglu(x, w13, w2):
    a, b = jnp.split(x @ w13, 2, axis=-1)
    return (jax.nn.silu(a) * b) @ w2


def ffn_half_step(x, shift, scale, gate, g_pre, g_post, w13, w2):
    h = swiglu(modulate(rmsnorm(x, g_pre), shift, scale), w13, w2)
    return x + 0.5 * gate * rmsnorm(h, g_post)


def centred_shift(p):
    zero = jnp.zeros_like(p[:, :1])
    prev = jnp.concatenate([zero, p[:, :-1]], axis=1)
    nxt = jnp.concatenate([p[:, 1:], zero], axis=1)
    return prev, nxt


def axial_rope_tables(rows, rot_dim):
    row = jnp.repeat(jnp.arange(rows, dtype=jnp.float32), GRID_W)
    col = jnp.tile(jnp.arange(GRID_W, dtype=jnp.float32), rows)
    axis_dim = rot_dim // 2
    inv_freq = ROPE_BASE ** (-jnp.arange(0, axis_dim, 2, dtype=jnp.float32) / axis_dim)
    ang_r = row[:, None] * inv_freq
    ang_c = col[:, None] * inv_freq
    return (jnp.cos(ang_r), jnp.sin(ang_r), jnp.cos(ang_c), jnp.sin(ang_c))


def rope_rotate(x, cos, sin):
    x1, x2 = jnp.split(x, 2, axis=-1)
    cos = cos[None, :, None, :].astype(x.dtype)
    sin = sin[None, :, None, :].astype(x.dtype)
    return jnp.concatenate([x1 * cos - x2 * sin, x1 * sin + x2 * cos], axis=-1)


def axial_rope(x, tabs):
    cos_r, sin_r, cos_c, sin_c = tabs
    x_row, x_col = jnp.split(x, 2, axis=-1)
    return jnp.concatenate([rope_rotate(x_row, cos_r, sin_r), rope_rotate(x_col, cos_c, sin_c)], axis=-1)


def softmax_attend(q, k, v, scale):
    s = jnp.einsum("bqhd,bkhd->bhqk", q, k).astype(jnp.float32) * scale
    p = jax.nn.softmax(s, axis=-1).astype(v.dtype)
    return jnp.einsum("bhqk,bkhd->bqhd", p, v)


def blocked_attend(q, k, v, scale):
    b, n, h, d = q.shape
    nb = n // BLOCK
    qb = jnp.moveaxis(q.reshape(b, nb, BLOCK, h, d), 1, 0)
    out = lax.map(lambda qi: softmax_attend(qi, k, v, scale), qb)
    return jnp.moveaxis(out, 0, 1).reshape(b, n, h, v.shape[-1])


def mla_project(p, lp, tabs):
    b, n, _ = p.shape
    c_q, c_kv, k_r = jnp.split(p, [MLA_Q_RANK, MLA_Q_RANK + MLA_KV_RANK], axis=-1)
    q = (rmsnorm(c_q, lp["mla_norm_q"]) @ lp["mla_w_uq"]).reshape(b, n, MLA_HEADS, MLA_NOPE + MLA_ROPE)
    kv = (rmsnorm(c_kv, lp["mla_norm_kv"]) @ lp["mla_w_ukv"]).reshape(b, n, MLA_HEADS, MLA_NOPE + MLA_V)
    q_nope, q_rope = jnp.split(q, [MLA_NOPE], axis=-1)
    k_nope, v = jnp.split(kv, [MLA_NOPE], axis=-1)
    k_r = k_r[:, :, None, :]
    if tabs is not None:
        q_rope = axial_rope(q_rope, tabs)
        k_r = axial_rope(k_r, tabs)
    q = jnp.concatenate([q_nope, q_rope], axis=-1)
    k = jnp.concatenate([k_nope, jnp.broadcast_to(k_r, (b, n, MLA_HEADS, MLA_ROPE))], axis=-1)
    return q, k, v


def mla_mixer(p_lat, p_ctx, lp, tabs, with_ctx):
    scale = (MLA_NOPE + MLA_ROPE) ** -0.5
    q, k, v = mla_project(p_lat, lp, tabs)
    qc, kc, vc = mla_project(p_ctx, lp, None)
    k_all = jnp.concatenate([k, kc], axis=1)
    v_all = jnp.concatenate([v, vc], axis=1)
    b, n = p_lat.shape[:2]
    y = blocked_attend(q, k_all, v_all, scale).reshape(b, n, MLA_HEADS * MLA_V)
    yc = None
    if with_ctx:
        yc = softmax_attend(qc, kc, vc, scale).reshape(b, p_ctx.shape[1], MLA_HEADS * MLA_V)
    return y, yc


def rwkv_prepare(p, lp):
    f32 = jnp.float32
    p = p.astype(f32)
    prev, nxt = centred_shift(p)
    mu = lp["rwkv_mu"].astype(f32)
    p = p + mu[0] * (prev - p) + mu[1] * (nxt - p)
    r, k, v, w_lo, a_lo, g_lo = jnp.split(p, RWKV_SPLITS, axis=-1)
    b, n, _ = p.shape

    def heads(t):
        return t.reshape(b, n, RWKV_HEADS, RWKV_HEAD)

    kk = heads(k * lp["rwkv_kvec"][0])
    kk = kk * lax.rsqrt(jnp.sum(kk * kk, axis=-1, keepdims=True) + 1e-12)
    g = jax.nn.sigmoid(g_lo) @ lp["rwkv_g_up"].astype(f32)
    per_dir = []
    for d in range(2):
        w_log = -jax.nn.softplus(-(lp["rwkv_w0"][d] + jnp.tanh(w_lo) @ lp["rwkv_w_up"][d])) - 0.5
        decay = jnp.exp(-jnp.exp(w_log))
        a = jax.nn.sigmoid(lp["rwkv_a0"][d] + a_lo @ lp["rwkv_a_up"][d])
        k_d = k * (1 + (a - 1) * lp["rwkv_kvec"][1])
        per_dir.append((heads(decay), heads(k_d), -kk, kk * heads(a)))
    return heads(r), heads(v), g, per_dir


def wkv_scan(s0, r, w, k, v, z, bb, reverse):
    def step(s, inp):
        r_t, w_t, k_t, v_t, z_t, b_t = inp
        sz = jnp.einsum("bhvk,bhk->bhv", s, z_t)
        s = s * w_t[:, :, None, :] + sz[..., None] * b_t[:, :, None, :] + v_t[..., None] * k_t[:, :, None, :]
        return s, jnp.einsum("bhvk,bhk->bhv", s, r_t)

    xs = tuple(jnp.moveaxis(t, 1, 0) for t in (r, w, k, v, z, bb))
    s_final, ys = lax.scan(step, s0, xs, reverse=reverse)
    return jnp.moveaxis(ys, 0, 1), s_final


def rwkv_readout(y, r, v, k_bonus, g, lp, dtype):
    b, n = y.shape[:2]
    mean = jnp.mean(y, axis=-1, keepdims=True)
    var = jnp.mean(jnp.square(y - mean), axis=-1, keepdims=True)
    yn = ((y - mean) * lax.rsqrt(var + RWKV_LN_EPS)).reshape(b, n, RWKV_DIM)
    yn = yn * lp["rwkv_ln_g"] + lp["rwkv_ln_b"]
    r_k = lp["rwkv_r_k"].astype(jnp.float32).reshape(RWKV_HEADS, RWKV_HEAD)
    bonus = (jnp.sum(r * k_bonus * r_k, axis=-1, keepdims=True) * v).reshape(b, n, RWKV_DIM)
    return ((yn + bonus) * g).astype(dtype)


def rwkv_mixer(p_lat, p_ctx, lp, with_ctx):
    r, v, g, dirs = rwkv_prepare(p_lat, lp)
    rc, vc, gc, dirs_c = rwkv_prepare(p_ctx, lp)
    s0 = jnp.zeros((p_lat.shape[0], RWKV_HEADS, RWKV_HEAD, RWKV_HEAD), jnp.float32)
    y = 0.0
    yc = 0.0
    for d, rev in enumerate((False, True)):
        dec_c, k_c, z_c, b_c = dirs_c[d]
        yc_d, s_ctx = wkv_scan(s0, rc, dec_c, k_c, vc, z_c, b_c, rev)
        dec, k_d, z_d, b_d = dirs[d]
        y_d, _ = wkv_scan(s_ctx, r, dec, k_d, v, z_d, b_d, rev)
        y = y + y_d
        yc = yc + yc_d
    out = rwkv_readout(y, r, v, 0.5 * (dirs[0][1] + dirs[1][1]), g, lp, p_lat.dtype)
    out_c = None
    if with_ctx:
        out_c = rwkv_readout(yc, rc, vc, 0.5 * (dirs_c[0][1] + dirs_c[1][1]), gc, lp, p_ctx.dtype)
    return out, out_c


def hyena_filters(n, lp):
    f32 = jnp.float32
    t = jnp.linspace(0.0, 1.0, n, dtype=f32)[:, None]
    bands = jnp.linspace(1e-4, HYENA_BANDS - 1, HYENA_BANDS, dtype=f32)
    ang = (2.0 * math.pi / n) * jnp.arange(n, dtype=f32)[:, None] * bands[None, :]
    feats = jnp.concatenate([t, jnp.cos(ang), -jnp.sin(ang)], axis=-1)
    freq = lp["hyena_freq"].astype(f32)
    h = jnp.sin(freq[0] * (feats @ lp["hyena_w1"].astype(f32) + lp["hyena_b1"].astype(f32)))
    h = jnp.sin(freq[1] * (h @ lp["hyena_w2"].astype(f32) + lp["hyena_b2"].astype(f32)))
    h = (h @ lp["hyena_w3"].astype(f32)).reshape(n, HYENA_ORDER, 2, HYENA_DIM)
    deltas = jnp.abs(jnp.linspace(math.log(HYENA_TARGET) / HYENA_SLOW,
                                  math.log(HYENA_TARGET) / HYENA_FAST, HYENA_DIM, dtype=f32))
    h = h * jnp.exp(-t * deltas)[:, None, None, :]
    return h / jnp.sum(jnp.abs(h), axis=(0, 2), keepdims=True)


def bidir_fftconv(u, h_fwd, h_bwd, bias):
    n = u.shape[1]
    kbuf = jnp.concatenate([h_fwd, jnp.zeros_like(h_fwd[:1]), h_bwd[1:][::-1]], axis=0)
    uf = jnp.fft.rfft(u.astype(jnp.float32), n=2 * n, axis=1)
    kf = jnp.fft.rfft(kbuf, n=2 * n, axis=0)
    y = jnp.fft.irfft(uf * kf[None], n=2 * n, axis=1)[:, :n]
    return (y + u.astype(jnp.float32) * bias.astype(jnp.float32)).astype(u.dtype)


def hyena_operator(p, lp):
    prev, nxt = centred_shift(p)
    ck = lp["hyena_conv"]
    p = ck[0] * prev + ck[1] * p + ck[2] * nxt + lp["hyena_conv_b"]
    v, x1, x2 = jnp.split(p, 3, axis=-1)
    h = hyena_filters(p.shape[1], lp)
    z = v
    for o, gate in enumerate((x1, x2)):
        z = gate * bidir_fftconv(z, h[:, o, 0], h[:, o, 1], lp["hyena_bias"][o])
    return z


def swa_project(p, tabs):
    b, n, _ = p.shape
    q, k, v = jnp.split(p, [SWA_HEADS * SWA_HEAD, (SWA_HEADS + SWA_KV_HEADS) * SWA_HEAD], axis=-1)
    q = q.reshape(b, n, SWA_HEADS, SWA_HEAD)
    k = k.reshape(b, n, SWA_KV_HEADS, SWA_HEAD)
    v = v.reshape(b, n, SWA_KV_HEADS, SWA_HEAD)
    if tabs is not None:
        q = axial_rope(q, tabs)
        k = axial_rope(k, tabs)
    return q.reshape(b, n, SWA_KV_HEADS, SWA_GROUP, SWA_HEAD), k, v


def sink_softmax(parts, sink):
    lead = parts[0].shape[:-1]
    s = jnp.concatenate(parts + [jnp.broadcast_to(sink[None, :, :, None, None], lead + (1,))], axis=-1)
    return jax.nn.softmax(s, axis=-1)[..., :-1]


def swa_mixer(p_lat, p_ctx, lp, tabs, with_ctx):
    f32 = jnp.float32
    scale = SWA_HEAD ** -0.5
    q, k, v = swa_project(p_lat, tabs)
    qc, kc, vc = swa_project(p_ctx, None)
    sink = lp["swa_sink"].astype(f32).reshape(SWA_KV_HEADS, SWA_GROUP)
    b, n = p_lat.shape[:2]
    nb = n // BLOCK
    pad = jnp.zeros((b, BLOCK, SWA_KV_HEADS, SWA_HEAD), k.dtype)
    kp = jnp.concatenate([pad, k, pad], axis=1)
    vp = jnp.concatenate([pad, v, pad], axis=1)
    qb = jnp.moveaxis(q.reshape(b, nb, BLOCK, SWA_KV_HEADS, SWA_GROUP, SWA_HEAD), 1, 0)
    offs = jnp.arange(3 * BLOCK)
    in_window = jnp.abs(offs[None, :] - BLOCK - jnp.arange(BLOCK)[:, None]) <= WINDOW

    def block(args):
        i, qi = args
        start = i * BLOCK
        ki = lax.dynamic_slice_in_dim(kp, start, 3 * BLOCK, axis=1)
        vi = lax.dynamic_slice_in_dim(vp, start, 3 * BLOCK, axis=1)
        key_pos = start - BLOCK + offs
        mask = in_window & ((key_pos >= 0) & (key_pos < n))[None, :]
        s_loc = jnp.einsum("bqkgd,bskd->bkgqs", qi, ki).astype(f32) * scale
        s_loc = jnp.where(mask, s_loc, NEG_INF)
        s_ctx = jnp.einsum("bqkgd,bckd->bkgqc", qi, kc).astype(f32) * scale
        pr = sink_softmax([s_loc, s_ctx], sink).astype(vi.dtype)
        return (jnp.einsum("bkgqs,bskd->bqkgd", pr[..., :3 * BLOCK], vi)
                + jnp.einsum("bkgqc,bckd->bqkgd", pr[..., 3 * BLOCK:], vc))

    out = lax.map(block, (jnp.arange(nb), qb))
    y = jnp.moveaxis(out, 0, 1).reshape(b, n, SWA_HEADS * SWA_HEAD)
    yc = None
    if with_ctx:
        s_cc = jnp.einsum("bqkgd,bckd->bkgqc", qc, kc).astype(f32) * scale
        pr = sink_softmax([s_cc], sink).astype(vc.dtype)
        yc = jnp.einsum("bkgqc,bckd->bqkgd", pr, vc).reshape(b, p_ctx.shape[1], SWA_HEADS * SWA_HEAD)
    return y, yc


def merge_branches(gates, branches, lp):
    d = gates.shape[-1] // N_BRANCH
    merged = None
    for i, y in enumerate(branches):
        gi = jax.nn.sigmoid(gates[..., i * d:(i + 1) * d] + lp["b_gate"][i])
        term = gi * (y @ lp["w_branch"][i])
        merged = term if merged is None else merged + term
    return merged @ lp["w_out"]


def token_mixing(u, uc, lp, tabs_mla, tabs_swa, with_ctx):
    gates, pa, pb, ph, pd = jnp.split(u @ lp["w_in"], IN_SPLITS, axis=-1)
    gates_c, pa_c, pb_c, ph_c, pd_c = jnp.split(uc @ lp["w_in"], IN_SPLITS, axis=-1)
    ya, ya_c = mla_mixer(pa, pa_c, lp, tabs_mla, with_ctx)
    yb, yb_c = rwkv_mixer(pb, pb_c, lp, with_ctx)
    yh = hyena_operator(ph, lp)
    yd, yd_c = swa_mixer(pd, pd_c, lp, tabs_swa, with_ctx)
    y = merge_branches(gates, (ya, yb, yh, yd), lp)
    y_ctx = None
    if with_ctx:
        yh_c = hyena_operator(ph_c, lp)
        y_ctx = merge_branches(gates_c, (ya_c, yb_c, yh_c, yd_c), lp)
    return y, y_ctx


def trunk_layer(x, xc, c, c_ctx, lp, tabs_mla, tabs_swa, with_ctx):
    b, d = c.shape
    m = (jax.nn.silu(c) @ lp["w_mod"] + lp["b_mod"]).reshape(b, N_MOD, 1, d)
    mc = (jax.nn.silu(c_ctx) @ lp["w_mod"] + lp["b_mod"]).reshape(N_MOD, d)
    g = lp["norm_g"]
    w13 = lp["ffn_w13"]
    w2 = lp["ffn_w2"]
    x = ffn_half_step(x, m[:, 0], m[:, 1], m[:, 2], g[0], g[1], w13[0], w2[0])
    xc = ffn_half_step(xc, mc[0], mc[1], mc[2], g[0], g[1], w13[0], w2[0])
    u = modulate(rmsnorm(x, g[2]), m[:, 3], m[:, 4])
    uc = modulate(rmsnorm(xc, g[2]), mc[3], mc[4])
    y, y_ctx = token_mixing(u, uc, lp, tabs_mla, tabs_swa, with_ctx)
    x = x + m[:, 5] * rmsnorm(y, g[3])
    x = ffn_half_step(x, m[:, 6], m[:, 7], m[:, 8], g[4], g[5], w13[1], w2[1])
    if with_ctx:
        xc = xc + mc[5] * rmsnorm(y_ctx, g[3])
        xc = ffn_half_step(xc, mc[6], mc[7], mc[8], g[4], g[5], w13[1], w2[1])
    return x, xc


def setup_inputs(seed: int = 0) -> dict:
    key = jax.random.key(seed)
    ks = iter(jax.random.split(key, 48))
    f32 = jnp.float32

    def nrm(shape, scale):
        return scale * jax.random.normal(next(ks), shape, f32)

    def gain(shape):
        return 1.0 + nrm(shape, 0.02)

    L, D = DEPTH, D_MODEL
    return {
        "x": nrm((BATCH, SEQ, D), 1.0),
        "c": nrm((BATCH, D), 1.0),
        "ctx": nrm((BATCH, CTX_LEN, D), 1.0),
        "c_ctx": nrm((D,), 1.0),
        "w_mod": nrm((L, D, N_MOD * D), 0.5 * D ** -0.5),
        "b_mod": nrm((L, N_MOD * D), 0.02),
        "norm_g": gain((L, 6, D)),
        "ffn_w13": nrm((L, 2, D, 2 * FF_DIM), D ** -0.5),
        "ffn_w2": nrm((L, 2, FF_DIM, D), FF_DIM ** -0.5),
        "w_in": nrm((L, D, IN_COLS), D ** -0.5),
        "b_gate": nrm((L, N_BRANCH, D), 0.02),
        "mla_norm_q": gain((L, MLA_Q_RANK)),
        "mla_norm_kv": gain((L, MLA_KV_RANK)),
        "mla_w_uq": nrm((L, MLA_Q_RANK, MLA_HEADS * (MLA_NOPE + MLA_ROPE)), MLA_Q_RANK ** -0.5),
        "mla_w_ukv": nrm((L, MLA_KV_RANK, MLA_HEADS * (MLA_NOPE + MLA_V)), MLA_KV_RANK ** -0.5),
        "rwkv_mu": jax.random.uniform(next(ks), (L, 2, RWKV_COLS), f32, 0.0, 0.5),
        "rwkv_w0": nrm((L, 2, RWKV_DIM), 0.5),
        "rwkv_w_up": nrm((L, 2, DECAY_LORA, RWKV_DIM), DECAY_LORA ** -0.5),
        "rwkv_a0": nrm((L, 2, RWKV_DIM), 0.5),
        "rwkv_a_up": nrm((L, 2, AAA_LORA, RWKV_DIM), AAA_LORA ** -0.5),
        "rwkv_g_up": nrm((L, GATE_LORA, RWKV_DIM), GATE_LORA ** -0.5),
        "rwkv_kvec": gain((L, 2, RWKV_DIM)),
        "rwkv_r_k": nrm((L, RWKV_DIM), 0.1),
        "rwkv_ln_g": gain((L, RWKV_DIM)),
        "rwkv_ln_b": nrm((L, RWKV_DIM), 0.02),
        "hyena_conv": nrm((L, SHORT_CONV, HYENA_COLS), SHORT_CONV ** -0.5),
        "hyena_conv_b": nrm((L, HYENA_COLS), 0.02),
        "hyena_w1": nrm((L, HYENA_EMB, HYENA_FW), HYENA_EMB ** -0.5),
        "hyena_b1": nrm((L, HYENA_FW), 0.02),
        "hyena_w2": nrm((L, HYENA_FW, HYENA_FW), HYENA_FW ** -0.5),
        "hyena_b2": nrm((L, HYENA_FW), 0.02),
        "hyena_w3": nrm((L, HYENA_FW, HYENA_ORDER * 2 * HYENA_DIM), HYENA_FW ** -0.5),
        "hyena_freq": gain((L, 2, HYENA_FW)),
        "hyena_bias": nrm((L, HYENA_ORDER, HYENA_DIM), 1.0),
        "swa_sink": nrm((L, SWA_HEADS), 0.5),
        "w_branch": nrm((L, N_BRANCH, BRANCH_DIM, D), BRANCH_DIM ** -0.5),
        "w_out": nrm((L, D, D), D ** -0.5),
    }


def reference(x, c, ctx, c_ctx, w_mod, b_mod, norm_g, ffn_w13, ffn_w2, w_in, b_gate,
              mla_norm_q, mla_norm_kv, mla_w_uq, mla_w_ukv,
              rwkv_mu, rwkv_w0, rwkv_w_up, rwkv_a0, rwkv_a_up, rwkv_g_up, rwkv_kvec, rwkv_r_k,
              rwkv_ln_g, rwkv_ln_b,
              hyena_conv, hyena_conv_b, hyena_w1, hyena_b1, hyena_w2, hyena_b2, hyena_w3,
              hyena_freq, hyena_bias,
              swa_sink, w_branch, w_out):
    ROWS = x.shape[1] // GRID_W
    tabs_mla = axial_rope_tables(ROWS, MLA_ROPE)
    tabs_swa = axial_rope_tables(ROWS, SWA_HEAD)
    xc = ctx
    for l in range(DEPTH):
        lp = {
            "w_mod": w_mod[l], "b_mod": b_mod[l], "norm_g": norm_g[l],
            "ffn_w13": ffn_w13[l], "ffn_w2": ffn_w2[l], "w_in": w_in[l], "b_gate": b_gate[l],
            "mla_norm_q": mla_norm_q[l], "mla_norm_kv": mla_norm_kv[l],
            "mla_w_uq": mla_w_uq[l], "mla_w_ukv": mla_w_ukv[l],
            "rwkv_mu": rwkv_mu[l], "rwkv_w0": rwkv_w0[l], "rwkv_w_up": rwkv_w_up[l],
            "rwkv_a0": rwkv_a0[l], "rwkv_a_up": rwkv_a_up[l], "rwkv_g_up": rwkv_g_up[l],
            "rwkv_kvec": rwkv_kvec[l], "rwkv_r_k": rwkv_r_k[l],
            "rwkv_ln_g": rwkv_ln_g[l], "rwkv_ln_b": rwkv_ln_b[l],
            "hyena_conv": hyena_conv[l], "hyena_conv_b": hyena_conv_b[l],
            "hyena_w1": hyena_w1[l], "hyena_b1": hyena_b1[l], "hyena_w2": hyena_w2[l],
            "hyena_b2": hyena_b2[l], "hyena_w3": hyena_w3[l], "hyena_freq": hyena_freq[l],
            "hyena_bias": hyena_bias[l],
            "swa_sink": swa_sink[l], "w_branch": w_branch[l], "w_out": w_out[l],
        }
        x, xc = trunk_layer(x, xc, c, c_ctx, lp, tabs_mla, tabs_swa, with_ctx=(l < DEPTH - 1))
    return x
```

```python
import numpy as np
import concourse.bass as bass
import concourse.mybir as mybir

F32 = mybir.dt.float32
BF16 = mybir.dt.bfloat16
AF = mybir.ActivationFunctionType
ALU = mybir.AluOpType
AX = mybir.AxisListType

NDSEM = 36
NHW = 28
SEM_LIMIT = 30000
SB_BASE = 16640
SB_TOP = 229376
SAME_ENG_GAP = 3


class Buf:
    __slots__ = ("w", "r", "name")

    def __init__(self, name=""):
        self.w = None
        self.r = {}
        self.name = name


class Tile:
    def __init__(self, t, buf):
        self.t = t
        self.b = buf

    def __getitem__(self, k):
        return self.t[k]


def _bufs(xs):
    return [x.b if isinstance(x, Tile) else x for x in xs]


class Prog:
    def __init__(self, nc):
        self.nc = nc
        self.eng = {"pe": nc.tensor, "dve": nc.vector, "act": nc.scalar,
                    "pool": nc.gpsimd, "sp": nc.sync}
        self.cnt = {e: 0 for e in self.eng}
        self.known = {e: {} for e in self.eng}
        self.esem = {}
        self.egen = {e: 0 for e in self.eng}
        self.ebase = {e: 0 for e in self.eng}
        self.dsem = []
        self.duse = []
        self.dgen = []
        self.semtab = {}
        self.dnext = 0
        self.dnext_sw = 0
        self.nwait = 0
        self.ninstr = 0
        self.sb_off = SB_BASE
        self.sb_mark = []
        self.uid = 0
        nc = self.nc
        for e in self.eng:
            self.esem[e] = nc.alloc_semaphore("es_%s_0" % e)
            self.semtab[("e", e, 0)] = self.esem[e]
        for i in range(NDSEM):
            self.dsem.append(nc.alloc_semaphore("ds%d_0" % i))
            self.semtab[("d", i, 0)] = self.dsem[i]
            self.duse.append(0)
            self.dgen.append(0)

    def _rot_e(self, e):
        if self.cnt[e] - self.ebase[e] >= SEM_LIMIT:
            self.egen[e] += 1
            self.ebase[e] = self.cnt[e]
            self.esem[e] = self.nc.alloc_semaphore("es_%s_%d" % (e, self.egen[e]))
            self.semtab[("e", e, self.egen[e])] = self.esem[e]

    def _rot_d(self, i):
        if 16 * self.duse[i] >= SEM_LIMIT:
            self.dgen[i] += 1
            self.duse[i] = 0
            self.dsem[i] = self.nc.alloc_semaphore("ds%d_%d" % (i, self.dgen[i]))
            self.semtab[("d", i, self.dgen[i])] = self.dsem[i]

    def sbuf(self, shape, dtype, name=None):
        self.uid += 1
        name = (name or "t") + "_%d" % self.uid
        nbytes = int(np.prod(shape[1:])) * mybir.dt.size(dtype)
        off = (self.sb_off + 31) // 32 * 32
        t = self.nc.alloc_sbuf_tensor_at(name, list(shape), dtype, offset=off)
        self.sb_off = off + nbytes
        assert self.sb_off <= SB_TOP, ("SBUF overflow", name, self.sb_off)
        return Tile(t, Buf(name))

    def mark(self):
        self.sb_mark.append(self.sb_off)

    def release(self):
        self.barrier()
        self.sb_off = self.sb_mark.pop()

    def _wait(self, e, kind, id_, gen, val):
        self.known[e][(kind, id_)] = (gen, val)
        self.eng[e].wait_ge(self.semtab[(kind, id_, gen)], val)
        self.nwait += 1

    def _need(self, e, toks):
        kn = self.known[e]
        req = {}
        for t in toks:
            if t is None:
                continue
            kind, id_, gen, val, absidx = t
            if kind == "e" and id_ == e:
                if e == "pe":
                    continue
                if self.cnt[e] + 1 - absidx >= SAME_ENG_GAP:
                    continue
            k = (kind, id_)
            if kn.get(k, (-1, 0)) >= (gen, val):
                continue
            if req.get(k, (-1, 0)) < (gen, val):
                req[k] = (gen, val)
        for (kind, id_), (gen, val) in req.items():
            self._wait(e, kind, id_, gen, val)

    @staticmethod
    def _deps(reads, writes):
        toks = []
        for b in reads:
            toks.append(b.w)
        for b in writes:
            toks.append(b.w)
            toks.extend(b.r.values())
        return toks

    @staticmethod
    def _commit(tok, reads, writes):
        k = (tok[0], tok[1])
        for b in reads:
            b.r[k] = tok
        for b in writes:
            b.w = tok
            b.r = {}

    def op(self, e, fn, reads=(), writes=()):
        reads = _bufs(reads)
        writes = _bufs(writes)
        self._rot_e(e)
        self._need(e, self._deps(reads, writes))
        ins = fn(self.eng[e])
        self.cnt[e] += 1
        ins.then_inc(self.esem[e], 1)
        tok = ("e", e, self.egen[e], self.cnt[e] - self.ebase[e], self.cnt[e])
        self._commit(tok, reads, writes)
        self.ninstr += 1
        return ins

    def dma(self, out, in_, reads=(), writes=(), q="sp", **kw):
        reads = _bufs(reads)
        writes = _bufs(writes)
        if q == "pool":
            i = NHW + self.dnext_sw
            self.dnext_sw = (self.dnext_sw + 1) % (NDSEM - NHW)
        else:
            i = self.dnext
            self.dnext = (self.dnext + 1) % NHW
        toks = self._deps(reads, writes)
        if self.duse[i]:
            toks.append(("d", i, self.dgen[i], 16 * self.duse[i], 0))
        self._need(q, toks)
        self._rot_d(i)
        self.duse[i] += 1
        ins = self.eng[q].dma_start(out=out, in_=in_, **kw)
        ins.then_inc(self.dsem[i], 16)
        tok = ("d", i, self.dgen[i], 16 * self.duse[i], 0)
        self._commit(tok, reads, writes)
        self.ninstr += 1
        return ins

    def barrier(self, engines=None):
        toks = []
        for e in self.eng:
            if self.cnt[e] > self.ebase[e]:
                toks.append(("e", e, self.egen[e], self.cnt[e] - self.ebase[e]))
            elif self.egen[e] > 0:
                toks.append(("e", e, self.egen[e] - 1, SEM_LIMIT))
        for i, u in enumerate(self.duse):
            if u:
                toks.append(("d", i, self.dgen[i], 16 * u))
        for e in (engines or self.eng):
            kn = self.known[e]
            for kind, id_, gen, val in toks:
                if kind == "e" and id_ == e and e in ("sp", "pe"):
                    continue
                if kn.get((kind, id_), (-1, 0)) >= (gen, val):
                    continue
                self._wait(e, kind, id_, gen, val)

    def mm(self, out, lhsT, rhs, start=True, stop=True, reads=(), writes=()):
        return self.op("pe", lambda g: g.matmul(out, lhsT, rhs, start=start, stop=stop),
                       reads=reads, writes=writes)


DM = 1024
NLAT = 4096
NCTX = 256
TT = NLAT + NCTX
NTILE = TT // 128
FF = 2816
NFC = FF // 128
EPS = 1e-6
I32 = mybir.dt.int32


class KB:
    def __init__(self, nc, ext_in=(), ext_out=()):
        self.nc = nc
        self.P = Prog(nc)
        self.ext_in = set(ext_in)
        self.ext_out = set(ext_out)
        self.d = {}
        self.dbufs = {}
        self.ps = [Tile(nc.alloc_psum_tensor("ps%d" % i, [128, 512], F32), Buf("ps%d" % i))
                   for i in range(8)]

    def inp(self, name, shape, dtype=F32):
        self.d[name] = self.nc.dram_tensor(name, list(shape), dtype, kind="ExternalInput").ap()
        return self.d[name]

    def outp(self, name, shape, dtype=F32):
        self.d[name] = self.nc.dram_tensor(name, list(shape), dtype, kind="ExternalOutput").ap()
        return self.d[name]

    def scr(self, name, shape, dtype=F32):
        kind = "Internal"
        if name in self.ext_in:
            kind = "ExternalInput"
        elif name in self.ext_out:
            kind = "ExternalOutput"
        self.d[name] = self.nc.dram_tensor(name, list(shape), dtype, kind=kind).ap()
        return self.d[name]

    def db(self, name, idx=0):
        k = (name, idx)
        if k not in self.dbufs:
            self.dbufs[k] = Buf("%s_%s" % (name, idx))
        return self.dbufs[k]

    def psb(self, i):
        return self.ps[i][:].bitcast(BF16)

    def rstd_from_ss(self, ss, n, eps, out):
        P = self.P
        ss_t, ss_ap = ss
        o_t, o_ap = out
        P.op("act", lambda g: g.activation(out=o_ap, in_=ss_ap, func=AF.Sqrt, scale=1.0 / n, bias=eps),
             reads=[ss_t], writes=[o_t])
        P.op("dve", lambda g: g.reciprocal(out=o_ap, in_=o_ap), reads=[o_t], writes=[o_t])

    def declare_common(self):
        L = 2
        self.inp("xall", [TT, DM])
        self.inp("cT", [128, 16])
        self.inp("ident", [128, 128])
        self.inp("w_mod", [L, DM, 9 * DM])
        self.inp("b_mod", [L, 9 * DM])
        self.inp("norm_g", [L, 6, DM])
        self.inp("norm_gT", [L, 128, 48])
        self.inp("ffn_w13", [L, 2, DM, 2 * FF])
        self.inp("ffn_w2", [L, 2, FF, DM])
        self.scr("gtrow", [L, 3, 2, DM])
        self.scr("xs", [TT, DM])

    def alloc_persist(self):
        P = self.P
        self.ident_f = P.sbuf([128, 128], F32, "identf")
        self.ident_b = P.sbuf([128, 128], BF16, "identb")
        P.dma(self.ident_f[:], self.d["ident"], writes=[self.ident_f])
        P.dma(self.ident_b[:], self.d["ident"], writes=[self.ident_b], q="pool")
        self.mcol = P.sbuf([128, 72, 2], F32, "mcol")
        self.AB = P.sbuf([128, 3, 2, 8, 2], F32, "AB")

    def phase_mod(self, l):
        P = self.P
        d = self.d
        P.mark()
        cT = P.sbuf([128, 16], F32, "cT")
        P.dma(cT[:], d["cT"], writes=[cT])
        sc = P.sbuf([128, 16], F32, "sc")
        P.op("act", lambda g: g.activation(out=sc[:], in_=cT[:], func=AF.Silu), reads=[cT], writes=[sc])
        mrow = P.sbuf([2, 9 * DM], F32, "mrow")
        brow = P.sbuf([2, 9 * DM], F32, "brow")
        for s in range(2):
            P.dma(brow[s:s + 1, :], d["b_mod"][l:l + 1, :], writes=[brow])
        wt = [P.sbuf([128, 8, 512], F32, "wmod%d" % i) for i in range(2)]
        wsrc = d["w_mod"][l].rearrange("(k p) n -> p k n", p=128)
        for jb in range(18):
            w = wt[jb % 2]
            P.dma(w[:], wsrc[:, :, jb * 512:(jb + 1) * 512], writes=[w])
            ps = self.ps[jb % 2]
            for k in range(8):
                P.mm(ps[0:2, :], sc[:, 2 * k:2 * k + 2], w[:, k, :], start=(k == 0), stop=(k == 7),
                     reads=[sc, w], writes=[ps])
            P.op("dve", lambda g: g.tensor_tensor(out=mrow[:, jb * 512:(jb + 1) * 512], in0=ps[0:2, :],
                                                  in1=brow[:, jb * 512:(jb + 1) * 512], op=ALU.add),
                 reads=[ps, brow], writes=[mrow])
        psc = self.ps[2]
        for c in range(72):
            P.mm(psc[:, 2 * c:2 * c + 2], mrow[0:2, c * 128:(c + 1) * 128], self.ident_f[0:2, 0:2],
                 reads=[mrow, self.ident_f], writes=[psc])
        mcol = self.mcol
        P.op("dve", lambda g: g.tensor_copy(out=mcol[:].rearrange("p c s -> p (c s)"), in_=psc[:, 0:144]),
             reads=[psc], writes=[mcol])
        gcol = P.sbuf([128, 6, 8], F32, "gcol")
        P.dma(gcol[:].rearrange("p n k -> p (n k)"), d["norm_gT"][l], writes=[gcol])
        mc4 = mcol[:].rearrange("p (j k) s -> p j k s", k=8)
        tmp = P.sbuf([128, 8, 2], F32, "abtmp")
        for sub in range(3):
            jsh, jsc, npre = 3 * sub, 3 * sub + 1, 2 * sub
            P.op("dve", lambda g: g.tensor_scalar(out=tmp[:], in0=mc4[:, jsc, :, :], scalar1=1.0, scalar2=None,
                                                  op0=ALU.add), reads=[mcol], writes=[tmp])
            P.op("dve", lambda g: g.tensor_tensor(out=self.AB[:, sub, 0, :, :], in0=tmp[:],
                                                  in1=gcol[:, npre, :].unsqueeze(2).to_broadcast([128, 8, 2]),
                                                  op=ALU.mult), reads=[tmp, gcol], writes=[self.AB])
            P.op("dve", lambda g: g.tensor_copy(out=self.AB[:, sub, 1, :, :], in_=mc4[:, jsh, :, :]),
                 reads=[mcol], writes=[self.AB])
        grow = [P.sbuf([2, DM], F32, "grow%d" % i) for i in range(3)]
        gto = [P.sbuf([2, DM], F32, "gto%d" % i) for i in range(3)]
        for sub in range(3):
            jg, npost = 3 * sub + 2, 2 * sub + 1
            fac = 1.0 if sub == 1 else 0.5
            for s in range(2):
                P.dma(grow[sub][s:s + 1, :], d["norm_g"][l, npost:npost + 1, :], writes=[grow[sub]])
            P.op("dve", lambda g: g.scalar_tensor_tensor(out=gto[sub][:], in0=mrow[:, jg * DM:(jg + 1) * DM],
                                                         scalar=fac, in1=grow[sub][:], op0=ALU.mult,
                                                         op1=ALU.mult),
                 reads=[mrow, grow[sub]], writes=[gto[sub]])
            P.dma(d["gtrow"][l, sub], gto[sub][:], reads=[gto[sub]], writes=[self.db("gtrow", (l, sub))])
        P.release()

    def norm_T(self, xt_t, x_ap, sub, s, xnT_t, xnT_ap, pst, wk):
        P = self.P
        junk, ss, rs, xs = wk
        P.op("act", lambda g: g.activation(out=junk[:], in_=x_ap, func=AF.Square, accum_out=ss[:]),
             reads=[xt_t], writes=[junk, ss])
        self.rstd_from_ss((ss, ss[:]), DM, EPS, (rs, rs[:]))
        P.op("dve", lambda g: g.tensor_scalar(out=xs[:], in0=x_ap, scalar1=rs[:, 0:1], scalar2=None,
                                              op0=ALU.mult), reads=[xt_t, rs], writes=[xs])
        pb = self.psb(pst)
        for k in range(8):
            P.op("pe", lambda g: g.transpose(pb[:, k * 128:(k + 1) * 128], xs[:, k * 128:(k + 1) * 128],
                                             self.ident_b[:]),
                 reads=[xs, self.ident_b], writes=[self.ps[pst]])
        for k in range(8):
            A = self.AB[:, sub, 0, k, s:s + 1]
            B = self.AB[:, sub, 1, k, s:s + 1]
            if k % 2 == 0:
                P.op("dve", lambda g: g.tensor_scalar(out=xnT_ap[:, k, :], in0=pb[:, k * 128:(k + 1) * 128],
                                                      scalar1=A, scalar2=B, op0=ALU.mult, op1=ALU.add),
                     reads=[self.ps[pst], self.AB], writes=[xnT_t])
            else:
                P.op("act", lambda g: g.activation(out=xnT_ap[:, k, :], in_=pb[:, k * 128:(k + 1) * 128],
                                                   func=AF.Identity, scale=A, bias=B),
                     reads=[self.ps[pst], self.AB], writes=[xnT_t])

    def norm_res_out(self, pso, xt_t, x_ap, gt, wk2, dst_ap, dst_buf):
        P = self.P
        junk, s2, r2, tmp = wk2
        for h in range(2):
            P.op("act", lambda g: g.activation(out=junk[:, 0:512], in_=self.ps[pso[h]][:], func=AF.Square,
                                               accum_out=s2[:, h:h + 1]),
                 reads=[self.ps[pso[h]]], writes=[junk, s2])
        P.op("dve", lambda g: g.tensor_tensor(out=s2[:, 2:3], in0=s2[:, 0:1], in1=s2[:, 1:2], op=ALU.add),
             reads=[s2], writes=[s2])
        self.rstd_from_ss((s2, s2[:, 2:3]), DM, EPS, (r2, r2[:]))
        for h in range(2):
            P.op("dve", lambda g: g.scalar_tensor_tensor(out=tmp[:, h * 512:(h + 1) * 512],
                                                         in0=self.ps[pso[h]][:], scalar=r2[:, 0:1],
                                                         in1=gt[:, h * 512:(h + 1) * 512],
                                                         op0=ALU.mult, op1=ALU.mult),
                 reads=[self.ps[pso[h]], r2, gt], writes=[tmp])
        P.op("pool", lambda g: g.tensor_tensor(out=x_ap, in0=x_ap, in1=tmp[:], op=ALU.add),
             reads=[tmp, xt_t], writes=[xt_t])
        P.dma(dst_ap, x_ap, reads=[xt_t], writes=[dst_buf])

    def phase_ffn(self, l, sub, src, dst, ntiles):
        P = self.P
        d = self.d
        wi = 0 if sub == 0 else 1
        P.mark()
        w13 = P.sbuf([128, 8, 2 * FF], BF16, "w13")
        w13b = [Buf("w13_%d" % k) for k in range(8)]
        for k in range(8):
            P.dma(w13[:, k, :], d["ffn_w13"][l, wi, k * 128:(k + 1) * 128, :], writes=[w13b[k]], q="pool")
        w2 = P.sbuf([128, NFC, DM], BF16, "w2")
        w2b = [Buf("w2_%d" % j) for j in range(NFC)]
        for j in range(NFC):
            P.dma(w2[:, j, :], d["ffn_w2"][l, wi, j * 128:(j + 1) * 128, :], writes=[w2b[j]], q="pool")
        gt = [P.sbuf([128, DM], F32, "gt%d" % s) for s in range(2)]
        for s in range(2):
            P.dma(gt[s][:], d["gtrow"][l, sub, s:s + 1, :].to_broadcast([128, DM]),
                  reads=[self.db("gtrow", (l, sub))], writes=[gt[s]])
        xbuf = [P.sbuf([128, 2, DM], F32, "xbuf%d" % i) for i in range(2)]
        xbb = [[Buf("xb%d_%d" % (i, j)) for j in range(2)] for i in range(2)]
        xnT = [P.sbuf([128, 8, 256], BF16, "xnT%d" % i) for i in range(2)]
        hT = P.sbuf([128, NFC, 256], BF16, "hT")
        hTb = [Buf("hT%d" % j) for j in range(NFC)]
        junk = P.sbuf([128, DM], BF16, "junk")
        ss = [P.sbuf([128, 1], F32, "ss%d" % i) for i in range(2)]
        rs = [P.sbuf([128, 1], F32, "rs%d" % i) for i in range(2)]
        xs = [P.sbuf([128, DM], BF16, "xs%d" % i) for i in range(2)]
        s2 = [P.sbuf([128, 3], F32, "s2%d" % i) for i in range(2)]
        r2 = [P.sbuf([128, 1], F32, "r2%d" % i) for i in range(2)]
        tmp = [P.sbuf([128, DM], F32, "tmp%d" % i) for i in range(2)]
        sa = [P.sbuf([128, 256], F32, "sa%d" % i) for i in range(2)]
        ngroups = ntiles // 2
        src_ap, src_name = src
        dst_ap, dst_name = dst

        def load(gi):
            for i in range(2):
                t = 2 * gi + i
                xt = Tile(xbuf[gi % 2].t, xbb[gi % 2][i])
                P.dma(xbuf[gi % 2][:, i, :], src_ap[t * 128:(t + 1) * 128, :],
                      reads=[self.db(src_name, t)], writes=[xt])

        load(0)
        for gi in range(ngroups):
            if gi + 1 < ngroups:
                load(gi + 1)
            xb = xbuf[gi % 2]
            xn = xnT[gi % 2]
            s = 1 if 2 * gi >= 32 else 0
            for i in range(2):
                xt = Tile(xb.t, xbb[gi % 2][i])
                self.norm_T(xt, xb[:, i, :], sub, s, xn, xn[:, :, i * 128:(i + 1) * 128], 0 if i == 0 else 7,
                            (junk, ss[i], rs[i], xs[i]))
            for j in range(NFC):
                pa = self.ps[1 + (j % 2) * 2]
                pbk = self.ps[2 + (j % 2) * 2]
                for k in range(8):
                    P.mm(pa[:, 0:256], w13[:, k, j * 128:(j + 1) * 128], xn[:, k, :], start=(k == 0),
                         stop=(k == 7), reads=[w13b[k], xn], writes=[pa])
                for k in range(8):
                    P.mm(pbk[:, 0:256], w13[:, k, FF + j * 128:FF + (j + 1) * 128], xn[:, k, :], start=(k == 0),
                         stop=(k == 7), reads=[w13b[k], xn], writes=[pbk])
                sj = sa[j % 2]
                P.op("act", lambda g: g.activation(out=sj[:], in_=pa[:, 0:256], func=AF.Silu),
                     reads=[pa], writes=[sj])
                P.op("dve", lambda g: g.tensor_tensor(out=hT[:, j, :], in0=sj[:], in1=pbk[:, 0:256], op=ALU.mult),
                     reads=[sj, pbk], writes=[hTb[j]])
            for i in range(2):
                t = 2 * gi + i
                xt = Tile(xb.t, xbb[gi % 2][i])
                for h in range(2):
                    po = self.ps[5 + h]
                    for j in range(NFC):
                        P.mm(po[:], hT[:, j, i * 128:(i + 1) * 128], w2[:, j, h * 512:(h + 1) * 512],
                             start=(j == 0), stop=(j == NFC - 1), reads=[hTb[j], w2b[j]], writes=[po])
                self.norm_res_out([5, 6], xt, xb[:, i, :], gt[s], (junk, s2[i], r2[i], tmp[i]),
                                  dst_ap[t * 128:(t + 1) * 128, :], self.db(dst_name, t))
        P.release()


def host_shared(inp):
    f32 = np.float32
    sh = {}
    sh["ident"] = np.eye(128, dtype=f32)
    for k in ("w_mod", "b_mod", "norm_g", "ffn_w13", "ffn_w2"):
        sh[k] = np.ascontiguousarray(inp[k], dtype=f32)
    ng = np.asarray(inp["norm_g"], dtype=f32)
    sh["norm_gT"] = np.ascontiguousarray(ng.reshape(2, 6, 8, 128).transpose(0, 3, 1, 2).reshape(2, 128, 48))
    sh["w_ext"] = build_w_ext(inp["w_in"])
    cm, sm = rope_tables(32)
    sh["rope_m"] = np.ascontiguousarray(np.stack([cm, sm], 0))
    cs, ss_ = rope_tables(64)
    sh["rope_s"] = np.ascontiguousarray(np.stack([np.concatenate([cs, cs], 0), np.concatenate([ss_, ss_], 0)], 0))
    nq = np.asarray(inp["mla_norm_q"], f32)
    nkv = np.asarray(inp["mla_norm_kv"], f32)
    sh["mla_nT"] = np.ascontiguousarray(np.stack([nq[:, 0:128], nq[:, 128:256], nkv], axis=2))
    wuq = np.asarray(inp["mla_w_uq"], f32).reshape(2, 256, 8, 96)
    pm, _ = rope_partner(32)
    sw = np.concatenate([wuq[..., 0:64], wuq[..., 64 + pm]], axis=-1)
    sh["mla_wq2"] = np.ascontiguousarray(np.stack([wuq, sw], axis=3).reshape(2, 256, 8 * 2 * 96))
    wukv = np.asarray(inp["mla_w_ukv"], f32).reshape(2, 128, 8, 128)
    sh["mla_wk"] = np.ascontiguousarray(wukv[..., 0:64].reshape(2, 128, 512))
    sh["mla_wv"] = np.ascontiguousarray(wukv[..., 64:128].reshape(2, 128, 512))
    sh["swa_sink"] = np.ascontiguousarray(inp["swa_sink"], dtype=f32)
    sh["w_branch"] = np.ascontiguousarray(inp["w_branch"], dtype=f32)
    for k in ("hyena_conv", "hyena_conv_b", "hyena_w1", "hyena_w2", "hyena_w3", "hyena_bias"):
        sh[k] = np.ascontiguousarray(inp[k], dtype=f32)
    sh["rwkv_mu"] = np.ascontiguousarray(inp["rwkv_mu"], dtype=f32)
    sh["rwkv_kvec"] = np.ascontiguousarray(inp["rwkv_kvec"], dtype=f32)
    sh["rwkv_lnp"] = np.ascontiguousarray(np.stack([inp["rwkv_ln_g"], inp["rwkv_ln_b"], inp["rwkv_r_k"]], axis=1), dtype=f32)
    wup = np.asarray(inp["rwkv_w_up"], f32)
    w0 = np.asarray(inp["rwkv_w0"], f32)
    sh["rwkv_wupA"] = np.ascontiguousarray(np.concatenate([wup.transpose(0, 2, 1, 3).reshape(2, 64, 1024),
                                                           w0.reshape(2, 1, 1024)], axis=1))
    aup = np.asarray(inp["rwkv_a_up"], f32)
    a0 = np.asarray(inp["rwkv_a0"], f32)
    sh["rwkv_aupA"] = np.ascontiguousarray(np.concatenate([aup.transpose(0, 2, 1, 3).reshape(2, 64, 1024),
                                                           a0.reshape(2, 1, 1024)], axis=1))
    sh["rwkv_g_up"] = np.ascontiguousarray(inp["rwkv_g_up"], dtype=f32)
    sh["rw_tri"] = rwkv_tables()
    hf = np.asarray(inp["hyena_freq"], f32)
    sh["hyT"] = np.ascontiguousarray(np.stack([hf[:, 0], hf[:, 1], np.asarray(inp["hyena_b1"], f32),
                                               np.asarray(inp["hyena_b2"], f32)], axis=2))
    tl = hy_tables(NLAT)
    tc = hy_tables(NCTX)
    sh["hy_D"] = np.ascontiguousarray(np.stack([tl["D2"], tl["D2sw"], tl["E"]], 0))
    sh["hyL_W1"] = np.ascontiguousarray(tl["W1"].reshape(128, -1))
    sh["hyL_W3"] = np.ascontiguousarray(tl["W3"].reshape(128, -1))
    sh["hyC_W1"] = np.ascontiguousarray(tc["W1"].reshape(8, -1))
    sh["hyC_W3"] = np.ascontiguousarray(tc["W3"].reshape(8, -1))
    sh["hyL_fK"], sh["hyL_wK"] = hy_feats(NLAT)
    sh["hyC_fK"], sh["hyC_wK"] = hy_feats(NCTX)
    sh["w_out"] = np.ascontiguousarray(inp["w_out"], dtype=f32)
    bgt = np.asarray(inp["b_gate"], f32).reshape(2, 4, 8, 128).transpose(0, 3, 1, 2).reshape(2, 128, 32)
    sh["b_gateT"] = np.ascontiguousarray(bgt)
    kk = np.arange(128)[:, None]
    qq = np.arange(128)[None, :]
    sh["swa_mask"] = np.ascontiguousarray(np.stack([(qq <= kk), (kk <= qq)], 0).astype(f32))
    return sh


def host_core(inp, b):
    f32 = np.float32
    pc = {}
    pc["xall"] = np.ascontiguousarray(np.concatenate([inp["x"][b], inp["ctx"][b]], axis=0), dtype=f32)
    cv = np.stack([np.asarray(inp["c"][b], f32), np.asarray(inp["c_ctx"], f32)], axis=0)
    pc["cT"] = np.ascontiguousarray(cv.reshape(2, 8, 128).transpose(2, 1, 0).reshape(128, 16))
    return pc


G0, PA0, PB0, PH0, PD0 = 0, 4096, 4512, 6304, 7840
FM_COLS = 1728
TM_COLS = 3456
WX_FM0 = 0
WX_TM0 = FM_COLS
WX_G0 = FM_COLS + TM_COLS
WX_COLS = WX_G0 + 4096
PAD_ROWS = TT + 3


def tm_row(t):
    return 1 + t if t < NLAT else 2 + t


def rope_partner(R):
    H = R // 2
    q = H // 2
    part = np.zeros(R, np.int64)
    sign = np.zeros(R, np.float32)
    for dd in range(R):
        base = (dd // H) * H
        o = dd % H
        if o < q:
            part[dd] = base + o + q
            sign[dd] = -1.0
        else:
            part[dd] = base + o - q
            sign[dd] = 1.0
    return part, sign


def rope_tables(R):
    H = R // 2
    q = H // 2
    t = np.arange(NLAT)
    row = (t // 64).astype(np.float32)
    col = (t % 64).astype(np.float32)
    inv = (10000.0 ** (-np.arange(0, H, 2, dtype=np.float32) / H)).astype(np.float32)
    _, sign = rope_partner(R)
    cos = np.zeros((R, NLAT), np.float32)
    sin = np.zeros((R, NLAT), np.float32)
    for dd in range(R):
        pos = row if dd < H else col
        ang = (pos * inv[(dd % H) % q]).astype(np.float32)
        cos[dd] = np.cos(ang)
        sin[dd] = sign[dd] * np.sin(ang)
    return cos, sin


def build_w_ext(w_in):
    pm, _ = rope_partner(32)
    ps_, _ = rope_partner(64)
    cols = []
    cols += list(range(PA0, PA0 + 384))
    kr0 = PA0 + 384
    cols += [kr0 + i for i in range(32)]
    cols += [kr0 + int(pm[i]) for i in range(32)]
    q0 = PD0
    cols += [q0 + i for i in range(512)]
    cols += [q0 + (i // 64) * 64 + int(ps_[i % 64]) for i in range(512)]
    k0 = PD0 + 512
    cols += [k0 + i for i in range(128)]
    cols += [k0 + (i // 64) * 64 + int(ps_[i % 64]) for i in range(128)]
    assert len(cols) == FM_COLS
    cols += list(range(PB0, PB0 + 1792))
    cols += list(range(PH0, PH0 + 1536))
    cols += list(range(PD0 + 640, PD0 + 768))
    assert len(cols) == FM_COLS + TM_COLS
    cols += list(range(0, 4096))
    return np.ascontiguousarray(np.asarray(w_in, np.float32)[:, :, np.asarray(cols)])


def _pin_methods():
    def declare_pin(self):
        L = 2
        self.inp("w_ext", [L, DM, WX_COLS])
        self.inp("rope_m", [2, 32, NLAT])
        self.inp("rope_s", [2, 128, NLAT])
        self.scr("uT", [8, 128, TT], BF16)
        self.scr("cqkvT", [3, 128, TT], BF16)
        self.scr("krT", [32, TT], BF16)
        self.scr("sqT", [4, 128, TT], BF16)
        self.scr("skT", [128, TT], BF16)
        self.scr("pb", [PAD_ROWS, 1792])
        self.scr("ph", [PAD_ROWS, 1536])
        self.scr("pv", [TT, 128])

    def phase_pin(self, l, src):
        P = self.P
        d = self.d
        src_ap, src_name = src
        P.mark()
        NW = FM_COLS + TM_COLS
        w = P.sbuf([128, 8, NW], BF16, "wpin")
        wb = [Buf("wpin%d" % k) for k in range(8)]
        for k in range(8):
            P.dma(w[:, k, :], d["w_ext"][l, k * 128:(k + 1) * 128, 0:NW], writes=[wb[k]], q="pool")
        z = P.sbuf([1, 1792], F32, "zrow")
        P.op("pool", lambda g: g.memset(z[:], 0.0), writes=[z])
        for r in (0, NLAT + 1, TT + 2):
            P.dma(d["pb"][r:r + 1, :], z[:], reads=[z], writes=[self.db("pb", "pad%d" % r)])
            P.dma(d["ph"][r:r + 1, :], z[:, 0:1536], reads=[z], writes=[self.db("ph", "pad%d" % r)])
        xbuf = [P.sbuf([128, 4, DM], F32, "xbuf%d" % i) for i in range(2)]
        xbb = [[Buf("xb%d_%d" % (i, j)) for j in range(4)] for i in range(2)]
        uT = [P.sbuf([128, 8, 512], BF16, "uT%d" % i) for i in range(2)]
        junk = P.sbuf([128, DM], BF16, "junk")
        ss = [P.sbuf([128, 1], F32, "ss%d" % i) for i in range(2)]
        rs = [P.sbuf([128, 1], F32, "rs%d" % i) for i in range(2)]
        xs = [P.sbuf([128, DM], BF16, "xs%d" % i) for i in range(2)]
        tabm = [P.sbuf([32, 2, 512], F32, "tabm%d" % i) for i in range(2)]
        tabs = [P.sbuf([128, 2, 512], F32, "tabs%d" % i) for i in range(2)]
        t1 = [P.sbuf([128, 512], F32, "t1_%d" % i) for i in range(2)]
        t2 = [P.sbuf([128, 512], F32, "t2_%d" % i) for i in range(2)]
        fo = [P.sbuf([128, 512], BF16, "fo%d" % i) for i in range(3)]
        tmo = [P.sbuf([128, TM_COLS], F32, "tmo%d" % i) for i in range(2)]
        groups = [list(range(4 * g, 4 * g + 4)) for g in range(8)] + [[32, 33]]

        def load(gi):
            for i, t in enumerate(groups[gi]):
                xt = Tile(xbuf[gi % 2].t, xbb[gi % 2][i])
                P.dma(xbuf[gi % 2][:, i, :], src_ap[t * 128:(t + 1) * 128, :],
                      reads=[self.db(src_name, t)], writes=[xt])
            if gi < 8:
                P.dma(tabm[gi % 2][:], d["rope_m"][:, :, gi * 512:(gi + 1) * 512].rearrange("c p t -> p c t"),
                      writes=[tabm[gi % 2]])
                P.dma(tabs[gi % 2][:], d["rope_s"][:, :, gi * 512:(gi + 1) * 512].rearrange("c p t -> p c t"),
                      writes=[tabs[gi % 2]])

        load(0)
        nfo = 0
        nev = 0
        for gi, tl in enumerate(groups):
            if gi + 1 < len(groups):
                load(gi + 1)
            n = 128 * len(tl)
            t0 = tl[0] * 128
            lat = gi < 8
            s = 0 if lat else 1
            xb = xbuf[gi % 2]
            u = uT[gi % 2]
            for i, t in enumerate(tl):
                xt = Tile(xb.t, xbb[gi % 2][i])
                self.norm_T(xt, xb[:, i, :], 1, s, u, u[:, :, i * 128:(i + 1) * 128], 0 if i % 2 == 0 else 7,
                            (junk, ss[i % 2], rs[i % 2], xs[i % 2]))
            P.dma(d["uT"][:, :, t0:t0 + n].rearrange("k p t -> p k t"), u[:, :, 0:n], reads=[u],
                  writes=[self.db("uT", gi)])

            def fm_mm(ps, c0, m):
                for k in range(8):
                    P.mm(ps[0:m, 0:n], w[:, k, c0:c0 + m], u[:, k, 0:n], start=(k == 0), stop=(k == 7),
                         reads=[wb[k], u], writes=[ps])

            for c in range(3):
                ps = self.ps[1 + (c % 2) * 2]
                fm_mm(ps, c * 128, 128)
                o = fo[nfo % 3]
                nfo += 1
                P.op("act", lambda g: g.activation(out=o[:, 0:n], in_=ps[:, 0:n], func=AF.Copy),
                     reads=[ps], writes=[o])
                P.dma(d["cqkvT"][c, :, t0:t0 + n], o[:, 0:n], reads=[o], writes=[self.db("cqkvT", (c, gi))])
            roped = [(384, 416, 32, tabm, d["krT"][:, t0:t0 + n], ("krT", gi))]
            for c in range(4):
                roped.append((448 + c * 128, 960 + c * 128, 128, tabs, d["sqT"][c, :, t0:t0 + n], ("sqT", (c, gi))))
            roped.append((1472, 1600, 128, tabs, d["skT"][:, t0:t0 + n], ("skT", gi)))
            for ri, (cx, csw, m, tab, dst, dk) in enumerate(roped):
                psx = self.ps[1 + (ri % 2) * 2]
                fm_mm(psx, cx, m)
                o = fo[nfo % 3]
                nfo += 1
                if lat:
                    pss = self.ps[2 + (ri % 2) * 2]
                    fm_mm(pss, csw, m)
                    tb = tab[gi % 2]
                    a1 = t1[ri % 2]
                    a2 = t2[ri % 2]
                    P.op("dve", lambda g: g.tensor_tensor(out=a1[0:m, :], in0=psx[0:m, :], in1=tb[0:m, 0, :],
                                                          op=ALU.mult), reads=[psx, tb], writes=[a1])
                    P.op("dve", lambda g: g.tensor_tensor(out=a2[0:m, :], in0=pss[0:m, :], in1=tb[0:m, 1, :],
                                                          op=ALU.mult), reads=[pss, tb], writes=[a2])
                    P.op("pool", lambda g: g.tensor_tensor(out=o[0:m, :], in0=a1[0:m, :], in1=a2[0:m, :],
                                                           op=ALU.add), reads=[a1, a2], writes=[o])
                else:
                    P.op("act", lambda g: g.activation(out=o[0:m, 0:n], in_=psx[0:m, 0:n], func=AF.Copy),
                         reads=[psx], writes=[o])
                P.dma(dst, o[0:m, 0:n], reads=[o], writes=[self.db(*dk)])
            for i, t in enumerate(tl):
                st = tmo[i % 2]
                for cb in range(7):
                    c0 = cb * 512
                    cw = min(512, TM_COLS - c0)
                    ps = self.ps[5 + (cb % 2)]
                    for k in range(8):
                        P.mm(ps[:, 0:cw], u[:, k, i * 128:(i + 1) * 128], w[:, k, FM_COLS + c0:FM_COLS + c0 + cw],
                             start=(k == 0), stop=(k == 7), reads=[u, wb[k]], writes=[ps])
                    if nev % 2 == 0:
                        P.op("act", lambda g: g.activation(out=st[:, c0:c0 + cw], in_=ps[:, 0:cw], func=AF.Copy),
                             reads=[ps], writes=[st])
                    else:
                        P.op("dve", lambda g: g.tensor_copy(out=st[:, c0:c0 + cw], in_=ps[:, 0:cw]),
                             reads=[ps], writes=[st])
                    nev += 1
                r0 = tm_row(t * 128)
                P.dma(d["pb"][r0:r0 + 128, :], st[:, 0:1792], reads=[st], writes=[self.db("pb", t)])
                P.dma(d["ph"][r0:r0 + 128, :], st[:, 1792:3328], reads=[st], writes=[self.db("ph", t)])
                P.dma(d["pv"][t * 128:(t + 1) * 128, :], st[:, 3328:3456], reads=[st], writes=[self.db("pv", t)])
        P.release()

    KB.declare_pin = declare_pin
    KB.phase_pin = phase_pin


_pin_methods()


def _attn_methods():
    def declare_attn(self):
        L = 2
        self.inp("mla_nT", [L, 128, 3])
        self.inp("mla_wq2", [L, 256, 8 * 2 * 96])
        self.inp("mla_wk", [L, 128, 512])
        self.inp("mla_wv", [L, 128, 512])
        self.inp("swa_sink", [L, 8])
        self.inp("swa_mask", [2, 128, 128])
        self.scr("yT", [4, 512, TT], BF16)

    def phase_mla(self, l, with_ctx):
        P = self.P
        d = self.d
        P.mark()
        scale = 96.0 ** -0.5
        wq = P.sbuf([128, 2, 8, 2, 96], BF16, "wq")
        for c in range(2):
            P.dma(wq[:, c].rearrange("p h s m -> p (h s m)"), d["mla_wq2"][l, c * 128:(c + 1) * 128, :],
                  writes=[wq], q="pool")
        wk = P.sbuf([128, 8, 64], BF16, "wk")
        P.dma(wk[:].rearrange("p h m -> p (h m)"), d["mla_wk"][l], writes=[wk], q="pool")
        wv = P.sbuf([128, 512], BF16, "wv")
        P.dma(wv[:], d["mla_wv"][l], writes=[wv], q="pool")
        nT = P.sbuf([128, 3], F32, "nT")
        P.dma(nT[:], d["mla_nT"][l], writes=[nT])
        ones_f = P.sbuf([128, 128], F32, "ones_f")
        P.op("pool", lambda g: g.memset(ones_f[:], 1.0), writes=[ones_f])
        cqn = P.sbuf([128, 2, TT], BF16, "cqn")
        ckvn = P.sbuf([128, TT], BF16, "ckvn")
        vaug = P.sbuf([128, NTILE, 8, 128], BF16, "vaug")
        P.op("pool", lambda g: g.memset(vaug[:, :, :, 64:128], 1.0), writes=[vaug])
        groups = [(g * 512, 512) for g in range(8)] + [(NLAT, 256)]
        P.mark()
        xin = [P.sbuf([128, 3, 512], BF16, "xin%d" % i) for i in range(2)]
        sq = [P.sbuf([128, 3, 512], F32, "sq%d" % i) for i in range(2)]
        rsb = [P.sbuf([128, 2, 512], F32, "rsb%d" % i) for i in range(2)]
        for gi, (t0, n) in enumerate(groups):
            xi = xin[gi % 2]
            P.dma(xi[:, :, 0:n], d["cqkvT"][:, :, t0:t0 + n].rearrange("c p t -> p c t"),
                  reads=[self.db("cqkvT", (c, gi)) for c in range(3)], writes=[xi])
            sqi = sq[gi % 2]
            P.op("pool", lambda g: g.tensor_tensor(out=sqi[:, :, 0:n], in0=xi[:, :, 0:n], in1=xi[:, :, 0:n],
                                                   op=ALU.mult), reads=[xi], writes=[sqi])
            psq = self.ps[6]
            psk = self.ps[7]
            for c in range(2):
                P.mm(psq[:, 0:n], ones_f[:], sqi[:, c, 0:n], start=(c == 0), stop=(c == 1),
                     reads=[ones_f, sqi], writes=[psq])
            P.mm(psk[:, 0:n], ones_f[:], sqi[:, 2, 0:n], reads=[ones_f, sqi], writes=[psk])
            r = rsb[gi % 2]
            P.op("act", lambda g: g.activation(out=r[:, 0, 0:n], in_=psq[:, 0:n], func=AF.Sqrt, scale=1.0 / 256,
                                               bias=EPS), reads=[psq], writes=[r])
            P.op("act", lambda g: g.activation(out=r[:, 1, 0:n], in_=psk[:, 0:n], func=AF.Sqrt, scale=1.0 / 128,
                                               bias=EPS), reads=[psk], writes=[r])
            P.op("dve", lambda g: g.reciprocal(out=r[:, :, 0:n], in_=r[:, :, 0:n]), reads=[r], writes=[r])
            for c in range(2):
                P.op("dve", lambda g: g.scalar_tensor_tensor(out=cqn[:, c, t0:t0 + n], in0=xi[:, c, 0:n],
                                                             scalar=nT[:, c:c + 1], in1=r[:, 0, 0:n],
                                                             op0=ALU.mult, op1=ALU.mult),
                     reads=[xi, nT, r], writes=[cqn])
            P.op("dve", lambda g: g.scalar_tensor_tensor(out=ckvn[:, t0:t0 + n], in0=xi[:, 2, 0:n],
                                                         scalar=nT[:, 2:3], in1=r[:, 1, 0:n],
                                                         op0=ALU.mult, op1=ALU.mult),
                 reads=[xi, nT, r], writes=[ckvn])
        P.release()
        for t in range(NTILE):
            ps = self.ps[5 + t % 2]
            P.mm(ps[:], ckvn[:, t * 128:(t + 1) * 128], wv[:], reads=[ckvn, wv], writes=[ps])
            eng = "act" if t % 2 == 0 else "dve"
            if eng == "act":
                P.op("act", lambda g: g.activation(out=vaug[:, t, :, 0:64],
                                                   in_=ps[:].rearrange("p (h m) -> p h m", m=64), func=AF.Copy),
                     reads=[ps], writes=[vaug])
            else:
                P.op("dve", lambda g: g.tensor_copy(out=vaug[:, t, :, 0:64],
                                                    in_=ps[:].rearrange("p (h m) -> p h m", m=64)),
                     reads=[ps], writes=[vaug])
        NQ = TT if with_ctx else NLAT
        KT = [P.sbuf([96, TT], BF16, "KT%d" % i) for i in range(2)]
        QT = [P.sbuf([96, TT], BF16, "QT%d" % i) for i in range(2)]
        tab = [P.sbuf([96, 2, 512], F32, "tab%d" % i) for i in range(2)]
        a1 = [P.sbuf([96, 512], F32, "a1_%d" % i) for i in range(2)]
        a2 = [P.sbuf([96, 512], F32, "a2_%d" % i) for i in range(2)]
        PT = [P.sbuf([128, 512], BF16, "PT%d" % i) for i in range(4)]
        rec = [P.sbuf([64, 512], F32, "rec%d" % i) for i in range(2)]
        yo = [P.sbuf([64, 512], BF16, "yo%d" % i) for i in range(2)]
        npt = 0
        nqg = 0
        for h in range(8):
            kt = KT[h % 2]
            qt = QT[h % 2]
            P.dma(kt[64:96, :], d["krT"], reads=[self.db("krT", gi) for gi in range(9)], writes=[kt])
            for gi, (t0, n) in enumerate(groups):
                lat = gi < 8
                if gi >= 8 and not with_ctx:
                    pass
                pk = self.ps[5]
                P.mm(pk[0:64, 0:n], wk[:, h, :], ckvn[:, t0:t0 + n], reads=[wk, ckvn], writes=[pk])
                P.op("act", lambda g: g.activation(out=kt[0:64, t0:t0 + n], in_=pk[0:64, 0:n], func=AF.Copy),
                     reads=[pk], writes=[kt])
                if gi >= 8 and not with_ctx:
                    continue
                p1 = self.ps[6]
                for c in range(2):
                    P.mm(p1[0:96, 0:n], wq[:, c, h, 0, :], cqn[:, c, t0:t0 + n], start=(c == 0), stop=(c == 1),
                         reads=[wq, cqn], writes=[p1])
                if lat:
                    p2 = self.ps[7]
                    for c in range(2):
                        P.mm(p2[0:96, 0:n], wq[:, c, h, 1, :], cqn[:, c, t0:t0 + n], start=(c == 0),
                             stop=(c == 1), reads=[wq, cqn], writes=[p2])
                    tb = tab[gi % 2]
                    P.dma(tb[64:96, :, :], d["rope_m"][:, :, t0:t0 + n].rearrange("c p t -> p c t"), writes=[tb])
                    P.op("act", lambda g: g.activation(out=qt[0:64, t0:t0 + n], in_=p1[0:64, 0:n], func=AF.Copy),
                         reads=[p1], writes=[qt])
                    b1 = a1[gi % 2]
                    b2 = a2[gi % 2]
                    P.op("dve", lambda g: g.tensor_tensor(out=b1[64:96, :], in0=p1[64:96, :], in1=tb[64:96, 0, :],
                                                          op=ALU.mult), reads=[p1, tb], writes=[b1])
                    P.op("dve", lambda g: g.tensor_tensor(out=b2[64:96, :], in0=p2[64:96, :], in1=tb[64:96, 1, :],
                                                          op=ALU.mult), reads=[p2, tb], writes=[b2])
                    P.op("pool", lambda g: g.tensor_tensor(out=qt[64:96, t0:t0 + n], in0=b1[64:96, :],
                                                           in1=b2[64:96, :], op=ALU.add),
                         reads=[b1, b2], writes=[qt])
                else:
                    P.op("act", lambda g: g.activation(out=qt[0:96, t0:t0 + n], in_=p1[0:96, 0:n], func=AF.Copy),
                         reads=[p1], writes=[qt])
            qgroups = [(g * 512, 512, list(range(NTILE))) for g in range(8)]
            if with_ctx:
                qgroups.append((NLAT, 256, [32, 33]))
            for (q0, n, kbs) in qgroups:
                po = self.ps[3 + nqg % 2]
                nqg += 1
                pend = []

                def pv(item, first, last):
                    kb, pt = item
                    P.mm(po[:, 0:n], vaug[:, kb, h, :], pt[:, 0:n], start=first, stop=last,
                         reads=[vaug, pt], writes=[po])

                for idx, kb in enumerate(kbs):
                    pss = self.ps[npt % 3]
                    pt = PT[npt % 4]
                    npt += 1
                    P.mm(pss[:, 0:n], kt[:, kb * 128:(kb + 1) * 128], qt[:, q0:q0 + n], reads=[kt, qt],
                         writes=[pss])
                    P.op("act", lambda g: g.activation(out=pt[:, 0:n], in_=pss[:, 0:n], func=AF.Exp, scale=scale),
                         reads=[pss], writes=[pt])
                    pend.append((kb, pt))
                    if len(pend) > 2:
                        pv(pend.pop(0), idx == 2, False)
                while pend:
                    first = (len(kbs) - len(pend) == 0)
                    pv(pend.pop(0), first, len(pend) == 0)
                rc = rec[nqg % 2]
                y = yo[nqg % 2]
                P.op("dve", lambda g: g.reciprocal(out=rc[:, 0:n], in_=po[64:128, 0:n]), reads=[po], writes=[rc])
                P.op("dve", lambda g: g.tensor_tensor(out=y[:, 0:n], in0=po[0:64, 0:n], in1=rc[:, 0:n],
                                                      op=ALU.mult), reads=[po, rc], writes=[y])
                P.dma(d["yT"][0, h * 64:(h + 1) * 64, q0:q0 + n], y[:, 0:n], reads=[y],
                      writes=[self.db("yT", (0, h, q0))])
        P.release()

    def phase_swa(self, l, with_ctx):
        P = self.P
        d = self.d
        P.mark()
        scale = 64.0 ** -0.5
        es = P.sbuf([128, 8], F32, "es")
        P.dma(es[:], d["swa_sink"][l:l + 1, :].to_broadcast([128, 8]), writes=[es])
        P.op("act", lambda g: g.activation(out=es[:], in_=es[:], func=AF.Exp), reads=[es], writes=[es])
        msk = P.sbuf([128, 2, 128], BF16, "msk")
        P.dma(msk[:], d["swa_mask"].rearrange("c p t -> p c t"), writes=[msk], q="pool")
        Kk = P.sbuf([64, TT], BF16, "Kk")
        Qk = P.sbuf([64, 4, TT], BF16, "Qk")
        va = P.sbuf([128, NTILE, 128], BF16, "va")
        P.op("pool", lambda g: g.memset(va[:, :, 64:128], 1.0), writes=[va])
        yd = P.sbuf([64, 4, TT], BF16, "yd")
        PT = [P.sbuf([128, 4, 128], BF16, "PT%d" % i) for i in range(4)]
        den = [P.sbuf([64, 4, 128], F32, "den%d" % i) for i in range(2)]
        npt = 0
        nblk = 0
        allsq = [self.db("sqT", (c, gi)) for c in range(4) for gi in range(9)]
        allsk = [self.db("skT", gi) for gi in range(9)]
        allpv = [self.db("pv", t) for t in range(NTILE)]
        for kh in range(2):
            P.dma(Kk[:], d["skT"][kh * 64:(kh + 1) * 64, :], reads=allsk, writes=[Kk])
            for g_ in range(4):
                hh = kh * 4 + g_
                P.dma(Qk[:, g_, :], d["sqT"][hh // 2, (hh % 2) * 64:(hh % 2) * 64 + 64, :], reads=allsq, writes=[Qk])
            P.dma(va[:, :, 0:64], d["pv"].rearrange("(t p) c -> p t c", p=128)[:, :, kh * 64:(kh + 1) * 64],
                  reads=allpv, writes=[va], q="pool")
            nq = NTILE if with_ctx else 32
            for i in range(nq):
                if i < 32:
                    kbs = []
                    if i > 0:
                        kbs.append((i - 1, 0))
                    kbs.append((i, None))
                    if i < 31:
                        kbs.append((i + 1, 1))
                    kbs += [(32, None), (33, None)]
                else:
                    kbs = [(32, None), (33, None)]
                po = self.ps[3 + nblk % 2]
                dn = den[nblk % 2]
                nblk += 1
                for idx, (kb, mk) in enumerate(kbs):
                    pss = self.ps[npt % 3]
                    pt = PT[npt % 4]
                    npt += 1
                    for g_ in range(4):
                        P.mm(pss[:, g_ * 128:(g_ + 1) * 128], Kk[:, kb * 128:(kb + 1) * 128],
                             Qk[:, g_, i * 128:(i + 1) * 128], reads=[Kk, Qk], writes=[pss])
                    P.op("act", lambda g: g.activation(out=pt[:].rearrange("p g t -> p (g t)"), in_=pss[:],
                                                       func=AF.Exp, scale=scale), reads=[pss], writes=[pt])
                    if mk is not None:
                        P.op("dve", lambda g: g.tensor_tensor(out=pt[:], in0=pt[:],
                                                              in1=msk[:, mk, :].unsqueeze(1).to_broadcast([128, 4, 128]),
                                                              op=ALU.mult), reads=[pt, msk], writes=[pt])
                    P.mm(po[:], va[:, kb, :], pt[:].rearrange("p g t -> p (g t)"), start=(idx == 0),
                         stop=(idx == len(kbs) - 1), reads=[va, pt], writes=[po])
                P.op("dve", lambda g: g.tensor_tensor(out=dn[:], in0=po[64:128, :].rearrange("p (g t) -> p g t", g=4),
                                                      in1=es[64:128, kh * 4:(kh + 1) * 4].unsqueeze(2).to_broadcast([64, 4, 128]),
                                                      op=ALU.add), reads=[po, es], writes=[dn])
                P.op("dve", lambda g: g.reciprocal(out=dn[:], in_=dn[:]), reads=[dn], writes=[dn])
                P.op("dve", lambda g: g.tensor_tensor(out=yd[:, :, i * 128:(i + 1) * 128],
                                                      in0=po[0:64, :].rearrange("p (g t) -> p g t", g=4), in1=dn[:],
                                                      op=ALU.mult), reads=[po, dn], writes=[yd])
            for g_ in range(4):
                hh = kh * 4 + g_
                P.dma(d["yT"][3, hh * 64:(hh + 1) * 64, 0:nq * 128], yd[:, g_, 0:nq * 128], reads=[yd],
                      writes=[self.db("yT", (3, hh))])
        P.release()

    KB.declare_attn = declare_attn
    KB.phase_mla = phase_mla
    KB.phase_swa = phase_swa


_attn_methods()


def _merge_methods():
    def declare_merge(self):
        L = 2
        self.inp("w_branch", [L, 4, 512, DM])
        self.inp("w_out", [L, DM, DM])
        self.inp("b_gateT", [L, 128, 32])

    def phase_merge(self, l, with_ctx, xname="xs"):
        P = self.P
        d = self.d
        P.mark()
        wg = P.sbuf([128, 8, 4096], BF16, "wg")
        wgb = [Buf("wg%d" % k) for k in range(8)]
        for k in range(8):
            P.dma(wg[:, k, :], d["w_ext"][l, k * 128:(k + 1) * 128, WX_G0:WX_G0 + 4096], writes=[wgb[k]], q="pool")
        wbr = P.sbuf([128, 4, 4, DM], BF16, "wbr")
        for br in range(4):
            P.dma(wbr[:, br], d["w_branch"][l, br].rearrange("(kc p) n -> p kc n", p=128), writes=[wbr], q="pool")
        wo = P.sbuf([128, 8, DM], BF16, "wo")
        P.dma(wo[:], d["w_out"][l].rearrange("(k p) n -> p k n", p=128), writes=[wo], q="pool")
        bg = P.sbuf([128, 4, 8], F32, "bg")
        P.dma(bg[:].rearrange("p b o -> p (b o)"), d["b_gateT"][l], writes=[bg])
        gt = [P.sbuf([128, DM], F32, "gt%d" % s) for s in range(2)]
        for s in range(2):
            P.dma(gt[s][:], d["gtrow"][l, 1, s:s + 1, :].to_broadcast([128, DM]),
                  reads=[self.db("gtrow", (l, 1))], writes=[gt[s]])
        groups = [(g * 512, 512) for g in range(8)] + ([(NLAT, 256)] if with_ctx else [])
        uT = [P.sbuf([128, 8, 512], BF16, "uT%d" % i) for i in range(2)]
        yg = [P.sbuf([128, 4, 4, 512], BF16, "yg%d" % i) for i in range(2)]
        mg = P.sbuf([128, 8, 512], BF16, "mg")
        mgb = [Buf("mg%d" % k) for k in range(8)]
        sg = [P.sbuf([128, 512], F32, "sg%d" % i) for i in range(2)]
        tm_ = [P.sbuf([128, 512], F32, "tm%d" % i) for i in range(2)]
        acc = [P.sbuf([128, 512], F32, "acc%d" % i) for i in range(2)]
        xbuf = [P.sbuf([128, DM], F32, "xb%d" % i) for i in range(2)]
        junk = P.sbuf([128, DM], BF16, "junk")
        s2 = [P.sbuf([128, 3], F32, "s2%d" % i) for i in range(2)]
        r2 = [P.sbuf([128, 1], F32, "r2%d" % i) for i in range(2)]
        tmp1 = P.sbuf([128, DM], F32, "tmp")
        tmp = [tmp1, tmp1]
        ally = [b for k, b in self.dbufs.items() if k[0] == "yT"]
        allu = [b for k, b in self.dbufs.items() if k[0] == "uT"]

        def load(gi):
            t0, n = groups[gi]
            P.dma(uT[gi % 2][:, :, 0:n], d["uT"][:, :, t0:t0 + n].rearrange("k p t -> p k t"), reads=allu,
                  writes=[uT[gi % 2]])
            for br in range(4):
                P.dma(yg[gi % 2][:, br, :, 0:n],
                      d["yT"][br].rearrange("(kc p) t -> p kc t", p=128)[:, :, t0:t0 + n], reads=ally,
                      writes=[yg[gi % 2]])

        load(0)
        nx = 0
        for gi, (t0, n) in enumerate(groups):
            if gi + 1 < len(groups):
                load(gi + 1)
            u = uT[gi % 2]
            y = yg[gi % 2]
            s = 0 if gi < 8 else 1
            for oc in range(8):
                ac = acc[oc % 2]
                for br in range(4):
                    psg = self.ps[1 + (br % 2) * 2]
                    psy = self.ps[2 + (br % 2) * 2]
                    c0 = br * 1024 + oc * 128
                    for k in range(8):
                        P.mm(psg[:, 0:n], wg[:, k, c0:c0 + 128], u[:, k, 0:n], start=(k == 0), stop=(k == 7),
                             reads=[wgb[k], u], writes=[psg])
                    for kc in range(4):
                        P.mm(psy[:, 0:n], wbr[:, br, kc, oc * 128:(oc + 1) * 128], y[:, br, kc, 0:n],
                             start=(kc == 0), stop=(kc == 3), reads=[wbr, y], writes=[psy])
                    sgt = sg[br % 2]
                    P.op("act", lambda g: g.activation(out=sgt[:, 0:n], in_=psg[:, 0:n], func=AF.Sigmoid,
                                                       bias=bg[:, br, oc:oc + 1]), reads=[psg, bg], writes=[sgt])
                    if br == 0:
                        P.op("dve", lambda g: g.tensor_tensor(out=ac[:, 0:n], in0=sgt[:, 0:n], in1=psy[:, 0:n],
                                                              op=ALU.mult), reads=[sgt, psy], writes=[ac])
                    else:
                        tt = tm_[br % 2]
                        P.op("dve", lambda g: g.tensor_tensor(out=tt[:, 0:n], in0=sgt[:, 0:n], in1=psy[:, 0:n],
                                                              op=ALU.mult), reads=[sgt, psy], writes=[tt])
                        if br < 3:
                            P.op("pool", lambda g: g.tensor_tensor(out=ac[:, 0:n], in0=ac[:, 0:n], in1=tt[:, 0:n],
                                                                   op=ALU.add), reads=[ac, tt], writes=[ac])
                        else:
                            P.op("pool", lambda g: g.tensor_tensor(out=mg[:, oc, 0:n], in0=ac[:, 0:n],
                                                                   in1=tt[:, 0:n], op=ALU.add),
                                 reads=[ac, tt], writes=[mgb[oc]])
            for i in range(n // 128):
                t = t0 // 128 + i
                xb = xbuf[nx % 2]
                P.dma(xb[:], d[xname][t * 128:(t + 1) * 128, :], reads=[self.db(xname, t)], writes=[xb])
                for h in range(2):
                    po = self.ps[5 + h]
                    for k in range(8):
                        P.mm(po[:], mg[:, k, i * 128:(i + 1) * 128], wo[:, k, h * 512:(h + 1) * 512],
                             start=(k == 0), stop=(k == 7), reads=[mgb[k], wo], writes=[po])
                self.norm_res_out([5, 6], xb, xb[:], gt[s], (junk, s2[nx % 2], r2[nx % 2], tmp[nx % 2]),
                                  d[xname][t * 128:(t + 1) * 128, :], self.db(xname, t))
                nx += 1
        P.release()

    KB.declare_merge = declare_merge
    KB.phase_merge = phase_merge


_merge_methods()


def hy_tables(n):
    M = 2 * n
    S1 = M // 64
    T1 = n // 64
    s1 = np.arange(S1)[:, None, None]
    s2 = np.arange(64)[None, :, None]
    f1 = np.arange(S1)[None, None, :]
    ang = 2.0 * np.pi * ((f1 * (64 * s1 + s2)) % M) / M
    W1 = np.stack([np.cos(ang), -np.sin(ang)], axis=2).astype(np.float32)
    s2v = np.arange(64)[:, None]
    f2v = np.arange(64)[None, :]
    th = 2.0 * np.pi * ((s2v * f2v) % 64) / 64
    c, s = np.cos(th), np.sin(th)
    D2 = np.block([[c, -s], [s, c]]).astype(np.float32)
    D2sw = np.concatenate([D2[:, 64:], D2[:, :64]], axis=1)
    E = np.block([[c, s], [-s, c]]).astype(np.float32)
    f1v = np.arange(S1)[:, None, None]
    t2v = np.arange(64)[None, :, None]
    t1v = np.arange(T1)[None, None, :]
    psi = 2.0 * np.pi * ((f1v * (64 * t1v + t2v)) % M) / M
    W3 = np.stack([np.cos(psi), -np.sin(psi)], axis=2).astype(np.float32)
    return dict(W1=W1, D2=D2, D2sw=D2sw, E=E, W3=W3, S1=S1, T1=T1, M=M)


def hy_feats(n):
    M = 2 * n
    f32 = np.float32
    t = np.linspace(0.0, 1.0, n, dtype=f32)
    bands = np.linspace(1e-4, 15.0, 16, dtype=f32)
    ang = (f32(2.0 * np.pi / n) * np.arange(n, dtype=f32)[:, None] * bands[None, :]).astype(f32)
    feats = np.concatenate([t[:, None], np.cos(ang), -np.sin(ang)], axis=-1).astype(f32)
    deltas = np.abs(np.linspace(np.log(1e-2) / 1.5, np.log(1e-2) / 0.3, 512, dtype=f32)).astype(f32)
    win = np.exp(-t[:, None] * deltas[None, :]).astype(f32)
    idx = np.zeros(M, np.int64)
    idx[:n] = np.arange(n)
    idx[n + 1:] = n - np.arange(1, n)
    fK = feats[idx].copy()
    wK = win[idx].copy()
    fK[n] = feats[0]
    wK[n] = win[0]
    return np.ascontiguousarray(fK.T), np.ascontiguousarray(wK)


def _hyena_methods():
    TWO_PI = 2.0 * np.pi

    def declare_hyena(self):
        L = 2
        self.inp("hyena_conv", [L, 3, 1536])
        self.inp("hyena_conv_b", [L, 1536])
        self.inp("hyena_w1", [L, 33, 64])
        self.inp("hyena_w2", [L, 64, 64])
        self.inp("hyena_w3", [L, 64, 2048])
        self.inp("hyT", [L, 64, 4])
        self.inp("hyena_bias", [L, 2, 512])
        self.inp("hy_D", [3, 128, 128])
        self.inp("hyL_W1", [128, 64 * 2 * 128])
        self.inp("hyL_W3", [128, 64 * 2 * 64])
        self.inp("hyL_fK", [33, 8192])
        self.inp("hyL_wK", [8192, 512])
        self.inp("hyC_W1", [8, 64 * 2 * 8])
        self.inp("hyC_W3", [8, 64 * 2 * 4])
        self.inp("hyC_fK", [33, 512])
        self.inp("hyC_wK", [512, 512])
        self.scr("hcs", [TT, 1536])
        self.scr("kbuf", [2, 8192, 512], BF16)
        self.scr("Bd", [128, 128, 512], BF16)
        self.scr("Dd", [128, 128, 512], BF16)
        self.scr("KAB_L", [2, 128, 2, 128, 512], BF16)
        self.scr("KAB_C", [2, 8, 2, 128, 512], BF16)
        self.scr("zt1", [TT, 512])
        self.scr("zt2", [TT, 512])

    def hy_shortconv(self, l, ntiles):
        P = self.P
        d = self.d
        P.mark()
        ck = P.sbuf([128, 3, 1536], F32, "ck")
        P.dma(ck[:].rearrange("p a c -> p (a c)"),
              d["hyena_conv"][l:l + 1].rearrange("o a c -> o (a c)").to_broadcast([128, 4608]), writes=[ck])
        cb = P.sbuf([128, 1536], F32, "cb")
        P.dma(cb[:], d["hyena_conv_b"][l:l + 1, :].to_broadcast([128, 1536]), writes=[cb])
        bufs = [[P.sbuf([128, 1536], F32, "sc%d_%d" % (i, j)) for j in range(3)] for i in range(2)]
        allph = [b for k, b in self.dbufs.items() if k[0] == "ph"]
        for t in range(ntiles):
            r0 = tm_row(t * 128)
            pv_, cu, nx = bufs[t % 2]
            P.dma(pv_[:], d["ph"][r0 - 1:r0 + 127, :], reads=allph, writes=[pv_])
            P.dma(cu[:], d["ph"][r0:r0 + 128, :], reads=allph, writes=[cu])
            P.dma(nx[:], d["ph"][r0 + 1:r0 + 129, :], reads=allph, writes=[nx])
            P.op("dve", lambda g: g.tensor_tensor(out=pv_[:], in0=pv_[:], in1=ck[:, 0, :], op=ALU.mult),
                 reads=[pv_, ck], writes=[pv_])
            P.op("pool", lambda g: g.tensor_tensor(out=cu[:], in0=cu[:], in1=ck[:, 1, :], op=ALU.mult),
                 reads=[cu, ck], writes=[cu])
            P.op("dve", lambda g: g.tensor_tensor(out=nx[:], in0=nx[:], in1=ck[:, 2, :], op=ALU.mult),
                 reads=[nx, ck], writes=[nx])
            P.op("pool", lambda g: g.tensor_tensor(out=cu[:], in0=cu[:], in1=pv_[:], op=ALU.add),
                 reads=[cu, pv_], writes=[cu])
            P.op("dve", lambda g: g.tensor_tensor(out=nx[:], in0=nx[:], in1=cb[:], op=ALU.add),
                 reads=[nx, cb], writes=[nx])
            P.op("pool", lambda g: g.tensor_tensor(out=cu[:], in0=cu[:], in1=nx[:], op=ALU.add),
                 reads=[cu, nx], writes=[cu])
            P.dma(d["hcs"][t * 128:(t + 1) * 128, :], cu[:], reads=[cu], writes=[self.db("hcs", t)])
        P.release()

    def hy_load_tabs(self, n):
        P = self.P
        d = self.d
        pre = "hyL" if n == NLAT else "hyC"
        S1 = 2 * n // 64
        T1 = n // 64
        W1 = P.sbuf([S1, 64, 2, S1], BF16, "W1")
        P.dma(W1[:].rearrange("p a b c -> p (a b c)"), d[pre + "_W1"], writes=[W1], q="pool")
        W3 = P.sbuf([S1, 64, 2, T1], BF16, "W3")
        P.dma(W3[:].rearrange("p a b c -> p (a b c)"), d[pre + "_W3"], writes=[W3], q="pool")
        Dm = P.sbuf([128, 3, 128], BF16, "Dm")
        P.dma(Dm[:], d["hy_D"].rearrange("a p c -> p a c"), writes=[Dm], q="pool")
        return dict(W1=W1, W3=W3, Dm=Dm, S1=S1, T1=T1, n=n)

    def hy_stage1(self, tb, src_ap, nz, src_reads, cast):
        P = self.P
        d = self.d
        S1 = tb["S1"]
        P.mark()
        U = P.sbuf([nz, 64, 512], BF16, "U")
        P.dma(U[:], src_ap.rearrange("(a s) c -> a s c", s=64), reads=src_reads, writes=[U],
              q=("pool" if cast else "sp"))
        bo = [P.sbuf([S1, 2, 512], BF16, "bo%d" % i) for i in range(3)]
        bdv = d["Bd"].rearrange("(r s) f c -> s f r c", r=2)
        for s2 in range(64):
            o = bo[s2 % 3]
            for ri in range(2):
                ps = self.ps[(2 * s2 + ri) % 4]
                P.mm(ps[0:S1, :], tb["W1"][0:nz, s2, ri, :], U[:, s2, :], reads=[tb["W1"], U], writes=[ps])
                if ri == 0:
                    P.op("act", lambda g: g.activation(out=o[:, ri, :], in_=ps[0:S1, :], func=AF.Copy),
                         reads=[ps], writes=[o])
                else:
                    P.op("dve", lambda g: g.tensor_copy(out=o[:, ri, :], in_=ps[0:S1, :]), reads=[ps], writes=[o])
            P.dma(bdv[s2, 0:S1], o[:], reads=[o], writes=[self.db("Bd", s2)])
        P.release()

    def hy_stage2(self, tb, cb):
        P = self.P
        d = self.d
        S1 = tb["S1"]
        allbd = [self.db("Bd", s2) for s2 in range(64)]
        FG = 4
        bins = [P.sbuf([128, FG, 512], BF16, "bin%d" % i) for i in range(2)]
        for fg in range(S1 // FG):
            b = bins[fg % 2]
            P.dma(b[:], d["Bd"][:, fg * FG:(fg + 1) * FG, :], reads=allbd, writes=[b])
            for j in range(FG):
                f1 = fg * FG + j
                p1 = self.ps[(f1 % 2) * 2]
                p2 = self.ps[(f1 % 2) * 2 + 1]
                P.mm(p1[:], tb["Dm"][:, 0, :], b[:, j, :], reads=[tb["Dm"], b], writes=[p1])
                P.mm(p2[:], tb["Dm"][:, 1, :], b[:, j, :], reads=[tb["Dm"], b], writes=[p2])
                cb(f1, p1, p2)

    def hy_filters(self, l, n, kab_name):
        P = self.P
        d = self.d
        pre = "hyL" if n == NLAT else "hyC"
        M = 2 * n
        P.mark()
        tb = self.hy_load_tabs(n)
        hyT = P.sbuf([64, 4], F32, "hyT")
        P.dma(hyT[:], d["hyT"][l], writes=[hyT])
        sc = P.sbuf([64, 4], F32, "hsc")
        for j in range(2):
            P.op("dve", lambda g: g.tensor_scalar(out=sc[:, 2 * j:2 * j + 1], in0=hyT[:, j:j + 1],
                                                  scalar1=1.0 / TWO_PI, scalar2=None, op0=ALU.mult),
                 reads=[hyT], writes=[sc])
            P.op("dve", lambda g: g.tensor_tensor(out=sc[:, 2 * j + 1:2 * j + 2], in0=hyT[:, 2 + j:3 + j],
                                                  in1=sc[:, 2 * j:2 * j + 1], op=ALU.mult),
                 reads=[hyT, sc], writes=[sc])
            P.op("dve", lambda g: g.tensor_scalar(out=sc[:, 2 * j + 1:2 * j + 2], in0=sc[:, 2 * j + 1:2 * j + 2],
                                                  scalar1=64.0, scalar2=None, op0=ALU.add),
                 reads=[sc], writes=[sc])
        w1 = P.sbuf([33, 64], F32, "hw1")
        P.dma(w1[:], d["hyena_w1"][l], writes=[w1])
        w2 = P.sbuf([64, 64], F32, "hw2")
        P.dma(w2[:], d["hyena_w2"][l], writes=[w2])
        w3 = P.sbuf([64, 2048], F32, "hw3")
        P.dma(w3[:], d["hyena_w3"][l], writes=[w3])
        ones_f = P.sbuf([128, 128], F32, "ones_f")
        P.op("pool", lambda g: g.memset(ones_f[:], 1.0), writes=[ones_f])
        G2T = P.sbuf([64, M], F32, "G2T")
        rn = [P.sbuf([128, 512], F32, "rn%d" % o) for o in range(2)]
        P.mark()
        fk = [P.sbuf([33, 512], F32, "fk%d" % i) for i in range(2)]
        vt = [P.sbuf([64, 512], F32, "vt%d" % i) for i in range(2)]
        vi = [P.sbuf([64, 512], I32, "vi%d" % i) for i in range(2)]
        vf = [P.sbuf([64, 512], F32, "vf%d" % i) for i in range(2)]
        g1 = [P.sbuf([64, 512], F32, "g1%d" % i) for i in range(2)]
        cnt = [0]

        def sin_reduce(ps, j, out_ap, out_t):
            i = cnt[0] % 2
            cnt[0] += 1
            P.op("dve", lambda g: g.tensor_scalar(out=vt[i][:], in0=ps[0:64, :], scalar1=sc[:, 2 * j:2 * j + 1],
                                                  scalar2=sc[:, 2 * j + 1:2 * j + 2], op0=ALU.mult, op1=ALU.add),
                 reads=[ps, sc], writes=[vt[i]])
            P.op("dve", lambda g: g.tensor_copy(out=vi[i][:], in_=vt[i][:]), reads=[vt[i]], writes=[vi[i]])
            P.op("pool", lambda g: g.tensor_copy(out=vf[i][:], in_=vi[i][:]), reads=[vi[i]], writes=[vf[i]])
            P.op("pool", lambda g: g.tensor_tensor(out=vt[i][:], in0=vt[i][:], in1=vf[i][:], op=ALU.subtract),
                 reads=[vt[i], vf[i]], writes=[vt[i]])
            P.op("act", lambda g: g.activation(out=out_ap, in_=vt[i][:], func=AF.Sin, scale=TWO_PI),
                 reads=[vt[i]], writes=[out_t])

        for cbk in range(M // 512):
            f = fk[cbk % 2]
            P.dma(f[:], d[pre + "_fK"][:, cbk * 512:(cbk + 1) * 512], writes=[f])
            ps = self.ps[cbk % 2]
            P.mm(ps[0:64, :], w1[:], f[:], reads=[w1, f], writes=[ps])
            gg = g1[cbk % 2]
            sin_reduce(ps, 0, gg[:], gg)
            ps2 = self.ps[2 + cbk % 2]
            P.mm(ps2[0:64, :], w2[:], gg[:], reads=[w2, gg], writes=[ps2])
            sin_reduce(ps2, 1, G2T[:, cbk * 512:(cbk + 1) * 512], G2T)
        P.release()
        P.mark()
        wk = [P.sbuf([128, 512], F32, "wk%d" % i) for i in range(2)]
        kbt = [P.sbuf([128, 512], F32, "kbt%d" % i) for i in range(4)]
        ab = [P.sbuf([128, 512], F32, "ab%d" % i) for i in range(2)]
        kbo = [P.sbuf([128, 512], BF16, "kbo%d" % i) for i in range(4)]
        nlt = M // 128
        c = 0
        for lt in range(nlt):
            dirn = 0 if lt < n // 128 else 1
            w = wk[lt % 2]
            P.dma(w[:], d[pre + "_wK"][lt * 128:(lt + 1) * 128, :], writes=[w])
            for o in range(2):
                ps = self.ps[c % 4]
                kt_ = kbt[c % 4]
                a = ab[c % 2]
                ko = kbo[c % 4]
                c += 1
                P.mm(ps[:], G2T[:, lt * 128:(lt + 1) * 128], w3[:, o * 1024 + dirn * 512:o * 1024 + dirn * 512 + 512],
                     reads=[G2T, w3], writes=[ps])
                P.op("dve", lambda g: g.tensor_tensor(out=kt_[:], in0=ps[:], in1=w[:], op=ALU.mult),
                     reads=[ps, w], writes=[kt_])
                P.op("act", lambda g: g.activation(out=a[:], in_=kt_[:], func=AF.Abs), reads=[kt_], writes=[a])
                P.mm(self.ps[6 + o][:], ones_f[:], a[:], start=(lt == 0), stop=(lt == nlt - 1),
                     reads=[ones_f, a], writes=[self.ps[6 + o]])
                if lt == n // 128:
                    P.op("pool", lambda g: g.memset(kt_[0:1, :], 0.0), reads=[kt_], writes=[kt_])
                P.op("pool", lambda g: g.tensor_copy(out=ko[:], in_=kt_[:]), reads=[kt_], writes=[ko])
                P.dma(d["kbuf"][o, lt * 128:(lt + 1) * 128, :], ko[:], reads=[ko], writes=[self.db("kbuf", (o, lt))])
        for o in range(2):
            P.op("dve", lambda g: g.tensor_scalar(out=rn[o][:], in0=self.ps[6 + o][:], scalar1=float(M), scalar2=None,
                                                  op0=ALU.mult), reads=[self.ps[6 + o]], writes=[rn[o]])
            P.op("dve", lambda g: g.reciprocal(out=rn[o][:], in_=rn[o][:]), reads=[rn[o]], writes=[rn[o]])
        P.release()
        for o in range(2):
            allkb = [self.db("kbuf", (o, lt)) for lt in range(nlt)]
            self.hy_stage1(tb, d["kbuf"][o, 0:M, :], tb["S1"], allkb, False)
            P.mark()
            kab = [P.sbuf([128, 2, 512], BF16, "kab%d" % i) for i in range(3)]

            def cbf(f1, p1, p2):
                k = kab[f1 % 3]
                r = rn[o]
                P.op("dve", lambda g: g.tensor_tensor(out=k[0:64, 0, :], in0=p1[0:64, :], in1=r[0:64, :], op=ALU.mult),
                     reads=[p1, r], writes=[k])
                P.op("dve", lambda g: g.tensor_tensor(out=k[64:128, 0, :], in0=p2[64:128, :], in1=r[64:128, :],
                                                      op=ALU.mult), reads=[p2, r], writes=[k])
                P.op("dve", lambda g: g.scalar_tensor_tensor(out=k[0:64, 1, :], in0=p2[0:64, :], scalar=-1.0,
                                                             in1=r[0:64, :], op0=ALU.mult, op1=ALU.mult),
                     reads=[p2, r], writes=[k])
                P.op("dve", lambda g: g.tensor_tensor(out=k[64:128, 1, :], in0=p1[64:128, :], in1=r[64:128, :],
                                                      op=ALU.mult), reads=[p1, r], writes=[k])
                P.dma(d[kab_name][o, f1].rearrange("a p c -> p a c"), k[:], reads=[k],
                      writes=[self.db(kab_name, (o, f1))])

            self.hy_stage2(tb, cbf)
            P.release()
        P.release()

    def hy_conv(self, l, tb, o, kab_name, src_ap, src_reads, gate_ap, gate_reads, dst_ap, dst_name):
        P = self.P
        d = self.d
        n, S1, T1 = tb["n"], tb["S1"], tb["T1"]
        self.hy_stage1(tb, src_ap, S1 // 2, src_reads, True)
        P.mark()
        kab = [P.sbuf([128, 2, 512], BF16, "kab%d" % i) for i in range(2)]
        ta = [P.sbuf([128, 512], F32, "ta%d" % i) for i in range(2)]
        tb2 = [P.sbuf([128, 512], F32, "tb%d" % i) for i in range(2)]
        yh = [P.sbuf([128, 512], BF16, "yh%d" % i) for i in range(2)]
        do = [P.sbuf([128, 512], BF16, "do%d" % i) for i in range(2)]
        allk = [self.db(kab_name, (o, f1)) for f1 in range(S1)]

        def cbf(f1, p1, p2):
            i = f1 % 2
            k = kab[i]
            P.dma(k[:], d[kab_name][o, f1].rearrange("a p c -> p a c"), reads=allk, writes=[k])
            P.op("dve", lambda g: g.tensor_tensor(out=ta[i][:], in0=p1[:], in1=k[:, 0, :], op=ALU.mult),
                 reads=[p1, k], writes=[ta[i]])
            P.op("dve", lambda g: g.tensor_tensor(out=tb2[i][:], in0=p2[:], in1=k[:, 1, :], op=ALU.mult),
                 reads=[p2, k], writes=[tb2[i]])
            P.op("pool", lambda g: g.tensor_tensor(out=yh[i][:], in0=ta[i][:], in1=tb2[i][:], op=ALU.add),
                 reads=[ta[i], tb2[i]], writes=[yh[i]])
            pd_ = self.ps[4 + i]
            P.mm(pd_[:], tb["Dm"][:, 2, :], yh[i][:], reads=[tb["Dm"], yh[i]], writes=[pd_])
            P.op("act", lambda g: g.activation(out=do[i][:], in_=pd_[:], func=AF.Copy), reads=[pd_], writes=[do[i]])
            P.dma(d["Dd"][:, f1, :], do[i][:], reads=[do[i]], writes=[self.db("Dd", f1)])

        self.hy_stage2(tb, cbf)
        P.release()
        P.mark()
        bias = P.sbuf([64, 512], F32, "hbias")
        P.dma(bias[:], d["hyena_bias"][l, o:o + 1, :].to_broadcast([64, 512]), writes=[bias])
        din = [P.sbuf([S1, 2, 512], BF16, "din%d" % i) for i in range(2)]
        gs = [P.sbuf([T1, 512], F32, "gs%d" % i) for i in range(2)]
        us = [P.sbuf([T1, 512], F32, "us%d" % i) for i in range(2)]
        zo = [P.sbuf([T1, 512], F32, "zo%d" % i) for i in range(2)]
        alld = [self.db("Dd", f1) for f1 in range(S1)]
        ddv = d["Dd"].rearrange("(r t) f c -> t f r c", r=2)
        gv = gate_ap.rearrange("(a s) c -> s a c", s=64)
        uv = src_ap.rearrange("(a s) c -> s a c", s=64)
        dv = dst_ap.rearrange("(a s) c -> s a c", s=64)
        for t2 in range(64):
            i = t2 % 2
            P.dma(din[i][:], ddv[t2, 0:S1], reads=alld, writes=[din[i]])
            P.dma(gs[i][:], gv[t2], reads=gate_reads, writes=[gs[i]])
            P.dma(us[i][:], uv[t2], reads=src_reads, writes=[us[i]])
            py = self.ps[6 + i]
            P.mm(py[0:T1, :], tb["W3"][:, t2, 0, :], din[i][:, 0, :], start=True, stop=False,
                 reads=[tb["W3"], din[i]], writes=[py])
            P.mm(py[0:T1, :], tb["W3"][:, t2, 1, :], din[i][:, 1, :], start=False, stop=True,
                 reads=[tb["W3"], din[i]], writes=[py])
            P.op("pool", lambda g: g.tensor_tensor(out=us[i][:], in0=us[i][:], in1=bias[0:T1, :], op=ALU.mult),
                 reads=[us[i], bias], writes=[us[i]])
            P.op("dve", lambda g: g.tensor_tensor(out=us[i][:], in0=us[i][:], in1=py[0:T1, :], op=ALU.add),
                 reads=[us[i], py], writes=[us[i]])
            P.op("pool", lambda g: g.tensor_tensor(out=zo[i][:], in0=us[i][:], in1=gs[i][:], op=ALU.mult),
                 reads=[us[i], gs[i]], writes=[zo[i]])
            P.dma(dv[t2], zo[i][:], reads=[zo[i]], writes=[self.db(dst_name, ("t2", t2, n))])
        P.release()

    def phase_hyena(self, l, with_ctx):
        P = self.P
        d = self.d
        self.hy_shortconv(l, NTILE if with_ctx else 32)
        self.hy_filters(l, NLAT, "KAB_L")
        if with_ctx:
            self.hy_filters(l, NCTX, "KAB_C")
        segs = [(NLAT, 0, "KAB_L")] + ([(NCTX, NLAT, "KAB_C")] if with_ctx else [])
        for (n, r0, kn) in segs:
            P.mark()
            tb = self.hy_load_tabs(n)
            hcs = [b for k, b in self.dbufs.items() if k[0] == "hcs"]
            self.hy_conv(l, tb, 0, kn, d["hcs"][r0:r0 + n, 0:512], hcs, d["hcs"][r0:r0 + n, 512:1024], hcs,
                         d["zt1"][r0:r0 + n, :], "zt1")
            z1 = [b for k, b in self.dbufs.items() if k[0] == "zt1"]
            self.hy_conv(l, tb, 1, kn, d["zt1"][r0:r0 + n, :], z1, d["hcs"][r0:r0 + n, 1024:1536], hcs,
                         d["zt2"][r0:r0 + n, :], "zt2")
            P.release()
        P.mark()
        zin = [P.sbuf([128, 512], F32, "zin%d" % i) for i in range(2)]
        zT = [P.sbuf([128, 4, 128], BF16, "zT%d" % i) for i in range(2)]
        z2 = [b for k, b in self.dbufs.items() if k[0] == "zt2"]
        yv = d["yT"][2].rearrange("(kc p) t -> p kc t", p=128)
        for t in range(NTILE if with_ctx else 32):
            zi = zin[t % 2]
            P.dma(zi[:], d["zt2"][t * 128:(t + 1) * 128, :], reads=z2, writes=[zi])
            ps = self.ps[t % 2]
            for kc in range(4):
                P.op("pe", lambda g: g.transpose(ps[:, kc * 128:(kc + 1) * 128], zi[:, kc * 128:(kc + 1) * 128],
                                                 self.ident_f[:]), reads=[zi, self.ident_f], writes=[ps])
            P.op("act", lambda g: g.activation(out=zT[t % 2][:].rearrange("p a b -> p (a b)"), in_=ps[:], func=AF.Copy),
                 reads=[ps], writes=[zT[t % 2]])
            P.dma(yv[:, :, t * 128:(t + 1) * 128], zT[t % 2][:], reads=[zT[t % 2]], writes=[self.db("yT", (2, t))])
        P.release()

    KB.declare_hyena = declare_hyena
    KB.hy_shortconv = hy_shortconv
    KB.hy_load_tabs = hy_load_tabs
    KB.hy_stage1 = hy_stage1
    KB.hy_stage2 = hy_stage2
    KB.hy_filters = hy_filters
    KB.hy_conv = hy_conv
    KB.phase_hyena = phase_hyena


_hyena_methods()


def rwkv_tables():
    idx = np.arange(128)
    out = np.zeros((2, 6, 128, 128), np.float32)
    for dd in range(2):
        incl = (idx[:, None] <= idx[None, :]) if dd == 0 else (idx[:, None] >= idx[None, :])
        incl = incl.astype(np.float32)
        ref = 63 if dd == 0 else 64
        out[dd, 0] = incl
        out[dd, 1] = incl - incl[:, ref:ref + 1]
        out[dd, 2] = 1.0 - incl
        out[dd, 3] = incl - np.eye(128, dtype=np.float32)
        out[dd, 4] = incl
        out[dd, 5] = out[dd, 3].T
    return out


def _rwkv_methods():
    def declare_rwkv(self):
        L = 2
        self.inp("rwkv_mu", [L, 2, 1792])
        self.inp("rwkv_kvec", [L, 2, 512])
        self.inp("rwkv_lnp", [L, 3, 512])
        self.inp("rwkv_wupA", [L, 65, 1024])
        self.inp("rwkv_aupA", [L, 65, 1024])
        self.inp("rwkv_g_up", [L, 128, 512])
        self.inp("rw_tri", [2, 6, 128, 128])
        self.scr("yf", [TT, 512])

    def phase_rwkv(self, l, with_ctx):
        P = self.P
        d = self.d
        idb = self.ident_b
        idf = self.ident_f
        P.mark()

        def dve(fn, r, w):
            return P.op("dve", fn, reads=r, writes=w)

        def act(fn, r, w):
            return P.op("act", fn, reads=r, writes=w)

        def pool(fn, r, w):
            return P.op("pool", fn, reads=r, writes=w)

        def T32(name, shape=(128, 512)):
            return P.sbuf(list(shape), F32, name)

        def T16(name, shape=(128, 512)):
            return P.sbuf(list(shape), BF16, name)

        mu = T32("mu", (128, 3, 1792))
        for j in range(2):
            P.dma(mu[:, 1 + j, :], d["rwkv_mu"][l, j:j + 1, :].to_broadcast([128, 1792]), writes=[mu])
        dve(lambda g: g.tensor_tensor(out=mu[:, 0, :], in0=mu[:, 1, :], in1=mu[:, 2, :], op=ALU.add), [mu], [mu])
        dve(lambda g: g.tensor_scalar(out=mu[:, 0, :], in0=mu[:, 0, :], scalar1=-1.0, scalar2=1.0, op0=ALU.mult,
                                      op1=ALU.add), [mu], [mu])
        kv = T32("kv", (128, 3, 512))
        for j in range(2):
            P.dma(kv[:, j, :], d["rwkv_kvec"][l, j:j + 1, :].to_broadcast([128, 512]), writes=[kv])
        dve(lambda g: g.tensor_scalar(out=kv[:, 2, :], in0=kv[:, 1, :], scalar1=-1.0, scalar2=1.0, op0=ALU.mult,
                                      op1=ALU.add), [kv], [kv])
        lnp = T32("lnp", (128, 3, 512))
        P.dma(lnp[:].rearrange("p a c -> p (a c)"),
              d["rwkv_lnp"][l:l + 1].rearrange("o a c -> o (a c)").to_broadcast([128, 1536]), writes=[lnp])
        wupA = T16("wupA", (65, 2, 512))
        P.dma(wupA[:].rearrange("p a c -> p (a c)"), d["rwkv_wupA"][l], writes=[wupA], q="pool")
        aupA = T16("aupA", (65, 2, 512))
        P.dma(aupA[:].rearrange("p a c -> p (a c)"), d["rwkv_aupA"][l], writes=[aupA], q="pool")
        gup = T16("gup", (128, 512))
        P.dma(gup[:], d["rwkv_g_up"][l], writes=[gup], q="pool")
        tri = T32("tri", (128, 2, 6, 128))
        P.dma(tri[:], d["rw_tri"].rearrange("a b p c -> p a b c"), writes=[tri])
        onec = T32("onec", (128, 1))
        pool(lambda g: g.memset(onec[:], 1.0), [], [onec])
        TWA = T16("TWA", (65, 128))
        ALA = T16("ALA", (65, 128))
        pool(lambda g: g.memset(TWA[:], 1.0), [], [TWA])
        pool(lambda g: g.memset(ALA[:], 1.0), [], [ALA])
        cur = [T32("cur%d" % i, (128, 1792)) for i in range(2)]
        prv = [T32("prv%d" % i, (128, 1792)) for i in range(2)]
        nxt = [T32("nxt%d" % i, (128, 1792)) for i in range(2)]
        kk = T32("kk")
        sq = T32("sq")
        s8 = T32("s8", (128, 8))
        r8 = T32("r8", (128, 8))
        vbf = T16("vbf")
        tw = T16("tw", (128, 64))
        al = T16("al", (128, 64))
        sgl = T16("sgl", (128, 128))
        sglT = T16("sglT", (128, 128))
        lw = T32("lw")
        av = T32("av")
        tt_ = T32("tt")
        kd = T32("kd")
        kd0 = T32("kd0")
        bb = T32("bb")
        eW, eWi, eLu, eD, elw, eWx, eLux = [T32(nm) for nm in ("eW", "eWi", "eLu", "eD", "elw", "eWx", "eLux")]
        rt, zt, bt, kt, bp, kp, ru, zu = [T16(nm) for nm in ("rt", "zt", "bt", "kt", "bp", "kp", "ru", "zu")]
        RTf, ZTf, BTf, KTf = [T16(nm, (64, 8, 128)) for nm in ("RTf", "ZTf", "BTf", "KTf")]
        Xs = [T16("Xs%d" % i, (128, 8, 128)) for i in range(2)]
        XTs = [T16("XTs%d" % i, (128, 8, 128)) for i in range(2)]
        TTs = [T16("TTs%d" % i, (128, 8, 128)) for i in range(2)]
        AzkT, ArbT, ArkT = [T16(nm, (128, 8, 128)) for nm in ("AzkT", "ArbT", "ArkT")]
        Zp, Gm, U0 = [T16(nm) for nm in ("Zp", "Gm", "U0")]
        Y0 = T32("Y0")
        RpT = T16("RpT", (64, 8, 128))
        Mm = T32("Mm", (64, 8, 64))
        NTt = T32("NTt", (64, 8, 64))
        STf = T32("STf", (64, 8, 64))
        STb = T16("STb", (64, 8, 64))
        WC = T32("WC", (64, 8))
        Yt = T32("Yt")
        yfl = T32("yfl")
        m8 = T32("m8", (128, 8))
        v8 = T32("v8", (128, 8))
        b8 = T32("b8", (128, 8))
        yc = T32("yc")
        ob = T16("ob", (128, 4, 128))
        allpb = [b for k, b in self.dbufs.items() if k[0] == "pb"]
        ps = self.ps

        def view8(ap):
            return ap.rearrange("p (h m) -> p h m", m=64)

        def b8c(t8):
            return t8[:].unsqueeze(2).to_broadcast([128, 8, 64])

        for pss in range(2):
            dd = pss
            order = [32, 33] + list(range(32)) if dd == 0 else [33, 32] + list(range(31, -1, -1))
            pool(lambda g: g.memset(STf[:], 0.0), [], [STf])
            pool(lambda g: g.memset(STb[:], 0.0), [], [STb])
            for ti, t in enumerate(order):
                need_y = with_ctx or t < 32
                r0 = tm_row(t * 128)
                cu, pv_, nx = cur[ti % 2], prv[ti % 2], nxt[ti % 2]
                P.dma(cu[:], d["pb"][r0:r0 + 128, :], reads=allpb, writes=[cu])
                P.dma(pv_[:], d["pb"][r0 - 1:r0 + 127, :], reads=allpb, writes=[pv_])
                P.dma(nx[:], d["pb"][r0 + 1:r0 + 129, :], reads=allpb, writes=[nx])
                dve(lambda g: g.tensor_tensor(out=cu[:], in0=cu[:], in1=mu[:, 0, :], op=ALU.mult), [cu, mu], [cu])
                pool(lambda g: g.tensor_tensor(out=pv_[:], in0=pv_[:], in1=mu[:, 1, :], op=ALU.mult), [pv_, mu], [pv_])
                pool(lambda g: g.tensor_tensor(out=nx[:], in0=nx[:], in1=mu[:, 2, :], op=ALU.mult), [nx, mu], [nx])
                dve(lambda g: g.tensor_tensor(out=cu[:], in0=cu[:], in1=pv_[:], op=ALU.add), [cu, pv_], [cu])
                dve(lambda g: g.tensor_tensor(out=cu[:], in0=cu[:], in1=nx[:], op=ALU.add), [cu, nx], [cu])
                r_ap, k_ap, v_ap = cu[:, 0:512], cu[:, 512:1024], cu[:, 1024:1536]
                dve(lambda g: g.tensor_tensor(out=kk[:], in0=k_ap, in1=kv[:, 0, :], op=ALU.mult), [cu, kv], [kk])
                pool(lambda g: g.tensor_tensor(out=sq[:], in0=kk[:], in1=kk[:], op=ALU.mult), [kk], [sq])
                dve(lambda g: g.tensor_reduce(out=s8[:], in_=view8(sq[:]), axis=AX.X, op=ALU.add), [sq], [s8])
                act(lambda g: g.activation(out=r8[:], in_=s8[:], func=AF.Sqrt, bias=1e-12), [s8], [r8])
                dve(lambda g: g.reciprocal(out=r8[:], in_=r8[:]), [r8], [r8])
                dve(lambda g: g.tensor_tensor(out=view8(kk[:]), in0=view8(kk[:]), in1=b8c(r8), op=ALU.mult),
                    [kk, r8], [kk])
                act(lambda g: g.activation(out=vbf[:], in_=v_ap, func=AF.Copy), [cu], [vbf])
                act(lambda g: g.activation(out=tw[:], in_=cu[:, 1536:1600], func=AF.Tanh), [cu], [tw])
                act(lambda g: g.activation(out=al[:], in_=cu[:, 1600:1664], func=AF.Copy), [cu], [al])
                pb0 = self.psb(0)
                P.op("pe", lambda g: g.transpose(pb0[0:64, 0:128], tw[:], idb[:]), reads=[tw, idb], writes=[ps[0]])
                P.op("pe", lambda g: g.transpose(pb0[0:64, 128:256], al[:], idb[:]), reads=[al, idb], writes=[ps[0]])
                dve(lambda g: g.tensor_copy(out=TWA[0:64, :], in_=pb0[0:64, 0:128]), [ps[0]], [TWA])
                dve(lambda g: g.tensor_copy(out=ALA[0:64, :], in_=pb0[0:64, 128:256]), [ps[0]], [ALA])
                if pss == 1:
                    act(lambda g: g.activation(out=sgl[:], in_=cu[:, 1664:1792], func=AF.Sigmoid), [cu], [sgl])
                    pb1 = self.psb(1)
                    P.op("pe", lambda g: g.transpose(pb1[:, 0:128], sgl[:], idb[:]), reads=[sgl, idb], writes=[ps[1]])
                    dve(lambda g: g.tensor_copy(out=sglT[:], in_=pb1[:, 0:128]), [ps[1]], [sglT])
                    P.mm(ps[2][:], ALA[:], aupA[:, 0, :], reads=[ALA, aupA], writes=[ps[2]])
                    act(lambda g: g.activation(out=av[:], in_=ps[2][:], func=AF.Sigmoid), [ps[2]], [av])
                    dve(lambda g: g.tensor_tensor(out=tt_[:], in0=av[:], in1=kv[:, 1, :], op=ALU.mult), [av, kv], [tt_])
                    pool(lambda g: g.tensor_tensor(out=tt_[:], in0=tt_[:], in1=kv[:, 2, :], op=ALU.add), [tt_, kv], [tt_])
                    dve(lambda g: g.tensor_tensor(out=kd0[:], in0=k_ap, in1=tt_[:], op=ALU.mult), [cu, tt_], [kd0])
                P.mm(ps[2][:], TWA[:], wupA[:, dd, :], reads=[TWA, wupA], writes=[ps[2]])
                act(lambda g: g.activation(out=lw[:], in_=ps[2][:], func=AF.Sigmoid), [ps[2]], [lw])
                dve(lambda g: g.tensor_scalar(out=lw[:], in0=lw[:], scalar1=-0.6065306597126334, scalar2=None,
                                              op0=ALU.mult), [lw], [lw])
                P.mm(ps[3][:], ALA[:], aupA[:, dd, :], reads=[ALA, aupA], writes=[ps[3]])
                act(lambda g: g.activation(out=av[:], in_=ps[3][:], func=AF.Sigmoid), [ps[3]], [av])
                dve(lambda g: g.tensor_tensor(out=tt_[:], in0=av[:], in1=kv[:, 1, :], op=ALU.mult), [av, kv], [tt_])
                pool(lambda g: g.tensor_tensor(out=tt_[:], in0=tt_[:], in1=kv[:, 2, :], op=ALU.add), [tt_, kv], [tt_])
                dve(lambda g: g.tensor_tensor(out=kd[:], in0=k_ap, in1=tt_[:], op=ALU.mult), [cu, tt_], [kd])
                pool(lambda g: g.tensor_tensor(out=bb[:], in0=kk[:], in1=av[:], op=ALU.mult), [kk, av], [bb])
                P.mm(ps[4][:], tri[:, dd, 0, :], lw[:], reads=[tri, lw], writes=[ps[4]])
                P.mm(ps[5][:], tri[:, dd, 1, :], lw[:], reads=[tri, lw], writes=[ps[5]])
                P.mm(ps[6][:], tri[:, dd, 2, :], lw[:], reads=[tri, lw], writes=[ps[6]])
                for h in range(8):
                    P.mm(ps[7][0:64, h:h + 1], lw[:, h * 64:(h + 1) * 64], onec[:], reads=[lw, onec], writes=[ps[7]])
                act(lambda g: g.activation(out=WC[:], in_=ps[7][0:64, 0:8], func=AF.Exp), [ps[7]], [WC])
                act(lambda g: g.activation(out=eLu[:], in_=ps[4][:], func=AF.Exp), [ps[4]], [eLu])
                act(lambda g: g.activation(out=eW[:], in_=ps[5][:], func=AF.Exp), [ps[5]], [eW])
                act(lambda g: g.activation(out=eWi[:], in_=ps[5][:], func=AF.Exp, scale=-1.0), [ps[5]], [eWi])
                act(lambda g: g.activation(out=eD[:], in_=ps[6][:], func=AF.Exp), [ps[6]], [eD])
                act(lambda g: g.activation(out=elw[:], in_=lw[:], func=AF.Exp, scale=-1.0), [lw], [elw])
                dve(lambda g: g.tensor_tensor(out=eWx[:], in0=eW[:], in1=elw[:], op=ALU.mult), [eW, elw], [eWx])
                pool(lambda g: g.tensor_tensor(out=eLux[:], in0=eLu[:], in1=elw[:], op=ALU.mult), [eLu, elw], [eLux])
                dve(lambda g: g.tensor_tensor(out=rt[:], in0=r_ap, in1=eW[:], op=ALU.mult), [cu, eW], [rt])
                dve(lambda g: g.scalar_tensor_tensor(out=zt[:], in0=kk[:], scalar=-1.0, in1=eWx[:], op0=ALU.mult,
                                                     op1=ALU.mult), [kk, eWx], [zt])
                pool(lambda g: g.tensor_tensor(out=bt[:], in0=bb[:], in1=eWi[:], op=ALU.mult), [bb, eWi], [bt])
                pool(lambda g: g.tensor_tensor(out=kt[:], in0=kd[:], in1=eWi[:], op=ALU.mult), [kd, eWi], [kt])
                pool(lambda g: g.tensor_tensor(out=bp[:], in0=bb[:], in1=eD[:], op=ALU.mult), [bb, eD], [bp])
                dve(lambda g: g.tensor_tensor(out=kp[:], in0=kd[:], in1=eD[:], op=ALU.mult), [kd, eD], [kp])
                pool(lambda g: g.tensor_tensor(out=ru[:], in0=r_ap, in1=eLu[:], op=ALU.mult), [cu, eLu], [ru])
                dve(lambda g: g.scalar_tensor_tensor(out=zu[:], in0=kk[:], scalar=-1.0, in1=eLux[:], op0=ALU.mult,
                                                     op1=ALU.mult), [kk, eLux], [zu])
                for qi, (src, dstf) in enumerate(((rt, RTf), (zt, ZTf), (bt, BTf), (kt, KTf))):
                    pbx = self.psb(qi % 2)
                    for h in range(8):
                        P.op("pe", lambda g: g.transpose(pbx[0:64, h * 128:(h + 1) * 128], src[:, h * 64:(h + 1) * 64],
                                                         idb[:]), reads=[src, idb], writes=[ps[qi % 2]])
                    if qi % 2 == 0:
                        act(lambda g: g.activation(out=dstf[:].rearrange("p h t -> p (h t)"), in_=pbx[0:64, :],
                                                   func=AF.Copy), [ps[qi % 2]], [dstf])
                    else:
                        dve(lambda g: g.tensor_copy(out=dstf[:].rearrange("p h t -> p (h t)"), in_=pbx[0:64, :]),
                            [ps[qi % 2]], [dstf])
                nb = [0]

                def amat(Lf, Rf, mi, dst, add_ident=None):
                    for hg in range(2):
                        pa = ps[2 + nb[0] % 4]
                        nb[0] += 1
                        for j in range(4):
                            h = hg * 4 + j
                            P.mm(pa[:, j * 128:(j + 1) * 128], Lf[:, h, :], Rf[:, h, :], reads=[Lf, Rf], writes=[pa])
                        dve(lambda g: g.tensor_tensor(out=dst[:, hg * 4:(hg + 1) * 4, :],
                                                      in0=pa[:].rearrange("p (h t) -> p h t", h=4),
                                                      in1=tri[:, dd, mi, :].unsqueeze(1).to_broadcast([128, 4, 128]),
                                                      op=ALU.mult), [pa, tri], [dst])

                amat(ZTf, BTf, 5, Xs[0])
                amat(BTf, ZTf, 3, XTs[0])
                amat(KTf, ZTf, 3, AzkT)
                amat(BTf, RTf, 4, ArbT)
                amat(KTf, RTf, 4, ArkT)
                pool(lambda g: g.tensor_tensor(out=TTs[0][:], in0=XTs[0][:],
                                               in1=idb[:].unsqueeze(1).to_broadcast([128, 8, 128]), op=ALU.add),
                     [XTs[0], idb], [TTs[0]])
                cx = 0
                for it in range(6):
                    Xc, XTc, TTc = Xs[cx], XTs[cx], TTs[cx]
                    Xn, XTn, TTn = Xs[1 - cx], XTs[1 - cx], TTs[1 - cx]
                    for hg in range(2):
                        p2 = ps[2 + hg]
                        for j in range(4):
                            h = hg * 4 + j
                            P.mm(p2[:, j * 128:(j + 1) * 128], XTc[:, h, :], Xc[:, h, :], reads=[XTc, Xc], writes=[p2])
                        act(lambda g: g.activation(out=Xn[:, hg * 4:(hg + 1) * 4, :].rearrange("p h t -> p (h t)"),
                                                   in_=p2[:], func=AF.Copy), [p2], [Xn])
                        if it < 5:
                            p3 = ps[4 + hg]
                            for j in range(4):
                                h = hg * 4 + j
                                P.mm(p3[:, j * 128:(j + 1) * 128], Xc[:, h, :], XTc[:, h, :], reads=[Xc, XTc],
                                     writes=[p3])
                            dve(lambda g: g.tensor_copy(out=XTn[:, hg * 4:(hg + 1) * 4, :].rearrange("p h t -> p (h t)"),
                                                        in_=p3[:]), [p3], [XTn])
                    for hg in range(2):
                        p4 = ps[6 + hg]
                        for j in range(4):
                            h = hg * 4 + j
                            P.mm(p4[:, j * 128:(j + 1) * 128], Xn[:, h, :], TTc[:, h, :], reads=[Xn, TTc], writes=[p4])
                        dve(lambda g: g.tensor_tensor(out=TTn[:, hg * 4:(hg + 1) * 4, :],
                                                      in0=TTc[:, hg * 4:(hg + 1) * 4, :],
                                                      in1=p4[:].rearrange("p (h t) -> p h t", h=4), op=ALU.add),
                            [TTc, p4], [TTn])
                    cx = 1 - cx
                TT = TTs[cx]
                for h in range(8):
                    P.mm(ps[2][:, h * 64:(h + 1) * 64], TT[:, h, :], zu[:, h * 64:(h + 1) * 64], reads=[TT, zu], writes=[ps[2]])
                act(lambda g: g.activation(out=Zp[:], in_=ps[2][:], func=AF.Copy), [ps[2]], [Zp])
                for h in range(8):
                    P.mm(ps[3][:, h * 64:(h + 1) * 64], AzkT[:, h, :], vbf[:, h * 64:(h + 1) * 64], reads=[AzkT, vbf],
                         writes=[ps[3]])
                dve(lambda g: g.tensor_copy(out=Gm[:], in_=ps[3][:]), [ps[3]], [Gm])
                for h in range(8):
                    P.mm(ps[4][:, h * 64:(h + 1) * 64], TT[:, h, :], Gm[:, h * 64:(h + 1) * 64], reads=[TT, Gm], writes=[ps[4]])
                act(lambda g: g.activation(out=U0[:], in_=ps[4][:], func=AF.Copy), [ps[4]], [U0])
                for h in range(8):
                    P.mm(ps[5][0:64, h * 64:(h + 1) * 64], Zp[:, h * 64:(h + 1) * 64], bp[:, h * 64:(h + 1) * 64],
                         reads=[Zp, bp], writes=[ps[5]])
                dve(lambda g: g.tensor_tensor(out=Mm[:], in0=idf[0:64, 0:64].unsqueeze(1).to_broadcast([64, 8, 64]),
                                              in1=WC[:].unsqueeze(2).to_broadcast([64, 8, 64]), op=ALU.mult),
                    [idf, WC], [Mm])
                dve(lambda g: g.tensor_tensor(out=Mm[:], in0=Mm[:], in1=ps[5][0:64, :].rearrange("p (h m) -> p h m", m=64),
                                              op=ALU.add), [Mm, ps[5]], [Mm])
                for h in range(8):
                    hs = slice(h * 64, (h + 1) * 64)
                    P.mm(ps[6][0:64, hs], bp[:, hs], U0[:, hs], start=True, stop=False, reads=[bp, U0], writes=[ps[6]])
                    P.mm(ps[6][0:64, hs], kp[:, hs], vbf[:, hs], start=False, stop=True, reads=[kp, vbf], writes=[ps[6]])
                act(lambda g: g.activation(out=NTt[:].rearrange("p h m -> p (h m)"), in_=ps[6][0:64, :], func=AF.Copy),
                    [ps[6]], [NTt])
                if need_y:
                    for h in range(8):
                        hs = slice(h * 64, (h + 1) * 64)
                        P.mm(ps[7][:, hs], ArbT[:, h, :], U0[:, hs], start=True, stop=False, reads=[ArbT, U0], writes=[ps[7]])
                        P.mm(ps[7][:, hs], ArkT[:, h, :], vbf[:, hs], start=False, stop=True, reads=[ArkT, vbf],
                             writes=[ps[7]])
                    act(lambda g: g.activation(out=Y0[:], in_=ps[7][:], func=AF.Copy), [ps[7]], [Y0])
                    for hg in range(2):
                        pr = ps[hg]
                        for j in range(4):
                            h = hg * 4 + j
                            hs = slice(h * 64, (h + 1) * 64)
                            P.mm(pr[0:64, j * 128:(j + 1) * 128], ru[:, hs], idb[:], start=True, stop=False,
                                 reads=[ru, idb], writes=[pr])
                            P.mm(pr[0:64, j * 128:(j + 1) * 128], Zp[:, hs], ArbT[:, h, :], start=False, stop=True,
                                 reads=[Zp, ArbT], writes=[pr])
                        dve(lambda g: g.tensor_copy(out=RpT[:, hg * 4:(hg + 1) * 4, :].rearrange("p h t -> p (h t)"),
                                                    in_=pr[0:64, :]), [pr], [RpT])
                    for h in range(8):
                        P.mm(ps[2][:, h * 64:(h + 1) * 64], RpT[:, h, :], STb[:, h, :], reads=[RpT, STb], writes=[ps[2]])
                    dve(lambda g: g.tensor_tensor(out=Yt[:], in0=ps[2][:], in1=Y0[:], op=ALU.add), [ps[2], Y0], [Yt])
                for h in range(8):
                    P.mm(ps[3][0:64, h * 64:(h + 1) * 64], Mm[:, h, :], STf[:, h, :], reads=[Mm, STf], writes=[ps[3]])
                dve(lambda g: g.tensor_tensor(out=STf[:], in0=ps[3][0:64, :].rearrange("p (h m) -> p h m", m=64),
                                              in1=NTt[:], op=ALU.add), [ps[3], NTt], [STf])
                act(lambda g: g.activation(out=STb[:], in_=STf[:], func=AF.Copy), [STf], [STb])
                if not need_y:
                    continue
                if pss == 0:
                    P.dma(d["yf"][t * 128:(t + 1) * 128, :], Yt[:], reads=[Yt], writes=[self.db("yf", t)])
                    continue
                P.dma(yfl[:], d["yf"][t * 128:(t + 1) * 128, :], reads=[self.db("yf", t)], writes=[yfl])
                dve(lambda g: g.tensor_tensor(out=Yt[:], in0=Yt[:], in1=yfl[:], op=ALU.add), [Yt, yfl], [Yt])
                dve(lambda g: g.tensor_reduce(out=m8[:], in_=view8(Yt[:]), axis=AX.X, op=ALU.add), [Yt], [m8])
                dve(lambda g: g.tensor_scalar(out=m8[:], in0=m8[:], scalar1=1.0 / 64, scalar2=None, op0=ALU.mult), [m8], [m8])
                dve(lambda g: g.tensor_tensor(out=view8(yc[:]), in0=view8(Yt[:]), in1=b8c(m8), op=ALU.subtract),
                    [Yt, m8], [yc])
                pool(lambda g: g.tensor_tensor(out=sq[:], in0=yc[:], in1=yc[:], op=ALU.mult), [yc], [sq])
                dve(lambda g: g.tensor_reduce(out=v8[:], in_=view8(sq[:]), axis=AX.X, op=ALU.add), [sq], [v8])
                act(lambda g: g.activation(out=v8[:], in_=v8[:], func=AF.Sqrt, scale=1.0 / 64, bias=64e-5), [v8], [v8])
                dve(lambda g: g.reciprocal(out=v8[:], in_=v8[:]), [v8], [v8])
                dve(lambda g: g.tensor_tensor(out=view8(yc[:]), in0=view8(yc[:]), in1=b8c(v8), op=ALU.mult), [yc, v8], [yc])
                pool(lambda g: g.tensor_tensor(out=yc[:], in0=yc[:], in1=lnp[:, 0, :], op=ALU.mult), [yc, lnp], [yc])
                pool(lambda g: g.tensor_tensor(out=yc[:], in0=yc[:], in1=lnp[:, 1, :], op=ALU.add), [yc, lnp], [yc])
                dve(lambda g: g.tensor_tensor(out=kd0[:], in0=kd0[:], in1=kd[:], op=ALU.add), [kd0, kd], [kd0])
                dve(lambda g: g.tensor_tensor(out=kd0[:], in0=kd0[:], in1=r_ap, op=ALU.mult), [kd0, cu], [kd0])
                dve(lambda g: g.scalar_tensor_tensor(out=sq[:], in0=kd0[:], scalar=0.5, in1=lnp[:, 2, :], op0=ALU.mult,
                                                     op1=ALU.mult), [kd0, lnp], [sq])
                dve(lambda g: g.tensor_reduce(out=b8[:], in_=view8(sq[:]), axis=AX.X, op=ALU.add), [sq], [b8])
                dve(lambda g: g.tensor_tensor(out=view8(sq[:]), in0=view8(v_ap), in1=b8c(b8), op=ALU.mult), [cu, b8], [sq])
                pool(lambda g: g.tensor_tensor(out=yc[:], in0=yc[:], in1=sq[:], op=ALU.add), [yc, sq], [yc])
                P.mm(ps[4][:], sglT[:], gup[:], reads=[sglT, gup], writes=[ps[4]])
                dve(lambda g: g.tensor_tensor(out=yc[:], in0=yc[:], in1=ps[4][:], op=ALU.mult), [yc, ps[4]], [yc])
                for kc in range(4):
                    P.op("pe", lambda g: g.transpose(ps[5][:, kc * 128:(kc + 1) * 128], yc[:, kc * 128:(kc + 1) * 128],
                                                     idf[:]), reads=[yc, idf], writes=[ps[5]])
                act(lambda g: g.activation(out=ob[:].rearrange("p a b -> p (a b)"), in_=ps[5][:], func=AF.Copy),
                    [ps[5]], [ob])
                P.dma(d["yT"][1].rearrange("(kc p) t -> p kc t", p=128)[:, :, t * 128:(t + 1) * 128], ob[:], reads=[ob],
                      writes=[self.db("yT", (1, t))])
        P.release()

    KB.declare_rwkv = declare_rwkv
    KB.phase_rwkv = phase_rwkv


_rwkv_methods()


def build_program():
    nc = bass.Bass("TRN2", target_bir_lowering=False)
    kb = KB(nc)
    kb.declare_common()
    kb.declare_pin()
    kb.declare_attn()
    kb.declare_merge()
    kb.declare_hyena()
    kb.declare_rwkv()
    kb.outp("out", [NLAT, DM])
    kb.alloc_persist()
    d = kb.d
    for l in range(2):
        with_ctx = (l == 0)
        kb.phase_mod(l)
        src = (d["xall"], "xall") if l == 0 else (d["xs"], "xs")
        kb.phase_ffn(l, 0, src, (d["xs"], "xs"), NTILE)
        kb.phase_pin(l, (d["xs"], "xs"))
        kb.phase_mla(l, with_ctx)
        kb.phase_swa(l, with_ctx)
        kb.phase_hyena(l, with_ctx)
        kb.phase_rwkv(l, with_ctx)
        kb.phase_merge(l, with_ctx)
        if l == 0:
            kb.phase_ffn(l, 2, (d["xs"], "xs"), (d["xs"], "xs"), NTILE)
        else:
            kb.phase_ffn(l, 2, (d["xs"], "xs"), (d["out"], "out"), 32)
    kb.P.barrier()
    return nc, kb


def kernel(**inputs):
    from concourse.bass_utils import run_bass_kernel_spmd
    nc, kb = build_program()
    sh = host_shared(inputs)
    names = [k for k in kb.d if k in sh]
    maps = []
    for b in range(8):
        pc = host_core(inputs, b)
        m = {k: sh[k] for k in names}
        m.update(pc)
        maps.append(m)
    res = run_bass_kernel_spmd(nc, maps, core_ids=list(range(8)))
    out = np.stack([np.asarray(res.results[b]["out"], dtype=np.float32) for b in range(8)], axis=0)
    return out
```

```python
import numpy as np
import concourse.bass as bass
import concourse.mybir as mybir

F32 = mybir.dt.float32
BF16 = mybir.dt.bfloat16
AF = mybir.ActivationFunctionType
ALU = mybir.AluOpType
AX = mybir.AxisListType

NDSEM = 36
NHW = 28
SEM_LIMIT = 30000
SB_BASE = 16640
SB_TOP = 229376
LAZY_D = 2
SAME_ENG_GAP = 3


class Buf:
    __slots__ = ("w", "r", "name")

    def __init__(self, name=""):
        self.w = None
        self.r = {}
        self.name = name


class Tile:
    def __init__(self, t, buf):
        self.t = t
        self.b = buf

    def __getitem__(self, k):
        return self.t[k]


def _bufs(xs):
    return [x.b if isinstance(x, Tile) else x for x in xs]


class Prog:
    def __init__(self, nc):
        self.nc = nc
        self.eng = {"pe": nc.tensor, "dve": nc.vector, "act": nc.scalar,
                    "pool": nc.gpsimd, "sp": nc.sync}
        self.cnt = {e: 0 for e in self.eng}
        self.known = {e: {} for e in self.eng}
        self.esem = {}
        self.egen = {e: 0 for e in self.eng}
        self.ebase = {e: 0 for e in self.eng}
        self.dsem = []
        self.duse = []
        self.dgen = []
        self.semtab = {}
        self.dnext = 0
        self.dnext_sw = 0
        self.pending = []
        self.nwait = 0
        self.ninstr = 0
        self.sb_off = SB_BASE
        self.sb_mark = []
        self.uid = 0
        nc = self.nc
        for e in self.eng:
            self.esem[e] = nc.alloc_semaphore("es_%s_0" % e)
            self.semtab[("e", e, 0)] = self.esem[e]
        for i in range(NDSEM):
            self.dsem.append(nc.alloc_semaphore("ds%d_0" % i))
            self.semtab[("d", i, 0)] = self.dsem[i]
            self.duse.append(0)
            self.dgen.append(0)

    def _rot_e(self, e):
        if self.cnt[e] - self.ebase[e] >= SEM_LIMIT:
            self.egen[e] += 1
            self.ebase[e] = self.cnt[e]
            self.esem[e] = self.nc.alloc_semaphore("es_%s_%d" % (e, self.egen[e]))
            self.semtab[("e", e, self.egen[e])] = self.esem[e]

    def _rot_d(self, i):
        if 16 * self.duse[i] >= SEM_LIMIT:
            self.dgen[i] += 1
            self.duse[i] = 0
            self.dsem[i] = self.nc.alloc_semaphore("ds%d_%d" % (i, self.dgen[i]))
            self.semtab[("d", i, self.dgen[i])] = self.dsem[i]

    def sbuf(self, shape, dtype, name=None):
        self.uid += 1
        name = (name or "t") + "_%d" % self.uid
        nbytes = int(np.prod(shape[1:])) * mybir.dt.size(dtype)
        off = (self.sb_off + 31) // 32 * 32
        t = self.nc.alloc_sbuf_tensor_at(name, list(shape), dtype, offset=off)
        self.sb_off = off + nbytes
        assert self.sb_off <= SB_TOP, ("SBUF overflow", name, self.sb_off)
        return Tile(t, Buf(name))

    def mark(self):
        self.sb_mark.append(self.sb_off)

    def release(self):
        self.barrier()
        self.sb_off = self.sb_mark.pop()

    def _wait(self, e, kind, id_, gen, val):
        self.known[e][(kind, id_)] = (gen, val)
        self.eng[e].wait_ge(self.semtab[(kind, id_, gen)], val)
        self.nwait += 1

    def _need(self, e, toks):
        kn = self.known[e]
        req = {}
        for t in toks:
            if t is None:
                continue
            kind, id_, gen, val, absidx = t
            if kind == "e" and id_ == e:
                if e == "pe":
                    continue
                if self.cnt[e] + 1 - absidx >= SAME_ENG_GAP:
                    continue
            k = (kind, id_)
            if kn.get(k, (-1, 0)) >= (gen, val):
                continue
            if req.get(k, (-1, 0)) < (gen, val):
                req[k] = (gen, val)
        for (kind, id_), (gen, val) in req.items():
            self._wait(e, kind, id_, gen, val)

    @staticmethod
    def _deps(reads, writes):
        toks = []
        for b in reads:
            toks.append(b.w)
        for b in writes:
            toks.append(b.w)
            toks.extend(b.r.values())
        return toks

    @staticmethod
    def _commit(tok, reads, writes):
        k = (tok[0], tok[1])
        for b in reads:
            b.r[k] = tok
        for b in writes:
            b.w = tok
            b.r = {}

    def op(self, e, fn, reads=(), writes=()):
        reads = _bufs(reads)
        writes = _bufs(writes)
        self._hazard_flush(reads, writes)
        self._rot_e(e)
        self._need(e, self._deps(reads, writes))
        ins = fn(self.eng[e])
        self.cnt[e] += 1
        ins.then_inc(self.esem[e], 1)
        tok = ("e", e, self.egen[e], self.cnt[e] - self.ebase[e], self.cnt[e])
        self._commit(tok, reads, writes)
        self.ninstr += 1
        return ins

    def _flush(self, upto=None):
        n = len(self.pending) if upto is None else upto
        todo, self.pending = self.pending[:n], self.pending[n:]
        for (out, in_, reads, writes, q, kw, _) in todo:
            self._dma_now(out, in_, reads, writes, q, kw)

    def _hazard_flush(self, reads, writes):
        if not self.pending:
            return
        rs = set(map(id, reads))
        ws = set(map(id, writes))
        last = -1
        for i, (_, _, pr, pw, _, _, _) in enumerate(self.pending):
            hit = False
            for b in pr:
                if id(b) in ws:
                    hit = True
            for b in pw:
                if id(b) in ws or id(b) in rs:
                    hit = True
            if hit:
                last = i
        if last >= 0:
            self._flush(last + 1)

    def dma(self, out, in_, reads=(), writes=(), q="sp", **kw):
        reads = _bufs(reads)
        writes = _bufs(writes)
        self._hazard_flush(reads, writes)
        is_store = type(out.tensor).__name__.startswith("DRam") and not type(in_.tensor).__name__.startswith("DRam")
        if is_store and q == "sp" and LAZY_D > 0:
            self.pending.append([out, in_, reads, writes, q, kw, 0])
            return None
        r = self._dma_now(out, in_, reads, writes, q, kw)
        if q == "sp" and self.pending:
            k = 0
            for p in self.pending:
                p[6] += 1
            while k < len(self.pending) and self.pending[k][6] >= LAZY_D:
                k += 1
            if k:
                self._flush(k)
        return r

    def _dma_now(self, out, in_, reads, writes, q, kw):
        if q == "pool":
            i = NHW + self.dnext_sw
            self.dnext_sw = (self.dnext_sw + 1) % (NDSEM - NHW)
        else:
            i = self.dnext
            self.dnext = (self.dnext + 1) % NHW
        toks = self._deps(reads, writes)
        if self.duse[i]:
            toks.append(("d", i, self.dgen[i], 16 * self.duse[i], 0))
        self._need(q, toks)
        self._rot_d(i)
        self.duse[i] += 1
        ins = self.eng[q].dma_start(out=out, in_=in_, **kw)
        ins.then_inc(self.dsem[i], 16)
        tok = ("d", i, self.dgen[i], 16 * self.duse[i], 0)
        self._commit(tok, reads, writes)
        self.ninstr += 1
        return ins

    def barrier(self, engines=None):
        self._flush()
        toks = []
        for e in self.eng:
            if self.cnt[e] > self.ebase[e]:
                toks.append(("e", e, self.egen[e], self.cnt[e] - self.ebase[e]))
            elif self.egen[e] > 0:
                toks.append(("e", e, self.egen[e] - 1, SEM_LIMIT))
        for i, u in enumerate(self.duse):
            if u:
                toks.append(("d", i, self.dgen[i], 16 * u))
        for e in (engines or self.eng):
            kn = self.known[e]
            for kind, id_, gen, val in toks:
                if kind == "e" and id_ == e and e in ("sp", "pe"):
                    continue
                if kn.get((kind, id_), (-1, 0)) >= (gen, val):
                    continue
                self._wait(e, kind, id_, gen, val)

    def mm(self, out, lhsT, rhs, start=True, stop=True, reads=(), writes=()):
        return self.op("pe", lambda g: g.matmul(out, lhsT, rhs, start=start, stop=stop),
                       reads=reads, writes=writes)


DM = 1024
NLAT = 4096
NCTX = 256
TT = NLAT + NCTX
NTILE = TT // 128
FF = 2816
NFC = FF // 128
EPS = 1e-6
I32 = mybir.dt.int32


class KB:
    def __init__(self, nc, ext_in=(), ext_out=()):
        self.nc = nc
        self.P = Prog(nc)
        self.ext_in = set(ext_in)
        self.ext_out = set(ext_out)
        self.d = {}
        self.dbufs = {}
        self.ps = [Tile(nc.alloc_psum_tensor("ps%d" % i, [128, 512], F32), Buf("ps%d" % i))
                   for i in range(8)]

    def inp(self, name, shape, dtype=F32):
        self.d[name] = self.nc.dram_tensor(name, list(shape), dtype, kind="ExternalInput").ap()
        return self.d[name]

    def outp(self, name, shape, dtype=F32):
        self.d[name] = self.nc.dram_tensor(name, list(shape), dtype, kind="ExternalOutput").ap()
        return self.d[name]

    def scr(self, name, shape, dtype=F32):
        kind = "Internal"
        if name in self.ext_in:
            kind = "ExternalInput"
        elif name in self.ext_out:
            kind = "ExternalOutput"
        self.d[name] = self.nc.dram_tensor(name, list(shape), dtype, kind=kind).ap()
        return self.d[name]

    def db(self, name, idx=0):
        k = (name, idx)
        if k not in self.dbufs:
            self.dbufs[k] = Buf("%s_%s" % (name, idx))
        return self.dbufs[k]

    def psb(self, i):
        return self.ps[i][:].bitcast(BF16)

    def rstd_from_ss(self, ss, n, eps, out):
        P = self.P
        ss_t, ss_ap = ss
        o_t, o_ap = out
        P.op("act", lambda g: g.activation(out=o_ap, in_=ss_ap, func=AF.Sqrt, scale=1.0 / n, bias=eps),
             reads=[ss_t], writes=[o_t])
        P.op("dve", lambda g: g.reciprocal(out=o_ap, in_=o_ap), reads=[o_t], writes=[o_t])

    def declare_common(self):
        L = 2
        self.inp("xall", [TT, DM])
        self.inp("cT", [128, 16])
        self.inp("ident", [128, 128])
        self.inp("w_mod", [L, DM, 9 * DM])
        self.inp("b_mod", [L, 9 * DM])
        self.inp("norm_g", [L, 6, DM])
        self.inp("norm_gT", [L, 128, 48])
        self.inp("ffn_w13", [L, 2, DM, 2 * FF])
        self.inp("ffn_w2", [L, 2, FF, DM])
        self.scr("gtrow", [L, 3, 2, DM])
        self.scr("xs", [TT, DM])

    def alloc_persist(self):
        P = self.P
        self.ident_f = P.sbuf([128, 128], F32, "identf")
        self.ident_b = P.sbuf([128, 128], BF16, "identb")
        P.dma(self.ident_f[:], self.d["ident"], writes=[self.ident_f])
        P.dma(self.ident_b[:], self.d["ident"], writes=[self.ident_b], q="pool")
        self.mcol = P.sbuf([128, 72, 2], F32, "mcol")
        self.AB = P.sbuf([128, 3, 2, 8, 2], F32, "AB")

    def phase_mod(self, l):
        P = self.P
        d = self.d
        P.mark()
        cT = P.sbuf([128, 16], F32, "cT")
        P.dma(cT[:], d["cT"], writes=[cT])
        sc = P.sbuf([128, 16], F32, "sc")
        P.op("act", lambda g: g.activation(out=sc[:], in_=cT[:], func=AF.Silu), reads=[cT], writes=[sc])
        mrow = P.sbuf([2, 9 * DM], F32, "mrow")
        brow = P.sbuf([2, 9 * DM], F32, "brow")
        for s in range(2):
            P.dma(brow[s:s + 1, :], d["b_mod"][l:l + 1, :], writes=[brow])
        wt = [P.sbuf([128, 8, 512], F32, "wmod%d" % i) for i in range(2)]
        wsrc = d["w_mod"][l].rearrange("(k p) n -> p k n", p=128)
        for jb in range(18):
            w = wt[jb % 2]
            P.dma(w[:], wsrc[:, :, jb * 512:(jb + 1) * 512], writes=[w])
            ps = self.ps[jb % 2]
            for k in range(8):
                P.mm(ps[0:2, :], sc[:, 2 * k:2 * k + 2], w[:, k, :], start=(k == 0), stop=(k == 7),
                     reads=[sc, w], writes=[ps])
            P.op("dve", lambda g: g.tensor_tensor(out=mrow[:, jb * 512:(jb + 1) * 512], in0=ps[0:2, :],
                                                  in1=brow[:, jb * 512:(jb + 1) * 512], op=ALU.add),
                 reads=[ps, brow], writes=[mrow])
        psc = self.ps[2]
        for c in range(72):
            P.mm(psc[:, 2 * c:2 * c + 2], mrow[0:2, c * 128:(c + 1) * 128], self.ident_f[0:2, 0:2],
                 reads=[mrow, self.ident_f], writes=[psc])
        mcol = self.mcol
        P.op("dve", lambda g: g.tensor_copy(out=mcol[:].rearrange("p c s -> p (c s)"), in_=psc[:, 0:144]),
             reads=[psc], writes=[mcol])
        gcol = P.sbuf([128, 6, 8], F32, "gcol")
        P.dma(gcol[:].rearrange("p n k -> p (n k)"), d["norm_gT"][l], writes=[gcol])
        mc4 = mcol[:].rearrange("p (j k) s -> p j k s", k=8)
        tmp = P.sbuf([128, 8, 2], F32, "abtmp")
        for sub in range(3):
            jsh, jsc, npre = 3 * sub, 3 * sub + 1, 2 * sub
            P.op("dve", lambda g: g.tensor_scalar(out=tmp[:], in0=mc4[:, jsc, :, :], scalar1=1.0, scalar2=None,
                                                  op0=ALU.add), reads=[mcol], writes=[tmp])
            P.op("dve", lambda g: g.tensor_tensor(out=self.AB[:, sub, 0, :, :], in0=tmp[:],
                                                  in1=gcol[:, npre, :].unsqueeze(2).to_broadcast([128, 8, 2]),
                                                  op=ALU.mult), reads=[tmp, gcol], writes=[self.AB])
            P.op("dve", lambda g: g.tensor_copy(out=self.AB[:, sub, 1, :, :], in_=mc4[:, jsh, :, :]),
                 reads=[mcol], writes=[self.AB])
        grow = [P.sbuf([2, DM], F32, "grow%d" % i) for i in range(3)]
        gto = [P.sbuf([2, DM], F32, "gto%d" % i) for i in range(3)]
        for sub in range(3):
            jg, npost = 3 * sub + 2, 2 * sub + 1
            fac = 1.0 if sub == 1 else 0.5
            for s in range(2):
                P.dma(grow[sub][s:s + 1, :], d["norm_g"][l, npost:npost + 1, :], writes=[grow[sub]])
            P.op("dve", lambda g: g.scalar_tensor_tensor(out=gto[sub][:], in0=mrow[:, jg * DM:(jg + 1) * DM],
                                                         scalar=fac, in1=grow[sub][:], op0=ALU.mult,
                                                         op1=ALU.mult),
                 reads=[mrow, grow[sub]], writes=[gto[sub]])
            P.dma(d["gtrow"][l, sub], gto[sub][:], reads=[gto[sub]], writes=[self.db("gtrow", (l, sub))])
        P.release()

    def norm_T(self, xt_t, x_ap, sub, s, xnT_t, xnT_ap, pst, wk):
        P = self.P
        junk, ss, rs, xs = wk
        P.op("act", lambda g: g.activation(out=junk[:], in_=x_ap, func=AF.Square, accum_out=ss[:]),
             reads=[xt_t], writes=[junk, ss])
        self.rstd_from_ss((ss, ss[:]), DM, EPS, (rs, rs[:]))
        P.op("dve", lambda g: g.tensor_scalar(out=xs[:], in0=x_ap, scalar1=rs[:, 0:1], scalar2=None,
                                              op0=ALU.mult), reads=[xt_t, rs], writes=[xs])
        pb = self.psb(pst)
        for k in range(8):
            P.op("pe", lambda g: g.transpose(pb[:, k * 128:(k + 1) * 128], xs[:, k * 128:(k + 1) * 128],
                                             self.ident_b[:]),
                 reads=[xs, self.ident_b], writes=[self.ps[pst]])
        for k in range(8):
            A = self.AB[:, sub, 0, k, s:s + 1]
            B = self.AB[:, sub, 1, k, s:s + 1]
            if k % 2 == 0:
                P.op("dve", lambda g: g.tensor_scalar(out=xnT_ap[:, k, :], in0=pb[:, k * 128:(k + 1) * 128],
                                                      scalar1=A, scalar2=B, op0=ALU.mult, op1=ALU.add),
                     reads=[self.ps[pst], self.AB], writes=[xnT_t])
            else:
                P.op("act", lambda g: g.activation(out=xnT_ap[:, k, :], in_=pb[:, k * 128:(k + 1) * 128],
                                                   func=AF.Identity, scale=A, bias=B),
                     reads=[self.ps[pst], self.AB], writes=[xnT_t])

    def norm_res_out(self, pso, xt_t, x_ap, gt, wk2, dst_ap, dst_buf):
        P = self.P
        junk, s2, r2, tmp = wk2
        for h in range(2):
            P.op("act", lambda g: g.activation(out=junk[:, 0:512], in_=self.ps[pso[h]][:], func=AF.Square,
                                               accum_out=s2[:, h:h + 1]),
                 reads=[self.ps[pso[h]]], writes=[junk, s2])
        P.op("dve", lambda g: g.tensor_tensor(out=s2[:, 2:3], in0=s2[:, 0:1], in1=s2[:, 1:2], op=ALU.add),
             reads=[s2], writes=[s2])
        self.rstd_from_ss((s2, s2[:, 2:3]), DM, EPS, (r2, r2[:]))
        for h in range(2):
            P.op("dve", lambda g: g.scalar_tensor_tensor(out=tmp[:, h * 512:(h + 1) * 512],
                                                         in0=self.ps[pso[h]][:], scalar=r2[:, 0:1],
                                                         in1=gt[:, h * 512:(h + 1) * 512],
                                                         op0=ALU.mult, op1=ALU.mult),
                 reads=[self.ps[pso[h]], r2, gt], writes=[tmp])
        P.op("pool", lambda g: g.tensor_tensor(out=x_ap, in0=x_ap, in1=tmp[:], op=ALU.add),
             reads=[tmp, xt_t], writes=[xt_t])
        P.dma(dst_ap, x_ap, reads=[xt_t], writes=[dst_buf])

    def phase_ffn(self, l, sub, src, dst, ntiles):
        P = self.P
        d = self.d
        wi = 0 if sub == 0 else 1
        P.mark()
        w13 = P.sbuf([128, 8, 2 * FF], BF16, "w13")
        w13b = [Buf("w13_%d" % k) for k in range(8)]
        for k in range(8):
            P.dma(w13[:, k, :], d["ffn_w13"][l, wi, k * 128:(k + 1) * 128, :], writes=[w13b[k]], q="pool")
        w2 = P.sbuf([128, NFC, DM], BF16, "w2")
        w2b = [Buf("w2_%d" % j) for j in range(NFC)]
        for j in range(NFC):
            P.dma(w2[:, j, :], d["ffn_w2"][l, wi, j * 128:(j + 1) * 128, :], writes=[w2b[j]], q="pool")
        gt = [P.sbuf([128, DM], F32, "gt%d" % s) for s in range(2)]
        for s in range(2):
            P.dma(gt[s][:], d["gtrow"][l, sub, s:s + 1, :].to_broadcast([128, DM]),
                  reads=[self.db("gtrow", (l, sub))], writes=[gt[s]])
        xbuf = [P.sbuf([128, 2, DM], F32, "xbuf%d" % i) for i in range(2)]
        xbb = [[Buf("xb%d_%d" % (i, j)) for j in range(2)] for i in range(2)]
        xnT = [P.sbuf([128, 8, 256], BF16, "xnT%d" % i) for i in range(2)]
        hT = P.sbuf([128, NFC, 256], BF16, "hT")
        hTb = [Buf("hT%d" % j) for j in range(NFC)]
        junk = P.sbuf([128, DM], BF16, "junk")
        ss = [P.sbuf([128, 1], F32, "ss%d" % i) for i in range(2)]
        rs = [P.sbuf([128, 1], F32, "rs%d" % i) for i in range(2)]
        xs = [P.sbuf([128, DM], BF16, "xs%d" % i) for i in range(2)]
        s2 = [P.sbuf([128, 3], F32, "s2%d" % i) for i in range(2)]
        r2 = [P.sbuf([128, 1], F32, "r2%d" % i) for i in range(2)]
        tmp = [P.sbuf([128, DM], F32, "tmp%d" % i) for i in range(2)]
        sa = [P.sbuf([128, 256], F32, "sa%d" % i) for i in range(2)]
        ngroups = ntiles // 2
        src_ap, src_name = src
        dst_ap, dst_name = dst

        def load(gi):
            for i in range(2):
                t = 2 * gi + i
                xt = Tile(xbuf[gi % 2].t, xbb[gi % 2][i])
                P.dma(xbuf[gi % 2][:, i, :], src_ap[t * 128:(t + 1) * 128, :],
                      reads=[self.db(src_name, t)], writes=[xt])

        load(0)
        for gi in range(ngroups):
            if gi + 1 < ngroups:
                load(gi + 1)
            xb = xbuf[gi % 2]
            xn = xnT[gi % 2]
            s = 1 if 2 * gi >= 32 else 0
            for i in range(2):
                xt = Tile(xb.t, xbb[gi % 2][i])
                self.norm_T(xt, xb[:, i, :], sub, s, xn, xn[:, :, i * 128:(i + 1) * 128], 0 if i == 0 else 7,
                            (junk, ss[i], rs[i], xs[i]))
            for j in range(NFC):
                pa = self.ps[1 + (j % 2) * 2]
                pbk = self.ps[2 + (j % 2) * 2]
                for k in range(8):
                    P.mm(pa[:, 0:256], w13[:, k, j * 128:(j + 1) * 128], xn[:, k, :], start=(k == 0),
                         stop=(k == 7), reads=[w13b[k], xn], writes=[pa])
                for k in range(8):
                    P.mm(pbk[:, 0:256], w13[:, k, FF + j * 128:FF + (j + 1) * 128], xn[:, k, :], start=(k == 0),
                         stop=(k == 7), reads=[w13b[k], xn], writes=[pbk])
                sj = sa[j % 2]
                P.op("act", lambda g: g.activation(out=sj[:], in_=pa[:, 0:256], func=AF.Silu),
                     reads=[pa], writes=[sj])
                P.op("dve", lambda g: g.tensor_tensor(out=hT[:, j, :], in0=sj[:], in1=pbk[:, 0:256], op=ALU.mult),
                     reads=[sj, pbk], writes=[hTb[j]])
            for i in range(2):
                t = 2 * gi + i
                xt = Tile(xb.t, xbb[gi % 2][i])
                for h in range(2):
                    po = self.ps[5 + h]
                    for j in range(NFC):
                        P.mm(po[:], hT[:, j, i * 128:(i + 1) * 128], w2[:, j, h * 512:(h + 1) * 512],
                             start=(j == 0), stop=(j == NFC - 1), reads=[hTb[j], w2b[j]], writes=[po])
                self.norm_res_out([5, 6], xt, xb[:, i, :], gt[s], (junk, s2[i], r2[i], tmp[i]),
                                  dst_ap[t * 128:(t + 1) * 128, :], self.db(dst_name, t))
        P.release()


def host_shared(inp):
    f32 = np.float32
    sh = {}
    sh["ident"] = np.eye(128, dtype=f32)
    for k in ("w_mod", "b_mod", "norm_g", "ffn_w13", "ffn_w2"):
        sh[k] = np.ascontiguousarray(inp[k], dtype=f32)
    ng = np.asarray(inp["norm_g"], dtype=f32)
    sh["norm_gT"] = np.ascontiguousarray(ng.reshape(2, 6, 8, 128).transpose(0, 3, 1, 2).reshape(2, 128, 48))
    sh["w_ext"] = build_w_ext(inp["w_in"])
    cm, sm = rope_tables(32)
    sh["rope_m"] = np.ascontiguousarray(np.stack([cm, sm], 0))
    cs, ss_ = rope_tables(64)
    sh["rope_s"] = np.ascontiguousarray(np.stack([np.concatenate([cs, cs], 0), np.concatenate([ss_, ss_], 0)], 0))
    nq = np.asarray(inp["mla_norm_q"], f32)
    nkv = np.asarray(inp["mla_norm_kv"], f32)
    sh["mla_nT"] = np.ascontiguousarray(np.stack([nq[:, 0:128], nq[:, 128:256], nkv], axis=2))
    wuq = np.asarray(inp["mla_w_uq"], f32).reshape(2, 256, 8, 96)
    pm, _ = rope_partner(32)
    sw = np.concatenate([wuq[..., 0:64], wuq[..., 64 + pm]], axis=-1)
    sh["mla_wq2"] = np.ascontiguousarray(np.stack([wuq, sw], axis=3).reshape(2, 256, 8 * 2 * 96))
    wukv = np.asarray(inp["mla_w_ukv"], f32).reshape(2, 128, 8, 128)
    sh["mla_wk"] = np.ascontiguousarray(wukv[..., 0:64].reshape(2, 128, 512))
    sh["mla_wv"] = np.ascontiguousarray(wukv[..., 64:128].reshape(2, 128, 512))
    sh["swa_sink"] = np.ascontiguousarray(inp["swa_sink"], dtype=f32)
    sh["w_branch"] = np.ascontiguousarray(inp["w_branch"], dtype=f32)
    for k in ("hyena_conv", "hyena_conv_b", "hyena_w1", "hyena_w2", "hyena_w3", "hyena_bias"):
        sh[k] = np.ascontiguousarray(inp[k], dtype=f32)
    sh["rwkv_mu"] = np.ascontiguousarray(inp["rwkv_mu"], dtype=f32)
    sh["rwkv_kvec"] = np.ascontiguousarray(inp["rwkv_kvec"], dtype=f32)
    sh["rwkv_lnp"] = np.ascontiguousarray(np.stack([inp["rwkv_ln_g"], inp["rwkv_ln_b"], inp["rwkv_r_k"]], axis=1), dtype=f32)
    wup = np.asarray(inp["rwkv_w_up"], f32)
    w0 = np.asarray(inp["rwkv_w0"], f32)
    sh["rwkv_wupA"] = np.ascontiguousarray(np.concatenate([wup.transpose(0, 2, 1, 3).reshape(2, 64, 1024),
                                                           w0.reshape(2, 1, 1024)], axis=1))
    aup = np.asarray(inp["rwkv_a_up"], f32)
    a0 = np.asarray(inp["rwkv_a0"], f32)
    sh["rwkv_aupA"] = np.ascontiguousarray(np.concatenate([aup.transpose(0, 2, 1, 3).reshape(2, 64, 1024),
                                                           a0.reshape(2, 1, 1024)], axis=1))
    sh["rwkv_g_up"] = np.ascontiguousarray(inp["rwkv_g_up"], dtype=f32)
    sh["rw_tri"] = rwkv_tables()
    hf = np.asarray(inp["hyena_freq"], f32)
    sh["hyT"] = np.ascontiguousarray(np.stack([hf[:, 0], hf[:, 1], np.asarray(inp["hyena_b1"], f32),
                                               np.asarray(inp["hyena_b2"], f32)], axis=2))
    tl = hy_tables(NLAT)
    tc = hy_tables(NCTX)
    sh["hy_D"] = np.ascontiguousarray(np.stack([tl["D2"], tl["D2sw"], tl["E"]], 0))
    sh["hyL_W1"] = np.ascontiguousarray(tl["W1"].reshape(128, -1))
    sh["hyL_W3"] = np.ascontiguousarray(tl["W3"].reshape(128, -1))
    sh["hyC_W1"] = np.ascontiguousarray(tc["W1"].reshape(8, -1))
    sh["hyC_W3"] = np.ascontiguousarray(tc["W3"].reshape(8, -1))
    sh["hyL_fK"], sh["hyL_wK"] = hy_feats(NLAT)
    sh["hyC_fK"], sh["hyC_wK"] = hy_feats(NCTX)
    sh["w_out"] = np.ascontiguousarray(inp["w_out"], dtype=f32)
    bgt = np.asarray(inp["b_gate"], f32).reshape(2, 4, 8, 128).transpose(0, 3, 1, 2).reshape(2, 128, 32)
    sh["b_gateT"] = np.ascontiguousarray(bgt)
    kk = np.arange(128)[:, None]
    qq = np.arange(128)[None, :]
    sh["swa_mask"] = np.ascontiguousarray(np.stack([(qq <= kk), (kk <= qq)], 0).astype(f32))
    return sh


def host_core(inp, b):
    f32 = np.float32
    pc = {}
    pc["xall"] = np.ascontiguousarray(np.concatenate([inp["x"][b], inp["ctx"][b]], axis=0), dtype=f32)
    cv = np.stack([np.asarray(inp["c"][b], f32), np.asarray(inp["c_ctx"], f32)], axis=0)
    pc["cT"] = np.ascontiguousarray(cv.reshape(2, 8, 128).transpose(2, 1, 0).reshape(128, 16))
    return pc


G0, PA0, PB0, PH0, PD0 = 0, 4096, 4512, 6304, 7840
FM_COLS = 1728
TM_COLS = 3456
WX_FM0 = 0
WX_TM0 = FM_COLS
WX_G0 = FM_COLS + TM_COLS
WX_COLS = WX_G0 + 4096
PAD_ROWS = TT + 3


def tm_row(t):
    return 1 + t if t < NLAT else 2 + t


def rope_partner(R):
    H = R // 2
    q = H // 2
    part = np.zeros(R, np.int64)
    sign = np.zeros(R, np.float32)
    for dd in range(R):
        base = (dd // H) * H
        o = dd % H
        if o < q:
            part[dd] = base + o + q
            sign[dd] = -1.0
        else:
            part[dd] = base + o - q
            sign[dd] = 1.0
    return part, sign


def rope_tables(R):
    H = R // 2
    q = H // 2
    t = np.arange(NLAT)
    row = (t // 64).astype(np.float32)
    col = (t % 64).astype(np.float32)
    inv = (10000.0 ** (-np.arange(0, H, 2, dtype=np.float32) / H)).astype(np.float32)
    _, sign = rope_partner(R)
    cos = np.zeros((R, NLAT), np.float32)
    sin = np.zeros((R, NLAT), np.float32)
    for dd in range(R):
        pos = row if dd < H else col
        ang = (pos * inv[(dd % H) % q]).astype(np.float32)
        cos[dd] = np.cos(ang)
        sin[dd] = sign[dd] * np.sin(ang)
    return cos, sin


def build_w_ext(w_in):
    pm, _ = rope_partner(32)
    ps_, _ = rope_partner(64)
    cols = []
    cols += list(range(PA0, PA0 + 384))
    kr0 = PA0 + 384
    cols += [kr0 + i for i in range(32)]
    cols += [kr0 + int(pm[i]) for i in range(32)]
    q0 = PD0
    cols += [q0 + i for i in range(512)]
    cols += [q0 + (i // 64) * 64 + int(ps_[i % 64]) for i in range(512)]
    k0 = PD0 + 512
    cols += [k0 + i for i in range(128)]
    cols += [k0 + (i // 64) * 64 + int(ps_[i % 64]) for i in range(128)]
    assert len(cols) == FM_COLS
    cols += list(range(PB0, PB0 + 1792))
    cols += list(range(PH0, PH0 + 1536))
    cols += list(range(PD0 + 640, PD0 + 768))
    assert len(cols) == FM_COLS + TM_COLS
    cols += list(range(0, 4096))
    return np.ascontiguousarray(np.asarray(w_in, np.float32)[:, :, np.asarray(cols)])


def _pin_methods():
    def declare_pin(self):
        L = 2
        self.inp("w_ext", [L, DM, WX_COLS])
        self.inp("rope_m", [2, 32, NLAT])
        self.inp("rope_s", [2, 128, NLAT])
        self.scr("uT", [8, 128, TT], BF16)
        self.scr("cqkvT", [3, 128, TT], BF16)
        self.scr("krT", [32, TT], BF16)
        self.scr("sqT", [4, 128, TT], BF16)
        self.scr("skT", [128, TT], BF16)
        self.scr("pb", [PAD_ROWS, 1792])
        self.scr("ph", [PAD_ROWS, 1536])
        self.scr("pv", [TT, 128])

    def phase_pin(self, l, src):
        P = self.P
        d = self.d
        src_ap, src_name = src
        P.mark()
        NW = FM_COLS + TM_COLS
        w = P.sbuf([128, 8, NW], BF16, "wpin")
        wb = [Buf("wpin%d" % k) for k in range(8)]
        for k in range(8):
            P.dma(w[:, k, :], d["w_ext"][l, k * 128:(k + 1) * 128, 0:NW], writes=[wb[k]], q="pool")
        z = P.sbuf([1, 1792], F32, "zrow")
        P.op("pool", lambda g: g.memset(z[:], 0.0), writes=[z])
        for r in (0, NLAT + 1, TT + 2):
            P.dma(d["pb"][r:r + 1, :], z[:], reads=[z], writes=[self.db("pb", "pad%d" % r)])
            P.dma(d["ph"][r:r + 1, :], z[:, 0:1536], reads=[z], writes=[self.db("ph", "pad%d" % r)])
        xbuf = [P.sbuf([128, 4, DM], F32, "xbuf%d" % i) for i in range(2)]
        xbb = [[Buf("xb%d_%d" % (i, j)) for j in range(4)] for i in range(2)]
        uT = [P.sbuf([128, 8, 512], BF16, "uT%d" % i) for i in range(2)]
        junk = P.sbuf([128, DM], BF16, "junk")
        ss = [P.sbuf([128, 1], F32, "ss%d" % i) for i in range(2)]
        rs = [P.sbuf([128, 1], F32, "rs%d" % i) for i in range(2)]
        xs = [P.sbuf([128, DM], BF16, "xs%d" % i) for i in range(2)]
        tabm = [P.sbuf([32, 2, 512], F32, "tabm%d" % i) for i in range(2)]
        tabs = [P.sbuf([128, 2, 512], F32, "tabs%d" % i) for i in range(2)]
        t1 = [P.sbuf([128, 512], F32, "t1_%d" % i) for i in range(2)]
        t2 = [P.sbuf([128, 512], F32, "t2_%d" % i) for i in range(2)]
        fo = [P.sbuf([128, 512], BF16, "fo%d" % i) for i in range(3)]
        tmo = [P.sbuf([128, TM_COLS], F32, "tmo%d" % i) for i in range(2)]
        groups = [list(range(4 * g, 4 * g + 4)) for g in range(8)] + [[32, 33]]

        def load(gi):
            for i, t in enumerate(groups[gi]):
                xt = Tile(xbuf[gi % 2].t, xbb[gi % 2][i])
                P.dma(xbuf[gi % 2][:, i, :], src_ap[t * 128:(t + 1) * 128, :],
                      reads=[self.db(src_name, t)], writes=[xt])
            if gi < 8:
                P.dma(tabm[gi % 2][:], d["rope_m"][:, :, gi * 512:(gi + 1) * 512].rearrange("c p t -> p c t"),
                      writes=[tabm[gi % 2]])
                P.dma(tabs[gi % 2][:], d["rope_s"][:, :, gi * 512:(gi + 1) * 512].rearrange("c p t -> p c t"),
                      writes=[tabs[gi % 2]])

        load(0)
        nfo = 0
        nev = 0
        for gi, tl in enumerate(groups):
            if gi + 1 < len(groups):
                load(gi + 1)
            n = 128 * len(tl)
            t0 = tl[0] * 128
            lat = gi < 8
            s = 0 if lat else 1
            xb = xbuf[gi % 2]
            u = uT[gi % 2]
            for i, t in enumerate(tl):
                xt = Tile(xb.t, xbb[gi % 2][i])
                self.norm_T(xt, xb[:, i, :], 1, s, u, u[:, :, i * 128:(i + 1) * 128], 0 if i % 2 == 0 else 7,
                            (junk, ss[i % 2], rs[i % 2], xs[i % 2]))
            P.dma(d["uT"][:, :, t0:t0 + n].rearrange("k p t -> p k t"), u[:, :, 0:n], reads=[u],
                  writes=[self.db("uT", gi)])

            def fm_mm(ps, c0, m):
                for k in range(8):
                    P.mm(ps[0:m, 0:n], w[:, k, c0:c0 + m], u[:, k, 0:n], start=(k == 0), stop=(k == 7),
                         reads=[wb[k], u], writes=[ps])

            for c in range(3):
                ps = self.ps[1 + (c % 2) * 2]
                fm_mm(ps, c * 128, 128)
                o = fo[nfo % 3]
                nfo += 1
                P.op("act", lambda g: g.activation(out=o[:, 0:n], in_=ps[:, 0:n], func=AF.Copy),
                     reads=[ps], writes=[o])
                P.dma(d["cqkvT"][c, :, t0:t0 + n], o[:, 0:n], reads=[o], writes=[self.db("cqkvT", (c, gi))])
            roped = [(384, 416, 32, tabm, d["krT"][:, t0:t0 + n], ("krT", gi))]
            for c in range(4):
                roped.append((448 + c * 128, 960 + c * 128, 128, tabs, d["sqT"][c, :, t0:t0 + n], ("sqT", (c, gi))))
            roped.append((1472, 1600, 128, tabs, d["skT"][:, t0:t0 + n], ("skT", gi)))
            for ri, (cx, csw, m, tab, dst, dk) in enumerate(roped):
                psx = self.ps[1 + (ri % 2) * 2]
                fm_mm(psx, cx, m)
                o = fo[nfo % 3]
                nfo += 1
                if lat:
                    pss = self.ps[2 + (ri % 2) * 2]
                    fm_mm(pss, csw, m)
                    tb = tab[gi % 2]
                    a1 = t1[ri % 2]
                    a2 = t2[ri % 2]
                    P.op("dve", lambda g: g.tensor_tensor(out=a1[0:m, :], in0=psx[0:m, :], in1=tb[0:m, 0, :],
                                                          op=ALU.mult), reads=[psx, tb], writes=[a1])
                    P.op("dve", lambda g: g.tensor_tensor(out=a2[0:m, :], in0=pss[0:m, :], in1=tb[0:m, 1, :],
                                                          op=ALU.mult), reads=[pss, tb], writes=[a2])
                    P.op("pool", lambda g: g.tensor_tensor(out=o[0:m, :], in0=a1[0:m, :], in1=a2[0:m, :],
                                                           op=ALU.add), reads=[a1, a2], writes=[o])
                else:
                    P.op("act", lambda g: g.activation(out=o[0:m, 0:n], in_=psx[0:m, 0:n], func=AF.Copy),
                         reads=[psx], writes=[o])
                P.dma(dst, o[0:m, 0:n], reads=[o], writes=[self.db(*dk)])
            for i, t in enumerate(tl):
                st = tmo[i % 2]
                for cb in range(7):
                    c0 = cb * 512
                    cw = min(512, TM_COLS - c0)
                    ps = self.ps[5 + (cb % 2)]
                    for k in range(8):
                        P.mm(ps[:, 0:cw], u[:, k, i * 128:(i + 1) * 128], w[:, k, FM_COLS + c0:FM_COLS + c0 + cw],
                             start=(k == 0), stop=(k == 7), reads=[u, wb[k]], writes=[ps])
                    if nev % 2 == 0:
                        P.op("act", lambda g: g.activation(out=st[:, c0:c0 + cw], in_=ps[:, 0:cw], func=AF.Copy),
                             reads=[ps], writes=[st])
                    else:
                        P.op("dve", lambda g: g.tensor_copy(out=st[:, c0:c0 + cw], in_=ps[:, 0:cw]),
                             reads=[ps], writes=[st])
                    nev += 1
                r0 = tm_row(t * 128)
                P.dma(d["pb"][r0:r0 + 128, :], st[:, 0:1792], reads=[st], writes=[self.db("pb", t)])
                P.dma(d["ph"][r0:r0 + 128, :], st[:, 1792:3328], reads=[st], writes=[self.db("ph", t)])
                P.dma(d["pv"][t * 128:(t + 1) * 128, :], st[:, 3328:3456], reads=[st], writes=[self.db("pv", t)])
        P.release()

    KB.declare_pin = declare_pin
    KB.phase_pin = phase_pin


_pin_methods()


def _attn_methods():
    def declare_attn(self):
        L = 2
        self.inp("mla_nT", [L, 128, 3])
        self.inp("mla_wq2", [L, 256, 8 * 2 * 96])
        self.inp("mla_wk", [L, 128, 512])
        self.inp("mla_wv", [L, 128, 512])
        self.inp("swa_sink", [L, 8])
        self.inp("swa_mask", [2, 128, 128])
        self.scr("yT", [4, 512, TT], BF16)

    def phase_mla(self, l, with_ctx):
        P = self.P
        d = self.d
        P.mark()
        scale = 96.0 ** -0.5
        wq = P.sbuf([128, 2, 8, 2, 96], BF16, "wq")
        for c in range(2):
            P.dma(wq[:, c].rearrange("p h s m -> p (h s m)"), d["mla_wq2"][l, c * 128:(c + 1) * 128, :],
                  writes=[wq], q="pool")
        wk = P.sbuf([128, 8, 64], BF16, "wk")
        P.dma(wk[:].rearrange("p h m -> p (h m)"), d["mla_wk"][l], writes=[wk], q="pool")
        wv = P.sbuf([128, 512], BF16, "wv")
        P.dma(wv[:], d["mla_wv"][l], writes=[wv], q="pool")
        nT = P.sbuf([128, 3], F32, "nT")
        P.dma(nT[:], d["mla_nT"][l], writes=[nT])
        ones_f = P.sbuf([128, 128], F32, "ones_f")
        P.op("pool", lambda g: g.memset(ones_f[:], 1.0), writes=[ones_f])
        cqn = P.sbuf([128, 2, TT], BF16, "cqn")
        ckvn = P.sbuf([128, TT], BF16, "ckvn")
        vaug = P.sbuf([128, NTILE, 8, 128], BF16, "vaug")
        P.op("pool", lambda g: g.memset(vaug[:, :, :, 64:128], 1.0), writes=[vaug])
        groups = [(g * 512, 512) for g in range(8)] + [(NLAT, 256)]
        P.mark()
        xin = [P.sbuf([128, 3, 512], BF16, "xin%d" % i) for i in range(2)]
        sq = [P.sbuf([128, 3, 512], F32, "sq%d" % i) for i in range(2)]
        rsb = [P.sbuf([128, 2, 512], F32, "rsb%d" % i) for i in range(2)]
        for gi, (t0, n) in enumerate(groups):
            xi = xin[gi % 2]
            P.dma(xi[:, :, 0:n], d["cqkvT"][:, :, t0:t0 + n].rearrange("c p t -> p c t"),
                  reads=[self.db("cqkvT", (c, gi)) for c in range(3)], writes=[xi])
            sqi = sq[gi % 2]
            P.op("pool", lambda g: g.tensor_tensor(out=sqi[:, :, 0:n], in0=xi[:, :, 0:n], in1=xi[:, :, 0:n],
                                                   op=ALU.mult), reads=[xi], writes=[sqi])
            psq = self.ps[6]
            psk = self.ps[7]
            for c in range(2):
                P.mm(psq[:, 0:n], ones_f[:], sqi[:, c, 0:n], start=(c == 0), stop=(c == 1),
                     reads=[ones_f, sqi], writes=[psq])
            P.mm(psk[:, 0:n], ones_f[:], sqi[:, 2, 0:n], reads=[ones_f, sqi], writes=[psk])
            r = rsb[gi % 2]
            P.op("act", lambda g: g.activation(out=r[:, 0, 0:n], in_=psq[:, 0:n], func=AF.Sqrt, scale=1.0 / 256,
                                               bias=EPS), reads=[psq], writes=[r])
            P.op("act", lambda g: g.activation(out=r[:, 1, 0:n], in_=psk[:, 0:n], func=AF.Sqrt, scale=1.0 / 128,
                                               bias=EPS), reads=[psk], writes=[r])
            P.op("dve", lambda g: g.reciprocal(out=r[:, :, 0:n], in_=r[:, :, 0:n]), reads=[r], writes=[r])
            for c in range(2):
                P.op("dve", lambda g: g.scalar_tensor_tensor(out=cqn[:, c, t0:t0 + n], in0=xi[:, c, 0:n],
                                                             scalar=nT[:, c:c + 1], in1=r[:, 0, 0:n],
                                                             op0=ALU.mult, op1=ALU.mult),
                     reads=[xi, nT, r], writes=[cqn])
            P.op("dve", lambda g: g.scalar_tensor_tensor(out=ckvn[:, t0:t0 + n], in0=xi[:, 2, 0:n],
                                                         scalar=nT[:, 2:3], in1=r[:, 1, 0:n],
                                                         op0=ALU.mult, op1=ALU.mult),
                 reads=[xi, nT, r], writes=[ckvn])
        P.release()
        for t in range(NTILE):
            ps = self.ps[5 + t % 2]
            P.mm(ps[:], ckvn[:, t * 128:(t + 1) * 128], wv[:], reads=[ckvn, wv], writes=[ps])
            eng = "act" if t % 2 == 0 else "dve"
            if eng == "act":
                P.op("act", lambda g: g.activation(out=vaug[:, t, :, 0:64],
                                                   in_=ps[:].rearrange("p (h m) -> p h m", m=64), func=AF.Copy),
                     reads=[ps], writes=[vaug])
            else:
                P.op("dve", lambda g: g.tensor_copy(out=vaug[:, t, :, 0:64],
                                                    in_=ps[:].rearrange("p (h m) -> p h m", m=64)),
                     reads=[ps], writes=[vaug])
        NQ = TT if with_ctx else NLAT
        KT = [P.sbuf([96, TT], BF16, "KT%d" % i) for i in range(2)]
        QT = [P.sbuf([96, TT], BF16, "QT%d" % i) for i in range(2)]
        tab = [P.sbuf([96, 2, 512], F32, "tab%d" % i) for i in range(2)]
        a1 = [P.sbuf([96, 512], F32, "a1_%d" % i) for i in range(2)]
        a2 = [P.sbuf([96, 512], F32, "a2_%d" % i) for i in range(2)]
        PT = [P.sbuf([128, 512], BF16, "PT%d" % i) for i in range(4)]
        rec = [P.sbuf([64, 512], F32, "rec%d" % i) for i in range(2)]
        yo = [P.sbuf([64, 512], BF16, "yo%d" % i) for i in range(2)]
        npt = 0
        nqg = 0
        for h in range(8):
            kt = KT[h % 2]
            qt = QT[h % 2]
            P.dma(kt[64:96, :], d["krT"], reads=[self.db("krT", gi) for gi in range(9)], writes=[kt])
            for gi, (t0, n) in enumerate(groups):
                lat = gi < 8
                if gi >= 8 and not with_ctx:
                    pass
                pk = self.ps[5]
                P.mm(pk[0:64, 0:n], wk[:, h, :], ckvn[:, t0:t0 + n], reads=[wk, ckvn], writes=[pk])
                P.op("act", lambda g: g.activation(out=kt[0:64, t0:t0 + n], in_=pk[0:64, 0:n], func=AF.Copy),
                     reads=[pk], writes=[kt])
                if gi >= 8 and not with_ctx:
                    continue
                p1 = self.ps[6]
                for c in range(2):
                    P.mm(p1[0:96, 0:n], wq[:, c, h, 0, :], cqn[:, c, t0:t0 + n], start=(c == 0), stop=(c == 1),
                         reads=[wq, cqn], writes=[p1])
                if lat:
                    p2 = self.ps[7]
                    for c in range(2):
                        P.mm(p2[0:96, 0:n], wq[:, c, h, 1, :], cqn[:, c, t0:t0 + n], start=(c == 0),
                             stop=(c == 1), reads=[wq, cqn], writes=[p2])
                    tb = tab[gi % 2]
                    P.dma(tb[64:96, :, :], d["rope_m"][:, :, t0:t0 + n].rearrange("c p t -> p c t"), writes=[tb])
                    P.op("act", lambda g: g.activation(out=qt[0:64, t0:t0 + n], in_=p1[0:64, 0:n], func=AF.Copy),
                         reads=[p1], writes=[qt])
                    b1 = a1[gi % 2]
                    b2 = a2[gi % 2]
                    P.op("dve", lambda g: g.tensor_tensor(out=b1[64:96, :], in0=p1[64:96, :], in1=tb[64:96, 0, :],
                                                          op=ALU.mult), reads=[p1, tb], writes=[b1])
                    P.op("dve", lambda g: g.tensor_tensor(out=b2[64:96, :], in0=p2[64:96, :], in1=tb[64:96, 1, :],
                                                          op=ALU.mult), reads=[p2, tb], writes=[b2])
                    P.op("pool", lambda g: g.tensor_tensor(out=qt[64:96, t0:t0 + n], in0=b1[64:96, :],
                                                           in1=b2[64:96, :], op=ALU.add),
                         reads=[b1, b2], writes=[qt])
                else:
                    P.op("act", lambda g: g.activation(out=qt[0:96, t0:t0 + n], in_=p1[0:96, 0:n], func=AF.Copy),
                         reads=[p1], writes=[qt])
            qgroups = [(g * 512, 512, list(range(NTILE))) for g in range(8)]
            if with_ctx:
                qgroups.append((NLAT, 256, [32, 33]))
            for (q0, n, kbs) in qgroups:
                po = self.ps[3 + nqg % 2]
                nqg += 1
                pend = []

                def pv(item, first, last):
                    kb, pt = item
                    P.mm(po[:, 0:n], vaug[:, kb, h, :], pt[:, 0:n], start=first, stop=last,
                         reads=[vaug, pt], writes=[po])

                for idx, kb in enumerate(kbs):
                    pss = self.ps[npt % 3]
                    pt = PT[npt % 4]
                    npt += 1
                    P.mm(pss[:, 0:n], kt[:, kb * 128:(kb + 1) * 128], qt[:, q0:q0 + n], reads=[kt, qt],
                         writes=[pss])
                    P.op("act", lambda g: g.activation(out=pt[:, 0:n], in_=pss[:, 0:n], func=AF.Exp, scale=scale),
                         reads=[pss], writes=[pt])
                    pend.append((kb, pt))
                    if len(pend) > 2:
                        pv(pend.pop(0), idx == 2, False)
                while pend:
                    first = (len(kbs) - len(pend) == 0)
                    pv(pend.pop(0), first, len(pend) == 0)
                rc = rec[nqg % 2]
                y = yo[nqg % 2]
                P.op("dve", lambda g: g.reciprocal(out=rc[:, 0:n], in_=po[64:128, 0:n]), reads=[po], writes=[rc])
                P.op("dve", lambda g: g.tensor_tensor(out=y[:, 0:n], in0=po[0:64, 0:n], in1=rc[:, 0:n],
                                                      op=ALU.mult), reads=[po, rc], writes=[y])
                P.dma(d["yT"][0, h * 64:(h + 1) * 64, q0:q0 + n], y[:, 0:n], reads=[y],
                      writes=[self.db("yT", (0, h, q0))])
        P.release()

    def phase_swa(self, l, with_ctx):
        P = self.P
        d = self.d
        P.mark()
        scale = 64.0 ** -0.5
        es = P.sbuf([128, 8], F32, "es")
        P.dma(es[:], d["swa_sink"][l:l + 1, :].to_broadcast([128, 8]), writes=[es])
        P.op("act", lambda g: g.activation(out=es[:], in_=es[:], func=AF.Exp), reads=[es], writes=[es])
        msk = P.sbuf([128, 2, 128], BF16, "msk")
        P.dma(msk[:], d["swa_mask"].rearrange("c p t -> p c t"), writes=[msk], q="pool")
        Kk = P.sbuf([64, TT], BF16, "Kk")
        Qk = P.sbuf([64, 4, TT], BF16, "Qk")
        va = P.sbuf([128, NTILE, 128], BF16, "va")
        P.op("pool", lambda g: g.memset(va[:, :, 64:128], 1.0), writes=[va])
        yd = P.sbuf([64, 4, TT], BF16, "yd")
        PT = [P.sbuf([128, 4, 128], BF16, "PT%d" % i) for i in range(6)]
        den = [P.sbuf([64, 4, 128], F32, "den%d" % i) for i in range(2)]
        npt = 0
        nblk = 0
        allsq = [self.db("sqT", (c, gi)) for c in range(4) for gi in range(9)]
        allsk = [self.db("skT", gi) for gi in range(9)]
        allpv = [self.db("pv", t) for t in range(NTILE)]
        for kh in range(2):
            P.dma(Kk[:], d["skT"][kh * 64:(kh + 1) * 64, :], reads=allsk, writes=[Kk])
            for g_ in range(4):
                hh = kh * 4 + g_
                P.dma(Qk[:, g_, :], d["sqT"][hh // 2, (hh % 2) * 64:(hh % 2) * 64 + 64, :], reads=allsq, writes=[Qk])
            P.dma(va[:, :, 0:64], d["pv"].rearrange("(t p) c -> p t c", p=128)[:, :, kh * 64:(kh + 1) * 64],
                  reads=allpv, writes=[va], q="pool")
            nq = NTILE if with_ctx else 32
            for i in range(nq):
                if i < 32:
                    kbs = []
                    if i > 0:
                        kbs.append((i - 1, 0))
                    kbs.append((i, None))
                    if i < 31:
                        kbs.append((i + 1, 1))
                    kbs += [(32, None), (33, None)]
                else:
                    kbs = [(32, None), (33, None)]
                po = self.ps[3 + nblk % 2]
                dn = den[nblk % 2]
                nblk += 1
                sbanks = [0, 1, 2, 5, 6, 7]
                items = []
                for idx, (kb, mk) in enumerate(kbs):
                    pss = self.ps[sbanks[npt % 6]]
                    pt = PT[npt % 6]
                    npt += 1
                    for g_ in range(4):
                        P.mm(pss[:, g_ * 128:(g_ + 1) * 128], Kk[:, kb * 128:(kb + 1) * 128],
                             Qk[:, g_, i * 128:(i + 1) * 128], reads=[Kk, Qk], writes=[pss])
                    items.append((kb, mk, pss, pt))
                for idx, (kb, mk, pss, pt) in enumerate(items):
                    P.op("act", lambda g: g.activation(out=pt[:].rearrange("p g t -> p (g t)"), in_=pss[:],
                                                       func=AF.Exp, scale=scale), reads=[pss], writes=[pt])
                    if mk is not None:
                        P.op("dve", lambda g: g.tensor_tensor(out=pt[:], in0=pt[:],
                                                              in1=msk[:, mk, :].unsqueeze(1).to_broadcast([128, 4, 128]),
                                                              op=ALU.mult), reads=[pt, msk], writes=[pt])
                for idx, (kb, mk, pss, pt) in enumerate(items):
                    P.mm(po[:], va[:, kb, :], pt[:].rearrange("p g t -> p (g t)"), start=(idx == 0),
                         stop=(idx == len(kbs) - 1), reads=[va, pt], writes=[po])
                P.op("dve", lambda g: g.tensor_tensor(out=dn[:], in0=po[64:128, :].rearrange("p (g t) -> p g t", g=4),
                                                      in1=es[64:128, kh * 4:(kh + 1) * 4].unsqueeze(2).to_broadcast([64, 4, 128]),
                                                      op=ALU.add), reads=[po, es], writes=[dn])
                P.op("dve", lambda g: g.reciprocal(out=dn[:], in_=dn[:]), reads=[dn], writes=[dn])
                P.op("dve", lambda g: g.tensor_tensor(out=yd[:, :, i * 128:(i + 1) * 128],
                                                      in0=po[0:64, :].rearrange("p (g t) -> p g t", g=4), in1=dn[:],
                                                      op=ALU.mult), reads=[po, dn], writes=[yd])
            for g_ in range(4):
                hh = kh * 4 + g_
                P.dma(d["yT"][3, hh * 64:(hh + 1) * 64, 0:nq * 128], yd[:, g_, 0:nq * 128], reads=[yd],
                      writes=[self.db("yT", (3, hh))])
        P.release()

    KB.declare_attn = declare_attn
    KB.phase_mla = phase_mla
    KB.phase_swa = phase_swa


_attn_methods()


def _merge_methods():
    def declare_merge(self):
        L = 2
        self.inp("w_branch", [L, 4, 512, DM])
        self.inp("w_out", [L, DM, DM])
        self.inp("b_gateT", [L, 128, 32])

    def phase_merge(self, l, with_ctx, xname="xs"):
        P = self.P
        d = self.d
        P.mark()
        wg = P.sbuf([128, 8, 4096], BF16, "wg")
        wgb = [Buf("wg%d" % k) for k in range(8)]
        for k in range(8):
            P.dma(wg[:, k, :], d["w_ext"][l, k * 128:(k + 1) * 128, WX_G0:WX_G0 + 4096], writes=[wgb[k]], q="pool")
        wbr = P.sbuf([128, 4, 4, DM], BF16, "wbr")
        for br in range(4):
            P.dma(wbr[:, br], d["w_branch"][l, br].rearrange("(kc p) n -> p kc n", p=128), writes=[wbr], q="pool")
        wo = P.sbuf([128, 8, DM], BF16, "wo")
        P.dma(wo[:], d["w_out"][l].rearrange("(k p) n -> p k n", p=128), writes=[wo], q="pool")
        bg = P.sbuf([128, 4, 8], F32, "bg")
        P.dma(bg[:].rearrange("p b o -> p (b o)"), d["b_gateT"][l], writes=[bg])
        gt = [P.sbuf([128, DM], F32, "gt%d" % s) for s in range(2)]
        for s in range(2):
            P.dma(gt[s][:], d["gtrow"][l, 1, s:s + 1, :].to_broadcast([128, DM]),
                  reads=[self.db("gtrow", (l, 1))], writes=[gt[s]])
        groups = [(g * 512, 512) for g in range(8)] + ([(NLAT, 256)] if with_ctx else [])
        uT = [P.sbuf([128, 8, 512], BF16, "uT%d" % i) for i in range(2)]
        yg = [P.sbuf([128, 4, 4, 512], BF16, "yg%d" % i) for i in range(2)]
        mg = P.sbuf([128, 8, 512], BF16, "mg")
        mgb = [Buf("mg%d" % k) for k in range(8)]
        sg = [P.sbuf([128, 512], F32, "sg%d" % i) for i in range(2)]
        tm_ = [P.sbuf([128, 512], F32, "tm%d" % i) for i in range(2)]
        acc = [P.sbuf([128, 512], F32, "acc%d" % i) for i in range(2)]
        xbuf = [P.sbuf([128, DM], F32, "xb%d" % i) for i in range(2)]
        junk = P.sbuf([128, DM], BF16, "junk")
        s2 = [P.sbuf([128, 3], F32, "s2%d" % i) for i in range(2)]
        r2 = [P.sbuf([128, 1], F32, "r2%d" % i) for i in range(2)]
        tmp1 = P.sbuf([128, DM], F32, "tmp")
        tmp = [tmp1, tmp1]
        ally = [b for k, b in self.dbufs.items() if k[0] == "yT"]
        allu = [b for k, b in self.dbufs.items() if k[0] == "uT"]

        def load(gi):
            t0, n = groups[gi]
            P.dma(uT[gi % 2][:, :, 0:n], d["uT"][:, :, t0:t0 + n].rearrange("k p t -> p k t"), reads=allu,
                  writes=[uT[gi % 2]])
            for br in range(4):
                P.dma(yg[gi % 2][:, br, :, 0:n],
                      d["yT"][br].rearrange("(kc p) t -> p kc t", p=128)[:, :, t0:t0 + n], reads=ally,
                      writes=[yg[gi % 2]])

        load(0)
        nx = 0
        for gi, (t0, n) in enumerate(groups):
            if gi + 1 < len(groups):
                load(gi + 1)
            u = uT[gi % 2]
            y = yg[gi % 2]
            s = 0 if gi < 8 else 1
            for oc in range(8):
                ac = acc[oc % 2]
                for br in range(4):
                    psg = self.ps[1 + (br % 2) * 2]
                    psy = self.ps[2 + (br % 2) * 2]
                    c0 = br * 1024 + oc * 128
                    for k in range(8):
                        P.mm(psg[:, 0:n], wg[:, k, c0:c0 + 128], u[:, k, 0:n], start=(k == 0), stop=(k == 7),
                             reads=[wgb[k], u], writes=[psg])
                    for kc in range(4):
                        P.mm(psy[:, 0:n], wbr[:, br, kc, oc * 128:(oc + 1) * 128], y[:, br, kc, 0:n],
                             start=(kc == 0), stop=(kc == 3), reads=[wbr, y], writes=[psy])
                    sgt = sg[br % 2]
                    P.op("act", lambda g: g.activation(out=sgt[:, 0:n], in_=psg[:, 0:n], func=AF.Sigmoid,
                                                       bias=bg[:, br, oc:oc + 1]), reads=[psg, bg], writes=[sgt])
                    if br == 0:
                        P.op("dve", lambda g: g.tensor_tensor(out=ac[:, 0:n], in0=sgt[:, 0:n], in1=psy[:, 0:n],
                                                              op=ALU.mult), reads=[sgt, psy], writes=[ac])
                    else:
                        tt = tm_[br % 2]
                        P.op("dve", lambda g: g.tensor_tensor(out=tt[:, 0:n], in0=sgt[:, 0:n], in1=psy[:, 0:n],
                                                              op=ALU.mult), reads=[sgt, psy], writes=[tt])
                        if br < 3:
                            P.op("pool", lambda g: g.tensor_tensor(out=ac[:, 0:n], in0=ac[:, 0:n], in1=tt[:, 0:n],
                                                                   op=ALU.add), reads=[ac, tt], writes=[ac])
                        else:
                            P.op("pool", lambda g: g.tensor_tensor(out=mg[:, oc, 0:n], in0=ac[:, 0:n],
                                                                   in1=tt[:, 0:n], op=ALU.add),
                                 reads=[ac, tt], writes=[mgb[oc]])
            for i in range(n // 128):
                t = t0 // 128 + i
                xb = xbuf[nx % 2]
                P.dma(xb[:], d[xname][t * 128:(t + 1) * 128, :], reads=[self.db(xname, t)], writes=[xb])
                for h in range(2):
                    po = self.ps[5 + h]
                    for k in range(8):
                        P.mm(po[:], mg[:, k, i * 128:(i + 1) * 128], wo[:, k, h * 512:(h + 1) * 512],
                             start=(k == 0), stop=(k == 7), reads=[mgb[k], wo], writes=[po])
                self.norm_res_out([5, 6], xb, xb[:], gt[s], (junk, s2[nx % 2], r2[nx % 2], tmp[nx % 2]),
                                  d[xname][t * 128:(t + 1) * 128, :], self.db(xname, t))
                nx += 1
        P.release()

    KB.declare_merge = declare_merge
    KB.phase_merge = phase_merge


_merge_methods()


def hy_tables(n):
    M = 2 * n
    S1 = M // 64
    T1 = n // 64
    s1 = np.arange(S1)[:, None, None]
    s2 = np.arange(64)[None, :, None]
    f1 = np.arange(S1)[None, None, :]
    ang = 2.0 * np.pi * ((f1 * (64 * s1 + s2)) % M) / M
    W1 = np.stack([np.cos(ang), -np.sin(ang)], axis=2).astype(np.float32)
    s2v = np.arange(64)[:, None]
    f2v = np.arange(64)[None, :]
    th = 2.0 * np.pi * ((s2v * f2v) % 64) / 64
    c, s = np.cos(th), np.sin(th)
    D2 = np.block([[c, -s], [s, c]]).astype(np.float32)
    D2sw = np.concatenate([D2[:, 64:], D2[:, :64]], axis=1)
    E = np.block([[c, s], [-s, c]]).astype(np.float32)
    f1v = np.arange(S1)[:, None, None]
    t2v = np.arange(64)[None, :, None]
    t1v = np.arange(T1)[None, None, :]
    psi = 2.0 * np.pi * ((f1v * (64 * t1v + t2v)) % M) / M
    W3 = np.stack([np.cos(psi), -np.sin(psi)], axis=2).astype(np.float32)
    return dict(W1=W1, D2=D2, D2sw=D2sw, E=E, W3=W3, S1=S1, T1=T1, M=M)


def hy_feats(n):
    M = 2 * n
    f32 = np.float32
    t = np.linspace(0.0, 1.0, n, dtype=f32)
    bands = np.linspace(1e-4, 15.0, 16, dtype=f32)
    ang = (f32(2.0 * np.pi / n) * np.arange(n, dtype=f32)[:, None] * bands[None, :]).astype(f32)
    feats = np.concatenate([t[:, None], np.cos(ang), -np.sin(ang)], axis=-1).astype(f32)
    deltas = np.abs(np.linspace(np.log(1e-2) / 1.5, np.log(1e-2) / 0.3, 512, dtype=f32)).astype(f32)
    win = np.exp(-t[:, None] * deltas[None, :]).astype(f32)
    idx = np.zeros(M, np.int64)
    idx[:n] = np.arange(n)
    idx[n + 1:] = n - np.arange(1, n)
    fK = feats[idx].copy()
    wK = win[idx].copy()
    fK[n] = feats[0]
    wK[n] = win[0]
    return np.ascontiguousarray(fK.T), np.ascontiguousarray(wK)


def _hyena_methods():
    TWO_PI = 2.0 * np.pi

    def declare_hyena(self):
        L = 2
        self.inp("hyena_conv", [L, 3, 1536])
        self.inp("hyena_conv_b", [L, 1536])
        self.inp("hyena_w1", [L, 33, 64])
        self.inp("hyena_w2", [L, 64, 64])
        self.inp("hyena_w3", [L, 64, 2048])
        self.inp("hyT", [L, 64, 4])
        self.inp("hyena_bias", [L, 2, 512])
        self.inp("hy_D", [3, 128, 128])
        self.inp("hyL_W1", [128, 64 * 2 * 128])
        self.inp("hyL_W3", [128, 64 * 2 * 64])
        self.inp("hyL_fK", [33, 8192])
        self.inp("hyL_wK", [8192, 512])
        self.inp("hyC_W1", [8, 64 * 2 * 8])
        self.inp("hyC_W3", [8, 64 * 2 * 4])
        self.inp("hyC_fK", [33, 512])
        self.inp("hyC_wK", [512, 512])
        self.scr("hcs", [TT, 1536])
        self.scr("kbuf", [2, 8192, 512], BF16)
        self.scr("Bd", [128, 128, 512], BF16)
        self.scr("Dd", [128, 128, 512], BF16)
        self.scr("KAB_L", [2, 128, 2, 128, 512], BF16)
        self.scr("KAB_C", [2, 8, 2, 128, 512], BF16)
        self.scr("zt1", [TT, 512])
        self.scr("zt2", [TT, 512])

    def hy_shortconv(self, l, ntiles):
        P = self.P
        d = self.d
        P.mark()
        ck = P.sbuf([128, 3, 1536], F32, "ck")
        P.dma(ck[:].rearrange("p a c -> p (a c)"),
              d["hyena_conv"][l:l + 1].rearrange("o a c -> o (a c)").to_broadcast([128, 4608]), writes=[ck])
        cb = P.sbuf([128, 1536], F32, "cb")
        P.dma(cb[:], d["hyena_conv_b"][l:l + 1, :].to_broadcast([128, 1536]), writes=[cb])
        bufs = [[P.sbuf([128, 1536], F32, "sc%d_%d" % (i, j)) for j in range(3)] for i in range(3)]
        allph = [b for k, b in self.dbufs.items() if k[0] == "ph"]
        for t in range(ntiles):
            r0 = tm_row(t * 128)
            pv_, cu, nx = bufs[t % 3]
            P.dma(pv_[:], d["ph"][r0 - 1:r0 + 127, :], reads=allph, writes=[pv_])
            P.dma(cu[:], d["ph"][r0:r0 + 128, :], reads=allph, writes=[cu])
            P.dma(nx[:], d["ph"][r0 + 1:r0 + 129, :], reads=allph, writes=[nx])
            P.op("dve", lambda g: g.tensor_tensor(out=pv_[:], in0=pv_[:], in1=ck[:, 0, :], op=ALU.mult),
                 reads=[pv_, ck], writes=[pv_])
            P.op("pool", lambda g: g.tensor_tensor(out=cu[:], in0=cu[:], in1=ck[:, 1, :], op=ALU.mult),
                 reads=[cu, ck], writes=[cu])
            P.op("dve", lambda g: g.tensor_tensor(out=nx[:], in0=nx[:], in1=ck[:, 2, :], op=ALU.mult),
                 reads=[nx, ck], writes=[nx])
            P.op("pool", lambda g: g.tensor_tensor(out=cu[:], in0=cu[:], in1=pv_[:], op=ALU.add),
                 reads=[cu, pv_], writes=[cu])
            P.op("dve", lambda g: g.tensor_tensor(out=nx[:], in0=nx[:], in1=cb[:], op=ALU.add),
                 reads=[nx, cb], writes=[nx])
            P.op("pool", lambda g: g.tensor_tensor(out=cu[:], in0=cu[:], in1=nx[:], op=ALU.add),
                 reads=[cu, nx], writes=[cu])
            P.dma(d["hcs"][t * 128:(t + 1) * 128, :], cu[:], reads=[cu], writes=[self.db("hcs", t)])
        P.release()

    def hy_load_tabs(self, n):
        P = self.P
        d = self.d
        pre = "hyL" if n == NLAT else "hyC"
        S1 = 2 * n // 64
        T1 = n // 64
        W1 = P.sbuf([S1, 64, 2, S1], BF16, "W1")
        P.dma(W1[:].rearrange("p a b c -> p (a b c)"), d[pre + "_W1"], writes=[W1], q="pool")
        W3 = P.sbuf([S1, 64, 2, T1], BF16, "W3")
        P.dma(W3[:].rearrange("p a b c -> p (a b c)"), d[pre + "_W3"], writes=[W3], q="pool")
        Dm = P.sbuf([128, 3, 128], BF16, "Dm")
        P.dma(Dm[:], d["hy_D"].rearrange("a p c -> p a c"), writes=[Dm], q="pool")
        return dict(W1=W1, W3=W3, Dm=Dm, S1=S1, T1=T1, n=n)

    def hy_stage1(self, tb, src_ap, nz, src_reads, cast):
        P = self.P
        d = self.d
        S1 = tb["S1"]
        P.mark()
        U = P.sbuf([nz, 64, 512], BF16, "U")
        P.dma(U[:], src_ap.rearrange("(a s) c -> a s c", s=64), reads=src_reads, writes=[U],
              q=("pool" if cast else "sp"))
        bo = [P.sbuf([S1, 2, 512], BF16, "bo%d" % i) for i in range(4)]
        bdv = d["Bd"].rearrange("(r s) f c -> s f r c", r=2)
        for s2 in range(64):
            o = bo[s2 % 4]
            for ri in range(2):
                ps = self.ps[(2 * s2 + ri) % 4]
                P.mm(ps[0:S1, :], tb["W1"][0:nz, s2, ri, :], U[:, s2, :], reads=[tb["W1"], U], writes=[ps])
                if ri == 0:
                    P.op("act", lambda g: g.activation(out=o[:, ri, :], in_=ps[0:S1, :], func=AF.Copy),
                         reads=[ps], writes=[o])
                else:
                    P.op("dve", lambda g: g.tensor_copy(out=o[:, ri, :], in_=ps[0:S1, :]), reads=[ps], writes=[o])
            P.dma(bdv[s2, 0:S1], o[:], reads=[o], writes=[self.db("Bd", s2)])
        P.release()

    def hy_stage2(self, tb, cb, cb2=None):
        P = self.P
        d = self.d
        S1 = tb["S1"]
        allbd = [self.db("Bd", s2) for s2 in range(64)]
        FG = 4
        bins = [P.sbuf([128, FG, 512], BF16, "bin%d" % i) for i in range(3)]
        prev = [None]
        for fg in range(S1 // FG):
            b = bins[fg % 3]
            P.dma(b[:], d["Bd"][:, fg * FG:(fg + 1) * FG, :], reads=allbd, writes=[b])
            for j in range(FG):
                f1 = fg * FG + j
                p1 = self.ps[(f1 % 3) * 2]
                p2 = self.ps[(f1 % 3) * 2 + 1]
                P.mm(p1[:], tb["Dm"][:, 0, :], b[:, j, :], reads=[tb["Dm"], b], writes=[p1])
                P.mm(p2[:], tb["Dm"][:, 1, :], b[:, j, :], reads=[tb["Dm"], b], writes=[p2])
                cb(f1, p1, p2)
                if cb2 is not None and prev[0] is not None:
                    cb2(prev[0])
                prev[0] = f1
        if cb2 is not None and prev[0] is not None:
            cb2(prev[0])

    def hy_filters(self, l, n, kab_name):
        P = self.P
        d = self.d
        pre = "hyL" if n == NLAT else "hyC"
        M = 2 * n
        P.mark()
        tb = self.hy_load_tabs(n)
        hyT = P.sbuf([64, 4], F32, "hyT")
        P.dma(hyT[:], d["hyT"][l], writes=[hyT])
        sc = P.sbuf([64, 4], F32, "hsc")
        for j in range(2):
            P.op("dve", lambda g: g.tensor_scalar(out=sc[:, 2 * j:2 * j + 1], in0=hyT[:, j:j + 1],
                                                  scalar1=1.0 / TWO_PI, scalar2=None, op0=ALU.mult),
                 reads=[hyT], writes=[sc])
            P.op("dve", lambda g: g.tensor_tensor(out=sc[:, 2 * j + 1:2 * j + 2], in0=hyT[:, 2 + j:3 + j],
                                                  in1=sc[:, 2 * j:2 * j + 1], op=ALU.mult),
                 reads=[hyT, sc], writes=[sc])
            P.op("dve", lambda g: g.tensor_scalar(out=sc[:, 2 * j + 1:2 * j + 2], in0=sc[:, 2 * j + 1:2 * j + 2],
                                                  scalar1=64.0, scalar2=None, op0=ALU.add),
                 reads=[sc], writes=[sc])
        w1 = P.sbuf([33, 64], F32, "hw1")
        P.dma(w1[:], d["hyena_w1"][l], writes=[w1])
        w2 = P.sbuf([64, 64], F32, "hw2")
        P.dma(w2[:], d["hyena_w2"][l], writes=[w2])
        w3 = P.sbuf([64, 2048], F32, "hw3")
        P.dma(w3[:], d["hyena_w3"][l], writes=[w3])
        ones_f = P.sbuf([128, 128], F32, "ones_f")
        P.op("pool", lambda g: g.memset(ones_f[:], 1.0), writes=[ones_f])
        G2T = P.sbuf([64, M], F32, "G2T")
        rn = [P.sbuf([128, 512], F32, "rn%d" % o) for o in range(2)]
        P.mark()
        fk = [P.sbuf([33, 512], F32, "fk%d" % i) for i in range(2)]
        vt = [P.sbuf([64, 512], F32, "vt%d" % i) for i in range(2)]
        vi = [P.sbuf([64, 512], I32, "vi%d" % i) for i in range(2)]
        vf = [P.sbuf([64, 512], F32, "vf%d" % i) for i in range(2)]
        g1 = [P.sbuf([64, 512], F32, "g1%d" % i) for i in range(2)]
        cnt = [0]

        def sin_reduce(ps, j, out_ap, out_t):
            i = cnt[0] % 2
            cnt[0] += 1
            P.op("dve", lambda g: g.tensor_scalar(out=vt[i][:], in0=ps[0:64, :], scalar1=sc[:, 2 * j:2 * j + 1],
                                                  scalar2=sc[:, 2 * j + 1:2 * j + 2], op0=ALU.mult, op1=ALU.add),
                 reads=[ps, sc], writes=[vt[i]])
            P.op("dve", lambda g: g.tensor_copy(out=vi[i][:], in_=vt[i][:]), reads=[vt[i]], writes=[vi[i]])
            P.op("pool", lambda g: g.tensor_copy(out=vf[i][:], in_=vi[i][:]), reads=[vi[i]], writes=[vf[i]])
            P.op("pool", lambda g: g.tensor_tensor(out=vt[i][:], in0=vt[i][:], in1=vf[i][:], op=ALU.subtract),
                 reads=[vt[i], vf[i]], writes=[vt[i]])
            P.op("act", lambda g: g.activation(out=out_ap, in_=vt[i][:], func=AF.Sin, scale=TWO_PI),
                 reads=[vt[i]], writes=[out_t])

        for cbk in range(M // 512):
            f = fk[cbk % 2]
            P.dma(f[:], d[pre + "_fK"][:, cbk * 512:(cbk + 1) * 512], writes=[f])
            ps = self.ps[cbk % 2]
            P.mm(ps[0:64, :], w1[:], f[:], reads=[w1, f], writes=[ps])
            gg = g1[cbk % 2]
            sin_reduce(ps, 0, gg[:], gg)
            ps2 = self.ps[2 + cbk % 2]
            P.mm(ps2[0:64, :], w2[:], gg[:], reads=[w2, gg], writes=[ps2])
            sin_reduce(ps2, 1, G2T[:, cbk * 512:(cbk + 1) * 512], G2T)
        P.release()
        P.mark()
        wk = [P.sbuf([128, 512], F32, "wk%d" % i) for i in range(3)]
        kbt = [P.sbuf([128, 512], F32, "kbt%d" % i) for i in range(4)]
        ab = [P.sbuf([128, 512], F32, "ab%d" % i) for i in range(4)]
        kbo = [P.sbuf([128, 512], BF16, "kbo%d" % i) for i in range(4)]
        nlt = M // 128
        c = 0
        for lt in range(nlt):
            dirn = 0 if lt < n // 128 else 1
            w = wk[lt % 3]
            P.dma(w[:], d[pre + "_wK"][lt * 128:(lt + 1) * 128, :], writes=[w])
            for o in range(2):
                ps = self.ps[c % 4]
                kt_ = kbt[c % 4]
                a = ab[c % 4]
                ko = kbo[c % 4]
                c += 1
                P.mm(ps[:], G2T[:, lt * 128:(lt + 1) * 128], w3[:, o * 1024 + dirn * 512:o * 1024 + dirn * 512 + 512],
                     reads=[G2T, w3], writes=[ps])
                P.op("dve", lambda g: g.tensor_tensor(out=kt_[:], in0=ps[:], in1=w[:], op=ALU.mult),
                     reads=[ps, w], writes=[kt_])
                P.op("act", lambda g: g.activation(out=a[:], in_=kt_[:], func=AF.Abs), reads=[kt_], writes=[a])
                P.mm(self.ps[6 + o][:], ones_f[:], a[:], start=(lt == 0), stop=(lt == nlt - 1),
                     reads=[ones_f, a], writes=[self.ps[6 + o]])
                if lt == n // 128:
                    P.op("pool", lambda g: g.memset(kt_[0:1, :], 0.0), reads=[kt_], writes=[kt_])
                P.op("pool", lambda g: g.tensor_copy(out=ko[:], in_=kt_[:]), reads=[kt_], writes=[ko])
                P.dma(d["kbuf"][o, lt * 128:(lt + 1) * 128, :], ko[:], reads=[ko], writes=[self.db("kbuf", (o, lt))])
        for o in range(2):
            P.op("dve", lambda g: g.tensor_scalar(out=rn[o][:], in0=self.ps[6 + o][:], scalar1=float(M), scalar2=None,
                                                  op0=ALU.mult), reads=[self.ps[6 + o]], writes=[rn[o]])
            P.op("dve", lambda g: g.reciprocal(out=rn[o][:], in_=rn[o][:]), reads=[rn[o]], writes=[rn[o]])
        P.release()
        for o in range(2):
            allkb = [self.db("kbuf", (o, lt)) for lt in range(nlt)]
            self.hy_stage1(tb, d["kbuf"][o, 0:M, :], tb["S1"], allkb, False)
            P.mark()
            kab = [P.sbuf([128, 2, 512], BF16, "kab%d" % i) for i in range(4)]

            def cbf(f1, p1, p2):
                k = kab[f1 % 4]
                r = rn[o]
                P.op("dve", lambda g: g.tensor_tensor(out=k[0:64, 0, :], in0=p1[0:64, :], in1=r[0:64, :], op=ALU.mult),
                     reads=[p1, r], writes=[k])
                P.op("dve", lambda g: g.tensor_tensor(out=k[64:128, 0, :], in0=p2[64:128, :], in1=r[64:128, :],
                                                      op=ALU.mult), reads=[p2, r], writes=[k])
                P.op("dve", lambda g: g.scalar_tensor_tensor(out=k[0:64, 1, :], in0=p2[0:64, :], scalar=-1.0,
                                                             in1=r[0:64, :], op0=ALU.mult, op1=ALU.mult),
                     reads=[p2, r], writes=[k])
                P.op("dve", lambda g: g.tensor_tensor(out=k[64:128, 1, :], in0=p1[64:128, :], in1=r[64:128, :],
                                                      op=ALU.mult), reads=[p1, r], writes=[k])
                P.dma(d[kab_name][o, f1].rearrange("a p c -> p a c"), k[:], reads=[k],
                      writes=[self.db(kab_name, (o, f1))])

            self.hy_stage2(tb, cbf)
            P.release()
        P.release()

    def hy_conv(self, l, tb, o, kab_name, src_ap, src_reads, gate_ap, gate_reads, dst_ap, dst_name):
        P = self.P
        d = self.d
        n, S1, T1 = tb["n"], tb["S1"], tb["T1"]
        self.hy_stage1(tb, src_ap, S1 // 2, src_reads, True)
        P.mark()
        kab = [P.sbuf([128, 2, 512], BF16, "kab%d" % i) for i in range(4)]
        ta = [P.sbuf([128, 512], F32, "ta%d" % i) for i in range(4)]
        tb2 = [P.sbuf([128, 512], F32, "tb%d" % i) for i in range(4)]
        yh = [P.sbuf([128, 512], BF16, "yh%d" % i) for i in range(4)]
        do = [P.sbuf([128, 512], BF16, "do%d" % i) for i in range(4)]
        allk = [self.db(kab_name, (o, f1)) for f1 in range(S1)]

        def cbf(f1, p1, p2):
            i = f1 % 4
            k = kab[i]
            P.dma(k[:], d[kab_name][o, f1].rearrange("a p c -> p a c"), reads=allk, writes=[k])
            P.op("dve", lambda g: g.tensor_tensor(out=ta[i][:], in0=p1[:], in1=k[:, 0, :], op=ALU.mult),
                 reads=[p1, k], writes=[ta[i]])
            P.op("dve", lambda g: g.tensor_tensor(out=tb2[i][:], in0=p2[:], in1=k[:, 1, :], op=ALU.mult),
                 reads=[p2, k], writes=[tb2[i]])
            P.op("pool", lambda g: g.tensor_tensor(out=yh[i][:], in0=ta[i][:], in1=tb2[i][:], op=ALU.add),
                 reads=[ta[i], tb2[i]], writes=[yh[i]])

        def cbf2(f1):
            i = f1 % 4
            pd_ = self.ps[6 + f1 % 2]
            P.mm(pd_[:], tb["Dm"][:, 2, :], yh[i][:], reads=[tb["Dm"], yh[i]], writes=[pd_])
            P.op("act", lambda g: g.activation(out=do[i][:], in_=pd_[:], func=AF.Copy), reads=[pd_], writes=[do[i]])
            P.dma(d["Dd"][:, f1, :], do[i][:], reads=[do[i]], writes=[self.db("Dd", f1)])

        self.hy_stage2(tb, cbf, cbf2)
        P.release()
        P.mark()
        bias = P.sbuf([64, 512], F32, "hbias")
        P.dma(bias[:], d["hyena_bias"][l, o:o + 1, :].to_broadcast([64, 512]), writes=[bias])
        din = [P.sbuf([S1, 2, 512], BF16, "din%d" % i) for i in range(4)]
        gs = [P.sbuf([T1, 512], F32, "gs%d" % i) for i in range(4)]
        us = [P.sbuf([T1, 512], F32, "us%d" % i) for i in range(4)]
        zo = [P.sbuf([T1, 512], F32, "zo%d" % i) for i in range(4)]
        alld = [self.db("Dd", f1) for f1 in range(S1)]
        ddv = d["Dd"].rearrange("(r t) f c -> t f r c", r=2)
        gv = gate_ap.rearrange("(a s) c -> s a c", s=64)
        uv = src_ap.rearrange("(a s) c -> s a c", s=64)
        dv = dst_ap.rearrange("(a s) c -> s a c", s=64)
        for t2 in range(64):
            i = t2 % 4
            P.dma(din[i][:], ddv[t2, 0:S1], reads=alld, writes=[din[i]])
            P.dma(gs[i][:], gv[t2], reads=gate_reads, writes=[gs[i]])
            P.dma(us[i][:], uv[t2], reads=src_reads, writes=[us[i]])
            py = self.ps[6 + i % 2]
            P.mm(py[0:T1, :], tb["W3"][:, t2, 0, :], din[i][:, 0, :], start=True, stop=False,
                 reads=[tb["W3"], din[i]], writes=[py])
            P.mm(py[0:T1, :], tb["W3"][:, t2, 1, :], din[i][:, 1, :], start=False, stop=True,
                 reads=[tb["W3"], din[i]], writes=[py])
            P.op("pool", lambda g: g.tensor_tensor(out=us[i][:], in0=us[i][:], in1=bias[0:T1, :], op=ALU.mult),
                 reads=[us[i], bias], writes=[us[i]])
            P.op("dve", lambda g: g.tensor_tensor(out=us[i][:], in0=us[i][:], in1=py[0:T1, :], op=ALU.add),
                 reads=[us[i], py], writes=[us[i]])
            P.op("pool", lambda g: g.tensor_tensor(out=zo[i][:], in0=us[i][:], in1=gs[i][:], op=ALU.mult),
                 reads=[us[i], gs[i]], writes=[zo[i]])
            P.dma(dv[t2], zo[i][:], reads=[zo[i]], writes=[self.db(dst_name, ("t2", t2, n))])
        P.release()

    def phase_hyena(self, l, with_ctx):
        P = self.P
        d = self.d
        self.hy_shortconv(l, NTILE if with_ctx else 32)
        self.hy_filters(l, NLAT, "KAB_L")
        if with_ctx:
            self.hy_filters(l, NCTX, "KAB_C")
        segs = [(NLAT, 0, "KAB_L")] + ([(NCTX, NLAT, "KAB_C")] if with_ctx else [])
        for (n, r0, kn) in segs:
            P.mark()
            tb = self.hy_load_tabs(n)
            hcs = [b for k, b in self.dbufs.items() if k[0] == "hcs"]
            self.hy_conv(l, tb, 0, kn, d["hcs"][r0:r0 + n, 0:512], hcs, d["hcs"][r0:r0 + n, 512:1024], hcs,
                         d["zt1"][r0:r0 + n, :], "zt1")
            z1 = [b for k, b in self.dbufs.items() if k[0] == "zt1"]
            self.hy_conv(l, tb, 1, kn, d["zt1"][r0:r0 + n, :], z1, d["hcs"][r0:r0 + n, 1024:1536], hcs,
                         d["zt2"][r0:r0 + n, :], "zt2")
            P.release()
        P.mark()
        zin = [P.sbuf([128, 512], F32, "zin%d" % i) for i in range(3)]
        zT = [P.sbuf([128, 4, 128], BF16, "zT%d" % i) for i in range(3)]
        z2 = [b for k, b in self.dbufs.items() if k[0] == "zt2"]
        yv = d["yT"][2].rearrange("(kc p) t -> p kc t", p=128)
        for t in range(NTILE if with_ctx else 32):
            zi = zin[t % 3]
            P.dma(zi[:], d["zt2"][t * 128:(t + 1) * 128, :], reads=z2, writes=[zi])
            ps = self.ps[t % 2]
            for kc in range(4):
                P.op("pe", lambda g: g.transpose(ps[:, kc * 128:(kc + 1) * 128], zi[:, kc * 128:(kc + 1) * 128],
                                                 self.ident_f[:]), reads=[zi, self.ident_f], writes=[ps])
            P.op("act", lambda g: g.activation(out=zT[t % 3][:].rearrange("p a b -> p (a b)"), in_=ps[:], func=AF.Copy),
                 reads=[ps], writes=[zT[t % 3]])
            P.dma(yv[:, :, t * 128:(t + 1) * 128], zT[t % 3][:], reads=[zT[t % 3]], writes=[self.db("yT", (2, t))])
        P.release()

    KB.declare_hyena = declare_hyena
    KB.hy_shortconv = hy_shortconv
    KB.hy_load_tabs = hy_load_tabs
    KB.hy_stage1 = hy_stage1
    KB.hy_stage2 = hy_stage2
    KB.hy_filters = hy_filters
    KB.hy_conv = hy_conv
    KB.phase_hyena = phase_hyena


_hyena_methods()


def rwkv_tables():
    idx = np.arange(128)
    out = np.zeros((2, 6, 128, 128), np.float32)
    for dd in range(2):
        incl = (idx[:, None] <= idx[None, :]) if dd == 0 else (idx[:, None] >= idx[None, :])
        incl = incl.astype(np.float32)
        ref = 63 if dd == 0 else 64
        out[dd, 0] = incl
        out[dd, 1] = incl - incl[:, ref:ref + 1]
        out[dd, 2] = 1.0 - incl
        out[dd, 3] = incl - np.eye(128, dtype=np.float32)
        out[dd, 4] = incl
        out[dd, 5] = out[dd, 3].T
    return out


def _rwkv_methods():
    def declare_rwkv(self):
        L = 2
        self.inp("rwkv_mu", [L, 2, 1792])
        self.inp("rwkv_kvec", [L, 2, 512])
        self.inp("rwkv_lnp", [L, 3, 512])
        self.inp("rwkv_wupA", [L, 65, 1024])
        self.inp("rwkv_aupA", [L, 65, 1024])
        self.inp("rwkv_g_up", [L, 128, 512])
        self.inp("rw_tri", [2, 6, 128, 128])
        self.scr("yf", [TT, 512])

    def phase_rwkv(self, l, with_ctx):
        P = self.P
        d = self.d
        idb = self.ident_b
        idf = self.ident_f
        P.mark()

        def dve(fn, r, w):
            return P.op("dve", fn, reads=r, writes=w)

        def act(fn, r, w):
            return P.op("act", fn, reads=r, writes=w)

        def pool(fn, r, w):
            return P.op("pool", fn, reads=r, writes=w)

        def T32(name, shape=(128, 512)):
            return P.sbuf(list(shape), F32, name)

        def T16(name, shape=(128, 512)):
            return P.sbuf(list(shape), BF16, name)

        mu = T32("mu", (128, 3, 1792))
        for j in range(2):
            P.dma(mu[:, 1 + j, :], d["rwkv_mu"][l, j:j + 1, :].to_broadcast([128, 1792]), writes=[mu])
        dve(lambda g: g.tensor_tensor(out=mu[:, 0, :], in0=mu[:, 1, :], in1=mu[:, 2, :], op=ALU.add), [mu], [mu])
        dve(lambda g: g.tensor_scalar(out=mu[:, 0, :], in0=mu[:, 0, :], scalar1=-1.0, scalar2=1.0, op0=ALU.mult,
                                      op1=ALU.add), [mu], [mu])
        kv = T32("kv", (128, 3, 512))
        for j in range(2):
            P.dma(kv[:, j, :], d["rwkv_kvec"][l, j:j + 1, :].to_broadcast([128, 512]), writes=[kv])
        dve(lambda g: g.tensor_scalar(out=kv[:, 2, :], in0=kv[:, 1, :], scalar1=-1.0, scalar2=1.0, op0=ALU.mult,
                                      op1=ALU.add), [kv], [kv])
        lnp = T32("lnp", (128, 3, 512))
        P.dma(lnp[:].rearrange("p a c -> p (a c)"),
              d["rwkv_lnp"][l:l + 1].rearrange("o a c -> o (a c)").to_broadcast([128, 1536]), writes=[lnp])
        wupA = T16("wupA", (65, 2, 512))
        P.dma(wupA[:].rearrange("p a c -> p (a c)"), d["rwkv_wupA"][l], writes=[wupA], q="pool")
        aupA = T16("aupA", (65, 2, 512))
        P.dma(aupA[:].rearrange("p a c -> p (a c)"), d["rwkv_aupA"][l], writes=[aupA], q="pool")
        gup = T16("gup", (128, 512))
        P.dma(gup[:], d["rwkv_g_up"][l], writes=[gup], q="pool")
        tri = T32("tri", (128, 2, 6, 128))
        P.dma(tri[:], d["rw_tri"].rearrange("a b p c -> p a b c"), writes=[tri])
        onec = T32("onec", (128, 1))
        pool(lambda g: g.memset(onec[:], 1.0), [], [onec])
        TWA = T16("TWA", (65, 128))
        ALA = T16("ALA", (65, 128))
        pool(lambda g: g.memset(TWA[:], 1.0), [], [TWA])
        pool(lambda g: g.memset(ALA[:], 1.0), [], [ALA])
        cur = [T32("cur%d" % i, (128, 1792)) for i in range(2)]
        prv = [T32("prv%d" % i, (128, 1792)) for i in range(2)]
        nxt = [T32("nxt%d" % i, (128, 1792)) for i in range(2)]
        kk = T32("kk")
        sq = T32("sq")
        s8 = T32("s8", (128, 8))
        r8 = T32("r8", (128, 8))
        vbf = T16("vbf")
        tw = T16("tw", (128, 64))
        al = T16("al", (128, 64))
        sgl = T16("sgl", (128, 128))
        sglT = T16("sglT", (128, 128))
        lw = T32("lw")
        av = T32("av")
        tt_ = T32("tt")
        kd = T32("kd")
        kd0 = T32("kd0")
        bb = T32("bb")
        eW, eWi, eLu, eD, elw, eWx, eLux = [T32(nm) for nm in ("eW", "eWi", "eLu", "eD", "elw", "eWx", "eLux")]
        rt, zt, bt, kt, bp, kp, ru, zu = [T16(nm) for nm in ("rt", "zt", "bt", "kt", "bp", "kp", "ru", "zu")]
        RTf, ZTf, BTf, KTf = [T16(nm, (64, 8, 128)) for nm in ("RTf", "ZTf", "BTf", "KTf")]
        Xs = [T16("Xs%d" % i, (128, 8, 128)) for i in range(2)]
        XTs = [T16("XTs%d" % i, (128, 8, 128)) for i in range(2)]
        TTs = [T16("TTs%d" % i, (128, 8, 128)) for i in range(2)]
        AzkT, ArbT, ArkT = [T16(nm, (128, 8, 128)) for nm in ("AzkT", "ArbT", "ArkT")]
        Zp, Gm, U0 = [T16(nm) for nm in ("Zp", "Gm", "U0")]
        Y0 = T32("Y0")
        RpT = T16("RpT", (64, 8, 128))
        Mm = T32("Mm", (64, 8, 64))
        NTt = T32("NTt", (64, 8, 64))
        STf = T32("STf", (64, 8, 64))
        STb = T16("STb", (64, 8, 64))
        WC = T32("WC", (64, 8))
        Yt = T32("Yt")
        yfl = T32("yfl")
        m8 = T32("m8", (128, 8))
        v8 = T32("v8", (128, 8))
        b8 = T32("b8", (128, 8))
        yc = T32("yc")
        ob = T16("ob", (128, 4, 128))
        allpb = [b for k, b in self.dbufs.items() if k[0] == "pb"]
        ps = self.ps

        def view8(ap):
            return ap.rearrange("p (h m) -> p h m", m=64)

        def b8c(t8):
            return t8[:].unsqueeze(2).to_broadcast([128, 8, 64])

        for pss in range(2):
            dd = pss
            order = [32, 33] + list(range(32)) if dd == 0 else [33, 32] + list(range(31, -1, -1))
            pool(lambda g: g.memset(STf[:], 0.0), [], [STf])
            pool(lambda g: g.memset(STb[:], 0.0), [], [STb])
            for ti, t in enumerate(order):
                need_y = with_ctx or t < 32
                r0 = tm_row(t * 128)
                cu, pv_, nx = cur[ti % 2], prv[ti % 2], nxt[ti % 2]

                def loads(tj, tn):
                    rr = tm_row(tn * 128)
                    P.dma(cur[tj % 2][:], d["pb"][rr:rr + 128, :], reads=allpb, writes=[cur[tj % 2]])
                    P.dma(prv[tj % 2][:], d["pb"][rr - 1:rr + 127, :], reads=allpb, writes=[prv[tj % 2]])
                    P.dma(nxt[tj % 2][:], d["pb"][rr + 1:rr + 129, :], reads=allpb, writes=[nxt[tj % 2]])

                if ti == 0:
                    loads(0, t)
                if pss == 1 and need_y:
                    P.dma(yfl[:], d["yf"][t * 128:(t + 1) * 128, :], reads=[self.db("yf", t)], writes=[yfl])
                if ti + 1 < len(order):
                    loads(ti + 1, order[ti + 1])
                dve(lambda g: g.tensor_tensor(out=cu[:], in0=cu[:], in1=mu[:, 0, :], op=ALU.mult), [cu, mu], [cu])
                pool(lambda g: g.tensor_tensor(out=pv_[:], in0=pv_[:], in1=mu[:, 1, :], op=ALU.mult), [pv_, mu], [pv_])
                pool(lambda g: g.tensor_tensor(out=nx[:], in0=nx[:], in1=mu[:, 2, :], op=ALU.mult), [nx, mu], [nx])
                dve(lambda g: g.tensor_tensor(out=cu[:], in0=cu[:], in1=pv_[:], op=ALU.add), [cu, pv_], [cu])
                dve(lambda g: g.tensor_tensor(out=cu[:], in0=cu[:], in1=nx[:], op=ALU.add), [cu, nx], [cu])
                r_ap, k_ap, v_ap = cu[:, 0:512], cu[:, 512:1024], cu[:, 1024:1536]
                dve(lambda g: g.tensor_tensor(out=kk[:], in0=k_ap, in1=kv[:, 0, :], op=ALU.mult), [cu, kv], [kk])
                pool(lambda g: g.tensor_tensor(out=sq[:], in0=kk[:], in1=kk[:], op=ALU.mult), [kk], [sq])
                dve(lambda g: g.tensor_reduce(out=s8[:], in_=view8(sq[:]), axis=AX.X, op=ALU.add), [sq], [s8])
                act(lambda g: g.activation(out=r8[:], in_=s8[:], func=AF.Sqrt, bias=1e-12), [s8], [r8])
                dve(lambda g: g.reciprocal(out=r8[:], in_=r8[:]), [r8], [r8])
                dve(lambda g: g.tensor_tensor(out=view8(kk[:]), in0=view8(kk[:]), in1=b8c(r8), op=ALU.mult),
                    [kk, r8], [kk])
                act(lambda g: g.activation(out=vbf[:], in_=v_ap, func=AF.Copy), [cu], [vbf])
                act(lambda g: g.activation(out=tw[:], in_=cu[:, 1536:1600], func=AF.Tanh), [cu], [tw])
                act(lambda g: g.activation(out=al[:], in_=cu[:, 1600:1664], func=AF.Copy), [cu], [al])
                pb0 = self.psb(0)
                P.op("pe", lambda g: g.transpose(pb0[0:64, 0:128], tw[:], idb[:]), reads=[tw, idb], writes=[ps[0]])
                P.op("pe", lambda g: g.transpose(pb0[0:64, 128:256], al[:], idb[:]), reads=[al, idb], writes=[ps[0]])
                dve(lambda g: g.tensor_copy(out=TWA[0:64, :], in_=pb0[0:64, 0:128]), [ps[0]], [TWA])
                dve(lambda g: g.tensor_copy(out=ALA[0:64, :], in_=pb0[0:64, 128:256]), [ps[0]], [ALA])
                if pss == 1:
                    act(lambda g: g.activation(out=sgl[:], in_=cu[:, 1664:1792], func=AF.Sigmoid), [cu], [sgl])
                    pb1 = self.psb(1)
                    P.op("pe", lambda g: g.transpose(pb1[:, 0:128], sgl[:], idb[:]), reads=[sgl, idb], writes=[ps[1]])
                    dve(lambda g: g.tensor_copy(out=sglT[:], in_=pb1[:, 0:128]), [ps[1]], [sglT])
                    P.mm(ps[2][:], ALA[:], aupA[:, 0, :], reads=[ALA, aupA], writes=[ps[2]])
                    act(lambda g: g.activation(out=av[:], in_=ps[2][:], func=AF.Sigmoid), [ps[2]], [av])
                    dve(lambda g: g.tensor_tensor(out=tt_[:], in0=av[:], in1=kv[:, 1, :], op=ALU.mult), [av, kv], [tt_])
                    pool(lambda g: g.tensor_tensor(out=tt_[:], in0=tt_[:], in1=kv[:, 2, :], op=ALU.add), [tt_, kv], [tt_])
                    dve(lambda g: g.tensor_tensor(out=kd0[:], in0=k_ap, in1=tt_[:], op=ALU.mult), [cu, tt_], [kd0])
                P.mm(ps[2][:], TWA[:], wupA[:, dd, :], reads=[TWA, wupA], writes=[ps[2]])
                act(lambda g: g.activation(out=lw[:], in_=ps[2][:], func=AF.Sigmoid), [ps[2]], [lw])
                dve(lambda g: g.tensor_scalar(out=lw[:], in0=lw[:], scalar1=-0.6065306597126334, scalar2=None,
                                              op0=ALU.mult), [lw], [lw])
                P.mm(ps[3][:], ALA[:], aupA[:, dd, :], reads=[ALA, aupA], writes=[ps[3]])
                act(lambda g: g.activation(out=av[:], in_=ps[3][:], func=AF.Sigmoid), [ps[3]], [av])
                dve(lambda g: g.tensor_tensor(out=tt_[:], in0=av[:], in1=kv[:, 1, :], op=ALU.mult), [av, kv], [tt_])
                pool(lambda g: g.tensor_tensor(out=tt_[:], in0=tt_[:], in1=kv[:, 2, :], op=ALU.add), [tt_, kv], [tt_])
                dve(lambda g: g.tensor_tensor(out=kd[:], in0=k_ap, in1=tt_[:], op=ALU.mult), [cu, tt_], [kd])
                pool(lambda g: g.tensor_tensor(out=bb[:], in0=kk[:], in1=av[:], op=ALU.mult), [kk, av], [bb])
                P.mm(ps[4][:], tri[:, dd, 0, :], lw[:], reads=[tri, lw], writes=[ps[4]])
                P.mm(ps[5][:], tri[:, dd, 1, :], lw[:], reads=[tri, lw], writes=[ps[5]])
                P.mm(ps[6][:], tri[:, dd, 2, :], lw[:], reads=[tri, lw], writes=[ps[6]])
                for h in range(8):
                    P.mm(ps[7][0:64, h:h + 1], lw[:, h * 64:(h + 1) * 64], onec[:], reads=[lw, onec], writes=[ps[7]])
                act(lambda g: g.activation(out=WC[:], in_=ps[7][0:64, 0:8], func=AF.Exp), [ps[7]], [WC])
                act(lambda g: g.activation(out=eLu[:], in_=ps[4][:], func=AF.Exp), [ps[4]], [eLu])
                act(lambda g: g.activation(out=eW[:], in_=ps[5][:], func=AF.Exp), [ps[5]], [eW])
                act(lambda g: g.activation(out=eWi[:], in_=ps[5][:], func=AF.Exp, scale=-1.0), [ps[5]], [eWi])
                act(lambda g: g.activation(out=eD[:], in_=ps[6][:], func=AF.Exp), [ps[6]], [eD])
                act(lambda g: g.activation(out=elw[:], in_=lw[:], func=AF.Exp, scale=-1.0), [lw], [elw])
                dve(lambda g: g.tensor_tensor(out=eWx[:], in0=eW[:], in1=elw[:], op=ALU.mult), [eW, elw], [eWx])
                pool(lambda g: g.tensor_tensor(out=eLux[:], in0=eLu[:], in1=elw[:], op=ALU.mult), [eLu, elw], [eLux])
                dve(lambda g: g.tensor_tensor(out=rt[:], in0=r_ap, in1=eW[:], op=ALU.mult), [cu, eW], [rt])
                dve(lambda g: g.scalar_tensor_tensor(out=zt[:], in0=kk[:], scalar=-1.0, in1=eWx[:], op0=ALU.mult,
                                                     op1=ALU.mult), [kk, eWx], [zt])
                pool(lambda g: g.tensor_tensor(out=bt[:], in0=bb[:], in1=eWi[:], op=ALU.mult), [bb, eWi], [bt])
                pool(lambda g: g.tensor_tensor(out=kt[:], in0=kd[:], in1=eWi[:], op=ALU.mult), [kd, eWi], [kt])
                pool(lambda g: g.tensor_tensor(out=bp[:], in0=bb[:], in1=eD[:], op=ALU.mult), [bb, eD], [bp])
                dve(lambda g: g.tensor_tensor(out=kp[:], in0=kd[:], in1=eD[:], op=ALU.mult), [kd, eD], [kp])
                pool(lambda g: g.tensor_tensor(out=ru[:], in0=r_ap, in1=eLu[:], op=ALU.mult), [cu, eLu], [ru])
                dve(lambda g: g.scalar_tensor_tensor(out=zu[:], in0=kk[:], scalar=-1.0, in1=eLux[:], op0=ALU.mult,
                                                     op1=ALU.mult), [kk, eLux], [zu])
                for qi, (src, dstf) in enumerate(((rt, RTf), (zt, ZTf), (bt, BTf), (kt, KTf))):
                    pbx = self.psb(qi % 2)
                    for h in range(8):
                        P.op("pe", lambda g: g.transpose(pbx[0:64, h * 128:(h + 1) * 128], src[:, h * 64:(h + 1) * 64],
                                                         idb[:]), reads=[src, idb], writes=[ps[qi % 2]])
                    if qi % 2 == 0:
                        act(lambda g: g.activation(out=dstf[:].rearrange("p h t -> p (h t)"), in_=pbx[0:64, :],
                                                   func=AF.Copy), [ps[qi % 2]], [dstf])
                    else:
                        dve(lambda g: g.tensor_copy(out=dstf[:].rearrange("p h t -> p (h t)"), in_=pbx[0:64, :]),
                            [ps[qi % 2]], [dstf])
                nb = [0]

                def amat(Lf, Rf, mi, dst, add_ident=None):
                    for hg in range(2):
                        pa = ps[2 + nb[0] % 4]
                        nb[0] += 1
                        for j in range(4):
                            h = hg * 4 + j
                            P.mm(pa[:, j * 128:(j + 1) * 128], Lf[:, h, :], Rf[:, h, :], reads=[Lf, Rf], writes=[pa])
                        dve(lambda g: g.tensor_tensor(out=dst[:, hg * 4:(hg + 1) * 4, :],
                                                      in0=pa[:].rearrange("p (h t) -> p h t", h=4),
                                                      in1=tri[:, dd, mi, :].unsqueeze(1).to_broadcast([128, 4, 128]),
                                                      op=ALU.mult), [pa, tri], [dst])

                amat(ZTf, BTf, 5, Xs[0])
                amat(BTf, ZTf, 3, XTs[0])
                amat(KTf, ZTf, 3, AzkT)
                amat(BTf, RTf, 4, ArbT)
                amat(KTf, RTf, 4, ArkT)
                pool(lambda g: g.tensor_tensor(out=TTs[0][:], in0=XTs[0][:],
                                               in1=idb[:].unsqueeze(1).to_broadcast([128, 8, 128]), op=ALU.add),
                     [XTs[0], idb], [TTs[0]])
                cx = 0
                for it in range(6):
                    Xc, XTc, TTc = Xs[cx], XTs[cx], TTs[cx]
                    Xn, XTn, TTn = Xs[1 - cx], XTs[1 - cx], TTs[1 - cx]
                    for hg in range(2):
                        p2 = ps[2 + hg]
                        for j in range(4):
                            h = hg * 4 + j
                            P.mm(p2[:, j * 128:(j + 1) * 128], XTc[:, h, :], Xc[:, h, :], reads=[XTc, Xc], writes=[p2])
                        act(lambda g: g.activation(out=Xn[:, hg * 4:(hg + 1) * 4, :].rearrange("p h t -> p (h t)"),
                                                   in_=p2[:], func=AF.Copy), [p2], [Xn])
                        if it < 5:
                            p3 = ps[4 + hg]
                            for j in range(4):
                                h = hg * 4 + j
                                P.mm(p3[:, j * 128:(j + 1) * 128], Xc[:, h, :], XTc[:, h, :], reads=[Xc, XTc],
                                     writes=[p3])
                            dve(lambda g: g.tensor_copy(out=XTn[:, hg * 4:(hg + 1) * 4, :].rearrange("p h t -> p (h t)"),
                                                        in_=p3[:]), [p3], [XTn])
                    for hg in range(2):
                        p4 = ps[6 + hg]
                        for j in range(4):
                            h = hg * 4 + j
                            P.mm(p4[:, j * 128:(j + 1) * 128], Xn[:, h, :], TTc[:, h, :], start=True, stop=False,
                                 reads=[Xn, TTc], writes=[p4])
                            P.mm(p4[:, j * 128:(j + 1) * 128], idb[:], TTc[:, h, :], start=False, stop=True,
                                 reads=[idb, TTc], writes=[p4])
                        act(lambda g: g.activation(out=TTn[:, hg * 4:(hg + 1) * 4, :].rearrange("p h t -> p (h t)"),
                                                   in_=p4[:], func=AF.Copy), [p4], [TTn])
                    cx = 1 - cx
                TT = TTs[cx]
                for h in range(8):
                    P.mm(ps[2][:, h * 64:(h + 1) * 64], TT[:, h, :], zu[:, h * 64:(h + 1) * 64], reads=[TT, zu], writes=[ps[2]])
                act(lambda g: g.activation(out=Zp[:], in_=ps[2][:], func=AF.Copy), [ps[2]], [Zp])
                for h in range(8):
                    P.mm(ps[3][:, h * 64:(h + 1) * 64], AzkT[:, h, :], vbf[:, h * 64:(h + 1) * 64], reads=[AzkT, vbf],
                         writes=[ps[3]])
                dve(lambda g: g.tensor_copy(out=Gm[:], in_=ps[3][:]), [ps[3]], [Gm])
                for h in range(8):
                    P.mm(ps[4][:, h * 64:(h + 1) * 64], TT[:, h, :], Gm[:, h * 64:(h + 1) * 64], reads=[TT, Gm], writes=[ps[4]])
                act(lambda g: g.activation(out=U0[:], in_=ps[4][:], func=AF.Copy), [ps[4]], [U0])
                for h in range(8):
                    P.mm(ps[5][0:64, h * 64:(h + 1) * 64], Zp[:, h * 64:(h + 1) * 64], bp[:, h * 64:(h + 1) * 64],
                         reads=[Zp, bp], writes=[ps[5]])
                dve(lambda g: g.tensor_tensor(out=Mm[:], in0=idf[0:64, 0:64].unsqueeze(1).to_broadcast([64, 8, 64]),
                                              in1=WC[:].unsqueeze(2).to_broadcast([64, 8, 64]), op=ALU.mult),
                    [idf, WC], [Mm])
                dve(lambda g: g.tensor_tensor(out=Mm[:], in0=Mm[:], in1=ps[5][0:64, :].rearrange("p (h m) -> p h m", m=64),
                                              op=ALU.add), [Mm, ps[5]], [Mm])
                for h in range(8):
                    hs = slice(h * 64, (h + 1) * 64)
                    P.mm(ps[6][0:64, hs], bp[:, hs], U0[:, hs], start=True, stop=False, reads=[bp, U0], writes=[ps[6]])
                    P.mm(ps[6][0:64, hs], kp[:, hs], vbf[:, hs], start=False, stop=True, reads=[kp, vbf], writes=[ps[6]])
                act(lambda g: g.activation(out=NTt[:].rearrange("p h m -> p (h m)"), in_=ps[6][0:64, :], func=AF.Copy),
                    [ps[6]], [NTt])
                if need_y:
                    for h in range(8):
                        hs = slice(h * 64, (h + 1) * 64)
                        P.mm(ps[7][:, hs], ArbT[:, h, :], U0[:, hs], start=True, stop=False, reads=[ArbT, U0], writes=[ps[7]])
                        P.mm(ps[7][:, hs], ArkT[:, h, :], vbf[:, hs], start=False, stop=True, reads=[ArkT, vbf],
                             writes=[ps[7]])
                    act(lambda g: g.activation(out=Y0[:], in_=ps[7][:], func=AF.Copy), [ps[7]], [Y0])
                    for hg in range(2):
                        pr = ps[hg]
                        for j in range(4):
                            h = hg * 4 + j
                            hs = slice(h * 64, (h + 1) * 64)
                            P.mm(pr[0:64, j * 128:(j + 1) * 128], ru[:, hs], idb[:], start=True, stop=False,
                                 reads=[ru, idb], writes=[pr])
                            P.mm(pr[0:64, j * 128:(j + 1) * 128], Zp[:, hs], ArbT[:, h, :], start=False, stop=True,
                                 reads=[Zp, ArbT], writes=[pr])
                        dve(lambda g: g.tensor_copy(out=RpT[:, hg * 4:(hg + 1) * 4, :].rearrange("p h t -> p (h t)"),
                                                    in_=pr[0:64, :]), [pr], [RpT])
                    for h in range(8):
                        P.mm(ps[2][:, h * 64:(h + 1) * 64], RpT[:, h, :], STb[:, h, :], reads=[RpT, STb], writes=[ps[2]])
                    dve(lambda g: g.tensor_tensor(out=Yt[:], in0=ps[2][:], in1=Y0[:], op=ALU.add), [ps[2], Y0], [Yt])
                for h in range(8):
                    P.mm(ps[3][0:64, h * 64:(h + 1) * 64], Mm[:, h, :], STf[:, h, :], reads=[Mm, STf], writes=[ps[3]])
                dve(lambda g: g.tensor_tensor(out=STf[:], in0=ps[3][0:64, :].rearrange("p (h m) -> p h m", m=64),
                                              in1=NTt[:], op=ALU.add), [ps[3], NTt], [STf])
                act(lambda g: g.activation(out=STb[:], in_=STf[:], func=AF.Copy), [STf], [STb])
                if not need_y:
                    continue
                if pss == 0:
                    P.dma(d["yf"][t * 128:(t + 1) * 128, :], Yt[:], reads=[Yt], writes=[self.db("yf", t)])
                    continue
                dve(lambda g: g.tensor_tensor(out=Yt[:], in0=Yt[:], in1=yfl[:], op=ALU.add), [Yt, yfl], [Yt])
                dve(lambda g: g.tensor_reduce(out=m8[:], in_=view8(Yt[:]), axis=AX.X, op=ALU.add), [Yt], [m8])
                dve(lambda g: g.tensor_scalar(out=m8[:], in0=m8[:], scalar1=1.0 / 64, scalar2=None, op0=ALU.mult), [m8], [m8])
                dve(lambda g: g.tensor_tensor(out=view8(yc[:]), in0=view8(Yt[:]), in1=b8c(m8), op=ALU.subtract),
                    [Yt, m8], [yc])
                pool(lambda g: g.tensor_tensor(out=sq[:], in0=yc[:], in1=yc[:], op=ALU.mult), [yc], [sq])
                dve(lambda g: g.tensor_reduce(out=v8[:], in_=view8(sq[:]), axis=AX.X, op=ALU.add), [sq], [v8])
                act(lambda g: g.activation(out=v8[:], in_=v8[:], func=AF.Sqrt, scale=1.0 / 64, bias=64e-5), [v8], [v8])
                dve(lambda g: g.reciprocal(out=v8[:], in_=v8[:]), [v8], [v8])
                dve(lambda g: g.tensor_tensor(out=view8(yc[:]), in0=view8(yc[:]), in1=b8c(v8), op=ALU.mult), [yc, v8], [yc])
                pool(lambda g: g.tensor_tensor(out=yc[:], in0=yc[:], in1=lnp[:, 0, :], op=ALU.mult), [yc, lnp], [yc])
                pool(lambda g: g.tensor_tensor(out=yc[:], in0=yc[:], in1=lnp[:, 1, :], op=ALU.add), [yc, lnp], [yc])
                dve(lambda g: g.tensor_tensor(out=kd0[:], in0=kd0[:], in1=kd[:], op=ALU.add), [kd0, kd], [kd0])
                dve(lambda g: g.tensor_tensor(out=kd0[:], in0=kd0[:], in1=r_ap, op=ALU.mult), [kd0, cu], [kd0])
                dve(lambda g: g.scalar_tensor_tensor(out=sq[:], in0=kd0[:], scalar=0.5, in1=lnp[:, 2, :], op0=ALU.mult,
                                                     op1=ALU.mult), [kd0, lnp], [sq])
                dve(lambda g: g.tensor_reduce(out=b8[:], in_=view8(sq[:]), axis=AX.X, op=ALU.add), [sq], [b8])
                dve(lambda g: g.tensor_tensor(out=view8(sq[:]), in0=view8(v_ap), in1=b8c(b8), op=ALU.mult), [cu, b8], [sq])
                pool(lambda g: g.tensor_tensor(out=yc[:], in0=yc[:], in1=sq[:], op=ALU.add), [yc, sq], [yc])
                P.mm(ps[4][:], sglT[:], gup[:], reads=[sglT, gup], writes=[ps[4]])
                dve(lambda g: g.tensor_tensor(out=yc[:], in0=yc[:], in1=ps[4][:], op=ALU.mult), [yc, ps[4]], [yc])
                for kc in range(4):
                    P.op("pe", lambda g: g.transpose(ps[5][:, kc * 128:(kc + 1) * 128], yc[:, kc * 128:(kc + 1) * 128],
                                                     idf[:]), reads=[yc, idf], writes=[ps[5]])
                act(lambda g: g.activation(out=ob[:].rearrange("p a b -> p (a b)"), in_=ps[5][:], func=AF.Copy),
                    [ps[5]], [ob])
                P.dma(d["yT"][1].rearrange("(kc p) t -> p kc t", p=128)[:, :, t * 128:(t + 1) * 128], ob[:], reads=[ob],
                      writes=[self.db("yT", (1, t))])
        P.release()

    KB.declare_rwkv = declare_rwkv
    KB.phase_rwkv = phase_rwkv


_rwkv_methods()


def build_program():
    nc = bass.Bass("TRN2", target_bir_lowering=False)
    kb = KB(nc)
    kb.declare_common()
    kb.declare_pin()
    kb.declare_attn()
    kb.declare_merge()
    kb.declare_hyena()
    kb.declare_rwkv()
    kb.outp("out", [NLAT, DM])
    kb.alloc_persist()
    d = kb.d
    for l in range(2):
        with_ctx = (l == 0)
        kb.phase_mod(l)
        src = (d["xall"], "xall") if l == 0 else (d["xs"], "xs")
        kb.phase_ffn(l, 0, src, (d["xs"], "xs"), NTILE)
        kb.phase_pin(l, (d["xs"], "xs"))
        kb.phase_mla(l, with_ctx)
        kb.phase_swa(l, with_ctx)
        kb.phase_hyena(l, with_ctx)
        kb.phase_rwkv(l, with_ctx)
        kb.phase_merge(l, with_ctx)
        if l == 0:
            kb.phase_ffn(l, 2, (d["xs"], "xs"), (d["xs"], "xs"), NTILE)
        else:
            kb.phase_ffn(l, 2, (d["xs"], "xs"), (d["out"], "out"), 32)
    kb.P.barrier()
    return nc, kb


def kernel(**inputs):
    from concourse.bass_utils import run_bass_kernel_spmd
    nc, kb = build_program()
    sh = host_shared(inputs)
    names = [k for k in kb.d if k in sh]
    maps = []
    for b in range(8):
        pc = host_core(inputs, b)
        m = {k: sh[k] for k in names}
        m.update(pc)
        maps.append(m)
    res = run_bass_kernel_spmd(nc, maps, core_ids=list(range(8)))
    out = np.stack([np.asarray(res.results[b]["out"], dtype=np.float32) for b in range(8)], axis=0)
    return out
```

```python
import numpy as np
import concourse.bass as bass
import concourse.mybir as mybir

F32 = mybir.dt.float32
BF16 = mybir.dt.bfloat16
AF = mybir.ActivationFunctionType
ALU = mybir.AluOpType
AX = mybir.AxisListType

NDSEM = 36
NHW = 28
SEM_LIMIT = 30000
SB_BASE = 16640
SB_TOP = 229376
LAZY_D = 2
SAME_ENG_GAP = 3


class Buf:
    __slots__ = ("w", "r", "name")

    def __init__(self, name=""):
        self.w = None
        self.r = {}
        self.name = name


class Tile:
    def __init__(self, t, buf):
        self.t = t
        self.b = buf

    def __getitem__(self, k):
        return self.t[k]


def _bufs(xs):
    return [x.b if isinstance(x, Tile) else x for x in xs]


class Prog:
    def __init__(self, nc):
        self.nc = nc
        self.eng = {"pe": nc.tensor, "dve": nc.vector, "act": nc.scalar,
                    "pool": nc.gpsimd, "sp": nc.sync}
        self.cnt = {e: 0 for e in self.eng}
        self.known = {e: {} for e in self.eng}
        self.esem = {}
        self.egen = {e: 0 for e in self.eng}
        self.ebase = {e: 0 for e in self.eng}
        self.dsem = []
        self.duse = []
        self.dgen = []
        self.semtab = {}
        self.dnext = 0
        self.dnext_sw = 0
        self.pending = []
        self.nwait = 0
        self.ninstr = 0
        self.sb_off = SB_BASE
        self.sb_mark = []
        self.uid = 0
        nc = self.nc
        for e in self.eng:
            self.esem[e] = nc.alloc_semaphore("es_%s_0" % e)
            self.semtab[("e", e, 0)] = self.esem[e]
        for i in range(NDSEM):
            self.dsem.append(nc.alloc_semaphore("ds%d_0" % i))
            self.semtab[("d", i, 0)] = self.dsem[i]
            self.duse.append(0)
            self.dgen.append(0)

    def _rot_e(self, e):
        if self.cnt[e] - self.ebase[e] >= SEM_LIMIT:
            self.egen[e] += 1
            self.ebase[e] = self.cnt[e]
            self.esem[e] = self.nc.alloc_semaphore("es_%s_%d" % (e, self.egen[e]))
            self.semtab[("e", e, self.egen[e])] = self.esem[e]

    def _rot_d(self, i):
        if 16 * self.duse[i] >= SEM_LIMIT:
            self.dgen[i] += 1
            self.duse[i] = 0
            self.dsem[i] = self.nc.alloc_semaphore("ds%d_%d" % (i, self.dgen[i]))
            self.semtab[("d", i, self.dgen[i])] = self.dsem[i]

    def sbuf(self, shape, dtype, name=None):
        self.uid += 1
        name = (name or "t") + "_%d" % self.uid
        nbytes = int(np.prod(shape[1:])) * mybir.dt.size(dtype)
        off = (self.sb_off + 31) // 32 * 32
        t = self.nc.alloc_sbuf_tensor_at(name, list(shape), dtype, offset=off)
        self.sb_off = off + nbytes
        assert self.sb_off <= SB_TOP, ("SBUF overflow", name, self.sb_off)
        return Tile(t, Buf(name))

    def mark(self):
        self.sb_mark.append(self.sb_off)

    def release(self):
        self.barrier()
        self.sb_off = self.sb_mark.pop()

    def _wait(self, e, kind, id_, gen, val):
        self.known[e][(kind, id_)] = (gen, val)
        self.eng[e].wait_ge(self.semtab[(kind, id_, gen)], val)
        self.nwait += 1

    def _need(self, e, toks):
        kn = self.known[e]
        req = {}
        for t in toks:
            if t is None:
                continue
            kind, id_, gen, val, absidx = t
            if kind == "e" and id_ == e:
                if e == "pe":
                    continue
                if self.cnt[e] + 1 - absidx >= SAME_ENG_GAP:
                    continue
            k = (kind, id_)
            if kn.get(k, (-1, 0)) >= (gen, val):
                continue
            if req.get(k, (-1, 0)) < (gen, val):
                req[k] = (gen, val)
        for (kind, id_), (gen, val) in req.items():
            self._wait(e, kind, id_, gen, val)

    @staticmethod
    def _deps(reads, writes):
        toks = []
        for b in reads:
            toks.append(b.w)
        for b in writes:
            toks.append(b.w)
            toks.extend(b.r.values())
        return toks

    @staticmethod
    def _commit(tok, reads, writes):
        k = (tok[0], tok[1])
        for b in reads:
            b.r[k] = tok
        for b in writes:
            b.w = tok
            b.r = {}

    def op(self, e, fn, reads=(), writes=()):
        reads = _bufs(reads)
        writes = _bufs(writes)
        self._hazard_flush(reads, writes)
        self._rot_e(e)
        self._need(e, self._deps(reads, writes))
        ins = fn(self.eng[e])
        self.cnt[e] += 1
        ins.then_inc(self.esem[e], 1)
        tok = ("e", e, self.egen[e], self.cnt[e] - self.ebase[e], self.cnt[e])
        self._commit(tok, reads, writes)
        self.ninstr += 1
        return ins

    def _flush(self, upto=None):
        n = len(self.pending) if upto is None else upto
        todo, self.pending = self.pending[:n], self.pending[n:]
        for (out, in_, reads, writes, q, kw, _) in todo:
            self._dma_now(out, in_, reads, writes, q, kw)

    def _hazard_flush(self, reads, writes):
        if not self.pending:
            return
        rs = set(map(id, reads))
        ws = set(map(id, writes))
        last = -1
        for i, (_, _, pr, pw, _, _, _) in enumerate(self.pending):
            hit = False
            for b in pr:
                if id(b) in ws:
                    hit = True
            for b in pw:
                if id(b) in ws or id(b) in rs:
                    hit = True
            if hit:
                last = i
        if last >= 0:
            self._flush(last + 1)

    def dma(self, out, in_, reads=(), writes=(), q="sp", **kw):
        reads = _bufs(reads)
        writes = _bufs(writes)
        self._hazard_flush(reads, writes)
        is_store = type(out.tensor).__name__.startswith("DRam") and not type(in_.tensor).__name__.startswith("DRam")
        if is_store and q == "sp" and LAZY_D > 0:
            self.pending.append([out, in_, reads, writes, q, kw, 0])
            return None
        r = self._dma_now(out, in_, reads, writes, q, kw)
        if q == "sp" and self.pending:
            k = 0
            for p in self.pending:
                p[6] += 1
            while k < len(self.pending) and self.pending[k][6] >= LAZY_D:
                k += 1
            if k:
                self._flush(k)
        return r

    def _dma_now(self, out, in_, reads, writes, q, kw):
        if q == "pool":
            i = NHW + self.dnext_sw
            self.dnext_sw = (self.dnext_sw + 1) % (NDSEM - NHW)
        else:
            i = self.dnext
            self.dnext = (self.dnext + 1) % NHW
        toks = self._deps(reads, writes)
        if self.duse[i]:
            toks.append(("d", i, self.dgen[i], 16 * self.duse[i], 0))
        self._need(q, toks)
        self._rot_d(i)
        self.duse[i] += 1
        ins = self.eng[q].dma_start(out=out, in_=in_, **kw)
        ins.then_inc(self.dsem[i], 16)
        tok = ("d", i, self.dgen[i], 16 * self.duse[i], 0)
        self._commit(tok, reads, writes)
        self.ninstr += 1
        return ins

    def barrier(self, engines=None):
        self._flush()
        toks = []
        for e in self.eng:
            if self.cnt[e] > self.ebase[e]:
                toks.append(("e", e, self.egen[e], self.cnt[e] - self.ebase[e]))
            elif self.egen[e] > 0:
                toks.append(("e", e, self.egen[e] - 1, SEM_LIMIT))
        for i, u in enumerate(self.duse):
            if u:
                toks.append(("d", i, self.dgen[i], 16 * u))
        for e in (engines or self.eng):
            kn = self.known[e]
            for kind, id_, gen, val in toks:
                if kind == "e" and id_ == e and e in ("sp", "pe"):
                    continue
                if kn.get((kind, id_), (-1, 0)) >= (gen, val):
                    continue
                self._wait(e, kind, id_, gen, val)

    def mm(self, out, lhsT, rhs, start=True, stop=True, reads=(), writes=()):
        return self.op("pe", lambda g: g.matmul(out, lhsT, rhs, start=start, stop=stop),
                       reads=reads, writes=writes)


DM = 1024
NLAT = 4096
NCTX = 256
TT = NLAT + NCTX
NTILE = TT // 128
FF = 2816
NFC = FF // 128
EPS = 1e-6
I32 = mybir.dt.int32


class KB:
    def __init__(self, nc, ext_in=(), ext_out=()):
        self.nc = nc
        self.P = Prog(nc)
        self.ext_in = set(ext_in)
        self.ext_out = set(ext_out)
        self.d = {}
        self.dbufs = {}
        self.ps = [Tile(nc.alloc_psum_tensor("ps%d" % i, [128, 512], F32), Buf("ps%d" % i))
                   for i in range(8)]

    def inp(self, name, shape, dtype=F32):
        self.d[name] = self.nc.dram_tensor(name, list(shape), dtype, kind="ExternalInput").ap()
        return self.d[name]

    def outp(self, name, shape, dtype=F32):
        self.d[name] = self.nc.dram_tensor(name, list(shape), dtype, kind="ExternalOutput").ap()
        return self.d[name]

    def scr(self, name, shape, dtype=F32):
        kind = "Internal"
        if name in self.ext_in:
            kind = "ExternalInput"
        elif name in self.ext_out:
            kind = "ExternalOutput"
        self.d[name] = self.nc.dram_tensor(name, list(shape), dtype, kind=kind).ap()
        return self.d[name]

    def db(self, name, idx=0):
        k = (name, idx)
        if k not in self.dbufs:
            self.dbufs[k] = Buf("%s_%s" % (name, idx))
        return self.dbufs[k]

    def psb(self, i):
        return self.ps[i][:].bitcast(BF16)

    def rstd_from_ss(self, ss, n, eps, out):
        P = self.P
        ss_t, ss_ap = ss
        o_t, o_ap = out
        P.op("act", lambda g: g.activation(out=o_ap, in_=ss_ap, func=AF.Sqrt, scale=1.0 / n, bias=eps),
             reads=[ss_t], writes=[o_t])
        P.op("dve", lambda g: g.reciprocal(out=o_ap, in_=o_ap), reads=[o_t], writes=[o_t])

    def declare_common(self):
        L = 2
        self.inp("xall", [TT, DM])
        self.inp("cT", [128, 16])
        self.inp("ident", [128, 128])
        self.inp("w_mod", [L, DM, 9 * DM])
        self.inp("b_mod", [L, 9 * DM])
        self.inp("norm_g", [L, 6, DM])
        self.inp("norm_gT", [L, 128, 48])
        self.inp("ffn_w13", [L, 2, DM, 2 * FF])
        self.inp("ffn_w2", [L, 2, FF, DM])
        self.scr("gtrow", [L, 3, 2, DM])
        self.scr("xs", [TT, DM])

    def alloc_persist(self):
        P = self.P
        self.ident_f = P.sbuf([128, 128], F32, "identf")
        self.ident_b = P.sbuf([128, 128], BF16, "identb")
        P.dma(self.ident_f[:], self.d["ident"], writes=[self.ident_f])
        P.dma(self.ident_b[:], self.d["ident"], writes=[self.ident_b], q="pool")
        self.mcol = P.sbuf([128, 72, 2], F32, "mcol")
        self.AB = P.sbuf([128, 3, 2, 8, 2], F32, "AB")

    def phase_mod(self, l):
        P = self.P
        d = self.d
        P.mark()
        cT = P.sbuf([128, 16], F32, "cT")
        P.dma(cT[:], d["cT"], writes=[cT])
        sc = P.sbuf([128, 16], F32, "sc")
        P.op("act", lambda g: g.activation(out=sc[:], in_=cT[:], func=AF.Silu), reads=[cT], writes=[sc])
        mrow = P.sbuf([2, 9 * DM], F32, "mrow")
        brow = P.sbuf([2, 9 * DM], F32, "brow")
        for s in range(2):
            P.dma(brow[s:s + 1, :], d["b_mod"][l:l + 1, :], writes=[brow])
        wt = [P.sbuf([128, 8, 512], F32, "wmod%d" % i) for i in range(2)]
        wsrc = d["w_mod"][l].rearrange("(k p) n -> p k n", p=128)
        for jb in range(18):
            w = wt[jb % 2]
            P.dma(w[:], wsrc[:, :, jb * 512:(jb + 1) * 512], writes=[w])
            ps = self.ps[jb % 2]
            for k in range(8):
                P.mm(ps[0:2, :], sc[:, 2 * k:2 * k + 2], w[:, k, :], start=(k == 0), stop=(k == 7),
                     reads=[sc, w], writes=[ps])
            P.op("dve", lambda g: g.tensor_tensor(out=mrow[:, jb * 512:(jb + 1) * 512], in0=ps[0:2, :],
                                                  in1=brow[:, jb * 512:(jb + 1) * 512], op=ALU.add),
                 reads=[ps, brow], writes=[mrow])
        psc = self.ps[2]
        for c in range(72):
            P.mm(psc[:, 2 * c:2 * c + 2], mrow[0:2, c * 128:(c + 1) * 128], self.ident_f[0:2, 0:2],
                 reads=[mrow, self.ident_f], writes=[psc])
        mcol = self.mcol
        P.op("dve", lambda g: g.tensor_copy(out=mcol[:].rearrange("p c s -> p (c s)"), in_=psc[:, 0:144]),
             reads=[psc], writes=[mcol])
        gcol = P.sbuf([128, 6, 8], F32, "gcol")
        P.dma(gcol[:].rearrange("p n k -> p (n k)"), d["norm_gT"][l], writes=[gcol])
        mc4 = mcol[:].rearrange("p (j k) s -> p j k s", k=8)
        tmp = P.sbuf([128, 8, 2], F32, "abtmp")
        for sub in range(3):
            jsh, jsc, npre = 3 * sub, 3 * sub + 1, 2 * sub
            P.op("dve", lambda g: g.tensor_scalar(out=tmp[:], in0=mc4[:, jsc, :, :], scalar1=1.0, scalar2=None,
                                                  op0=ALU.add), reads=[mcol], writes=[tmp])
            P.op("dve", lambda g: g.tensor_tensor(out=self.AB[:, sub, 0, :, :], in0=tmp[:],
                                                  in1=gcol[:, npre, :].unsqueeze(2).to_broadcast([128, 8, 2]),
                                                  op=ALU.mult), reads=[tmp, gcol], writes=[self.AB])
            P.op("dve", lambda g: g.tensor_copy(out=self.AB[:, sub, 1, :, :], in_=mc4[:, jsh, :, :]),
                 reads=[mcol], writes=[self.AB])
        grow = [P.sbuf([2, DM], F32, "grow%d" % i) for i in range(3)]
        gto = [P.sbuf([2, DM], F32, "gto%d" % i) for i in range(3)]
        for sub in range(3):
            jg, npost = 3 * sub + 2, 2 * sub + 1
            fac = 1.0 if sub == 1 else 0.5
            for s in range(2):
                P.dma(grow[sub][s:s + 1, :], d["norm_g"][l, npost:npost + 1, :], writes=[grow[sub]])
            P.op("dve", lambda g: g.scalar_tensor_tensor(out=gto[sub][:], in0=mrow[:, jg * DM:(jg + 1) * DM],
                                                         scalar=fac, in1=grow[sub][:], op0=ALU.mult,
                                                         op1=ALU.mult),
                 reads=[mrow, grow[sub]], writes=[gto[sub]])
            P.dma(d["gtrow"][l, sub], gto[sub][:], reads=[gto[sub]], writes=[self.db("gtrow", (l, sub))])
        P.release()

    def norm_T(self, xt_t, x_ap, sub, s, xnT_t, xnT_ap, pst, wk):
        P = self.P
        junk, ss, rs, xs = wk
        P.op("act", lambda g: g.activation(out=junk[:], in_=x_ap, func=AF.Square, accum_out=ss[:]),
             reads=[xt_t], writes=[junk, ss])
        self.rstd_from_ss((ss, ss[:]), DM, EPS, (rs, rs[:]))
        P.op("dve", lambda g: g.tensor_scalar(out=xs[:], in0=x_ap, scalar1=rs[:, 0:1], scalar2=None,
                                              op0=ALU.mult), reads=[xt_t, rs], writes=[xs])
        pb = self.psb(pst)
        for k in range(8):
            P.op("pe", lambda g: g.transpose(pb[:, k * 128:(k + 1) * 128], xs[:, k * 128:(k + 1) * 128],
                                             self.ident_b[:]),
                 reads=[xs, self.ident_b], writes=[self.ps[pst]])
        for k in range(8):
            A = self.AB[:, sub, 0, k, s:s + 1]
            B = self.AB[:, sub, 1, k, s:s + 1]
            if k % 2 == 0:
                P.op("dve", lambda g: g.tensor_scalar(out=xnT_ap[:, k, :], in0=pb[:, k * 128:(k + 1) * 128],
                                                      scalar1=A, scalar2=B, op0=ALU.mult, op1=ALU.add),
                     reads=[self.ps[pst], self.AB], writes=[xnT_t])
            else:
                P.op("act", lambda g: g.activation(out=xnT_ap[:, k, :], in_=pb[:, k * 128:(k + 1) * 128],
                                                   func=AF.Identity, scale=A, bias=B),
                     reads=[self.ps[pst], self.AB], writes=[xnT_t])

    def norm_res_out(self, pso, xt_t, x_ap, gt, wk2, dst_ap, dst_buf):
        P = self.P
        junk, s2, r2, tmp = wk2
        for h in range(2):
            P.op("act", lambda g: g.activation(out=junk[:, 0:512], in_=self.ps[pso[h]][:], func=AF.Square,
                                               accum_out=s2[:, h:h + 1]),
                 reads=[self.ps[pso[h]]], writes=[junk, s2])
        P.op("dve", lambda g: g.tensor_tensor(out=s2[:, 2:3], in0=s2[:, 0:1], in1=s2[:, 1:2], op=ALU.add),
             reads=[s2], writes=[s2])
        self.rstd_from_ss((s2, s2[:, 2:3]), DM, EPS, (r2, r2[:]))
        for h in range(2):
            P.op("dve", lambda g: g.scalar_tensor_tensor(out=tmp[:, h * 512:(h + 1) * 512],
                                                         in0=self.ps[pso[h]][:], scalar=r2[:, 0:1],
                                                         in1=gt[:, h * 512:(h + 1) * 512],
                                                         op0=ALU.mult, op1=ALU.mult),
                 reads=[self.ps[pso[h]], r2, gt], writes=[tmp])
        P.op("pool", lambda g: g.tensor_tensor(out=x_ap, in0=x_ap, in1=tmp[:], op=ALU.add),
             reads=[tmp, xt_t], writes=[xt_t])
        P.dma(dst_ap, x_ap, reads=[xt_t], writes=[dst_buf])

    def phase_ffn(self, l, sub, src, dst, ntiles):
        P = self.P
        d = self.d
        wi = 0 if sub == 0 else 1
        P.mark()
        w13 = P.sbuf([128, 8, 2 * FF], BF16, "w13")
        w13b = [Buf("w13_%d" % k) for k in range(8)]
        for k in range(8):
            P.dma(w13[:, k, :], d["ffn_w13"][l, wi, k * 128:(k + 1) * 128, :], writes=[w13b[k]], q="pool")
        w2 = P.sbuf([128, NFC, DM], BF16, "w2")
        w2b = [Buf("w2_%d" % j) for j in range(NFC)]
        for j in range(NFC):
            P.dma(w2[:, j, :], d["ffn_w2"][l, wi, j * 128:(j + 1) * 128, :], writes=[w2b[j]], q="pool")
        gt = [P.sbuf([128, DM], F32, "gt%d" % s) for s in range(2)]
        for s in range(2):
            P.dma(gt[s][:], d["gtrow"][l, sub, s:s + 1, :].to_broadcast([128, DM]),
                  reads=[self.db("gtrow", (l, sub))], writes=[gt[s]])
        xbuf = [P.sbuf([128, 2, DM], F32, "xbuf%d" % i) for i in range(2)]
        xbb = [[Buf("xb%d_%d" % (i, j)) for j in range(2)] for i in range(2)]
        xnT = [P.sbuf([128, 8, 256], BF16, "xnT%d" % i) for i in range(2)]
        hT = P.sbuf([128, NFC, 256], BF16, "hT")
        hTb = [Buf("hT%d" % j) for j in range(NFC)]
        junk = P.sbuf([128, DM], BF16, "junk")
        ss = [P.sbuf([128, 1], F32, "ss%d" % i) for i in range(2)]
        rs = [P.sbuf([128, 1], F32, "rs%d" % i) for i in range(2)]
        xs = [P.sbuf([128, DM], BF16, "xs%d" % i) for i in range(2)]
        s2 = [P.sbuf([128, 3], F32, "s2%d" % i) for i in range(2)]
        r2 = [P.sbuf([128, 1], F32, "r2%d" % i) for i in range(2)]
        tmp = [P.sbuf([128, DM], F32, "tmp%d" % i) for i in range(2)]
        sa = [P.sbuf([128, 256], F32, "sa%d" % i) for i in range(2)]
        ngroups = ntiles // 2
        src_ap, src_name = src
        dst_ap, dst_name = dst

        def load(gi):
            for i in range(2):
                t = 2 * gi + i
                xt = Tile(xbuf[gi % 2].t, xbb[gi % 2][i])
                P.dma(xbuf[gi % 2][:, i, :], src_ap[t * 128:(t + 1) * 128, :],
                      reads=[self.db(src_name, t)], writes=[xt])

        load(0)
        for gi in range(ngroups):
            if gi + 1 < ngroups:
                load(gi + 1)
            xb = xbuf[gi % 2]
            xn = xnT[gi % 2]
            s = 1 if 2 * gi >= 32 else 0
            for i in range(2):
                xt = Tile(xb.t, xbb[gi % 2][i])
                self.norm_T(xt, xb[:, i, :], sub, s, xn, xn[:, :, i * 128:(i + 1) * 128], 0 if i == 0 else 7,
                            (junk, ss[i], rs[i], xs[i]))
            for j in range(NFC):
                pa = self.ps[1 + (j % 2) * 2]
                pbk = self.ps[2 + (j % 2) * 2]
                for k in range(8):
                    P.mm(pa[:, 0:256], w13[:, k, j * 128:(j + 1) * 128], xn[:, k, :], start=(k == 0),
                         stop=(k == 7), reads=[w13b[k], xn], writes=[pa])
                for k in range(8):
                    P.mm(pbk[:, 0:256], w13[:, k, FF + j * 128:FF + (j + 1) * 128], xn[:, k, :], start=(k == 0),
                         stop=(k == 7), reads=[w13b[k], xn], writes=[pbk])
                sj = sa[j % 2]
                P.op("act", lambda g: g.activation(out=sj[:], in_=pa[:, 0:256], func=AF.Silu),
                     reads=[pa], writes=[sj])
                P.op("dve", lambda g: g.tensor_tensor(out=hT[:, j, :], in0=sj[:], in1=pbk[:, 0:256], op=ALU.mult),
                     reads=[sj, pbk], writes=[hTb[j]])
            for i in range(2):
                t = 2 * gi + i
                xt = Tile(xb.t, xbb[gi % 2][i])
                for h in range(2):
                    po = self.ps[5 + h]
                    for j in range(NFC):
                        P.mm(po[:], hT[:, j, i * 128:(i + 1) * 128], w2[:, j, h * 512:(h + 1) * 512],
                             start=(j == 0), stop=(j == NFC - 1), reads=[hTb[j], w2b[j]], writes=[po])
                self.norm_res_out([5, 6], xt, xb[:, i, :], gt[s], (junk, s2[i], r2[i], tmp[i]),
                                  dst_ap[t * 128:(t + 1) * 128, :], self.db(dst_name, t))
        P.release()


def host_shared(inp):
    f32 = np.float32
    sh = {}
    sh["ident"] = np.eye(128, dtype=f32)
    for k in ("w_mod", "b_mod", "norm_g", "ffn_w13", "ffn_w2"):
        sh[k] = np.ascontiguousarray(inp[k], dtype=f32)
    ng = np.asarray(inp["norm_g"], dtype=f32)
    sh["norm_gT"] = np.ascontiguousarray(ng.reshape(2, 6, 8, 128).transpose(0, 3, 1, 2).reshape(2, 128, 48))
    sh["w_ext"] = build_w_ext(inp["w_in"])
    cm, sm = rope_tables(32)
    sh["rope_m"] = np.ascontiguousarray(np.stack([cm, sm], 0))
    cs, ss_ = rope_tables(64)
    sh["rope_s"] = np.ascontiguousarray(np.stack([np.concatenate([cs, cs], 0), np.concatenate([ss_, ss_], 0)], 0))
    nq = np.asarray(inp["mla_norm_q"], f32)
    nkv = np.asarray(inp["mla_norm_kv"], f32)
    sh["mla_nT"] = np.ascontiguousarray(np.stack([nq[:, 0:128], nq[:, 128:256], nkv], axis=2))
    wuq = np.asarray(inp["mla_w_uq"], f32).reshape(2, 256, 8, 96)
    pm, _ = rope_partner(32)
    sw = np.concatenate([wuq[..., 0:64], wuq[..., 64 + pm]], axis=-1)
    sh["mla_wq2"] = np.ascontiguousarray(np.stack([wuq, sw], axis=3).reshape(2, 256, 8 * 2 * 96))
    wukv = np.asarray(inp["mla_w_ukv"], f32).reshape(2, 128, 8, 128)
    sh["mla_wk"] = np.ascontiguousarray(wukv[..., 0:64].reshape(2, 128, 512))
    sh["mla_wv"] = np.ascontiguousarray(wukv[..., 64:128].reshape(2, 128, 512))
    sh["swa_sink"] = np.ascontiguousarray(inp["swa_sink"], dtype=f32)
    sh["w_branch"] = np.ascontiguousarray(inp["w_branch"], dtype=f32)
    for k in ("hyena_conv", "hyena_conv_b", "hyena_w1", "hyena_w2", "hyena_w3", "hyena_bias"):
        sh[k] = np.ascontiguousarray(inp[k], dtype=f32)
    sh["rwkv_mu"] = np.ascontiguousarray(inp["rwkv_mu"], dtype=f32)
    sh["rwkv_kvec"] = np.ascontiguousarray(inp["rwkv_kvec"], dtype=f32)
    sh["rwkv_lnp"] = np.ascontiguousarray(np.stack([inp["rwkv_ln_g"], inp["rwkv_ln_b"], inp["rwkv_r_k"]], axis=1), dtype=f32)
    wup = np.asarray(inp["rwkv_w_up"], f32)
    w0 = np.asarray(inp["rwkv_w0"], f32)
    sh["rwkv_wupA"] = np.ascontiguousarray(np.concatenate([wup.transpose(0, 2, 1, 3).reshape(2, 64, 1024),
                                                           w0.reshape(2, 1, 1024)], axis=1))
    aup = np.asarray(inp["rwkv_a_up"], f32)
    a0 = np.asarray(inp["rwkv_a0"], f32)
    sh["rwkv_aupA"] = np.ascontiguousarray(np.concatenate([aup.transpose(0, 2, 1, 3).reshape(2, 64, 1024),
                                                           a0.reshape(2, 1, 1024)], axis=1))
    sh["rwkv_g_up"] = np.ascontiguousarray(inp["rwkv_g_up"], dtype=f32)
    sh["rw_tri"] = rwkv_tables()
    hf = np.asarray(inp["hyena_freq"], f32)
    sh["hyT"] = np.ascontiguousarray(np.stack([hf[:, 0], hf[:, 1], np.asarray(inp["hyena_b1"], f32),
                                               np.asarray(inp["hyena_b2"], f32)], axis=2))
    tl = hy_tables(NLAT)
    tc = hy_tables(NCTX)
    sh["hy_D"] = np.ascontiguousarray(np.stack([tl["D2"], tl["D2sw"], tl["E"]], 0))
    sh["hyL_W1"] = np.ascontiguousarray(tl["W1"].reshape(128, -1))
    sh["hyL_W3"] = np.ascontiguousarray(tl["W3"].reshape(65, -1))
    sh["hyC_W1"] = np.ascontiguousarray(tc["W1"].reshape(8, -1))
    sh["hyC_W3"] = np.ascontiguousarray(tc["W3"].reshape(5, -1))
    sh["hyL_fK"], sh["hyL_wK"] = hy_feats(NLAT)
    sh["hyC_fK"], sh["hyC_wK"] = hy_feats(NCTX)
    sh["w_out"] = np.ascontiguousarray(inp["w_out"], dtype=f32)
    bgt = np.asarray(inp["b_gate"], f32).reshape(2, 4, 8, 128).transpose(0, 3, 1, 2).reshape(2, 128, 32)
    sh["b_gateT"] = np.ascontiguousarray(bgt)
    kk = np.arange(128)[:, None]
    qq = np.arange(128)[None, :]
    sh["swa_mask"] = np.ascontiguousarray(np.stack([(qq <= kk), (kk <= qq)], 0).astype(f32))
    return sh


def host_core(inp, b):
    f32 = np.float32
    pc = {}
    pc["xall"] = np.ascontiguousarray(np.concatenate([inp["x"][b], inp["ctx"][b]], axis=0), dtype=f32)
    cv = np.stack([np.asarray(inp["c"][b], f32), np.asarray(inp["c_ctx"], f32)], axis=0)
    pc["cT"] = np.ascontiguousarray(cv.reshape(2, 8, 128).transpose(2, 1, 0).reshape(128, 16))
    return pc


G0, PA0, PB0, PH0, PD0 = 0, 4096, 4512, 6304, 7840
FM_COLS = 1728
TM_COLS = 3456
WX_FM0 = 0
WX_TM0 = FM_COLS
WX_G0 = FM_COLS + TM_COLS
WX_COLS = WX_G0 + 4096
PAD_ROWS = TT + 3


def tm_row(t):
    return 1 + t if t < NLAT else 2 + t


def rope_partner(R):
    H = R // 2
    q = H // 2
    part = np.zeros(R, np.int64)
    sign = np.zeros(R, np.float32)
    for dd in range(R):
        base = (dd // H) * H
        o = dd % H
        if o < q:
            part[dd] = base + o + q
            sign[dd] = -1.0
        else:
            part[dd] = base + o - q
            sign[dd] = 1.0
    return part, sign


def rope_tables(R):
    H = R // 2
    q = H // 2
    t = np.arange(NLAT)
    row = (t // 64).astype(np.float32)
    col = (t % 64).astype(np.float32)
    inv = (10000.0 ** (-np.arange(0, H, 2, dtype=np.float32) / H)).astype(np.float32)
    _, sign = rope_partner(R)
    cos = np.zeros((R, NLAT), np.float32)
    sin = np.zeros((R, NLAT), np.float32)
    for dd in range(R):
        pos = row if dd < H else col
        ang = (pos * inv[(dd % H) % q]).astype(np.float32)
        cos[dd] = np.cos(ang)
        sin[dd] = sign[dd] * np.sin(ang)
    return cos, sin


def build_w_ext(w_in):
    pm, _ = rope_partner(32)
    ps_, _ = rope_partner(64)
    cols = []
    cols += list(range(PA0, PA0 + 384))
    kr0 = PA0 + 384
    cols += [kr0 + i for i in range(32)]
    cols += [kr0 + int(pm[i]) for i in range(32)]
    q0 = PD0
    cols += [q0 + i for i in range(512)]
    cols += [q0 + (i // 64) * 64 + int(ps_[i % 64]) for i in range(512)]
    k0 = PD0 + 512
    cols += [k0 + i for i in range(128)]
    cols += [k0 + (i // 64) * 64 + int(ps_[i % 64]) for i in range(128)]
    assert len(cols) == FM_COLS
    cols += list(range(PB0, PB0 + 1792))
    cols += list(range(PH0, PH0 + 1536))
    cols += list(range(PD0 + 640, PD0 + 768))
    assert len(cols) == FM_COLS + TM_COLS
    cols += list(range(0, 4096))
    return np.ascontiguousarray(np.asarray(w_in, np.float32)[:, :, np.asarray(cols)])


def _pin_methods():
    def declare_pin(self):
        L = 2
        self.inp("w_ext", [L, DM, WX_COLS])
        self.inp("rope_m", [2, 32, NLAT])
        self.inp("rope_s", [2, 128, NLAT])
        self.scr("uT", [8, 128, TT], BF16)
        self.scr("cqkvT", [3, 128, TT], BF16)
        self.scr("krT", [32, TT], BF16)
        self.scr("sqT", [4, 128, TT], BF16)
        self.scr("skT", [128, TT], BF16)
        self.scr("pb", [PAD_ROWS, 1792])
        self.scr("ph", [PAD_ROWS, 1536])
        self.scr("pv", [TT, 128])

    def phase_pin(self, l, src):
        P = self.P
        d = self.d
        src_ap, src_name = src
        P.mark()
        NW = FM_COLS + TM_COLS
        w = P.sbuf([128, 8, NW], BF16, "wpin")
        wb = [Buf("wpin%d" % k) for k in range(8)]
        for k in range(8):
            P.dma(w[:, k, :], d["w_ext"][l, k * 128:(k + 1) * 128, 0:NW], writes=[wb[k]], q="pool")
        z = P.sbuf([1, 1792], F32, "zrow")
        P.op("pool", lambda g: g.memset(z[:], 0.0), writes=[z])
        for r in (0, NLAT + 1, TT + 2):
            P.dma(d["pb"][r:r + 1, :], z[:], reads=[z], writes=[self.db("pb", "pad%d" % r)])
            P.dma(d["ph"][r:r + 1, :], z[:, 0:1536], reads=[z], writes=[self.db("ph", "pad%d" % r)])
        xbuf = [P.sbuf([128, 4, DM], F32, "xbuf%d" % i) for i in range(2)]
        xbb = [[Buf("xb%d_%d" % (i, j)) for j in range(4)] for i in range(2)]
        uT = [P.sbuf([128, 8, 512], BF16, "uT%d" % i) for i in range(2)]
        junk = P.sbuf([128, DM], BF16, "junk")
        ss = [P.sbuf([128, 1], F32, "ss%d" % i) for i in range(2)]
        rs = [P.sbuf([128, 1], F32, "rs%d" % i) for i in range(2)]
        xs = [P.sbuf([128, DM], BF16, "xs%d" % i) for i in range(2)]
        tabm = [P.sbuf([32, 2, 512], F32, "tabm%d" % i) for i in range(2)]
        tabs = [P.sbuf([128, 2, 512], F32, "tabs%d" % i) for i in range(2)]
        t1 = [P.sbuf([128, 512], F32, "t1_%d" % i) for i in range(2)]
        t2 = [P.sbuf([128, 512], F32, "t2_%d" % i) for i in range(2)]
        fo = [P.sbuf([128, 512], BF16, "fo%d" % i) for i in range(3)]
        tmo = [P.sbuf([128, TM_COLS], F32, "tmo%d" % i) for i in range(2)]
        groups = [list(range(4 * g, 4 * g + 4)) for g in range(8)] + [[32, 33]]

        def load(gi):
            for i, t in enumerate(groups[gi]):
                xt = Tile(xbuf[gi % 2].t, xbb[gi % 2][i])
                P.dma(xbuf[gi % 2][:, i, :], src_ap[t * 128:(t + 1) * 128, :],
                      reads=[self.db(src_name, t)], writes=[xt])
            if gi < 8:
                P.dma(tabm[gi % 2][:], d["rope_m"][:, :, gi * 512:(gi + 1) * 512].rearrange("c p t -> p c t"),
                      writes=[tabm[gi % 2]])
                P.dma(tabs[gi % 2][:], d["rope_s"][:, :, gi * 512:(gi + 1) * 512].rearrange("c p t -> p c t"),
                      writes=[tabs[gi % 2]])

        load(0)
        nfo = 0
        nev = 0
        for gi, tl in enumerate(groups):
            if gi + 1 < len(groups):
                load(gi + 1)
            n = 128 * len(tl)
            t0 = tl[0] * 128
            lat = gi < 8
            s = 0 if lat else 1
            xb = xbuf[gi % 2]
            u = uT[gi % 2]
            for i, t in enumerate(tl):
                xt = Tile(xb.t, xbb[gi % 2][i])
                self.norm_T(xt, xb[:, i, :], 1, s, u, u[:, :, i * 128:(i + 1) * 128], 0 if i % 2 == 0 else 7,
                            (junk, ss[i % 2], rs[i % 2], xs[i % 2]))
            P.dma(d["uT"][:, :, t0:t0 + n].rearrange("k p t -> p k t"), u[:, :, 0:n], reads=[u],
                  writes=[self.db("uT", gi)])

            def fm_mm(ps, c0, m):
                for k in range(8):
                    P.mm(ps[0:m, 0:n], w[:, k, c0:c0 + m], u[:, k, 0:n], start=(k == 0), stop=(k == 7),
                         reads=[wb[k], u], writes=[ps])

            for c in range(3):
                ps = self.ps[1 + (c % 2) * 2]
                fm_mm(ps, c * 128, 128)
                o = fo[nfo % 3]
                nfo += 1
                P.op("act", lambda g: g.activation(out=o[:, 0:n], in_=ps[:, 0:n], func=AF.Copy),
                     reads=[ps], writes=[o])
                P.dma(d["cqkvT"][c, :, t0:t0 + n], o[:, 0:n], reads=[o], writes=[self.db("cqkvT", (c, gi))])
            roped = [(384, 416, 32, tabm, d["krT"][:, t0:t0 + n], ("krT", gi))]
            for c in range(4):
                roped.append((448 + c * 128, 960 + c * 128, 128, tabs, d["sqT"][c, :, t0:t0 + n], ("sqT", (c, gi))))
            roped.append((1472, 1600, 128, tabs, d["skT"][:, t0:t0 + n], ("skT", gi)))
            for ri, (cx, csw, m, tab, dst, dk) in enumerate(roped):
                psx = self.ps[1 + (ri % 2) * 2]
                fm_mm(psx, cx, m)
                o = fo[nfo % 3]
                nfo += 1
                if lat:
                    pss = self.ps[2 + (ri % 2) * 2]
                    fm_mm(pss, csw, m)
                    tb = tab[gi % 2]
                    a1 = t1[ri % 2]
                    a2 = t2[ri % 2]
                    P.op("dve", lambda g: g.tensor_tensor(out=a1[0:m, :], in0=psx[0:m, :], in1=tb[0:m, 0, :],
                                                          op=ALU.mult), reads=[psx, tb], writes=[a1])
                    P.op("dve", lambda g: g.tensor_tensor(out=a2[0:m, :], in0=pss[0:m, :], in1=tb[0:m, 1, :],
                                                          op=ALU.mult), reads=[pss, tb], writes=[a2])
                    P.op("pool", lambda g: g.tensor_tensor(out=o[0:m, :], in0=a1[0:m, :], in1=a2[0:m, :],
                                                           op=ALU.add), reads=[a1, a2], writes=[o])
                else:
                    P.op("act", lambda g: g.activation(out=o[0:m, 0:n], in_=psx[0:m, 0:n], func=AF.Copy),
                         reads=[psx], writes=[o])
                P.dma(dst, o[0:m, 0:n], reads=[o], writes=[self.db(*dk)])
            for i, t in enumerate(tl):
                st = tmo[i % 2]
                for cb in range(7):
                    c0 = cb * 512
                    cw = min(512, TM_COLS - c0)
                    ps = self.ps[5 + (cb % 2)]
                    for k in range(8):
                        P.mm(ps[:, 0:cw], u[:, k, i * 128:(i + 1) * 128], w[:, k, FM_COLS + c0:FM_COLS + c0 + cw],
                             start=(k == 0), stop=(k == 7), reads=[u, wb[k]], writes=[ps])
                    if nev % 2 == 0:
                        P.op("act", lambda g: g.activation(out=st[:, c0:c0 + cw], in_=ps[:, 0:cw], func=AF.Copy),
                             reads=[ps], writes=[st])
                    else:
                        P.op("dve", lambda g: g.tensor_copy(out=st[:, c0:c0 + cw], in_=ps[:, 0:cw]),
                             reads=[ps], writes=[st])
                    nev += 1
                r0 = tm_row(t * 128)
                P.dma(d["pb"][r0:r0 + 128, :], st[:, 0:1792], reads=[st], writes=[self.db("pb", t)])
                P.dma(d["ph"][r0:r0 + 128, :], st[:, 1792:3328], reads=[st], writes=[self.db("ph", t)])
                P.dma(d["pv"][t * 128:(t + 1) * 128, :], st[:, 3328:3456], reads=[st], writes=[self.db("pv", t)])
        P.release()

    KB.declare_pin = declare_pin
    KB.phase_pin = phase_pin


_pin_methods()


def _attn_methods():
    def declare_attn(self):
        L = 2
        self.inp("mla_nT", [L, 128, 3])
        self.inp("mla_wq2", [L, 256, 8 * 2 * 96])
        self.inp("mla_wk", [L, 128, 512])
        self.inp("mla_wv", [L, 128, 512])
        self.inp("swa_sink", [L, 8])
        self.inp("swa_mask", [2, 128, 128])
        self.scr("yT", [4, 512, TT], BF16)

    def phase_mla(self, l, with_ctx):
        P = self.P
        d = self.d
        P.mark()
        scale = 96.0 ** -0.5
        wq = P.sbuf([128, 2, 8, 2, 96], BF16, "wq")
        for c in range(2):
            P.dma(wq[:, c].rearrange("p h s m -> p (h s m)"), d["mla_wq2"][l, c * 128:(c + 1) * 128, :],
                  writes=[wq], q="pool")
        wk = P.sbuf([128, 8, 64], BF16, "wk")
        P.dma(wk[:].rearrange("p h m -> p (h m)"), d["mla_wk"][l], writes=[wk], q="pool")
        wv = P.sbuf([128, 512], BF16, "wv")
        P.dma(wv[:], d["mla_wv"][l], writes=[wv], q="pool")
        nT = P.sbuf([128, 3], F32, "nT")
        P.dma(nT[:], d["mla_nT"][l], writes=[nT])
        ones_f = P.sbuf([128, 128], F32, "ones_f")
        P.op("pool", lambda g: g.memset(ones_f[:], 1.0), writes=[ones_f])
        cqn = P.sbuf([128, 2, TT], BF16, "cqn")
        ckvn = P.sbuf([128, TT], BF16, "ckvn")
        vaug = P.sbuf([128, NTILE, 8, 128], BF16, "vaug")
        P.op("pool", lambda g: g.memset(vaug[:, :, :, 64:128], 1.0), writes=[vaug])
        groups = [(g * 512, 512) for g in range(8)] + [(NLAT, 256)]
        P.mark()
        xin = [P.sbuf([128, 3, 512], BF16, "xin%d" % i) for i in range(2)]
        sq = [P.sbuf([128, 3, 512], F32, "sq%d" % i) for i in range(2)]
        rsb = [P.sbuf([128, 2, 512], F32, "rsb%d" % i) for i in range(2)]
        for gi, (t0, n) in enumerate(groups):
            xi = xin[gi % 2]
            P.dma(xi[:, :, 0:n], d["cqkvT"][:, :, t0:t0 + n].rearrange("c p t -> p c t"),
                  reads=[self.db("cqkvT", (c, gi)) for c in range(3)], writes=[xi])
            sqi = sq[gi % 2]
            P.op("pool", lambda g: g.tensor_tensor(out=sqi[:, :, 0:n], in0=xi[:, :, 0:n], in1=xi[:, :, 0:n],
                                                   op=ALU.mult), reads=[xi], writes=[sqi])
            psq = self.ps[6]
            psk = self.ps[7]
            for c in range(2):
                P.mm(psq[:, 0:n], ones_f[:], sqi[:, c, 0:n], start=(c == 0), stop=(c == 1),
                     reads=[ones_f, sqi], writes=[psq])
            P.mm(psk[:, 0:n], ones_f[:], sqi[:, 2, 0:n], reads=[ones_f, sqi], writes=[psk])
            r = rsb[gi % 2]
            P.op("act", lambda g: g.activation(out=r[:, 0, 0:n], in_=psq[:, 0:n], func=AF.Sqrt, scale=1.0 / 256,
                                               bias=EPS), reads=[psq], writes=[r])
            P.op("act", lambda g: g.activation(out=r[:, 1, 0:n], in_=psk[:, 0:n], func=AF.Sqrt, scale=1.0 / 128,
                                               bias=EPS), reads=[psk], writes=[r])
            P.op("dve", lambda g: g.reciprocal(out=r[:, :, 0:n], in_=r[:, :, 0:n]), reads=[r], writes=[r])
            for c in range(2):
                P.op("dve", lambda g: g.scalar_tensor_tensor(out=cqn[:, c, t0:t0 + n], in0=xi[:, c, 0:n],
                                                             scalar=nT[:, c:c + 1], in1=r[:, 0, 0:n],
                                                             op0=ALU.mult, op1=ALU.mult),
                     reads=[xi, nT, r], writes=[cqn])
            P.op("dve", lambda g: g.scalar_tensor_tensor(out=ckvn[:, t0:t0 + n], in0=xi[:, 2, 0:n],
                                                         scalar=nT[:, 2:3], in1=r[:, 1, 0:n],
                                                         op0=ALU.mult, op1=ALU.mult),
                 reads=[xi, nT, r], writes=[ckvn])
        P.release()
        for t in range(NTILE):
            ps = self.ps[5 + t % 2]
            P.mm(ps[:], ckvn[:, t * 128:(t + 1) * 128], wv[:], reads=[ckvn, wv], writes=[ps])
            eng = "act" if t % 2 == 0 else "dve"
            if eng == "act":
                P.op("act", lambda g: g.activation(out=vaug[:, t, :, 0:64],
                                                   in_=ps[:].rearrange("p (h m) -> p h m", m=64), func=AF.Copy),
                     reads=[ps], writes=[vaug])
            else:
                P.op("dve", lambda g: g.tensor_copy(out=vaug[:, t, :, 0:64],
                                                    in_=ps[:].rearrange("p (h m) -> p h m", m=64)),
                     reads=[ps], writes=[vaug])
        NQ = TT if with_ctx else NLAT
        KT = [P.sbuf([96, TT], BF16, "KT%d" % i) for i in range(2)]
        QT = [P.sbuf([96, TT], BF16, "QT%d" % i) for i in range(2)]
        tab = [P.sbuf([96, 2, 512], F32, "tab%d" % i) for i in range(2)]
        a1 = [P.sbuf([96, 512], F32, "a1_%d" % i) for i in range(2)]
        a2 = [P.sbuf([96, 512], F32, "a2_%d" % i) for i in range(2)]
        PT = [P.sbuf([128, 512], BF16, "PT%d" % i) for i in range(4)]
        rec = [P.sbuf([64, 512], F32, "rec%d" % i) for i in range(2)]
        yo = [P.sbuf([64, 512], BF16, "yo%d" % i) for i in range(2)]
        npt = 0
        nqg = 0
        for h in range(8):
            kt = KT[h % 2]
            qt = QT[h % 2]
            P.dma(kt[64:96, :], d["krT"], reads=[self.db("krT", gi) for gi in range(9)], writes=[kt])
            for gi, (t0, n) in enumerate(groups):
                lat = gi < 8
                if gi >= 8 and not with_ctx:
                    pass
                pk = self.ps[5]
                P.mm(pk[0:64, 0:n], wk[:, h, :], ckvn[:, t0:t0 + n], reads=[wk, ckvn], writes=[pk])
                P.op("act", lambda g: g.activation(out=kt[0:64, t0:t0 + n], in_=pk[0:64, 0:n], func=AF.Copy),
                     reads=[pk], writes=[kt])
                if gi >= 8 and not with_ctx:
                    continue
                p1 = self.ps[6]
                for c in range(2):
                    P.mm(p1[0:96, 0:n], wq[:, c, h, 0, :], cqn[:, c, t0:t0 + n], start=(c == 0), stop=(c == 1),
                         reads=[wq, cqn], writes=[p1])
                if lat:
                    p2 = self.ps[7]
                    for c in range(2):
                        P.mm(p2[0:96, 0:n], wq[:, c, h, 1, :], cqn[:, c, t0:t0 + n], start=(c == 0),
                             stop=(c == 1), reads=[wq, cqn], writes=[p2])
                    tb = tab[gi % 2]
                    P.dma(tb[64:96, :, :], d["rope_m"][:, :, t0:t0 + n].rearrange("c p t -> p c t"), writes=[tb])
                    P.op("act", lambda g: g.activation(out=qt[0:64, t0:t0 + n], in_=p1[0:64, 0:n], func=AF.Copy),
                         reads=[p1], writes=[qt])
                    b1 = a1[gi % 2]
                    b2 = a2[gi % 2]
                    P.op("dve", lambda g: g.tensor_tensor(out=b1[64:96, :], in0=p1[64:96, :], in1=tb[64:96, 0, :],
                                                          op=ALU.mult), reads=[p1, tb], writes=[b1])
                    P.op("dve", lambda g: g.tensor_tensor(out=b2[64:96, :], in0=p2[64:96, :], in1=tb[64:96, 1, :],
                                                          op=ALU.mult), reads=[p2, tb], writes=[b2])
                    P.op("pool", lambda g: g.tensor_tensor(out=qt[64:96, t0:t0 + n], in0=b1[64:96, :],
                                                           in1=b2[64:96, :], op=ALU.add),
                         reads=[b1, b2], writes=[qt])
                else:
                    P.op("act", lambda g: g.activation(out=qt[0:96, t0:t0 + n], in_=p1[0:96, 0:n], func=AF.Copy),
                         reads=[p1], writes=[qt])
            qgroups = [(g * 512, 512, list(range(NTILE))) for g in range(8)]
            if with_ctx:
                qgroups.append((NLAT, 256, [32, 33]))
            for (q0, n, kbs) in qgroups:
                po = self.ps[3 + nqg % 2]
                nqg += 1
                pend = []

                def pv(item, first, last):
                    kb, pt = item
                    P.mm(po[:, 0:n], vaug[:, kb, h, :], pt[:, 0:n], start=first, stop=last,
                         reads=[vaug, pt], writes=[po])

                for idx, kb in enumerate(kbs):
                    pss = self.ps[npt % 3]
                    pt = PT[npt % 4]
                    npt += 1
                    P.mm(pss[:, 0:n], kt[:, kb * 128:(kb + 1) * 128], qt[:, q0:q0 + n], reads=[kt, qt],
                         writes=[pss])
                    P.op("act", lambda g: g.activation(out=pt[:, 0:n], in_=pss[:, 0:n], func=AF.Exp, scale=scale),
                         reads=[pss], writes=[pt])
                    pend.append((kb, pt))
                    if len(pend) > 2:
                        pv(pend.pop(0), idx == 2, False)
                while pend:
                    first = (len(kbs) - len(pend) == 0)
                    pv(pend.pop(0), first, len(pend) == 0)
                rc = rec[nqg % 2]
                y = yo[nqg % 2]
                P.op("dve", lambda g: g.reciprocal(out=rc[:, 0:n], in_=po[64:128, 0:n]), reads=[po], writes=[rc])
                P.op("dve", lambda g: g.tensor_tensor(out=y[:, 0:n], in0=po[0:64, 0:n], in1=rc[:, 0:n],
                                                      op=ALU.mult), reads=[po, rc], writes=[y])
                P.dma(d["yT"][0, h * 64:(h + 1) * 64, q0:q0 + n], y[:, 0:n], reads=[y],
                      writes=[self.db("yT", (0, h, q0))])
        P.release()

    def phase_swa(self, l, with_ctx):
        P = self.P
        d = self.d
        P.mark()
        scale = 64.0 ** -0.5
        es = P.sbuf([128, 8], F32, "es")
        P.dma(es[:], d["swa_sink"][l:l + 1, :].to_broadcast([128, 8]), writes=[es])
        P.op("act", lambda g: g.activation(out=es[:], in_=es[:], func=AF.Exp), reads=[es], writes=[es])
        msk = P.sbuf([128, 2, 128], BF16, "msk")
        P.dma(msk[:], d["swa_mask"].rearrange("c p t -> p c t"), writes=[msk], q="pool")
        Kk = P.sbuf([64, TT], BF16, "Kk")
        Qk = P.sbuf([64, 4, TT], BF16, "Qk")
        va = P.sbuf([128, NTILE, 128], BF16, "va")
        P.op("pool", lambda g: g.memset(va[:, :, 64:128], 1.0), writes=[va])
        yd = P.sbuf([64, 4, TT], BF16, "yd")
        PT = [P.sbuf([128, 4, 128], BF16, "PT%d" % i) for i in range(6)]
        den = [P.sbuf([64, 4, 128], F32, "den%d" % i) for i in range(2)]
        npt = 0
        nblk = 0
        allsq = [self.db("sqT", (c, gi)) for c in range(4) for gi in range(9)]
        allsk = [self.db("skT", gi) for gi in range(9)]
        allpv = [self.db("pv", t) for t in range(NTILE)]
        for kh in range(2):
            P.dma(Kk[:], d["skT"][kh * 64:(kh + 1) * 64, :], reads=allsk, writes=[Kk])
            for g_ in range(4):
                hh = kh * 4 + g_
                P.dma(Qk[:, g_, :], d["sqT"][hh // 2, (hh % 2) * 64:(hh % 2) * 64 + 64, :], reads=allsq, writes=[Qk])
            P.dma(va[:, :, 0:64], d["pv"].rearrange("(t p) c -> p t c", p=128)[:, :, kh * 64:(kh + 1) * 64],
                  reads=allpv, writes=[va], q="pool")
            nq = NTILE if with_ctx else 32
            for i in range(nq):
                if i < 32:
                    kbs = []
                    if i > 0:
                        kbs.append((i - 1, 0))
                    kbs.append((i, None))
                    if i < 31:
                        kbs.append((i + 1, 1))
                    kbs += [(32, None), (33, None)]
                else:
                    kbs = [(32, None), (33, None)]
                po = self.ps[3 + nblk % 2]
                dn = den[nblk % 2]
                nblk += 1
                sbanks = [0, 1, 2, 5, 6, 7]
                items = []
                for idx, (kb, mk) in enumerate(kbs):
                    pss = self.ps[sbanks[npt % 6]]
                    pt = PT[npt % 6]
                    npt += 1
                    for g_ in range(4):
                        P.mm(pss[:, g_ * 128:(g_ + 1) * 128], Kk[:, kb * 128:(kb + 1) * 128],
                             Qk[:, g_, i * 128:(i + 1) * 128], reads=[Kk, Qk], writes=[pss])
                    items.append((kb, mk, pss, pt))
                for idx, (kb, mk, pss, pt) in enumerate(items):
                    P.op("act", lambda g: g.activation(out=pt[:].rearrange("p g t -> p (g t)"), in_=pss[:],
                                                       func=AF.Exp, scale=scale), reads=[pss], writes=[pt])
                    if mk is not None:
                        P.op("dve", lambda g: g.tensor_tensor(out=pt[:], in0=pt[:],
                                                              in1=msk[:, mk, :].unsqueeze(1).to_broadcast([128, 4, 128]),
                                                              op=ALU.mult), reads=[pt, msk], writes=[pt])
                for idx, (kb, mk, pss, pt) in enumerate(items):
                    P.mm(po[:], va[:, kb, :], pt[:].rearrange("p g t -> p (g t)"), start=(idx == 0),
                         stop=(idx == len(kbs) - 1), reads=[va, pt], writes=[po])
                P.op("dve", lambda g: g.tensor_tensor(out=dn[:], in0=po[64:128, :].rearrange("p (g t) -> p g t", g=4),
                                                      in1=es[64:128, kh * 4:(kh + 1) * 4].unsqueeze(2).to_broadcast([64, 4, 128]),
                                                      op=ALU.add), reads=[po, es], writes=[dn])
                P.op("dve", lambda g: g.reciprocal(out=dn[:], in_=dn[:]), reads=[dn], writes=[dn])
                P.op("dve", lambda g: g.tensor_tensor(out=yd[:, :, i * 128:(i + 1) * 128],
                                                      in0=po[0:64, :].rearrange("p (g t) -> p g t", g=4), in1=dn[:],
                                                      op=ALU.mult), reads=[po, dn], writes=[yd])
            for g_ in range(4):
                hh = kh * 4 + g_
                P.dma(d["yT"][3, hh * 64:(hh + 1) * 64, 0:nq * 128], yd[:, g_, 0:nq * 128], reads=[yd],
                      writes=[self.db("yT", (3, hh))])
        P.release()

    KB.declare_attn = declare_attn
    KB.phase_mla = phase_mla
    KB.phase_swa = phase_swa


_attn_methods()


def _merge_methods():
    def declare_merge(self):
        L = 2
        self.inp("w_branch", [L, 4, 512, DM])
        self.inp("w_out", [L, DM, DM])
        self.inp("b_gateT", [L, 128, 32])

    def phase_merge(self, l, with_ctx, xname="xs"):
        P = self.P
        d = self.d
        P.mark()
        wg = P.sbuf([128, 8, 4096], BF16, "wg")
        wgb = [Buf("wg%d" % k) for k in range(8)]
        for k in range(8):
            P.dma(wg[:, k, :], d["w_ext"][l, k * 128:(k + 1) * 128, WX_G0:WX_G0 + 4096], writes=[wgb[k]], q="pool")
        wbr = P.sbuf([128, 4, 4, DM], BF16, "wbr")
        for br in range(4):
            P.dma(wbr[:, br], d["w_branch"][l, br].rearrange("(kc p) n -> p kc n", p=128), writes=[wbr], q="pool")
        wo = P.sbuf([128, 8, DM], BF16, "wo")
        P.dma(wo[:], d["w_out"][l].rearrange("(k p) n -> p k n", p=128), writes=[wo], q="pool")
        bg = P.sbuf([128, 4, 8], F32, "bg")
        P.dma(bg[:].rearrange("p b o -> p (b o)"), d["b_gateT"][l], writes=[bg])
        gt = [P.sbuf([128, DM], F32, "gt%d" % s) for s in range(2)]
        for s in range(2):
            P.dma(gt[s][:], d["gtrow"][l, 1, s:s + 1, :].to_broadcast([128, DM]),
                  reads=[self.db("gtrow", (l, 1))], writes=[gt[s]])
        groups = [(g * 512, 512) for g in range(8)] + ([(NLAT, 256)] if with_ctx else [])
        uT = [P.sbuf([128, 8, 512], BF16, "uT%d" % i) for i in range(2)]
        yg = [P.sbuf([128, 4, 4, 512], BF16, "yg%d" % i) for i in range(2)]
        mg = P.sbuf([128, 8, 512], BF16, "mg")
        mgb = [Buf("mg%d" % k) for k in range(8)]
        sg = [P.sbuf([128, 512], F32, "sg%d" % i) for i in range(2)]
        tm_ = [P.sbuf([128, 512], F32, "tm%d" % i) for i in range(2)]
        acc = [P.sbuf([128, 512], F32, "acc%d" % i) for i in range(2)]
        xbuf = [P.sbuf([128, DM], F32, "xb%d" % i) for i in range(2)]
        junk = P.sbuf([128, DM], BF16, "junk")
        s2 = [P.sbuf([128, 3], F32, "s2%d" % i) for i in range(2)]
        r2 = [P.sbuf([128, 1], F32, "r2%d" % i) for i in range(2)]
        tmp1 = P.sbuf([128, DM], F32, "tmp")
        tmp = [tmp1, tmp1]
        ally = [b for k, b in self.dbufs.items() if k[0] == "yT"]
        allu = [b for k, b in self.dbufs.items() if k[0] == "uT"]

        def load(gi):
            t0, n = groups[gi]
            P.dma(uT[gi % 2][:, :, 0:n], d["uT"][:, :, t0:t0 + n].rearrange("k p t -> p k t"), reads=allu,
                  writes=[uT[gi % 2]])
            for br in range(4):
                P.dma(yg[gi % 2][:, br, :, 0:n],
                      d["yT"][br].rearrange("(kc p) t -> p kc t", p=128)[:, :, t0:t0 + n], reads=ally,
                      writes=[yg[gi % 2]])

        load(0)
        nx = 0
        for gi, (t0, n) in enumerate(groups):
            if gi + 1 < len(groups):
                load(gi + 1)
            u = uT[gi % 2]
            y = yg[gi % 2]
            s = 0 if gi < 8 else 1
            for oc in range(8):
                ac = acc[oc % 2]
                for br in range(4):
                    psg = self.ps[1 + (br % 2) * 2]
                    psy = self.ps[2 + (br % 2) * 2]
                    c0 = br * 1024 + oc * 128
                    for k in range(8):
                        P.mm(psg[:, 0:n], wg[:, k, c0:c0 + 128], u[:, k, 0:n], start=(k == 0), stop=(k == 7),
                             reads=[wgb[k], u], writes=[psg])
                    for kc in range(4):
                        P.mm(psy[:, 0:n], wbr[:, br, kc, oc * 128:(oc + 1) * 128], y[:, br, kc, 0:n],
                             start=(kc == 0), stop=(kc == 3), reads=[wbr, y], writes=[psy])
                    sgt = sg[br % 2]
                    P.op("act", lambda g: g.activation(out=sgt[:, 0:n], in_=psg[:, 0:n], func=AF.Sigmoid,
                                                       bias=bg[:, br, oc:oc + 1]), reads=[psg, bg], writes=[sgt])
                    if br == 0:
                        P.op("dve", lambda g: g.tensor_tensor(out=ac[:, 0:n], in0=sgt[:, 0:n], in1=psy[:, 0:n],
                                                              op=ALU.mult), reads=[sgt, psy], writes=[ac])
                    else:
                        tt = tm_[br % 2]
                        P.op("dve", lambda g: g.tensor_tensor(out=tt[:, 0:n], in0=sgt[:, 0:n], in1=psy[:, 0:n],
                                                              op=ALU.mult), reads=[sgt, psy], writes=[tt])
                        if br < 3:
                            P.op("pool", lambda g: g.tensor_tensor(out=ac[:, 0:n], in0=ac[:, 0:n], in1=tt[:, 0:n],
                                                                   op=ALU.add), reads=[ac, tt], writes=[ac])
                        else:
                            P.op("pool", lambda g: g.tensor_tensor(out=mg[:, oc, 0:n], in0=ac[:, 0:n],
                                                                   in1=tt[:, 0:n], op=ALU.add),
                                 reads=[ac, tt], writes=[mgb[oc]])
            for i in range(n // 128):
                t = t0 // 128 + i
                xb = xbuf[nx % 2]
                P.dma(xb[:], d[xname][t * 128:(t + 1) * 128, :], reads=[self.db(xname, t)], writes=[xb])
                for h in range(2):
                    po = self.ps[5 + h]
                    for k in range(8):
                        P.mm(po[:], mg[:, k, i * 128:(i + 1) * 128], wo[:, k, h * 512:(h + 1) * 512],
                             start=(k == 0), stop=(k == 7), reads=[mgb[k], wo], writes=[po])
                self.norm_res_out([5, 6], xb, xb[:], gt[s], (junk, s2[nx % 2], r2[nx % 2], tmp[nx % 2]),
                                  d[xname][t * 128:(t + 1) * 128, :], self.db(xname, t))
                nx += 1
        P.release()

    KB.declare_merge = declare_merge
    KB.phase_merge = phase_merge


_merge_methods()


def hy_tables(n):
    M = 2 * n
    S1 = M // 64
    T1 = n // 64
    s1 = np.arange(S1)[:, None, None]
    s2 = np.arange(64)[None, :, None]
    f1 = np.arange(S1)[None, None, :]
    ang = 2.0 * np.pi * ((f1 * (64 * s1 + s2)) % M) / M
    F1n = S1 // 2 + 1
    W1 = np.stack([np.cos(ang), -np.sin(ang)], axis=2).astype(np.float32)[..., :F1n]
    s2v = np.arange(64)[:, None]
    f2v = np.arange(64)[None, :]
    th = 2.0 * np.pi * ((s2v * f2v) % 64) / 64
    c, s = np.cos(th), np.sin(th)
    D2 = np.block([[c, -s], [s, c]]).astype(np.float32)
    D2sw = np.concatenate([D2[:, 64:], D2[:, :64]], axis=1)
    E = np.block([[c, s], [-s, c]]).astype(np.float32)
    f1v = np.arange(S1)[:, None, None]
    t2v = np.arange(64)[None, :, None]
    t1v = np.arange(T1)[None, None, :]
    psi = 2.0 * np.pi * ((f1v * (64 * t1v + t2v)) % M) / M
    W3 = np.stack([np.cos(psi), -np.sin(psi)], axis=2).astype(np.float32)
    cw = np.full(F1n, 2.0, np.float32)
    cw[0] = 1.0
    cw[-1] = 1.0
    W3 = np.ascontiguousarray(W3[:F1n] * cw[:, None, None, None])
    return dict(W1=W1, D2=D2, D2sw=D2sw, E=E, W3=W3, S1=S1, T1=T1, M=M, F1n=F1n)


def hy_feats(n):
    M = 2 * n
    f32 = np.float32
    t = np.linspace(0.0, 1.0, n, dtype=f32)
    bands = np.linspace(1e-4, 15.0, 16, dtype=f32)
    ang = (f32(2.0 * np.pi / n) * np.arange(n, dtype=f32)[:, None] * bands[None, :]).astype(f32)
    feats = np.concatenate([t[:, None], np.cos(ang), -np.sin(ang)], axis=-1).astype(f32)
    deltas = np.abs(np.linspace(np.log(1e-2) / 1.5, np.log(1e-2) / 0.3, 512, dtype=f32)).astype(f32)
    win = np.exp(-t[:, None] * deltas[None, :]).astype(f32)
    idx = np.zeros(M, np.int64)
    idx[:n] = np.arange(n)
    idx[n + 1:] = n - np.arange(1, n)
    fK = feats[idx].copy()
    wK = win[idx].copy()
    fK[n] = feats[0]
    wK[n] = win[0]
    return np.ascontiguousarray(fK.T), np.ascontiguousarray(wK)


def _hyena_methods():
    TWO_PI = 2.0 * np.pi

    def declare_hyena(self):
        L = 2
        self.inp("hyena_conv", [L, 3, 1536])
        self.inp("hyena_conv_b", [L, 1536])
        self.inp("hyena_w1", [L, 33, 64])
        self.inp("hyena_w2", [L, 64, 64])
        self.inp("hyena_w3", [L, 64, 2048])
        self.inp("hyT", [L, 64, 4])
        self.inp("hyena_bias", [L, 2, 512])
        self.inp("hy_D", [3, 128, 128])
        self.inp("hyL_W1", [128, 64 * 2 * 65])
        self.inp("hyL_W3", [65, 64 * 2 * 64])
        self.inp("hyL_fK", [33, 8192])
        self.inp("hyL_wK", [8192, 512])
        self.inp("hyC_W1", [8, 64 * 2 * 5])
        self.inp("hyC_W3", [5, 64 * 2 * 4])
        self.inp("hyC_fK", [33, 512])
        self.inp("hyC_wK", [512, 512])
        self.scr("hcs", [TT, 1536])
        self.scr("kbuf", [2, 8192, 512], BF16)
        self.scr("Bd", [128, 128, 512], BF16)
        self.scr("Dd", [128, 128, 512], BF16)
        self.scr("KAB_L", [2, 128, 2, 128, 512], BF16)
        self.scr("KAB_C", [2, 8, 2, 128, 512], BF16)
        self.scr("zt1", [TT, 512])
        self.scr("zt2", [TT, 512])

    def hy_shortconv(self, l, ntiles):
        P = self.P
        d = self.d
        P.mark()
        ck = P.sbuf([128, 3, 1536], F32, "ck")
        P.dma(ck[:].rearrange("p a c -> p (a c)"),
              d["hyena_conv"][l:l + 1].rearrange("o a c -> o (a c)").to_broadcast([128, 4608]), writes=[ck])
        cb = P.sbuf([128, 1536], F32, "cb")
        P.dma(cb[:], d["hyena_conv_b"][l:l + 1, :].to_broadcast([128, 1536]), writes=[cb])
        bufs = [[P.sbuf([128, 1536], F32, "sc%d_%d" % (i, j)) for j in range(3)] for i in range(3)]
        allph = [b for k, b in self.dbufs.items() if k[0] == "ph"]
        for t in range(ntiles):
            r0 = tm_row(t * 128)
            pv_, cu, nx = bufs[t % 3]
            P.dma(pv_[:], d["ph"][r0 - 1:r0 + 127, :], reads=allph, writes=[pv_])
            P.dma(cu[:], d["ph"][r0:r0 + 128, :], reads=allph, writes=[cu])
            P.dma(nx[:], d["ph"][r0 + 1:r0 + 129, :], reads=allph, writes=[nx])
            P.op("dve", lambda g: g.tensor_tensor(out=pv_[:], in0=pv_[:], in1=ck[:, 0, :], op=ALU.mult),
                 reads=[pv_, ck], writes=[pv_])
            P.op("pool", lambda g: g.tensor_tensor(out=cu[:], in0=cu[:], in1=ck[:, 1, :], op=ALU.mult),
                 reads=[cu, ck], writes=[cu])
            P.op("dve", lambda g: g.tensor_tensor(out=nx[:], in0=nx[:], in1=ck[:, 2, :], op=ALU.mult),
                 reads=[nx, ck], writes=[nx])
            P.op("pool", lambda g: g.tensor_tensor(out=cu[:], in0=cu[:], in1=pv_[:], op=ALU.add),
                 reads=[cu, pv_], writes=[cu])
            P.op("dve", lambda g: g.tensor_tensor(out=nx[:], in0=nx[:], in1=cb[:], op=ALU.add),
                 reads=[nx, cb], writes=[nx])
            P.op("pool", lambda g: g.tensor_tensor(out=cu[:], in0=cu[:], in1=nx[:], op=ALU.add),
                 reads=[cu, nx], writes=[cu])
            P.dma(d["hcs"][t * 128:(t + 1) * 128, :], cu[:], reads=[cu], writes=[self.db("hcs", t)])
        P.release()

    def hy_load_tabs(self, n):
        P = self.P
        d = self.d
        pre = "hyL" if n == NLAT else "hyC"
        S1 = 2 * n // 64
        T1 = n // 64
        F1n = S1 // 2 + 1
        W1 = P.sbuf([S1, 64, 2, F1n], BF16, "W1")
        P.dma(W1[:].rearrange("p a b c -> p (a b c)"), d[pre + "_W1"], writes=[W1], q="pool")
        W3 = P.sbuf([F1n, 64, 2, T1], BF16, "W3")
        P.dma(W3[:].rearrange("p a b c -> p (a b c)"), d[pre + "_W3"], writes=[W3], q="pool")
        Dm = P.sbuf([128, 3, 128], BF16, "Dm")
        P.dma(Dm[:], d["hy_D"].rearrange("a p c -> p a c"), writes=[Dm], q="pool")
        return dict(W1=W1, W3=W3, Dm=Dm, S1=S1, T1=T1, n=n, F1n=F1n)

    def hy_stage1(self, tb, src_ap, nz, src_reads, cast):
        P = self.P
        d = self.d
        S1 = tb["F1n"]
        P.mark()
        U = P.sbuf([nz, 64, 512], BF16, "U")
        P.dma(U[:], src_ap.rearrange("(a s) c -> a s c", s=64), reads=src_reads, writes=[U],
              q=("pool" if cast else "sp"))
        bo = [P.sbuf([S1, 2, 512], BF16, "bo%d" % i) for i in range(4)]
        bdv = d["Bd"].rearrange("(r s) f c -> s f r c", r=2)
        for s2 in range(64):
            o = bo[s2 % 4]
            for ri in range(2):
                ps = self.ps[(2 * s2 + ri) % 4]
                P.mm(ps[0:S1, :], tb["W1"][0:nz, s2, ri, :], U[:, s2, :], reads=[tb["W1"], U], writes=[ps])
                if ri == 0:
                    P.op("act", lambda g: g.activation(out=o[:, ri, :], in_=ps[0:S1, :], func=AF.Copy),
                         reads=[ps], writes=[o])
                else:
                    P.op("dve", lambda g: g.tensor_copy(out=o[:, ri, :], in_=ps[0:S1, :]), reads=[ps], writes=[o])
            P.dma(bdv[s2, 0:S1], o[:], reads=[o], writes=[self.db("Bd", s2)])
        P.release()

    def hy_stage2(self, tb, cb, cb2=None):
        P = self.P
        d = self.d
        S1 = tb["F1n"]
        allbd = [self.db("Bd", s2) for s2 in range(64)]
        FG = 5
        bins = [P.sbuf([128, FG, 512], BF16, "bin%d" % i) for i in range(3)]
        prev = [None]
        for fg in range(S1 // FG):
            b = bins[fg % 3]
            P.dma(b[:], d["Bd"][:, fg * FG:(fg + 1) * FG, :], reads=allbd, writes=[b])
            for j in range(FG):
                f1 = fg * FG + j
                p1 = self.ps[(f1 % 3) * 2]
                p2 = self.ps[(f1 % 3) * 2 + 1]
                P.mm(p1[:], tb["Dm"][:, 0, :], b[:, j, :], reads=[tb["Dm"], b], writes=[p1])
                P.mm(p2[:], tb["Dm"][:, 1, :], b[:, j, :], reads=[tb["Dm"], b], writes=[p2])
                cb(f1, p1, p2)
                if cb2 is not None and prev[0] is not None:
                    cb2(prev[0])
                prev[0] = f1
        if cb2 is not None and prev[0] is not None:
            cb2(prev[0])

    def hy_filters(self, l, n, kab_name):
        P = self.P
        d = self.d
        pre = "hyL" if n == NLAT else "hyC"
        M = 2 * n
        P.mark()
        tb = self.hy_load_tabs(n)
        hyT = P.sbuf([64, 4], F32, "hyT")
        P.dma(hyT[:], d["hyT"][l], writes=[hyT])
        sc = P.sbuf([64, 4], F32, "hsc")
        for j in range(2):
            P.op("dve", lambda g: g.tensor_scalar(out=sc[:, 2 * j:2 * j + 1], in0=hyT[:, j:j + 1],
                                                  scalar1=1.0 / TWO_PI, scalar2=None, op0=ALU.mult),
                 reads=[hyT], writes=[sc])
            P.op("dve", lambda g: g.tensor_tensor(out=sc[:, 2 * j + 1:2 * j + 2], in0=hyT[:, 2 + j:3 + j],
                                                  in1=sc[:, 2 * j:2 * j + 1], op=ALU.mult),
                 reads=[hyT, sc], writes=[sc])
            P.op("dve", lambda g: g.tensor_scalar(out=sc[:, 2 * j + 1:2 * j + 2], in0=sc[:, 2 * j + 1:2 * j + 2],
                                                  scalar1=64.0, scalar2=None, op0=ALU.add),
                 reads=[sc], writes=[sc])
        w1 = P.sbuf([33, 64], F32, "hw1")
        P.dma(w1[:], d["hyena_w1"][l], writes=[w1])
        w2 = P.sbuf([64, 64], F32, "hw2")
        P.dma(w2[:], d["hyena_w2"][l], writes=[w2])
        w3 = P.sbuf([64, 2048], F32, "hw3")
        P.dma(w3[:], d["hyena_w3"][l], writes=[w3])
        ones_f = P.sbuf([128, 128], F32, "ones_f")
        P.op("pool", lambda g: g.memset(ones_f[:], 1.0), writes=[ones_f])
        G2T = P.sbuf([64, M], F32, "G2T")
        rn = [P.sbuf([128, 512], F32, "rn%d" % o) for o in range(2)]
        P.mark()
        fk = [P.sbuf([33, 512], F32, "fk%d" % i) for i in range(2)]
        vt = [P.sbuf([64, 512], F32, "vt%d" % i) for i in range(2)]
        vi = [P.sbuf([64, 512], I32, "vi%d" % i) for i in range(2)]
        vf = [P.sbuf([64, 512], F32, "vf%d" % i) for i in range(2)]
        g1 = [P.sbuf([64, 512], F32, "g1%d" % i) for i in range(2)]
        cnt = [0]

        def sin_reduce(ps, j, out_ap, out_t):
            i = cnt[0] % 2
            cnt[0] += 1
            P.op("dve", lambda g: g.tensor_scalar(out=vt[i][:], in0=ps[0:64, :], scalar1=sc[:, 2 * j:2 * j + 1],
                                                  scalar2=sc[:, 2 * j + 1:2 * j + 2], op0=ALU.mult, op1=ALU.add),
                 reads=[ps, sc], writes=[vt[i]])
            P.op("dve", lambda g: g.tensor_copy(out=vi[i][:], in_=vt[i][:]), reads=[vt[i]], writes=[vi[i]])
            P.op("pool", lambda g: g.tensor_copy(out=vf[i][:], in_=vi[i][:]), reads=[vi[i]], writes=[vf[i]])
            P.op("pool", lambda g: g.tensor_tensor(out=vt[i][:], in0=vt[i][:], in1=vf[i][:], op=ALU.subtract),
                 reads=[vt[i], vf[i]], writes=[vt[i]])
            P.op("act", lambda g: g.activation(out=out_ap, in_=vt[i][:], func=AF.Sin, scale=TWO_PI),
                 reads=[vt[i]], writes=[out_t])

        for cbk in range(M // 512):
            f = fk[cbk % 2]
            P.dma(f[:], d[pre + "_fK"][:, cbk * 512:(cbk + 1) * 512], writes=[f])
            ps = self.ps[cbk % 2]
            P.mm(ps[0:64, :], w1[:], f[:], reads=[w1, f], writes=[ps])
            gg = g1[cbk % 2]
            sin_reduce(ps, 0, gg[:], gg)
            ps2 = self.ps[2 + cbk % 2]
            P.mm(ps2[0:64, :], w2[:], gg[:], reads=[w2, gg], writes=[ps2])
            sin_reduce(ps2, 1, G2T[:, cbk * 512:(cbk + 1) * 512], G2T)
        P.release()
        P.mark()
        wk = [P.sbuf([128, 512], F32, "wk%d" % i) for i in range(3)]
        kbt = [P.sbuf([128, 512], F32, "kbt%d" % i) for i in range(4)]
        ab = [P.sbuf([128, 512], F32, "ab%d" % i) for i in range(4)]
        kbo = [P.sbuf([128, 512], BF16, "kbo%d" % i) for i in range(4)]
        nlt = M // 128
        c = 0
        for lt in range(nlt):
            dirn = 0 if lt < n // 128 else 1
            w = wk[lt % 3]
            P.dma(w[:], d[pre + "_wK"][lt * 128:(lt + 1) * 128, :], writes=[w])
            for o in range(2):
                ps = self.ps[c % 4]
                kt_ = kbt[c % 4]
                a = ab[c % 4]
                ko = kbo[c % 4]
                c += 1
                P.mm(ps[:], G2T[:, lt * 128:(lt + 1) * 128], w3[:, o * 1024 + dirn * 512:o * 1024 + dirn * 512 + 512],
                     reads=[G2T, w3], writes=[ps])
                P.op("dve", lambda g: g.tensor_tensor(out=kt_[:], in0=ps[:], in1=w[:], op=ALU.mult),
                     reads=[ps, w], writes=[kt_])
                P.op("act", lambda g: g.activation(out=a[:], in_=kt_[:], func=AF.Abs), reads=[kt_], writes=[a])
                P.mm(self.ps[6 + o][:], ones_f[:], a[:], start=(lt == 0), stop=(lt == nlt - 1),
                     reads=[ones_f, a], writes=[self.ps[6 + o]])
                if lt == n // 128:
                    P.op("pool", lambda g: g.memset(kt_[0:1, :], 0.0), reads=[kt_], writes=[kt_])
                P.op("pool", lambda g: g.tensor_copy(out=ko[:], in_=kt_[:]), reads=[kt_], writes=[ko])
                P.dma(d["kbuf"][o, lt * 128:(lt + 1) * 128, :], ko[:], reads=[ko], writes=[self.db("kbuf", (o, lt))])
        for o in range(2):
            P.op("dve", lambda g: g.tensor_scalar(out=rn[o][:], in0=self.ps[6 + o][:], scalar1=float(M), scalar2=None,
                                                  op0=ALU.mult), reads=[self.ps[6 + o]], writes=[rn[o]])
            P.op("dve", lambda g: g.reciprocal(out=rn[o][:], in_=rn[o][:]), reads=[rn[o]], writes=[rn[o]])
        P.release()
        for o in range(2):
            allkb = [self.db("kbuf", (o, lt)) for lt in range(nlt)]
            self.hy_stage1(tb, d["kbuf"][o, 0:M, :], tb["S1"], allkb, False)
            P.mark()
            kab = [P.sbuf([128, 2, 512], BF16, "kab%d" % i) for i in range(4)]

            def cbf(f1, p1, p2):
                k = kab[f1 % 4]
                r = rn[o]
                P.op("dve", lambda g: g.tensor_tensor(out=k[0:64, 0, :], in0=p1[0:64, :], in1=r[0:64, :], op=ALU.mult),
                     reads=[p1, r], writes=[k])
                P.op("dve", lambda g: g.tensor_tensor(out=k[64:128, 0, :], in0=p2[64:128, :], in1=r[64:128, :],
                                                      op=ALU.mult), reads=[p2, r], writes=[k])
                P.op("dve", lambda g: g.scalar_tensor_tensor(out=k[0:64, 1, :], in0=p2[0:64, :], scalar=-1.0,
                                                             in1=r[0:64, :], op0=ALU.mult, op1=ALU.mult),
                     reads=[p2, r], writes=[k])
                P.op("dve", lambda g: g.tensor_tensor(out=k[64:128, 1, :], in0=p1[64:128, :], in1=r[64:128, :],
                                                      op=ALU.mult), reads=[p1, r], writes=[k])
                P.dma(d[kab_name][o, f1].rearrange("a p c -> p a c"), k[:], reads=[k],
                      writes=[self.db(kab_name, (o, f1))])

            self.hy_stage2(tb, cbf)
            P.release()
        P.release()

    def hy_conv(self, l, tb, o, kab_name, src_ap, src_reads, gate_ap, gate_reads, dst_ap, dst_name):
        P = self.P
        d = self.d
        n, S1, T1 = tb["n"], tb["F1n"], tb["T1"]
        self.hy_stage1(tb, src_ap, tb["S1"] // 2, src_reads, True)
        P.mark()
        kab = [P.sbuf([128, 2, 512], BF16, "kab%d" % i) for i in range(4)]
        ta = [P.sbuf([128, 512], F32, "ta%d" % i) for i in range(4)]
        tb2 = [P.sbuf([128, 512], F32, "tb%d" % i) for i in range(4)]
        yh = [P.sbuf([128, 512], BF16, "yh%d" % i) for i in range(4)]
        do = [P.sbuf([128, 512], BF16, "do%d" % i) for i in range(4)]
        allk = [self.db(kab_name, (o, f1)) for f1 in range(S1)]

        def cbf(f1, p1, p2):
            i = f1 % 4
            k = kab[i]
            P.dma(k[:], d[kab_name][o, f1].rearrange("a p c -> p a c"), reads=allk, writes=[k])
            P.op("dve", lambda g: g.tensor_tensor(out=ta[i][:], in0=p1[:], in1=k[:, 0, :], op=ALU.mult),
                 reads=[p1, k], writes=[ta[i]])
            P.op("dve", lambda g: g.tensor_tensor(out=tb2[i][:], in0=p2[:], in1=k[:, 1, :], op=ALU.mult),
                 reads=[p2, k], writes=[tb2[i]])
            P.op("pool", lambda g: g.tensor_tensor(out=yh[i][:], in0=ta[i][:], in1=tb2[i][:], op=ALU.add),
                 reads=[ta[i], tb2[i]], writes=[yh[i]])

        def cbf2(f1):
            i = f1 % 4
            pd_ = self.ps[6 + f1 % 2]
            P.mm(pd_[:], tb["Dm"][:, 2, :], yh[i][:], reads=[tb["Dm"], yh[i]], writes=[pd_])
            P.op("act", lambda g: g.activation(out=do[i][:], in_=pd_[:], func=AF.Copy), reads=[pd_], writes=[do[i]])
            P.dma(d["Dd"][:, f1, :], do[i][:], reads=[do[i]], writes=[self.db("Dd", f1)])

        self.hy_stage2(tb, cbf, cbf2)
        P.release()
        P.mark()
        bias = P.sbuf([64, 512], F32, "hbias")
        P.dma(bias[:], d["hyena_bias"][l, o:o + 1, :].to_broadcast([64, 512]), writes=[bias])
        din = [P.sbuf([S1, 2, 512], BF16, "din%d" % i) for i in range(4)]
        gs = [P.sbuf([T1, 512], F32, "gs%d" % i) for i in range(4)]
        us = [P.sbuf([T1, 512], F32, "us%d" % i) for i in range(4)]
        zo = [P.sbuf([T1, 512], F32, "zo%d" % i) for i in range(4)]
        alld = [self.db("Dd", f1) for f1 in range(S1)]
        ddv = d["Dd"].rearrange("(r t) f c -> t f r c", r=2)
        gv = gate_ap.rearrange("(a s) c -> s a c", s=64)
        uv = src_ap.rearrange("(a s) c -> s a c", s=64)
        dv = dst_ap.rearrange("(a s) c -> s a c", s=64)
        for t2 in range(64):
            i = t2 % 4
            P.dma(din[i][:], ddv[t2, 0:S1], reads=alld, writes=[din[i]])
            P.dma(gs[i][:], gv[t2], reads=gate_reads, writes=[gs[i]])
            P.dma(us[i][:], uv[t2], reads=src_reads, writes=[us[i]])
            py = self.ps[6 + i % 2]
            P.mm(py[0:T1, :], tb["W3"][:, t2, 0, :], din[i][:, 0, :], start=True, stop=False,
                 reads=[tb["W3"], din[i]], writes=[py])
            P.mm(py[0:T1, :], tb["W3"][:, t2, 1, :], din[i][:, 1, :], start=False, stop=True,
                 reads=[tb["W3"], din[i]], writes=[py])
            P.op("pool", lambda g: g.tensor_tensor(out=us[i][:], in0=us[i][:], in1=bias[0:T1, :], op=ALU.mult),
                 reads=[us[i], bias], writes=[us[i]])
            P.op("dve", lambda g: g.tensor_tensor(out=us[i][:], in0=us[i][:], in1=py[0:T1, :], op=ALU.add),
                 reads=[us[i], py], writes=[us[i]])
            P.op("pool", lambda g: g.tensor_tensor(out=zo[i][:], in0=us[i][:], in1=gs[i][:], op=ALU.mult),
                 reads=[us[i], gs[i]], writes=[zo[i]])
            P.dma(dv[t2], zo[i][:], reads=[zo[i]], writes=[self.db(dst_name, ("t2", t2, n))])
        P.release()

    def phase_hyena(self, l, with_ctx):
        P = self.P
        d = self.d
        self.hy_shortconv(l, NTILE if with_ctx else 32)
        self.hy_filters(l, NLAT, "KAB_L")
        if with_ctx:
            self.hy_filters(l, NCTX, "KAB_C")
        segs = [(NLAT, 0, "KAB_L")] + ([(NCTX, NLAT, "KAB_C")] if with_ctx else [])
        for (n, r0, kn) in segs:
            P.mark()
            tb = self.hy_load_tabs(n)
            hcs = [b for k, b in self.dbufs.items() if k[0] == "hcs"]
            self.hy_conv(l, tb, 0, kn, d["hcs"][r0:r0 + n, 0:512], hcs, d["hcs"][r0:r0 + n, 512:1024], hcs,
                         d["zt1"][r0:r0 + n, :], "zt1")
            z1 = [b for k, b in self.dbufs.items() if k[0] == "zt1"]
            self.hy_conv(l, tb, 1, kn, d["zt1"][r0:r0 + n, :], z1, d["hcs"][r0:r0 + n, 1024:1536], hcs,
                         d["zt2"][r0:r0 + n, :], "zt2")
            P.release()
        P.mark()
        zin = [P.sbuf([128, 512], F32, "zin%d" % i) for i in range(3)]
        zT = [P.sbuf([128, 4, 128], BF16, "zT%d" % i) for i in range(3)]
        z2 = [b for k, b in self.dbufs.items() if k[0] == "zt2"]
        yv = d["yT"][2].rearrange("(kc p) t -> p kc t", p=128)
        for t in range(NTILE if with_ctx else 32):
            zi = zin[t % 3]
            P.dma(zi[:], d["zt2"][t * 128:(t + 1) * 128, :], reads=z2, writes=[zi])
            ps = self.ps[t % 2]
            for kc in range(4):
                P.op("pe", lambda g: g.transpose(ps[:, kc * 128:(kc + 1) * 128], zi[:, kc * 128:(kc + 1) * 128],
                                                 self.ident_f[:]), reads=[zi, self.ident_f], writes=[ps])
            P.op("act", lambda g: g.activation(out=zT[t % 3][:].rearrange("p a b -> p (a b)"), in_=ps[:], func=AF.Copy),
                 reads=[ps], writes=[zT[t % 3]])
            P.dma(yv[:, :, t * 128:(t + 1) * 128], zT[t % 3][:], reads=[zT[t % 3]], writes=[self.db("yT", (2, t))])
        P.release()

    KB.declare_hyena = declare_hyena
    KB.hy_shortconv = hy_shortconv
    KB.hy_load_tabs = hy_load_tabs
    KB.hy_stage1 = hy_stage1
    KB.hy_stage2 = hy_stage2
    KB.hy_filters = hy_filters
    KB.hy_conv = hy_conv
    KB.phase_hyena = phase_hyena


_hyena_methods()


def rwkv_tables():
    idx = np.arange(128)
    out = np.zeros((2, 6, 128, 128), np.float32)
    for dd in range(2):
        incl = (idx[:, None] <= idx[None, :]) if dd == 0 else (idx[:, None] >= idx[None, :])
        incl = incl.astype(np.float32)
        ref = 63 if dd == 0 else 64
        out[dd, 0] = incl
        out[dd, 1] = incl - incl[:, ref:ref + 1]
        out[dd, 2] = 1.0 - incl
        out[dd, 3] = incl - np.eye(128, dtype=np.float32)
        out[dd, 4] = incl
        out[dd, 5] = out[dd, 3].T
    return out


def _rwkv_methods():
    def declare_rwkv(self):
        L = 2
        self.inp("rwkv_mu", [L, 2, 1792])
        self.inp("rwkv_kvec", [L, 2, 512])
        self.inp("rwkv_lnp", [L, 3, 512])
        self.inp("rwkv_wupA", [L, 65, 1024])
        self.inp("rwkv_aupA", [L, 65, 1024])
        self.inp("rwkv_g_up", [L, 128, 512])
        self.inp("rw_tri", [2, 6, 128, 128])
        self.scr("yf", [TT, 512])

    def phase_rwkv(self, l, with_ctx):
        P = self.P
        d = self.d
        idb = self.ident_b
        idf = self.ident_f
        P.mark()

        def dve(fn, r, w):
            return P.op("dve", fn, reads=r, writes=w)

        def act(fn, r, w):
            return P.op("act", fn, reads=r, writes=w)

        def pool(fn, r, w):
            return P.op("pool", fn, reads=r, writes=w)

        def T32(name, shape=(128, 512)):
            return P.sbuf(list(shape), F32, name)

        def T16(name, shape=(128, 512)):
            return P.sbuf(list(shape), BF16, name)

        mu = T32("mu", (128, 3, 1792))
        for j in range(2):
            P.dma(mu[:, 1 + j, :], d["rwkv_mu"][l, j:j + 1, :].to_broadcast([128, 1792]), writes=[mu])
        dve(lambda g: g.tensor_tensor(out=mu[:, 0, :], in0=mu[:, 1, :], in1=mu[:, 2, :], op=ALU.add), [mu], [mu])
        dve(lambda g: g.tensor_scalar(out=mu[:, 0, :], in0=mu[:, 0, :], scalar1=-1.0, scalar2=1.0, op0=ALU.mult,
                                      op1=ALU.add), [mu], [mu])
        kv = T32("kv", (128, 3, 512))
        for j in range(2):
            P.dma(kv[:, j, :], d["rwkv_kvec"][l, j:j + 1, :].to_broadcast([128, 512]), writes=[kv])
        dve(lambda g: g.tensor_scalar(out=kv[:, 2, :], in0=kv[:, 1, :], scalar1=-1.0, scalar2=1.0, op0=ALU.mult,
                                      op1=ALU.add), [kv], [kv])
        lnp = T32("lnp", (128, 3, 512))
        P.dma(lnp[:].rearrange("p a c -> p (a c)"),
              d["rwkv_lnp"][l:l + 1].rearrange("o a c -> o (a c)").to_broadcast([128, 1536]), writes=[lnp])
        wupA = T16("wupA", (65, 2, 512))
        P.dma(wupA[:].rearrange("p a c -> p (a c)"), d["rwkv_wupA"][l], writes=[wupA], q="pool")
        aupA = T16("aupA", (65, 2, 512))
        P.dma(aupA[:].rearrange("p a c -> p (a c)"), d["rwkv_aupA"][l], writes=[aupA], q="pool")
        gup = T16("gup", (128, 512))
        P.dma(gup[:], d["rwkv_g_up"][l], writes=[gup], q="pool")
        tri = T32("tri", (128, 2, 6, 128))
        P.dma(tri[:], d["rw_tri"].rearrange("a b p c -> p a b c"), writes=[tri])
        onec = T32("onec", (128, 1))
        pool(lambda g: g.memset(onec[:], 1.0), [], [onec])
        TWA = T16("TWA", (65, 128))
        ALA = T16("ALA", (65, 128))
        pool(lambda g: g.memset(TWA[:], 1.0), [], [TWA])
        pool(lambda g: g.memset(ALA[:], 1.0), [], [ALA])
        cur = [T32("cur%d" % i, (128, 1792)) for i in range(3)]
        prv = T32("prv", (128, 1792))
        nxt = T32("nxt", (128, 1792))
        kk = T32("kk")
        sq = T32("sq")
        s8 = T32("s8", (128, 8))
        r8 = T32("r8", (128, 8))
        tw = T16("tw", (128, 64))
        al = T16("al", (128, 64))
        sgl = T16("sgl", (128, 128))
        lw = T32("lw")
        av = T32("av")
        tt_ = T32("tt")
        bb = T32("bb")
        eW, eWi, eLu, eD, elw, eWx, eLux = [T32(nm) for nm in ("eW", "eWi", "eLu", "eD", "elw", "eWx", "eLux")]
        rt, zt, bt, kt = [T16(nm) for nm in ("rt", "zt", "bt", "kt")]
        RTf, ZTf, BTf, KTf = [T16(nm, (64, 8, 128)) for nm in ("RTf", "ZTf", "BTf", "KTf")]
        X0 = [T16("X0%d" % i, (128, 8, 128)) for i in range(2)]
        XT0 = [T16("XT0%d" % i, (128, 8, 128)) for i in range(2)]
        TT0 = [T16("TT0%d" % i, (128, 8, 128)) for i in range(2)]
        AzkT = [T16("AzkT%d" % i, (128, 8, 128)) for i in range(2)]
        ArbT = [T16("ArbT%d" % i, (128, 8, 128)) for i in range(2)]
        ArkT = [T16("ArkT%d" % i, (128, 8, 128)) for i in range(2)]
        vbf = [T16("vbf%d" % i) for i in range(2)]
        bp = [T16("bp%d" % i) for i in range(2)]
        kp = [T16("kp%d" % i) for i in range(2)]
        ru = [T16("ru%d" % i) for i in range(2)]
        zu = [T16("zu%d" % i) for i in range(2)]
        kd = [T32("kd%d" % i) for i in range(2)]
        kd0 = [T32("kd0%d" % i) for i in range(2)]
        sglT = [T16("sglT%d" % i, (128, 128)) for i in range(2)]
        WC = [T32("WC%d" % i, (64, 8)) for i in range(2)]
        yfl = [T32("yfl%d" % i) for i in range(2)]
        Xs = [T16("Xs%d" % i, (128, 8, 128)) for i in range(2)]
        XTs = [T16("XTs%d" % i, (128, 8, 128)) for i in range(2)]
        TTs = [T16("TTs%d" % i, (128, 8, 128)) for i in range(2)]
        Zp, Gm, U0 = [T16(nm) for nm in ("Zp", "Gm", "U0")]
        Y0 = T32("Y0")
        RpT = T16("RpT", (64, 8, 128))
        Mm = T32("Mm", (64, 8, 64))
        NTt = T32("NTt", (64, 8, 64))
        STf = T32("STf", (64, 8, 64))
        STb = T16("STb", (64, 8, 64))
        Yt = T32("Yt")
        m8 = T32("m8", (128, 8))
        v8 = T32("v8", (128, 8))
        b8 = T32("b8", (128, 8))
        yc = T32("yc")
        sq2 = T32("sq2")
        ob = T16("ob", (128, 4, 128))
        allpb = [b for k, b in self.dbufs.items() if k[0] == "pb"]
        ps = self.ps

        def view8(ap):
            return ap.rearrange("p (h m) -> p h m", m=64)

        def b8c(t8):
            return t8[:].unsqueeze(2).to_broadcast([128, 8, 64])

        for pss in range(2):
            dd = pss
            order = [32, 33] + list(range(32)) if dd == 0 else [33, 32] + list(range(31, -1, -1))
            pool(lambda g: g.memset(STf[:], 0.0), [], [STf])
            pool(lambda g: g.memset(STb[:], 0.0), [], [STb])

            def loads(tj, tn):
                rr = tm_row(tn * 128)
                P.dma(cur[tj % 3][:], d["pb"][rr:rr + 128, :], reads=allpb, writes=[cur[tj % 3]])
                P.dma(prv[:], d["pb"][rr - 1:rr + 127, :], reads=allpb, writes=[prv])
                P.dma(nxt[:], d["pb"][rr + 1:rr + 129, :], reads=allpb, writes=[nxt])

            def stageA(ti):
                t = order[ti]
                pr = ti % 2
                need_y = with_ctx or t < 32
                cu, pv_, nx = cur[ti % 3], prv, nxt
                if pss == 1 and need_y:
                    P.dma(yfl[pr][:], d["yf"][t * 128:(t + 1) * 128, :], reads=[self.db("yf", t)], writes=[yfl[pr]])
                pool(lambda g: g.tensor_tensor(out=nx[:], in0=nx[:], in1=mu[:, 2, :], op=ALU.mult), [nx, mu], [nx])
                dve(lambda g: g.tensor_tensor(out=cu[:], in0=cu[:], in1=mu[:, 0, :], op=ALU.mult), [cu, mu], [cu])
                dve(lambda g: g.tensor_tensor(out=pv_[:], in0=pv_[:], in1=mu[:, 1, :], op=ALU.mult), [pv_, mu], [pv_])
                yield
                dve(lambda g: g.tensor_tensor(out=cu[:], in0=cu[:], in1=pv_[:], op=ALU.add), [cu, pv_], [cu])
                dve(lambda g: g.tensor_tensor(out=cu[:], in0=cu[:], in1=nx[:], op=ALU.add), [cu, nx], [cu])
                if ti + 1 < len(order):
                    loads(ti + 1, order[ti + 1])
                r_ap, k_ap, v_ap = cu[:, 0:512], cu[:, 512:1024], cu[:, 1024:1536]
                dve(lambda g: g.tensor_tensor(out=kk[:], in0=k_ap, in1=kv[:, 0, :], op=ALU.mult), [cu, kv], [kk])
                pool(lambda g: g.tensor_tensor(out=sq[:], in0=kk[:], in1=kk[:], op=ALU.mult), [kk], [sq])
                dve(lambda g: g.tensor_reduce(out=s8[:], in_=view8(sq[:]), axis=AX.X, op=ALU.add), [sq], [s8])
                act(lambda g: g.activation(out=r8[:], in_=s8[:], func=AF.Sqrt, bias=1e-12), [s8], [r8])
                dve(lambda g: g.reciprocal(out=r8[:], in_=r8[:]), [r8], [r8])
                dve(lambda g: g.tensor_tensor(out=view8(kk[:]), in0=view8(kk[:]), in1=b8c(r8), op=ALU.mult),
                    [kk, r8], [kk])
                act(lambda g: g.activation(out=vbf[pr][:], in_=v_ap, func=AF.Copy), [cu], [vbf[pr]])
                act(lambda g: g.activation(out=tw[:], in_=cu[:, 1536:1600], func=AF.Tanh), [cu], [tw])
                act(lambda g: g.activation(out=al[:], in_=cu[:, 1600:1664], func=AF.Copy), [cu], [al])
                yield
                pb0 = self.psb(0)
                P.op("pe", lambda g: g.transpose(pb0[0:64, 0:128], tw[:], idb[:]), reads=[tw, idb], writes=[ps[0]])
                P.op("pe", lambda g: g.transpose(pb0[0:64, 128:256], al[:], idb[:]), reads=[al, idb], writes=[ps[0]])
                dve(lambda g: g.tensor_copy(out=TWA[0:64, :], in_=pb0[0:64, 0:128]), [ps[0]], [TWA])
                dve(lambda g: g.tensor_copy(out=ALA[0:64, :], in_=pb0[0:64, 128:256]), [ps[0]], [ALA])
                if pss == 1:
                    act(lambda g: g.activation(out=sgl[:], in_=cu[:, 1664:1792], func=AF.Sigmoid), [cu], [sgl])
                    pb1 = self.psb(1)
                    P.op("pe", lambda g: g.transpose(pb1[:, 0:128], sgl[:], idb[:]), reads=[sgl, idb], writes=[ps[1]])
                    dve(lambda g: g.tensor_copy(out=sglT[pr][:], in_=pb1[:, 0:128]), [ps[1]], [sglT[pr]])
                    P.mm(ps[2][:], ALA[:], aupA[:, 0, :], reads=[ALA, aupA], writes=[ps[2]])
                    act(lambda g: g.activation(out=av[:], in_=ps[2][:], func=AF.Sigmoid), [ps[2]], [av])
                    dve(lambda g: g.tensor_tensor(out=tt_[:], in0=av[:], in1=kv[:, 1, :], op=ALU.mult), [av, kv], [tt_])
                    pool(lambda g: g.tensor_tensor(out=tt_[:], in0=tt_[:], in1=kv[:, 2, :], op=ALU.add), [tt_, kv], [tt_])
                    dve(lambda g: g.tensor_tensor(out=kd0[pr][:], in0=k_ap, in1=tt_[:], op=ALU.mult), [cu, tt_], [kd0[pr]])
                yield
                P.mm(ps[2][:], TWA[:], wupA[:, dd, :], reads=[TWA, wupA], writes=[ps[2]])
                act(lambda g: g.activation(out=lw[:], in_=ps[2][:], func=AF.Sigmoid), [ps[2]], [lw])
                dve(lambda g: g.tensor_scalar(out=lw[:], in0=lw[:], scalar1=-0.6065306597126334, scalar2=None,
                                              op0=ALU.mult), [lw], [lw])
                P.mm(ps[3][:], ALA[:], aupA[:, dd, :], reads=[ALA, aupA], writes=[ps[3]])
                act(lambda g: g.activation(out=av[:], in_=ps[3][:], func=AF.Sigmoid), [ps[3]], [av])
                dve(lambda g: g.tensor_tensor(out=tt_[:], in0=av[:], in1=kv[:, 1, :], op=ALU.mult), [av, kv], [tt_])
                pool(lambda g: g.tensor_tensor(out=tt_[:], in0=tt_[:], in1=kv[:, 2, :], op=ALU.add), [tt_, kv], [tt_])
                dve(lambda g: g.tensor_tensor(out=kd[pr][:], in0=k_ap, in1=tt_[:], op=ALU.mult), [cu, tt_], [kd[pr]])
                pool(lambda g: g.tensor_tensor(out=bb[:], in0=kk[:], in1=av[:], op=ALU.mult), [kk, av], [bb])
                yield
                P.mm(ps[0][:], tri[:, dd, 0, :], lw[:], reads=[tri, lw], writes=[ps[0]])
                P.mm(ps[1][:], tri[:, dd, 1, :], lw[:], reads=[tri, lw], writes=[ps[1]])
                P.mm(ps[2][:], tri[:, dd, 2, :], lw[:], reads=[tri, lw], writes=[ps[2]])
                for h in range(8):
                    P.mm(ps[3][0:64, h:h + 1], lw[:, h * 64:(h + 1) * 64], onec[:], reads=[lw, onec], writes=[ps[3]])
                act(lambda g: g.activation(out=WC[pr][:], in_=ps[3][0:64, 0:8], func=AF.Exp), [ps[3]], [WC[pr]])
                act(lambda g: g.activation(out=eLu[:], in_=ps[0][:], func=AF.Exp), [ps[0]], [eLu])
                act(lambda g: g.activation(out=eW[:], in_=ps[1][:], func=AF.Exp), [ps[1]], [eW])
                act(lambda g: g.activation(out=eWi[:], in_=ps[1][:], func=AF.Exp, scale=-1.0), [ps[1]], [eWi])
                act(lambda g: g.activation(out=eD[:], in_=ps[2][:], func=AF.Exp), [ps[2]], [eD])
                act(lambda g: g.activation(out=elw[:], in_=lw[:], func=AF.Exp, scale=-1.0), [lw], [elw])
                yield
                dve(lambda g: g.tensor_tensor(out=eWx[:], in0=eW[:], in1=elw[:], op=ALU.mult), [eW, elw], [eWx])
                pool(lambda g: g.tensor_tensor(out=eLux[:], in0=eLu[:], in1=elw[:], op=ALU.mult), [eLu, elw], [eLux])
                dve(lambda g: g.tensor_tensor(out=rt[:], in0=r_ap, in1=eW[:], op=ALU.mult), [cu, eW], [rt])
                dve(lambda g: g.scalar_tensor_tensor(out=zt[:], in0=kk[:], scalar=-1.0, in1=eWx[:], op0=ALU.mult,
                                                     op1=ALU.mult), [kk, eWx], [zt])
                pool(lambda g: g.tensor_tensor(out=bt[:], in0=bb[:], in1=eWi[:], op=ALU.mult), [bb, eWi], [bt])
                dve(lambda g: g.tensor_tensor(out=kt[:], in0=kd[pr][:], in1=eWi[:], op=ALU.mult), [kd[pr], eWi], [kt])
                yield
                pool(lambda g: g.tensor_tensor(out=bp[pr][:], in0=bb[:], in1=eD[:], op=ALU.mult), [bb, eD], [bp[pr]])
                dve(lambda g: g.tensor_tensor(out=kp[pr][:], in0=kd[pr][:], in1=eD[:], op=ALU.mult), [kd[pr], eD], [kp[pr]])
                pool(lambda g: g.tensor_tensor(out=ru[pr][:], in0=r_ap, in1=eLu[:], op=ALU.mult), [cu, eLu], [ru[pr]])
                dve(lambda g: g.scalar_tensor_tensor(out=zu[pr][:], in0=kk[:], scalar=-1.0, in1=eLux[:], op0=ALU.mult,
                                                     op1=ALU.mult), [kk, eLux], [zu[pr]])
                for qi, (src, dstf) in enumerate(((rt, RTf), (zt, ZTf), (bt, BTf), (kt, KTf))):
                    pbx = self.psb(qi % 2)
                    for h in range(8):
                        P.op("pe", lambda g: g.transpose(pbx[0:64, h * 128:(h + 1) * 128], src[:, h * 64:(h + 1) * 64],
                                                         idb[:]), reads=[src, idb], writes=[ps[qi % 2]])
                    if qi % 2 == 0:
                        act(lambda g: g.activation(out=dstf[:].rearrange("p h t -> p (h t)"), in_=pbx[0:64, :],
                                                   func=AF.Copy), [ps[qi % 2]], [dstf])
                    else:
                        dve(lambda g: g.tensor_copy(out=dstf[:].rearrange("p h t -> p (h t)"), in_=pbx[0:64, :]),
                            [ps[qi % 2]], [dstf])
                    if qi == 1:
                        yield
                yield
                nb = [0]

                def amat(Lf, Rf, mi, dst):
                    for hg in range(2):
                        pa = ps[nb[0] % 4]
                        nb[0] += 1
                        for j in range(4):
                            h = hg * 4 + j
                            P.mm(pa[:, j * 128:(j + 1) * 128], Lf[:, h, :], Rf[:, h, :], reads=[Lf, Rf], writes=[pa])
                        dve(lambda g: g.tensor_tensor(out=dst[:, hg * 4:(hg + 1) * 4, :],
                                                      in0=pa[:].rearrange("p (h t) -> p h t", h=4),
                                                      in1=tri[:, dd, mi, :].unsqueeze(1).to_broadcast([128, 4, 128]),
                                                      op=ALU.mult), [pa, tri], [dst])

                amat(ZTf, BTf, 5, X0[pr])
                amat(BTf, ZTf, 3, XT0[pr])
                yield
                amat(KTf, ZTf, 3, AzkT[pr])
                amat(BTf, RTf, 4, ArbT[pr])
                yield
                amat(KTf, RTf, 4, ArkT[pr])
                pool(lambda g: g.tensor_tensor(out=TT0[pr][:], in0=XT0[pr][:],
                                               in1=idb[:].unsqueeze(1).to_broadcast([128, 8, 128]), op=ALU.add),
                     [XT0[pr], idb], [TT0[pr]])

            def stageB(ti):
                t = order[ti]
                pr = ti % 2
                need_y = with_ctx or t < 32
                cu = cur[ti % 3]
                r_ap, v_ap = cu[:, 0:512], cu[:, 1024:1536]
                Xc, XTc, TTc = X0[pr], XT0[pr], TT0[pr]
                cx = 0
                for it in range(6):
                    Xn, XTn, TTn = Xs[cx], XTs[cx], TTs[cx]
                    for hg in range(2):
                        p2 = ps[4 + hg]
                        for j in range(4):
                            h = hg * 4 + j
                            P.mm(p2[:, j * 128:(j + 1) * 128], XTc[:, h, :], Xc[:, h, :], reads=[XTc, Xc], writes=[p2])
                        act(lambda g: g.activation(out=Xn[:, hg * 4:(hg + 1) * 4, :].rearrange("p h t -> p (h t)"),
                                                   in_=p2[:], func=AF.Copy), [p2], [Xn])
                        if it < 5:
                            p3 = ps[6 + hg]
                            for j in range(4):
                                h = hg * 4 + j
                                P.mm(p3[:, j * 128:(j + 1) * 128], Xc[:, h, :], XTc[:, h, :], reads=[Xc, XTc],
                                     writes=[p3])
                            dve(lambda g: g.tensor_copy(out=XTn[:, hg * 4:(hg + 1) * 4, :].rearrange("p h t -> p (h t)"),
                                                        in_=p3[:]), [p3], [XTn])
                    yield
                    for hg in range(2):
                        p4 = ps[4 + hg]
                        for j in range(4):
                            h = hg * 4 + j
                            P.mm(p4[:, j * 128:(j + 1) * 128], Xn[:, h, :], TTc[:, h, :], start=True, stop=False,
                                 reads=[Xn, TTc], writes=[p4])
                            P.mm(p4[:, j * 128:(j + 1) * 128], idb[:], TTc[:, h, :], start=False, stop=True,
                                 reads=[idb, TTc], writes=[p4])
                        act(lambda g: g.activation(out=TTn[:, hg * 4:(hg + 1) * 4, :].rearrange("p h t -> p (h t)"),
                                                   in_=p4[:], func=AF.Copy), [p4], [TTn])
                    Xc, XTc, TTc = Xn, XTn, TTn
                    cx = 1 - cx
                    yield
                TT = TTc
                for h in range(8):
                    P.mm(ps[4][:, h * 64:(h + 1) * 64], TT[:, h, :], zu[pr][:, h * 64:(h + 1) * 64], reads=[TT, zu[pr]],
                         writes=[ps[4]])
                act(lambda g: g.activation(out=Zp[:], in_=ps[4][:], func=AF.Copy), [ps[4]], [Zp])
                for h in range(8):
                    P.mm(ps[5][:, h * 64:(h + 1) * 64], AzkT[pr][:, h, :], vbf[pr][:, h * 64:(h + 1) * 64],
                         reads=[AzkT[pr], vbf[pr]], writes=[ps[5]])
                dve(lambda g: g.tensor_copy(out=Gm[:], in_=ps[5][:]), [ps[5]], [Gm])
                for h in range(8):
                    P.mm(ps[6][:, h * 64:(h + 1) * 64], TT[:, h, :], Gm[:, h * 64:(h + 1) * 64], reads=[TT, Gm], writes=[ps[6]])
                act(lambda g: g.activation(out=U0[:], in_=ps[6][:], func=AF.Copy), [ps[6]], [U0])
                yield
                for h in range(8):
                    P.mm(ps[7][0:64, h * 64:(h + 1) * 64], Zp[:, h * 64:(h + 1) * 64], bp[pr][:, h * 64:(h + 1) * 64],
                         reads=[Zp, bp[pr]], writes=[ps[7]])
                dve(lambda g: g.tensor_tensor(out=Mm[:], in0=idf[0:64, 0:64].unsqueeze(1).to_broadcast([64, 8, 64]),
                                              in1=WC[pr][:].unsqueeze(2).to_broadcast([64, 8, 64]), op=ALU.mult),
                    [idf, WC[pr]], [Mm])
                dve(lambda g: g.tensor_tensor(out=Mm[:], in0=Mm[:], in1=ps[7][0:64, :].rearrange("p (h m) -> p h m", m=64),
                                              op=ALU.add), [Mm, ps[7]], [Mm])
                for h in range(8):
                    hs = slice(h * 64, (h + 1) * 64)
                    P.mm(ps[4][0:64, hs], bp[pr][:, hs], U0[:, hs], start=True, stop=False, reads=[bp[pr], U0], writes=[ps[4]])
                    P.mm(ps[4][0:64, hs], kp[pr][:, hs], vbf[pr][:, hs], start=False, stop=True, reads=[kp[pr], vbf[pr]],
                         writes=[ps[4]])
                act(lambda g: g.activation(out=NTt[:].rearrange("p h m -> p (h m)"), in_=ps[4][0:64, :], func=AF.Copy),
                    [ps[4]], [NTt])
                yield
                if need_y:
                    for h in range(8):
                        hs = slice(h * 64, (h + 1) * 64)
                        P.mm(ps[5][:, hs], ArbT[pr][:, h, :], U0[:, hs], start=True, stop=False, reads=[ArbT[pr], U0],
                             writes=[ps[5]])
                        P.mm(ps[5][:, hs], ArkT[pr][:, h, :], vbf[pr][:, hs], start=False, stop=True,
                             reads=[ArkT[pr], vbf[pr]], writes=[ps[5]])
                    act(lambda g: g.activation(out=Y0[:], in_=ps[5][:], func=AF.Copy), [ps[5]], [Y0])
                    for hg in range(2):
                        prr = ps[6 + hg]
                        for j in range(4):
                            h = hg * 4 + j
                            hs = slice(h * 64, (h + 1) * 64)
                            P.mm(prr[0:64, j * 128:(j + 1) * 128], ru[pr][:, hs], idb[:], start=True, stop=False,
                                 reads=[ru[pr], idb], writes=[prr])
                            P.mm(prr[0:64, j * 128:(j + 1) * 128], Zp[:, hs], ArbT[pr][:, h, :], start=False, stop=True,
                                 reads=[Zp, ArbT[pr]], writes=[prr])
                        dve(lambda g: g.tensor_copy(out=RpT[:, hg * 4:(hg + 1) * 4, :].rearrange("p h t -> p (h t)"),
                                                    in_=prr[0:64, :]), [prr], [RpT])
                    yield
                    for h in range(8):
                        P.mm(ps[4][:, h * 64:(h + 1) * 64], RpT[:, h, :], STb[:, h, :], reads=[RpT, STb], writes=[ps[4]])
                    dve(lambda g: g.tensor_tensor(out=Yt[:], in0=ps[4][:], in1=Y0[:], op=ALU.add), [ps[4], Y0], [Yt])
                for h in range(8):
                    P.mm(ps[5][0:64, h * 64:(h + 1) * 64], Mm[:, h, :], STf[:, h, :], reads=[Mm, STf], writes=[ps[5]])
                dve(lambda g: g.tensor_tensor(out=STf[:], in0=ps[5][0:64, :].rearrange("p (h m) -> p h m", m=64),
                                              in1=NTt[:], op=ALU.add), [ps[5], NTt], [STf])
                act(lambda g: g.activation(out=STb[:], in_=STf[:], func=AF.Copy), [STf], [STb])
                yield
                if not need_y:
                    return
                if pss == 0:
                    P.dma(d["yf"][t * 128:(t + 1) * 128, :], Yt[:], reads=[Yt], writes=[self.db("yf", t)])
                    return
                dve(lambda g: g.tensor_tensor(out=Yt[:], in0=Yt[:], in1=yfl[pr][:], op=ALU.add), [Yt, yfl[pr]], [Yt])
                dve(lambda g: g.tensor_reduce(out=m8[:], in_=view8(Yt[:]), axis=AX.X, op=ALU.add), [Yt], [m8])
                dve(lambda g: g.tensor_scalar(out=m8[:], in0=m8[:], scalar1=1.0 / 64, scalar2=None, op0=ALU.mult), [m8], [m8])
                dve(lambda g: g.tensor_tensor(out=view8(yc[:]), in0=view8(Yt[:]), in1=b8c(m8), op=ALU.subtract),
                    [Yt, m8], [yc])
                pool(lambda g: g.tensor_tensor(out=sq2[:], in0=yc[:], in1=yc[:], op=ALU.mult), [yc], [sq2])
                dve(lambda g: g.tensor_reduce(out=v8[:], in_=view8(sq2[:]), axis=AX.X, op=ALU.add), [sq2], [v8])
                act(lambda g: g.activation(out=v8[:], in_=v8[:], func=AF.Sqrt, scale=1.0 / 64, bias=64e-5), [v8], [v8])
                dve(lambda g: g.reciprocal(out=v8[:], in_=v8[:]), [v8], [v8])
                yield
                dve(lambda g: g.tensor_tensor(out=view8(yc[:]), in0=view8(yc[:]), in1=b8c(v8), op=ALU.mult), [yc, v8], [yc])
                pool(lambda g: g.tensor_tensor(out=yc[:], in0=yc[:], in1=lnp[:, 0, :], op=ALU.mult), [yc, lnp], [yc])
                pool(lambda g: g.tensor_tensor(out=yc[:], in0=yc[:], in1=lnp[:, 1, :], op=ALU.add), [yc, lnp], [yc])
                dve(lambda g: g.tensor_tensor(out=kd0[pr][:], in0=kd0[pr][:], in1=kd[pr][:], op=ALU.add),
                    [kd0[pr], kd[pr]], [kd0[pr]])
                dve(lambda g: g.tensor_tensor(out=kd0[pr][:], in0=kd0[pr][:], in1=r_ap, op=ALU.mult), [kd0[pr], cu], [kd0[pr]])
                dve(lambda g: g.scalar_tensor_tensor(out=sq2[:], in0=kd0[pr][:], scalar=0.5, in1=lnp[:, 2, :], op0=ALU.mult,
                                                     op1=ALU.mult), [kd0[pr], lnp], [sq2])
                dve(lambda g: g.tensor_reduce(out=b8[:], in_=view8(sq2[:]), axis=AX.X, op=ALU.add), [sq2], [b8])
                dve(lambda g: g.tensor_tensor(out=view8(sq2[:]), in0=view8(v_ap), in1=b8c(b8), op=ALU.mult), [cu, b8], [sq2])
                pool(lambda g: g.tensor_tensor(out=yc[:], in0=yc[:], in1=sq2[:], op=ALU.add), [yc, sq2], [yc])
                yield
                P.mm(ps[6][:], sglT[pr][:], gup[:], reads=[sglT[pr], gup], writes=[ps[6]])
                dve(lambda g: g.tensor_tensor(out=yc[:], in0=yc[:], in1=ps[6][:], op=ALU.mult), [yc, ps[6]], [yc])
                for kc in range(4):
                    P.op("pe", lambda g: g.transpose(ps[7][:, kc * 128:(kc + 1) * 128], yc[:, kc * 128:(kc + 1) * 128],
                                                     idf[:]), reads=[yc, idf], writes=[ps[7]])
                act(lambda g: g.activation(out=ob[:].rearrange("p a b -> p (a b)"), in_=ps[7][:], func=AF.Copy),
                    [ps[7]], [ob])
                P.dma(d["yT"][1].rearrange("(kc p) t -> p kc t", p=128)[:, :, t * 128:(t + 1) * 128], ob[:], reads=[ob],
                      writes=[self.db("yT", (1, t))])

            def run2(ga, gb):
                live = [g for g in (ga, gb) if g is not None]
                while live:
                    for g in list(live):
                        try:
                            next(g)
                        except StopIteration:
                            live.remove(g)

            loads(0, order[0])
            run2(stageA(0), None)
            for ti in range(len(order)):
                nxa = stageA(ti + 1) if ti + 1 < len(order) else None
                run2(stageB(ti), nxa)
        P.release()

    KB.declare_rwkv = declare_rwkv
    KB.phase_rwkv = phase_rwkv


_rwkv_methods()


def build_program():
    nc = bass.Bass("TRN2", target_bir_lowering=False)
    kb = KB(nc)
    kb.declare_common()
    kb.declare_pin()
    kb.declare_attn()
    kb.declare_merge()
    kb.declare_hyena()
    kb.declare_rwkv()
    kb.outp("out", [NLAT, DM])
    kb.alloc_persist()
    d = kb.d
    for l in range(2):
        with_ctx = (l == 0)
        kb.phase_mod(l)
        src = (d["xall"], "xall") if l == 0 else (d["xs"], "xs")
        kb.phase_ffn(l, 0, src, (d["xs"], "xs"), NTILE)
        kb.phase_pin(l, (d["xs"], "xs"))
        kb.phase_mla(l, with_ctx)
        kb.phase_swa(l, with_ctx)
        kb.phase_hyena(l, with_ctx)
        kb.phase_rwkv(l, with_ctx)
        kb.phase_merge(l, with_ctx)
        if l == 0:
            kb.phase_ffn(l, 2, (d["xs"], "xs"), (d["xs"], "xs"), NTILE)
        else:
            kb.phase_ffn(l, 2, (d["xs"], "xs"), (d["out"], "out"), 32)
    kb.P.barrier()
    return nc, kb


def kernel(**inputs):
    from concourse.bass_utils import run_bass_kernel_spmd
    nc, kb = build_program()
    sh = host_shared(inputs)
    names = [k for k in kb.d if k in sh]
    maps = []
    for b in range(8):
        pc = host_core(inputs, b)
        m = {k: sh[k] for k in names}
        m.update(pc)
        maps.append(m)
    res = run_bass_kernel_spmd(nc, maps, core_ids=list(range(8)))
    out = np.stack([np.asarray(res.results[b]["out"], dtype=np.float32) for b in range(8)], axis=0)
    return out
```

```python
import numpy as np
import concourse.bass as bass
import concourse.mybir as mybir

F32 = mybir.dt.float32
BF16 = mybir.dt.bfloat16
AF = mybir.ActivationFunctionType
ALU = mybir.AluOpType
AX = mybir.AxisListType

NDSEM = 36
NHW = 28
SEM_LIMIT = 30000
SB_BASE = 16640
SB_TOP = 229376
LAZY_D = 2
SAME_ENG_GAP = 3


class Buf:
    __slots__ = ("w", "r", "name")

    def __init__(self, name=""):
        self.w = None
        self.r = {}
        self.name = name


class Tile:
    def __init__(self, t, buf):
        self.t = t
        self.b = buf

    def __getitem__(self, k):
        return self.t[k]


def _bufs(xs):
    return [x.b if isinstance(x, Tile) else x for x in xs]


class Prog:
    def __init__(self, nc):
        self.nc = nc
        self.eng = {"pe": nc.tensor, "dve": nc.vector, "act": nc.scalar,
                    "pool": nc.gpsimd, "sp": nc.sync}
        self.cnt = {e: 0 for e in self.eng}
        self.known = {e: {} for e in self.eng}
        self.esem = {}
        self.egen = {e: 0 for e in self.eng}
        self.ebase = {e: 0 for e in self.eng}
        self.dsem = []
        self.duse = []
        self.dgen = []
        self.semtab = {}
        self.dnext = 0
        self.dnext_sw = 0
        self.pending = []
        self.nwait = 0
        self.ninstr = 0
        self.sb_off = SB_BASE
        self.sb_mark = []
        self.uid = 0
        nc = self.nc
        for e in self.eng:
            self.esem[e] = nc.alloc_semaphore("es_%s_0" % e)
            self.semtab[("e", e, 0)] = self.esem[e]
        for i in range(NDSEM):
            self.dsem.append(nc.alloc_semaphore("ds%d_0" % i))
            self.semtab[("d", i, 0)] = self.dsem[i]
            self.duse.append(0)
            self.dgen.append(0)

    def _rot_e(self, e):
        if self.cnt[e] - self.ebase[e] >= SEM_LIMIT:
            self.egen[e] += 1
            self.ebase[e] = self.cnt[e]
            self.esem[e] = self.nc.alloc_semaphore("es_%s_%d" % (e, self.egen[e]))
            self.semtab[("e", e, self.egen[e])] = self.esem[e]

    def _rot_d(self, i):
        if 16 * self.duse[i] >= SEM_LIMIT:
            self.dgen[i] += 1
            self.duse[i] = 0
            self.dsem[i] = self.nc.alloc_semaphore("ds%d_%d" % (i, self.dgen[i]))
            self.semtab[("d", i, self.dgen[i])] = self.dsem[i]

    def sbuf(self, shape, dtype, name=None):
        self.uid += 1
        name = (name or "t") + "_%d" % self.uid
        nbytes = int(np.prod(shape[1:])) * mybir.dt.size(dtype)
        off = (self.sb_off + 31) // 32 * 32
        t = self.nc.alloc_sbuf_tensor_at(name, list(shape), dtype, offset=off)
        self.sb_off = off + nbytes
        assert self.sb_off <= SB_TOP, ("SBUF overflow", name, self.sb_off)
        return Tile(t, Buf(name))

    def mark(self):
        self.sb_mark.append(self.sb_off)

    def release(self):
        self.barrier()
        self.sb_off = self.sb_mark.pop()

    def _wait(self, e, kind, id_, gen, val):
        self.known[e][(kind, id_)] = (gen, val)
        self.eng[e].wait_ge(self.semtab[(kind, id_, gen)], val)
        self.nwait += 1

    def _need(self, e, toks):
        kn = self.known[e]
        req = {}
        for t in toks:
            if t is None:
                continue
            kind, id_, gen, val, absidx = t
            if kind == "e" and id_ == e:
                if e == "pe":
                    continue
                if self.cnt[e] + 1 - absidx >= SAME_ENG_GAP:
                    continue
            k = (kind, id_)
            if kn.get(k, (-1, 0)) >= (gen, val):
                continue
            if req.get(k, (-1, 0)) < (gen, val):
                req[k] = (gen, val)
        for (kind, id_), (gen, val) in req.items():
            self._wait(e, kind, id_, gen, val)

    @staticmethod
    def _deps(reads, writes):
        toks = []
        for b in reads:
            toks.append(b.w)
        for b in writes:
            toks.append(b.w)
            toks.extend(b.r.values())
        return toks

    @staticmethod
    def _commit(tok, reads, writes):
        k = (tok[0], tok[1])
        for b in reads:
            b.r[k] = tok
        for b in writes:
            b.w = tok
            b.r = {}

    def op(self, e, fn, reads=(), writes=()):
        reads = _bufs(reads)
        writes = _bufs(writes)
        self._hazard_flush(reads, writes)
        self._rot_e(e)
        self._need(e, self._deps(reads, writes))
        ins = fn(self.eng[e])
        self.cnt[e] += 1
        ins.then_inc(self.esem[e], 1)
        tok = ("e", e, self.egen[e], self.cnt[e] - self.ebase[e], self.cnt[e])
        self._commit(tok, reads, writes)
        self.ninstr += 1
        return ins

    def _flush(self, upto=None):
        n = len(self.pending) if upto is None else upto
        todo, self.pending = self.pending[:n], self.pending[n:]
        for (out, in_, reads, writes, q, kw, _) in todo:
            self._dma_now(out, in_, reads, writes, q, kw)

    def _hazard_flush(self, reads, writes):
        if not self.pending:
            return
        rs = set(map(id, reads))
        ws = set(map(id, writes))
        last = -1
        for i, (_, _, pr, pw, _, _, _) in enumerate(self.pending):
            hit = False
            for b in pr:
                if id(b) in ws:
                    hit = True
            for b in pw:
                if id(b) in ws or id(b) in rs:
                    hit = True
            if hit:
                last = i
        if last >= 0:
            self._flush(last + 1)

    def dma(self, out, in_, reads=(), writes=(), q="sp", **kw):
        reads = _bufs(reads)
        writes = _bufs(writes)
        self._hazard_flush(reads, writes)
        is_store = type(out.tensor).__name__.startswith("DRam") and not type(in_.tensor).__name__.startswith("DRam")
        if is_store and q == "sp" and LAZY_D > 0:
            self.pending.append([out, in_, reads, writes, q, kw, 0])
            return None
        r = self._dma_now(out, in_, reads, writes, q, kw)
        if q == "sp" and self.pending:
            k = 0
            for p in self.pending:
                p[6] += 1
            while k < len(self.pending) and self.pending[k][6] >= LAZY_D:
                k += 1
            if k:
                self._flush(k)
        return r

    def _dma_now(self, out, in_, reads, writes, q, kw):
        if q == "pool":
            i = NHW + self.dnext_sw
            self.dnext_sw = (self.dnext_sw + 1) % (NDSEM - NHW)
        else:
            i = self.dnext
            self.dnext = (self.dnext + 1) % NHW
        toks = self._deps(reads, writes)
        if self.duse[i]:
            toks.append(("d", i, self.dgen[i], 16 * self.duse[i], 0))
        self._need(q, toks)
        self._rot_d(i)
        self.duse[i] += 1
        ins = self.eng[q].dma_start(out=out, in_=in_, **kw)
        ins.then_inc(self.dsem[i], 16)
        tok = ("d", i, self.dgen[i], 16 * self.duse[i], 0)
        self._commit(tok, reads, writes)
        self.ninstr += 1
        return ins

    def barrier(self, engines=None):
        self._flush()
        toks = []
        for e in self.eng:
            if self.cnt[e] > self.ebase[e]:
                toks.append(("e", e, self.egen[e], self.cnt[e] - self.ebase[e]))
            elif self.egen[e] > 0:
                toks.append(("e", e, self.egen[e] - 1, SEM_LIMIT))
        for i, u in enumerate(self.duse):
            if u:
                toks.append(("d", i, self.dgen[i], 16 * u))
        for e in (engines or self.eng):
            kn = self.known[e]
            for kind, id_, gen, val in toks:
                if kind == "e" and id_ == e and e in ("sp", "pe"):
                    continue
                if kn.get((kind, id_), (-1, 0)) >= (gen, val):
                    continue
                self._wait(e, kind, id_, gen, val)

    def mm(self, out, lhsT, rhs, start=True, stop=True, reads=(), writes=()):
        return self.op("pe", lambda g: g.matmul(out, lhsT, rhs, start=start, stop=stop),
                       reads=reads, writes=writes)


DM = 1024
NLAT = 4096
NCTX = 256
TT = NLAT + NCTX
NTILE = TT // 128
FF = 2816
NFC = FF // 128
EPS = 1e-6
I32 = mybir.dt.int32


class KB:
    def __init__(self, nc, ext_in=(), ext_out=()):
        self.nc = nc
        self.P = Prog(nc)
        self.ext_in = set(ext_in)
        self.ext_out = set(ext_out)
        self.d = {}
        self.dbufs = {}
        self.ps = [Tile(nc.alloc_psum_tensor("ps%d" % i, [128, 512], F32), Buf("ps%d" % i))
                   for i in range(8)]

    def inp(self, name, shape, dtype=F32):
        self.d[name] = self.nc.dram_tensor(name, list(shape), dtype, kind="ExternalInput").ap()
        return self.d[name]

    def outp(self, name, shape, dtype=F32):
        self.d[name] = self.nc.dram_tensor(name, list(shape), dtype, kind="ExternalOutput").ap()
        return self.d[name]

    def scr(self, name, shape, dtype=F32):
        kind = "Internal"
        if name in self.ext_in:
            kind = "ExternalInput"
        elif name in self.ext_out:
            kind = "ExternalOutput"
        self.d[name] = self.nc.dram_tensor(name, list(shape), dtype, kind=kind).ap()
        return self.d[name]

    def db(self, name, idx=0):
        k = (name, idx)
        if k not in self.dbufs:
            self.dbufs[k] = Buf("%s_%s" % (name, idx))
        return self.dbufs[k]

    def psb(self, i):
        return self.ps[i][:].bitcast(BF16)

    def rstd_from_ss(self, ss, n, eps, out):
        P = self.P
        ss_t, ss_ap = ss
        o_t, o_ap = out
        P.op("act", lambda g: g.activation(out=o_ap, in_=ss_ap, func=AF.Sqrt, scale=1.0 / n, bias=eps),
             reads=[ss_t], writes=[o_t])
        P.op("dve", lambda g: g.reciprocal(out=o_ap, in_=o_ap), reads=[o_t], writes=[o_t])

    def declare_common(self):
        L = 2
        self.inp("xall", [TT, DM])
        self.inp("cT", [128, 16])
        self.inp("ident", [128, 128])
        self.inp("w_mod", [L, DM, 9 * DM])
        self.inp("b_mod", [L, 9 * DM])
        self.inp("norm_g", [L, 6, DM])
        self.inp("norm_gT", [L, 128, 48])
        self.inp("ffn_w13", [L, 2, DM, 2 * FF])
        self.inp("ffn_w2", [L, 2, FF, DM])
        self.scr("gtrow", [L, 3, 2, DM])
        self.scr("xs", [TT, DM])

    def alloc_persist(self):
        P = self.P
        self.ident_f = P.sbuf([128, 128], F32, "identf")
        self.ident_b = P.sbuf([128, 128], BF16, "identb")
        P.dma(self.ident_f[:], self.d["ident"], writes=[self.ident_f])
        P.dma(self.ident_b[:], self.d["ident"], writes=[self.ident_b], q="pool")
        self.mcol = P.sbuf([128, 72, 2], F32, "mcol")
        self.AB = P.sbuf([128, 3, 2, 8, 2], F32, "AB")

    def phase_mod(self, l):
        P = self.P
        d = self.d
        P.mark()
        cT = P.sbuf([128, 16], F32, "cT")
        P.dma(cT[:], d["cT"], writes=[cT])
        sc = P.sbuf([128, 16], F32, "sc")
        P.op("act", lambda g: g.activation(out=sc[:], in_=cT[:], func=AF.Silu), reads=[cT], writes=[sc])
        mrow = P.sbuf([2, 9 * DM], F32, "mrow")
        brow = P.sbuf([2, 9 * DM], F32, "brow")
        for s in range(2):
            P.dma(brow[s:s + 1, :], d["b_mod"][l:l + 1, :], writes=[brow])
        wt = [P.sbuf([128, 8, 512], F32, "wmod%d" % i) for i in range(2)]
        wsrc = d["w_mod"][l].rearrange("(k p) n -> p k n", p=128)
        for jb in range(18):
            w = wt[jb % 2]
            P.dma(w[:], wsrc[:, :, jb * 512:(jb + 1) * 512], writes=[w])
            ps = self.ps[jb % 2]
            for k in range(8):
                P.mm(ps[0:2, :], sc[:, 2 * k:2 * k + 2], w[:, k, :], start=(k == 0), stop=(k == 7),
                     reads=[sc, w], writes=[ps])
            P.op("dve", lambda g: g.tensor_tensor(out=mrow[:, jb * 512:(jb + 1) * 512], in0=ps[0:2, :],
                                                  in1=brow[:, jb * 512:(jb + 1) * 512], op=ALU.add),
                 reads=[ps, brow], writes=[mrow])
        psc = self.ps[2]
        for c in range(72):
            P.mm(psc[:, 2 * c:2 * c + 2], mrow[0:2, c * 128:(c + 1) * 128], self.ident_f[0:2, 0:2],
                 reads=[mrow, self.ident_f], writes=[psc])
        mcol = self.mcol
        P.op("dve", lambda g: g.tensor_copy(out=mcol[:].rearrange("p c s -> p (c s)"), in_=psc[:, 0:144]),
             reads=[psc], writes=[mcol])
        gcol = P.sbuf([128, 6, 8], F32, "gcol")
        P.dma(gcol[:].rearrange("p n k -> p (n k)"), d["norm_gT"][l], writes=[gcol])
        mc4 = mcol[:].rearrange("p (j k) s -> p j k s", k=8)
        tmp = P.sbuf([128, 8, 2], F32, "abtmp")
        for sub in range(3):
            jsh, jsc, npre = 3 * sub, 3 * sub + 1, 2 * sub
            P.op("dve", lambda g: g.tensor_scalar(out=tmp[:], in0=mc4[:, jsc, :, :], scalar1=1.0, scalar2=None,
                                                  op0=ALU.add), reads=[mcol], writes=[tmp])
            P.op("dve", lambda g: g.tensor_tensor(out=self.AB[:, sub, 0, :, :], in0=tmp[:],
                                                  in1=gcol[:, npre, :].unsqueeze(2).to_broadcast([128, 8, 2]),
                                                  op=ALU.mult), reads=[tmp, gcol], writes=[self.AB])
            P.op("dve", lambda g: g.tensor_copy(out=self.AB[:, sub, 1, :, :], in_=mc4[:, jsh, :, :]),
                 reads=[mcol], writes=[self.AB])
        grow = [P.sbuf([2, DM], F32, "grow%d" % i) for i in range(3)]
        gto = [P.sbuf([2, DM], F32, "gto%d" % i) for i in range(3)]
        for sub in range(3):
            jg, npost = 3 * sub + 2, 2 * sub + 1
            fac = 1.0 if sub == 1 else 0.5
            for s in range(2):
                P.dma(grow[sub][s:s + 1, :], d["norm_g"][l, npost:npost + 1, :], writes=[grow[sub]])
            P.op("dve", lambda g: g.scalar_tensor_tensor(out=gto[sub][:], in0=mrow[:, jg * DM:(jg + 1) * DM],
                                                         scalar=fac, in1=grow[sub][:], op0=ALU.mult,
                                                         op1=ALU.mult),
                 reads=[mrow, grow[sub]], writes=[gto[sub]])
            P.dma(d["gtrow"][l, sub], gto[sub][:], reads=[gto[sub]], writes=[self.db("gtrow", (l, sub))])
        P.release()

    def norm_T(self, xt_t, x_ap, sub, s, xnT_t, xnT_ap, pst, wk):
        self.norm_A(xt_t, x_ap, wk)
        self.norm_B(sub, s, xnT_t, xnT_ap, pst, wk)

    def norm_A(self, xt_t, x_ap, wk):
        P = self.P
        junk, ss, rs, xs = wk
        P.op("act", lambda g: g.activation(out=junk[:], in_=x_ap, func=AF.Square, accum_out=ss[:]),
             reads=[xt_t], writes=[junk, ss])
        self.rstd_from_ss((ss, ss[:]), DM, EPS, (rs, rs[:]))
        P.op("dve", lambda g: g.tensor_scalar(out=xs[:], in0=x_ap, scalar1=rs[:, 0:1], scalar2=None,
                                              op0=ALU.mult), reads=[xt_t, rs], writes=[xs])

    def norm_B(self, sub, s, xnT_t, xnT_ap, pst, wk):
        P = self.P
        junk, ss, rs, xs = wk
        pb = self.psb(pst)
        for k in range(8):
            P.op("pe", lambda g: g.transpose(pb[:, k * 128:(k + 1) * 128], xs[:, k * 128:(k + 1) * 128],
                                             self.ident_b[:]),
                 reads=[xs, self.ident_b], writes=[self.ps[pst]])
        for k in range(8):
            A = self.AB[:, sub, 0, k, s:s + 1]
            B = self.AB[:, sub, 1, k, s:s + 1]
            if k % 2 == 0:
                P.op("dve", lambda g: g.tensor_scalar(out=xnT_ap[:, k, :], in0=pb[:, k * 128:(k + 1) * 128],
                                                      scalar1=A, scalar2=B, op0=ALU.mult, op1=ALU.add),
                     reads=[self.ps[pst], self.AB], writes=[xnT_t])
            else:
                P.op("act", lambda g: g.activation(out=xnT_ap[:, k, :], in_=pb[:, k * 128:(k + 1) * 128],
                                                   func=AF.Identity, scale=A, bias=B),
                     reads=[self.ps[pst], self.AB], writes=[xnT_t])

    def norm_res_out(self, pso, xt_t, x_ap, gt, wk2, dst_ap, dst_buf):
        P = self.P
        junk, s2, r2, tmp = wk2
        for h in range(2):
            P.op("act", lambda g: g.activation(out=junk[:, 0:512], in_=self.ps[pso[h]][:], func=AF.Square,
                                               accum_out=s2[:, h:h + 1]),
                 reads=[self.ps[pso[h]]], writes=[junk, s2])
        P.op("dve", lambda g: g.tensor_tensor(out=s2[:, 2:3], in0=s2[:, 0:1], in1=s2[:, 1:2], op=ALU.add),
             reads=[s2], writes=[s2])
        self.rstd_from_ss((s2, s2[:, 2:3]), DM, EPS, (r2, r2[:]))
        for h in range(2):
            P.op("dve", lambda g: g.scalar_tensor_tensor(out=tmp[:, h * 512:(h + 1) * 512],
                                                         in0=self.ps[pso[h]][:], scalar=r2[:, 0:1],
                                                         in1=gt[:, h * 512:(h + 1) * 512],
                                                         op0=ALU.mult, op1=ALU.mult),
                 reads=[self.ps[pso[h]], r2, gt], writes=[tmp])
        P.op("pool", lambda g: g.tensor_tensor(out=x_ap, in0=x_ap, in1=tmp[:], op=ALU.add),
             reads=[tmp, xt_t], writes=[xt_t])
        P.dma(dst_ap, x_ap, reads=[xt_t], writes=[dst_buf])

    def phase_ffn(self, l, sub, src, dst, ntiles):
        P = self.P
        d = self.d
        wi = 0 if sub == 0 else 1
        P.mark()
        w13 = P.sbuf([128, 8, 2 * FF], BF16, "w13")
        w13b = [Buf("w13_%d" % k) for k in range(8)]
        for k in range(8):
            P.dma(w13[:, k, :], d["ffn_w13"][l, wi, k * 128:(k + 1) * 128, :], writes=[w13b[k]], q="pool")
        w2 = P.sbuf([128, NFC, DM], BF16, "w2")
        w2b = [Buf("w2_%d" % j) for j in range(NFC)]
        for j in range(NFC):
            P.dma(w2[:, j, :], d["ffn_w2"][l, wi, j * 128:(j + 1) * 128, :], writes=[w2b[j]], q="pool")
        gt = [P.sbuf([128, DM], F32, "gt%d" % s) for s in range(2)]
        for s in range(2):
            P.dma(gt[s][:], d["gtrow"][l, sub, s:s + 1, :].to_broadcast([128, DM]),
                  reads=[self.db("gtrow", (l, sub))], writes=[gt[s]])
        xbuf = [P.sbuf([128, 2, DM], F32, "xbuf%d" % i) for i in range(2)]
        xbb = [[Buf("xb%d_%d" % (i, j)) for j in range(2)] for i in range(2)]
        xnT = [P.sbuf([128, 8, 256], BF16, "xnT%d" % i) for i in range(2)]
        hT = P.sbuf([128, NFC, 256], BF16, "hT")
        hTb = [Buf("hT%d" % j) for j in range(NFC)]
        junk = P.sbuf([128, DM], BF16, "junk")
        junk2 = P.sbuf([128, 512], BF16, "junk2")
        s2 = [P.sbuf([128, 3], F32, "s2%d" % i) for i in range(2)]
        r2 = [P.sbuf([128, 1], F32, "r2%d" % i) for i in range(2)]
        tmp = [P.sbuf([128, DM], F32, "tmp%d" % i) for i in range(2)]
        sa = [P.sbuf([128, 256], F32, "sa%d" % i) for i in range(2)]
        ngroups = ntiles // 2
        src_ap, src_name = src
        dst_ap, dst_name = dst

        def load(gi):
            for i in range(2):
                t = 2 * gi + i
                xt = Tile(xbuf[gi % 2].t, xbb[gi % 2][i])
                P.dma(xbuf[gi % 2][:, i, :], src_ap[t * 128:(t + 1) * 128, :],
                      reads=[self.db(src_name, t)], writes=[xt])

        xs4 = [P.sbuf([128, DM], BF16, "xs4_%d" % i) for i in range(4)]
        ss4 = [P.sbuf([128, 1], F32, "ss4_%d" % i) for i in range(4)]
        rs4 = [P.sbuf([128, 1], F32, "rs4_%d" % i) for i in range(4)]

        def wkof(gi, i):
            k = (gi % 2) * 2 + i
            return (junk, ss4[k], rs4[k], xs4[k])

        def normA(gi):
            for i in range(2):
                xt = Tile(xbuf[gi % 2].t, xbb[gi % 2][i])
                self.norm_A(xt, xbuf[gi % 2][:, i, :], wkof(gi, i))

        def normB(gi):
            s_ = 1 if 2 * gi >= 32 else 0
            for i in range(2):
                self.norm_B(sub, s_, xnT[gi % 2], xnT[gi % 2][:, :, i * 128:(i + 1) * 128], 0 if i == 0 else 7,
                            wkof(gi, i))

        load(0)
        normA(0)
        normB(0)
        for gi in range(ngroups):
            if gi + 1 < ngroups:
                load(gi + 1)
            xb = xbuf[gi % 2]
            xn = xnT[gi % 2]
            s = 1 if 2 * gi >= 32 else 0
            for j in range(NFC):
                pa = self.ps[1 + (j % 2) * 2]
                pbk = self.ps[2 + (j % 2) * 2]
                for k in range(8):
                    P.mm(pa[:, 0:256], w13[:, k, j * 128:(j + 1) * 128], xn[:, k, :], start=(k == 0),
                         stop=(k == 7), reads=[w13b[k], xn], writes=[pa])
                for k in range(8):
                    P.mm(pbk[:, 0:256], w13[:, k, FF + j * 128:FF + (j + 1) * 128], xn[:, k, :], start=(k == 0),
                         stop=(k == 7), reads=[w13b[k], xn], writes=[pbk])
                sj = sa[j % 2]
                P.op("act", lambda g: g.activation(out=sj[:], in_=pa[:, 0:256], func=AF.Silu),
                     reads=[pa], writes=[sj])
                P.op("dve", lambda g: g.tensor_tensor(out=hT[:, j, :], in0=sj[:], in1=pbk[:, 0:256], op=ALU.mult),
                     reads=[sj, pbk], writes=[hTb[j]])
            if gi + 1 < ngroups:
                normA(gi + 1)
            for i in range(2):
                t = 2 * gi + i
                xt = Tile(xb.t, xbb[gi % 2][i])
                for h in range(2):
                    po = self.ps[5 + h]
                    for j in range(NFC):
                        P.mm(po[:], hT[:, j, i * 128:(i + 1) * 128], w2[:, j, h * 512:(h + 1) * 512],
                             start=(j == 0), stop=(j == NFC - 1), reads=[hTb[j], w2b[j]], writes=[po])
                self.norm_res_out([5, 6], xt, xb[:, i, :], gt[s], (junk2, s2[i], r2[i], tmp[i]),
                                  dst_ap[t * 128:(t + 1) * 128, :], self.db(dst_name, t))
            if gi + 1 < ngroups:
                normB(gi + 1)
        P.release()


def host_shared(inp):
    f32 = np.float32
    sh = {}
    sh["ident"] = np.eye(128, dtype=f32)
    for k in ("w_mod", "b_mod", "norm_g", "ffn_w13", "ffn_w2"):
        sh[k] = np.ascontiguousarray(inp[k], dtype=f32)
    ng = np.asarray(inp["norm_g"], dtype=f32)
    sh["norm_gT"] = np.ascontiguousarray(ng.reshape(2, 6, 8, 128).transpose(0, 3, 1, 2).reshape(2, 128, 48))
    sh["w_ext"] = build_w_ext(inp["w_in"])
    cm, sm = rope_tables(32)
    sh["rope_m"] = np.ascontiguousarray(np.stack([cm, sm], 0))
    cs, ss_ = rope_tables(64)
    sh["rope_s"] = np.ascontiguousarray(np.stack([np.concatenate([cs, cs], 0), np.concatenate([ss_, ss_], 0)], 0))
    nq = np.asarray(inp["mla_norm_q"], f32)
    nkv = np.asarray(inp["mla_norm_kv"], f32)
    sh["mla_nT"] = np.ascontiguousarray(np.stack([nq[:, 0:128], nq[:, 128:256], nkv], axis=2))
    wuq = np.asarray(inp["mla_w_uq"], f32).reshape(2, 256, 8, 96)
    pm, _ = rope_partner(32)
    sw = np.concatenate([wuq[..., 0:64], wuq[..., 64 + pm]], axis=-1)
    sh["mla_wq2"] = np.ascontiguousarray(np.stack([wuq, sw], axis=3).reshape(2, 256, 8 * 2 * 96))
    wukv = np.asarray(inp["mla_w_ukv"], f32).reshape(2, 128, 8, 128)
    sh["mla_wk"] = np.ascontiguousarray(wukv[..., 0:64].reshape(2, 128, 512))
    sh["mla_wv"] = np.ascontiguousarray(wukv[..., 64:128].reshape(2, 128, 512))
    sh["swa_sink"] = np.ascontiguousarray(inp["swa_sink"], dtype=f32)
    sh["w_branch"] = np.ascontiguousarray(inp["w_branch"], dtype=f32)
    for k in ("hyena_conv", "hyena_conv_b", "hyena_w1", "hyena_w2", "hyena_w3", "hyena_bias"):
        sh[k] = np.ascontiguousarray(inp[k], dtype=f32)
    sh["rwkv_mu"] = np.ascontiguousarray(inp["rwkv_mu"], dtype=f32)
    sh["rwkv_kvec"] = np.ascontiguousarray(inp["rwkv_kvec"], dtype=f32)
    sh["rwkv_lnp"] = np.ascontiguousarray(np.stack([inp["rwkv_ln_g"], inp["rwkv_ln_b"], inp["rwkv_r_k"]], axis=1), dtype=f32)
    wup = np.asarray(inp["rwkv_w_up"], f32)
    w0 = np.asarray(inp["rwkv_w0"], f32)
    sh["rwkv_wupA"] = np.ascontiguousarray(np.concatenate([wup.transpose(0, 2, 1, 3).reshape(2, 64, 1024),
                                                           w0.reshape(2, 1, 1024)], axis=1))
    aup = np.asarray(inp["rwkv_a_up"], f32)
    a0 = np.asarray(inp["rwkv_a0"], f32)
    sh["rwkv_aupA"] = np.ascontiguousarray(np.concatenate([aup.transpose(0, 2, 1, 3).reshape(2, 64, 1024),
                                                           a0.reshape(2, 1, 1024)], axis=1))
    sh["rwkv_g_up"] = np.ascontiguousarray(inp["rwkv_g_up"], dtype=f32)
    sh["rw_tri"] = rwkv_tables()
    hf = np.asarray(inp["hyena_freq"], f32)
    sh["hyT"] = np.ascontiguousarray(np.stack([hf[:, 0], hf[:, 1], np.asarray(inp["hyena_b1"], f32),
                                               np.asarray(inp["hyena_b2"], f32)], axis=2))
    tl = hy_tables(NLAT)
    tc = hy_tables(NCTX)
    sh["hy_D"] = np.ascontiguousarray(np.stack([tl["D2"], tl["D2sw"], tl["E"]], 0))
    sh["hyL_W1"] = np.ascontiguousarray(tl["W1"].reshape(128, -1))
    sh["hyL_W3"] = np.ascontiguousarray(tl["W3"].reshape(65, -1))
    sh["hyC_W1"] = np.ascontiguousarray(tc["W1"].reshape(8, -1))
    sh["hyC_W3"] = np.ascontiguousarray(tc["W3"].reshape(5, -1))
    sh["hyL_fK"], sh["hyL_wK"] = hy_feats(NLAT)
    sh["hyC_fK"], sh["hyC_wK"] = hy_feats(NCTX)
    sh["w_out"] = np.ascontiguousarray(inp["w_out"], dtype=f32)
    bgt = np.asarray(inp["b_gate"], f32).reshape(2, 4, 8, 128).transpose(0, 3, 1, 2).reshape(2, 128, 32)
    sh["b_gateT"] = np.ascontiguousarray(bgt)
    kk = np.arange(128)[:, None]
    qq = np.arange(128)[None, :]
    sh["swa_mask"] = np.ascontiguousarray(np.stack([(qq <= kk), (kk <= qq)], 0).astype(f32))
    return sh


def host_core(inp, b):
    f32 = np.float32
    pc = {}
    pc["xall"] = np.ascontiguousarray(np.concatenate([inp["x"][b], inp["ctx"][b]], axis=0), dtype=f32)
    cv = np.stack([np.asarray(inp["c"][b], f32), np.asarray(inp["c_ctx"], f32)], axis=0)
    pc["cT"] = np.ascontiguousarray(cv.reshape(2, 8, 128).transpose(2, 1, 0).reshape(128, 16))
    return pc


G0, PA0, PB0, PH0, PD0 = 0, 4096, 4512, 6304, 7840
FM_COLS = 1728
TM_COLS = 3456
WX_FM0 = 0
WX_TM0 = FM_COLS
WX_G0 = FM_COLS + TM_COLS
WX_COLS = WX_G0 + 4096
PAD_ROWS = TT + 3


def tm_row(t):
    return 1 + t if t < NLAT else 2 + t


def rope_partner(R):
    H = R // 2
    q = H // 2
    part = np.zeros(R, np.int64)
    sign = np.zeros(R, np.float32)
    for dd in range(R):
        base = (dd // H) * H
        o = dd % H
        if o < q:
            part[dd] = base + o + q
            sign[dd] = -1.0
        else:
            part[dd] = base + o - q
            sign[dd] = 1.0
    return part, sign


def rope_tables(R):
    H = R // 2
    q = H // 2
    t = np.arange(NLAT)
    row = (t // 64).astype(np.float32)
    col = (t % 64).astype(np.float32)
    inv = (10000.0 ** (-np.arange(0, H, 2, dtype=np.float32) / H)).astype(np.float32)
    _, sign = rope_partner(R)
    cos = np.zeros((R, NLAT), np.float32)
    sin = np.zeros((R, NLAT), np.float32)
    for dd in range(R):
        pos = row if dd < H else col
        ang = (pos * inv[(dd % H) % q]).astype(np.float32)
        cos[dd] = np.cos(ang)
        sin[dd] = sign[dd] * np.sin(ang)
    return cos, sin


def build_w_ext(w_in):
    pm, _ = rope_partner(32)
    ps_, _ = rope_partner(64)
    cols = []
    cols += list(range(PA0, PA0 + 384))
    kr0 = PA0 + 384
    cols += [kr0 + i for i in range(32)]
    cols += [kr0 + int(pm[i]) for i in range(32)]
    q0 = PD0
    cols += [q0 + i for i in range(512)]
    cols += [q0 + (i // 64) * 64 + int(ps_[i % 64]) for i in range(512)]
    k0 = PD0 + 512
    cols += [k0 + i for i in range(128)]
    cols += [k0 + (i // 64) * 64 + int(ps_[i % 64]) for i in range(128)]
    assert len(cols) == FM_COLS
    cols += list(range(PB0, PB0 + 1792))
    cols += list(range(PH0, PH0 + 1536))
    cols += list(range(PD0 + 640, PD0 + 768))
    assert len(cols) == FM_COLS + TM_COLS
    cols += list(range(0, 4096))
    return np.ascontiguousarray(np.asarray(w_in, np.float32)[:, :, np.asarray(cols)])


def _pin_methods():
    def declare_pin(self):
        L = 2
        self.inp("w_ext", [L, DM, WX_COLS])
        self.inp("rope_m", [2, 32, NLAT])
        self.inp("rope_s", [2, 128, NLAT])
        self.scr("uT", [8, 128, TT], BF16)
        self.scr("cqkvT", [3, 128, TT], BF16)
        self.scr("krT", [32, TT], BF16)
        self.scr("sqT", [4, 128, TT], BF16)
        self.scr("skT", [128, TT], BF16)
        self.scr("pb", [PAD_ROWS, 1792])
        self.scr("ph", [PAD_ROWS, 1536])
        self.scr("pv", [TT, 128])

    def phase_pin(self, l, src):
        P = self.P
        d = self.d
        src_ap, src_name = src
        P.mark()
        NW = FM_COLS + TM_COLS
        w = P.sbuf([128, 8, NW], BF16, "wpin")
        wb = [Buf("wpin%d" % k) for k in range(8)]
        for k in range(8):
            P.dma(w[:, k, :], d["w_ext"][l, k * 128:(k + 1) * 128, 0:NW], writes=[wb[k]], q="pool")
        z = P.sbuf([1, 1792], F32, "zrow")
        P.op("pool", lambda g: g.memset(z[:], 0.0), writes=[z])
        for r in (0, NLAT + 1, TT + 2):
            P.dma(d["pb"][r:r + 1, :], z[:], reads=[z], writes=[self.db("pb", "pad%d" % r)])
            P.dma(d["ph"][r:r + 1, :], z[:, 0:1536], reads=[z], writes=[self.db("ph", "pad%d" % r)])
        xbuf = [P.sbuf([128, 4, DM], F32, "xbuf%d" % i) for i in range(2)]
        xbb = [[Buf("xb%d_%d" % (i, j)) for j in range(4)] for i in range(2)]
        uT = [P.sbuf([128, 8, 512], BF16, "uT%d" % i) for i in range(2)]
        junk = P.sbuf([128, DM], BF16, "junk")
        ss = [P.sbuf([128, 1], F32, "ss%d" % i) for i in range(8)]
        rs = [P.sbuf([128, 1], F32, "rs%d" % i) for i in range(8)]
        xs = [P.sbuf([128, DM], BF16, "xs%d" % i) for i in range(8)]
        tabm = [P.sbuf([32, 2, 512], F32, "tabm%d" % i) for i in range(2)]
        tabs = [P.sbuf([128, 2, 512], F32, "tabs%d" % i) for i in range(2)]
        t1a = P.sbuf([128, 512], F32, "t1_0")
        t2a = P.sbuf([128, 512], F32, "t2_0")
        t1 = [t1a, t1a]
        t2 = [t2a, t2a]
        fo = [P.sbuf([128, 512], BF16, "fo%d" % i) for i in range(3)]
        tmo = [P.sbuf([128, TM_COLS], F32, "tmo%d" % i) for i in range(2)]
        groups = [list(range(4 * g, 4 * g + 4)) for g in range(8)] + [[32, 33]]

        def load(gi):
            for i, t in enumerate(groups[gi]):
                xt = Tile(xbuf[gi % 2].t, xbb[gi % 2][i])
                P.dma(xbuf[gi % 2][:, i, :], src_ap[t * 128:(t + 1) * 128, :],
                      reads=[self.db(src_name, t)], writes=[xt])
            if gi < 8:
                P.dma(tabm[gi % 2][:], d["rope_m"][:, :, gi * 512:(gi + 1) * 512].rearrange("c p t -> p c t"),
                      writes=[tabm[gi % 2]])
                P.dma(tabs[gi % 2][:], d["rope_s"][:, :, gi * 512:(gi + 1) * 512].rearrange("c p t -> p c t"),
                      writes=[tabs[gi % 2]])

        def wkof(gi, i):
            k = (gi % 2) * 4 + i
            return (junk, ss[k], rs[k], xs[k])

        def normA(gi):
            for i, t in enumerate(groups[gi]):
                xt = Tile(xbuf[gi % 2].t, xbb[gi % 2][i])
                self.norm_A(xt, xbuf[gi % 2][:, i, :], wkof(gi, i))

        def normB(gi):
            s_ = 0 if gi < 8 else 1
            for i, t in enumerate(groups[gi]):
                self.norm_B(1, s_, uT[gi % 2], uT[gi % 2][:, :, i * 128:(i + 1) * 128], 0 if i % 2 == 0 else 7,
                            wkof(gi, i))

        load(0)
        normA(0)
        normB(0)
        nfo = 0
        nev = 0
        for gi, tl in enumerate(groups):
            if gi + 1 < len(groups):
                load(gi + 1)
            n = 128 * len(tl)
            t0 = tl[0] * 128
            lat = gi < 8
            s = 0 if lat else 1
            xb = xbuf[gi % 2]
            u = uT[gi % 2]
            P.dma(d["uT"][:, :, t0:t0 + n].rearrange("k p t -> p k t"), u[:, :, 0:n], reads=[u],
                  writes=[self.db("uT", gi)])

            def fm_mm(ps, c0, m):
                for k in range(8):
                    P.mm(ps[0:m, 0:n], w[:, k, c0:c0 + m], u[:, k, 0:n], start=(k == 0), stop=(k == 7),
                         reads=[wb[k], u], writes=[ps])

            for c in range(3):
                ps = self.ps[1 + (c % 2) * 2]
                fm_mm(ps, c * 128, 128)
                o = fo[nfo % 3]
                nfo += 1
                P.op("act", lambda g: g.activation(out=o[:, 0:n], in_=ps[:, 0:n], func=AF.Copy),
                     reads=[ps], writes=[o])
                P.dma(d["cqkvT"][c, :, t0:t0 + n], o[:, 0:n], reads=[o], writes=[self.db("cqkvT", (c, gi))])
            roped = [(384, 416, 32, tabm, d["krT"][:, t0:t0 + n], ("krT", gi))]
            for c in range(4):
                roped.append((448 + c * 128, 960 + c * 128, 128, tabs, d["sqT"][c, :, t0:t0 + n], ("sqT", (c, gi))))
            roped.append((1472, 1600, 128, tabs, d["skT"][:, t0:t0 + n], ("skT", gi)))
            for ri, (cx, csw, m, tab, dst, dk) in enumerate(roped):
                psx = self.ps[1 + (ri % 2) * 2]
                fm_mm(psx, cx, m)
                o = fo[nfo % 3]
                nfo += 1
                if lat:
                    pss = self.ps[2 + (ri % 2) * 2]
                    fm_mm(pss, csw, m)
                    tb = tab[gi % 2]
                    a1 = t1[ri % 2]
                    a2 = t2[ri % 2]
                    P.op("dve", lambda g: g.tensor_tensor(out=a1[0:m, :], in0=psx[0:m, :], in1=tb[0:m, 0, :],
                                                          op=ALU.mult), reads=[psx, tb], writes=[a1])
                    P.op("dve", lambda g: g.tensor_tensor(out=a2[0:m, :], in0=pss[0:m, :], in1=tb[0:m, 1, :],
                                                          op=ALU.mult), reads=[pss, tb], writes=[a2])
                    P.op("pool", lambda g: g.tensor_tensor(out=o[0:m, :], in0=a1[0:m, :], in1=a2[0:m, :],
                                                           op=ALU.add), reads=[a1, a2], writes=[o])
                else:
                    P.op("act", lambda g: g.activation(out=o[0:m, 0:n], in_=psx[0:m, 0:n], func=AF.Copy),
                         reads=[psx], writes=[o])
                P.dma(dst, o[0:m, 0:n], reads=[o], writes=[self.db(*dk)])
            if gi + 1 < len(groups):
                normA(gi + 1)
            for i, t in enumerate(tl):
                st = tmo[i % 2]
                for cb in range(7):
                    c0 = cb * 512
                    cw = min(512, TM_COLS - c0)
                    ps = self.ps[5 + (cb % 2)]
                    for k in range(8):
                        P.mm(ps[:, 0:cw], u[:, k, i * 128:(i + 1) * 128], w[:, k, FM_COLS + c0:FM_COLS + c0 + cw],
                             start=(k == 0), stop=(k == 7), reads=[u, wb[k]], writes=[ps])
                    if nev % 2 == 0:
                        P.op("act", lambda g: g.activation(out=st[:, c0:c0 + cw], in_=ps[:, 0:cw], func=AF.Copy),
                             reads=[ps], writes=[st])
                    else:
                        P.op("dve", lambda g: g.tensor_copy(out=st[:, c0:c0 + cw], in_=ps[:, 0:cw]),
                             reads=[ps], writes=[st])
                    nev += 1
                r0 = tm_row(t * 128)
                P.dma(d["pb"][r0:r0 + 128, :], st[:, 0:1792], reads=[st], writes=[self.db("pb", t)])
                P.dma(d["ph"][r0:r0 + 128, :], st[:, 1792:3328], reads=[st], writes=[self.db("ph", t)])
                P.dma(d["pv"][t * 128:(t + 1) * 128, :], st[:, 3328:3456], reads=[st], writes=[self.db("pv", t)])
            if gi + 1 < len(groups):
                normB(gi + 1)
        P.release()

    KB.declare_pin = declare_pin
    KB.phase_pin = phase_pin


_pin_methods()


def _attn_methods():
    def declare_attn(self):
        L = 2
        self.inp("mla_nT", [L, 128, 3])
        self.inp("mla_wq2", [L, 256, 8 * 2 * 96])
        self.inp("mla_wk", [L, 128, 512])
        self.inp("mla_wv", [L, 128, 512])
        self.inp("swa_sink", [L, 8])
        self.inp("swa_mask", [2, 128, 128])
        self.scr("yT", [4, 512, TT], BF16)

    def phase_mla(self, l, with_ctx):
        P = self.P
        d = self.d
        P.mark()
        scale = 96.0 ** -0.5
        wq = P.sbuf([128, 2, 8, 2, 96], BF16, "wq")
        for c in range(2):
            P.dma(wq[:, c].rearrange("p h s m -> p (h s m)"), d["mla_wq2"][l, c * 128:(c + 1) * 128, :],
                  writes=[wq], q="pool")
        wk = P.sbuf([128, 8, 64], BF16, "wk")
        P.dma(wk[:].rearrange("p h m -> p (h m)"), d["mla_wk"][l], writes=[wk], q="pool")
        wv = P.sbuf([128, 512], BF16, "wv")
        P.dma(wv[:], d["mla_wv"][l], writes=[wv], q="pool")
        nT = P.sbuf([128, 3], F32, "nT")
        P.dma(nT[:], d["mla_nT"][l], writes=[nT])
        ones_f = P.sbuf([128, 128], F32, "ones_f")
        P.op("pool", lambda g: g.memset(ones_f[:], 1.0), writes=[ones_f])
        cqn = P.sbuf([128, 2, TT], BF16, "cqn")
        ckvn = P.sbuf([128, TT], BF16, "ckvn")
        vaug = P.sbuf([128, NTILE, 8, 128], BF16, "vaug")
        P.op("pool", lambda g: g.memset(vaug[:, :, :, 64:128], 1.0), writes=[vaug])
        groups = [(g * 512, 512) for g in range(8)] + [(NLAT, 256)]
        P.mark()
        xin = [P.sbuf([128, 3, 512], BF16, "xin%d" % i) for i in range(2)]
        sq = [P.sbuf([128, 3, 512], F32, "sq%d" % i) for i in range(2)]
        rsb = [P.sbuf([128, 2, 512], F32, "rsb%d" % i) for i in range(2)]
        for gi, (t0, n) in enumerate(groups):
            xi = xin[gi % 2]
            P.dma(xi[:, :, 0:n], d["cqkvT"][:, :, t0:t0 + n].rearrange("c p t -> p c t"),
                  reads=[self.db("cqkvT", (c, gi)) for c in range(3)], writes=[xi])
            sqi = sq[gi % 2]
            P.op("pool", lambda g: g.tensor_tensor(out=sqi[:, :, 0:n], in0=xi[:, :, 0:n], in1=xi[:, :, 0:n],
                                                   op=ALU.mult), reads=[xi], writes=[sqi])
            psq = self.ps[6]
            psk = self.ps[7]
            for c in range(2):
                P.mm(psq[:, 0:n], ones_f[:], sqi[:, c, 0:n], start=(c == 0), stop=(c == 1),
                     reads=[ones_f, sqi], writes=[psq])
            P.mm(psk[:, 0:n], ones_f[:], sqi[:, 2, 0:n], reads=[ones_f, sqi], writes=[psk])
            r = rsb[gi % 2]
            P.op("act", lambda g: g.activation(out=r[:, 0, 0:n], in_=psq[:, 0:n], func=AF.Sqrt, scale=1.0 / 256,
                                               bias=EPS), reads=[psq], writes=[r])
            P.op("act", lambda g: g.activation(out=r[:, 1, 0:n], in_=psk[:, 0:n], func=AF.Sqrt, scale=1.0 / 128,
                                               bias=EPS), reads=[psk], writes=[r])
            P.op("dve", lambda g: g.reciprocal(out=r[:, :, 0:n], in_=r[:, :, 0:n]), reads=[r], writes=[r])
            for c in range(2):
                P.op("dve", lambda g: g.scalar_tensor_tensor(out=cqn[:, c, t0:t0 + n], in0=xi[:, c, 0:n],
                                                             scalar=nT[:, c:c + 1], in1=r[:, 0, 0:n],
                                                             op0=ALU.mult, op1=ALU.mult),
                     reads=[xi, nT, r], writes=[cqn])
            P.op("dve", lambda g: g.scalar_tensor_tensor(out=ckvn[:, t0:t0 + n], in0=xi[:, 2, 0:n],
                                                         scalar=nT[:, 2:3], in1=r[:, 1, 0:n],
                                                         op0=ALU.mult, op1=ALU.mult),
                 reads=[xi, nT, r], writes=[ckvn])
        P.release()
        for t in range(NTILE):
            ps = self.ps[5 + t % 2]
            P.mm(ps[:], ckvn[:, t * 128:(t + 1) * 128], wv[:], reads=[ckvn, wv], writes=[ps])
            eng = "act" if t % 2 == 0 else "dve"
            if eng == "act":
                P.op("act", lambda g: g.activation(out=vaug[:, t, :, 0:64],
                                                   in_=ps[:].rearrange("p (h m) -> p h m", m=64), func=AF.Copy),
                     reads=[ps], writes=[vaug])
            else:
                P.op("dve", lambda g: g.tensor_copy(out=vaug[:, t, :, 0:64],
                                                    in_=ps[:].rearrange("p (h m) -> p h m", m=64)),
                     reads=[ps], writes=[vaug])
        NQ = TT if with_ctx else NLAT
        KT = [P.sbuf([96, TT], BF16, "KT%d" % i) for i in range(2)]
        QT = [P.sbuf([96, TT], BF16, "QT%d" % i) for i in range(2)]
        tab = [P.sbuf([96, 2, 512], F32, "tab%d" % i) for i in range(2)]
        a1 = [P.sbuf([96, 512], F32, "a1_%d" % i) for i in range(2)]
        a2 = [P.sbuf([96, 512], F32, "a2_%d" % i) for i in range(2)]
        PT = [P.sbuf([128, 512], BF16, "PT%d" % i) for i in range(4)]
        rec = [P.sbuf([64, 512], F32, "rec%d" % i) for i in range(2)]
        yo = [P.sbuf([64, 512], BF16, "yo%d" % i) for i in range(2)]
        npt = 0
        nqg = 0
        for h in range(8):
            kt = KT[h % 2]
            qt = QT[h % 2]
            P.dma(kt[64:96, :], d["krT"], reads=[self.db("krT", gi) for gi in range(9)], writes=[kt])
            for gi, (t0, n) in enumerate(groups):
                lat = gi < 8
                if gi >= 8 and not with_ctx:
                    pass
                pk = self.ps[5]
                P.mm(pk[0:64, 0:n], wk[:, h, :], ckvn[:, t0:t0 + n], reads=[wk, ckvn], writes=[pk])
                P.op("act", lambda g: g.activation(out=kt[0:64, t0:t0 + n], in_=pk[0:64, 0:n], func=AF.Copy),
                     reads=[pk], writes=[kt])
                if gi >= 8 and not with_ctx:
                    continue
                p1 = self.ps[6]
                for c in range(2):
                    P.mm(p1[0:96, 0:n], wq[:, c, h, 0, :], cqn[:, c, t0:t0 + n], start=(c == 0), stop=(c == 1),
                         reads=[wq, cqn], writes=[p1])
                if lat:
                    p2 = self.ps[7]
                    for c in range(2):
                        P.mm(p2[0:96, 0:n], wq[:, c, h, 1, :], cqn[:, c, t0:t0 + n], start=(c == 0),
                             stop=(c == 1), reads=[wq, cqn], writes=[p2])
                    tb = tab[gi % 2]
                    P.dma(tb[64:96, :, :], d["rope_m"][:, :, t0:t0 + n].rearrange("c p t -> p c t"), writes=[tb])
                    P.op("act", lambda g: g.activation(out=qt[0:64, t0:t0 + n], in_=p1[0:64, 0:n], func=AF.Copy),
                         reads=[p1], writes=[qt])
                    b1 = a1[gi % 2]
                    b2 = a2[gi % 2]
                    P.op("dve", lambda g: g.tensor_tensor(out=b1[64:96, :], in0=p1[64:96, :], in1=tb[64:96, 0, :],
                                                          op=ALU.mult), reads=[p1, tb], writes=[b1])
                    P.op("dve", lambda g: g.tensor_tensor(out=b2[64:96, :], in0=p2[64:96, :], in1=tb[64:96, 1, :],
                                                          op=ALU.mult), reads=[p2, tb], writes=[b2])
                    P.op("pool", lambda g: g.tensor_tensor(out=qt[64:96, t0:t0 + n], in0=b1[64:96, :],
                                                           in1=b2[64:96, :], op=ALU.add),
                         reads=[b1, b2], writes=[qt])
                else:
                    P.op("act", lambda g: g.activation(out=qt[0:96, t0:t0 + n], in_=p1[0:96, 0:n], func=AF.Copy),
                         reads=[p1], writes=[qt])
            qgroups = [(g * 512, 512, list(range(NTILE))) for g in range(8)]
            if with_ctx:
                qgroups.append((NLAT, 256, [32, 33]))
            for (q0, n, kbs) in qgroups:
                po = self.ps[3 + nqg % 2]
                nqg += 1
                pend = []

                def pv(item, first, last):
                    kb, pt = item
                    P.mm(po[:, 0:n], vaug[:, kb, h, :], pt[:, 0:n], start=first, stop=last,
                         reads=[vaug, pt], writes=[po])

                for idx, kb in enumerate(kbs):
                    pss = self.ps[npt % 3]
                    pt = PT[npt % 4]
                    npt += 1
                    P.mm(pss[:, 0:n], kt[:, kb * 128:(kb + 1) * 128], qt[:, q0:q0 + n], reads=[kt, qt],
                         writes=[pss])
                    P.op("act", lambda g: g.activation(out=pt[:, 0:n], in_=pss[:, 0:n], func=AF.Exp, scale=scale),
                         reads=[pss], writes=[pt])
                    pend.append((kb, pt))
                    if len(pend) > 2:
                        pv(pend.pop(0), idx == 2, False)
                while pend:
                    first = (len(kbs) - len(pend) == 0)
                    pv(pend.pop(0), first, len(pend) == 0)
                rc = rec[nqg % 2]
                y = yo[nqg % 2]
                P.op("dve", lambda g: g.reciprocal(out=rc[:, 0:n], in_=po[64:128, 0:n]), reads=[po], writes=[rc])
                P.op("dve", lambda g: g.tensor_tensor(out=y[:, 0:n], in0=po[0:64, 0:n], in1=rc[:, 0:n],
                                                      op=ALU.mult), reads=[po, rc], writes=[y])
                P.dma(d["yT"][0, h * 64:(h + 1) * 64, q0:q0 + n], y[:, 0:n], reads=[y],
                      writes=[self.db("yT", (0, h, q0))])
        P.release()

    def phase_swa(self, l, with_ctx):
        P = self.P
        d = self.d
        P.mark()
        scale = 64.0 ** -0.5
        es = P.sbuf([128, 8], F32, "es")
        P.dma(es[:], d["swa_sink"][l:l + 1, :].to_broadcast([128, 8]), writes=[es])
        P.op("act", lambda g: g.activation(out=es[:], in_=es[:], func=AF.Exp), reads=[es], writes=[es])
        msk = P.sbuf([128, 2, 128], BF16, "msk")
        P.dma(msk[:], d["swa_mask"].rearrange("c p t -> p c t"), writes=[msk], q="pool")
        Kk = P.sbuf([64, TT], BF16, "Kk")
        Qk = P.sbuf([64, 4, TT], BF16, "Qk")
        va = P.sbuf([128, NTILE, 128], BF16, "va")
        P.op("pool", lambda g: g.memset(va[:, :, 64:128], 1.0), writes=[va])
        yd = P.sbuf([64, 4, TT], BF16, "yd")
        PT = [P.sbuf([128, 4, 128], BF16, "PT%d" % i) for i in range(6)]
        den = [P.sbuf([64, 4, 128], F32, "den%d" % i) for i in range(2)]
        npt = 0
        nblk = 0
        allsq = [self.db("sqT", (c, gi)) for c in range(4) for gi in range(9)]
        allsk = [self.db("skT", gi) for gi in range(9)]
        allpv = [self.db("pv", t) for t in range(NTILE)]
        for kh in range(2):
            P.dma(Kk[:], d["skT"][kh * 64:(kh + 1) * 64, :], reads=allsk, writes=[Kk])
            for g_ in range(4):
                hh = kh * 4 + g_
                P.dma(Qk[:, g_, :], d["sqT"][hh // 2, (hh % 2) * 64:(hh % 2) * 64 + 64, :], reads=allsq, writes=[Qk])
            P.dma(va[:, :, 0:64], d["pv"].rearrange("(t p) c -> p t c", p=128)[:, :, kh * 64:(kh + 1) * 64],
                  reads=allpv, writes=[va], q="pool")
            nq = NTILE if with_ctx else 32
            for i in range(nq):
                if i < 32:
                    kbs = []
                    if i > 0:
                        kbs.append((i - 1, 0))
                    kbs.append((i, None))
                    if i < 31:
                        kbs.append((i + 1, 1))
                    kbs += [(32, None), (33, None)]
                else:
                    kbs = [(32, None), (33, None)]
                po = self.ps[3 + nblk % 2]
                dn = den[nblk % 2]
                nblk += 1
                sbanks = [0, 1, 2, 5, 6, 7]
                items = []
                for idx, (kb, mk) in enumerate(kbs):
                    pss = self.ps[sbanks[npt % 6]]
                    pt = PT[npt % 6]
                    npt += 1
                    for g_ in range(4):
                        P.mm(pss[:, g_ * 128:(g_ + 1) * 128], Kk[:, kb * 128:(kb + 1) * 128],
                             Qk[:, g_, i * 128:(i + 1) * 128], reads=[Kk, Qk], writes=[pss])
                    items.append((kb, mk, pss, pt))
                for idx, (kb, mk, pss, pt) in enumerate(items):
                    P.op("act", lambda g: g.activation(out=pt[:].rearrange("p g t -> p (g t)"), in_=pss[:],
                                                       func=AF.Exp, scale=scale), reads=[pss], writes=[pt])
                    if mk is not None:
                        P.op("dve", lambda g: g.tensor_tensor(out=pt[:], in0=pt[:],
                                                              in1=msk[:, mk, :].unsqueeze(1).to_broadcast([128, 4, 128]),
                                                              op=ALU.mult), reads=[pt, msk], writes=[pt])
                for idx, (kb, mk, pss, pt) in enumerate(items):
                    P.mm(po[:], va[:, kb, :], pt[:].rearrange("p g t -> p (g t)"), start=(idx == 0),
                         stop=(idx == len(kbs) - 1), reads=[va, pt], writes=[po])
                P.op("dve", lambda g: g.tensor_tensor(out=dn[:], in0=po[64:128, :].rearrange("p (g t) -> p g t", g=4),
                                                      in1=es[64:128, kh * 4:(kh + 1) * 4].unsqueeze(2).to_broadcast([64, 4, 128]),
                                                      op=ALU.add), reads=[po, es], writes=[dn])
                P.op("dve", lambda g: g.reciprocal(out=dn[:], in_=dn[:]), reads=[dn], writes=[dn])
                P.op("dve", lambda g: g.tensor_tensor(out=yd[:, :, i * 128:(i + 1) * 128],
                                                      in0=po[0:64, :].rearrange("p (g t) -> p g t", g=4), in1=dn[:],
                                                      op=ALU.mult), reads=[po, dn], writes=[yd])
            for g_ in range(4):
                hh = kh * 4 + g_
                P.dma(d["yT"][3, hh * 64:(hh + 1) * 64, 0:nq * 128], yd[:, g_, 0:nq * 128], reads=[yd],
                      writes=[self.db("yT", (3, hh))])
        P.release()

    KB.declare_attn = declare_attn
    KB.phase_mla = phase_mla
    KB.phase_swa = phase_swa


_attn_methods()


def _merge_methods():
    def declare_merge(self):
        L = 2
        self.inp("w_branch", [L, 4, 512, DM])
        self.inp("w_out", [L, DM, DM])
        self.inp("b_gateT", [L, 128, 32])

    def phase_merge(self, l, with_ctx, xname="xs"):
        P = self.P
        d = self.d
        P.mark()
        wg = P.sbuf([128, 8, 4096], BF16, "wg")
        wgb = [Buf("wg%d" % k) for k in range(8)]
        for k in range(8):
            P.dma(wg[:, k, :], d["w_ext"][l, k * 128:(k + 1) * 128, WX_G0:WX_G0 + 4096], writes=[wgb[k]], q="pool")
        wbr = P.sbuf([128, 4, 4, DM], BF16, "wbr")
        for br in range(4):
            P.dma(wbr[:, br], d["w_branch"][l, br].rearrange("(kc p) n -> p kc n", p=128), writes=[wbr], q="pool")
        wo = P.sbuf([128, 8, DM], BF16, "wo")
        P.dma(wo[:], d["w_out"][l].rearrange("(k p) n -> p k n", p=128), writes=[wo], q="pool")
        bg = P.sbuf([128, 4, 8], F32, "bg")
        P.dma(bg[:].rearrange("p b o -> p (b o)"), d["b_gateT"][l], writes=[bg])
        gt = [P.sbuf([128, DM], F32, "gt%d" % s) for s in range(2)]
        for s in range(2):
            P.dma(gt[s][:], d["gtrow"][l, 1, s:s + 1, :].to_broadcast([128, DM]),
                  reads=[self.db("gtrow", (l, 1))], writes=[gt[s]])
        groups = [(g * 512, 512) for g in range(8)] + ([(NLAT, 256)] if with_ctx else [])
        uT = [P.sbuf([128, 8, 512], BF16, "uT%d" % i) for i in range(2)]
        yg = [P.sbuf([128, 4, 4, 512], BF16, "yg%d" % i) for i in range(2)]
        mg = P.sbuf([128, 8, 512], BF16, "mg")
        mgb = [Buf("mg%d" % k) for k in range(8)]
        sg = [P.sbuf([128, 512], F32, "sg%d" % i) for i in range(2)]
        tm_ = [P.sbuf([128, 512], F32, "tm%d" % i) for i in range(2)]
        acc = [P.sbuf([128, 512], F32, "acc%d" % i) for i in range(2)]
        xbuf = [P.sbuf([128, DM], F32, "xb%d" % i) for i in range(2)]
        junk = P.sbuf([128, DM], BF16, "junk")
        s2 = [P.sbuf([128, 3], F32, "s2%d" % i) for i in range(2)]
        r2 = [P.sbuf([128, 1], F32, "r2%d" % i) for i in range(2)]
        tmp1 = P.sbuf([128, DM], F32, "tmp")
        tmp = [tmp1, tmp1]
        ally = [b for k, b in self.dbufs.items() if k[0] == "yT"]
        allu = [b for k, b in self.dbufs.items() if k[0] == "uT"]

        def load(gi):
            t0, n = groups[gi]
            P.dma(uT[gi % 2][:, :, 0:n], d["uT"][:, :, t0:t0 + n].rearrange("k p t -> p k t"), reads=allu,
                  writes=[uT[gi % 2]])
            for br in range(4):
                P.dma(yg[gi % 2][:, br, :, 0:n],
                      d["yT"][br].rearrange("(kc p) t -> p kc t", p=128)[:, :, t0:t0 + n], reads=ally,
                      writes=[yg[gi % 2]])

        load(0)
        nx = 0
        for gi, (t0, n) in enumerate(groups):
            if gi + 1 < len(groups):
                load(gi + 1)
            u = uT[gi % 2]
            y = yg[gi % 2]
            s = 0 if gi < 8 else 1
            for oc in range(8):
                ac = acc[oc % 2]
                for br in range(4):
                    psg = self.ps[1 + (br % 2) * 2]
                    psy = self.ps[2 + (br % 2) * 2]
                    c0 = br * 1024 + oc * 128
                    for k in range(8):
                        P.mm(psg[:, 0:n], wg[:, k, c0:c0 + 128], u[:, k, 0:n], start=(k == 0), stop=(k == 7),
                             reads=[wgb[k], u], writes=[psg])
                    for kc in range(4):
                        P.mm(psy[:, 0:n], wbr[:, br, kc, oc * 128:(oc + 1) * 128], y[:, br, kc, 0:n],
                             start=(kc == 0), stop=(kc == 3), reads=[wbr, y], writes=[psy])
                    sgt = sg[br % 2]
                    P.op("act", lambda g: g.activation(out=sgt[:, 0:n], in_=psg[:, 0:n], func=AF.Sigmoid,
                                                       bias=bg[:, br, oc:oc + 1]), reads=[psg, bg], writes=[sgt])
                    if br == 0:
                        P.op("dve", lambda g: g.tensor_tensor(out=ac[:, 0:n], in0=sgt[:, 0:n], in1=psy[:, 0:n],
                                                              op=ALU.mult), reads=[sgt, psy], writes=[ac])
                    else:
                        tt = tm_[br % 2]
                        P.op("dve", lambda g: g.tensor_tensor(out=tt[:, 0:n], in0=sgt[:, 0:n], in1=psy[:, 0:n],
                                                              op=ALU.mult), reads=[sgt, psy], writes=[tt])
                        if br < 3:
                            P.op("pool", lambda g: g.tensor_tensor(out=ac[:, 0:n], in0=ac[:, 0:n], in1=tt[:, 0:n],
                                                                   op=ALU.add), reads=[ac, tt], writes=[ac])
                        else:
                            P.op("pool", lambda g: g.tensor_tensor(out=mg[:, oc, 0:n], in0=ac[:, 0:n],
                                                                   in1=tt[:, 0:n], op=ALU.add),
                                 reads=[ac, tt], writes=[mgb[oc]])
            for i in range(n // 128):
                t = t0 // 128 + i
                xb = xbuf[nx % 2]
                P.dma(xb[:], d[xname][t * 128:(t + 1) * 128, :], reads=[self.db(xname, t)], writes=[xb])
                for h in range(2):
                    po = self.ps[5 + h]
                    for k in range(8):
                        P.mm(po[:], mg[:, k, i * 128:(i + 1) * 128], wo[:, k, h * 512:(h + 1) * 512],
                             start=(k == 0), stop=(k == 7), reads=[mgb[k], wo], writes=[po])
                self.norm_res_out([5, 6], xb, xb[:], gt[s], (junk, s2[nx % 2], r2[nx % 2], tmp[nx % 2]),
                                  d[xname][t * 128:(t + 1) * 128, :], self.db(xname, t))
                nx += 1
        P.release()

    KB.declare_merge = declare_merge
    KB.phase_merge = phase_merge


_merge_methods()


def hy_tables(n):
    M = 2 * n
    S1 = M // 64
    T1 = n // 64
    s1 = np.arange(S1)[:, None, None]
    s2 = np.arange(64)[None, :, None]
    f1 = np.arange(S1)[None, None, :]
    ang = 2.0 * np.pi * ((f1 * (64 * s1 + s2)) % M) / M
    F1n = S1 // 2 + 1
    W1 = np.stack([np.cos(ang), -np.sin(ang)], axis=2).astype(np.float32)[..., :F1n]
    s2v = np.arange(64)[:, None]
    f2v = np.arange(64)[None, :]
    th = 2.0 * np.pi * ((s2v * f2v) % 64) / 64
    c, s = np.cos(th), np.sin(th)
    D2 = np.block([[c, -s], [s, c]]).astype(np.float32)
    D2sw = np.concatenate([D2[:, 64:], D2[:, :64]], axis=1)
    E = np.block([[c, s], [-s, c]]).astype(np.float32)
    f1v = np.arange(S1)[:, None, None]
    t2v = np.arange(64)[None, :, None]
    t1v = np.arange(T1)[None, None, :]
    psi = 2.0 * np.pi * ((f1v * (64 * t1v + t2v)) % M) / M
    W3 = np.stack([np.cos(psi), -np.sin(psi)], axis=2).astype(np.float32)
    cw = np.full(F1n, 2.0, np.float32)
    cw[0] = 1.0
    cw[-1] = 1.0
    W3 = np.ascontiguousarray(W3[:F1n] * cw[:, None, None, None])
    return dict(W1=W1, D2=D2, D2sw=D2sw, E=E, W3=W3, S1=S1, T1=T1, M=M, F1n=F1n)


def hy_feats(n):
    M = 2 * n
    f32 = np.float32
    t = np.linspace(0.0, 1.0, n, dtype=f32)
    bands = np.linspace(1e-4, 15.0, 16, dtype=f32)
    ang = (f32(2.0 * np.pi / n) * np.arange(n, dtype=f32)[:, None] * bands[None, :]).astype(f32)
    feats = np.concatenate([t[:, None], np.cos(ang), -np.sin(ang)], axis=-1).astype(f32)
    deltas = np.abs(np.linspace(np.log(1e-2) / 1.5, np.log(1e-2) / 0.3, 512, dtype=f32)).astype(f32)
    win = np.exp(-t[:, None] * deltas[None, :]).astype(f32)
    idx = np.zeros(M, np.int64)
    idx[:n] = np.arange(n)
    idx[n + 1:] = n - np.arange(1, n)
    fK = feats[idx].copy()
    wK = win[idx].copy()
    fK[n] = feats[0]
    wK[n] = win[0]
    return np.ascontiguousarray(fK.T), np.ascontiguousarray(wK)


def _hyena_methods():
    TWO_PI = 2.0 * np.pi

    def declare_hyena(self):
        L = 2
        self.inp("hyena_conv", [L, 3, 1536])
        self.inp("hyena_conv_b", [L, 1536])
        self.inp("hyena_w1", [L, 33, 64])
        self.inp("hyena_w2", [L, 64, 64])
        self.inp("hyena_w3", [L, 64, 2048])
        self.inp("hyT", [L, 64, 4])
        self.inp("hyena_bias", [L, 2, 512])
        self.inp("hy_D", [3, 128, 128])
        self.inp("hyL_W1", [128, 64 * 2 * 65])
        self.inp("hyL_W3", [65, 64 * 2 * 64])
        self.inp("hyL_fK", [33, 8192])
        self.inp("hyL_wK", [8192, 512])
        self.inp("hyC_W1", [8, 64 * 2 * 5])
        self.inp("hyC_W3", [5, 64 * 2 * 4])
        self.inp("hyC_fK", [33, 512])
        self.inp("hyC_wK", [512, 512])
        self.scr("hcs", [TT, 1536])
        self.scr("kbuf", [2, 8192, 512], BF16)
        self.scr("Bd", [128, 128, 512], BF16)
        self.scr("Dd", [128, 128, 512], BF16)
        self.scr("KAB_L", [2, 128, 2, 128, 512], BF16)
        self.scr("KAB_C", [2, 8, 2, 128, 512], BF16)
        self.scr("zt1", [TT, 512])
        self.scr("zt2", [TT, 512])

    def hy_shortconv(self, l, ntiles):
        P = self.P
        d = self.d
        P.mark()
        ck = P.sbuf([128, 3, 1536], F32, "ck")
        P.dma(ck[:].rearrange("p a c -> p (a c)"),
              d["hyena_conv"][l:l + 1].rearrange("o a c -> o (a c)").to_broadcast([128, 4608]), writes=[ck])
        cb = P.sbuf([128, 1536], F32, "cb")
        P.dma(cb[:], d["hyena_conv_b"][l:l + 1, :].to_broadcast([128, 1536]), writes=[cb])
        bufs = [[P.sbuf([128, 1536], F32, "sc%d_%d" % (i, j)) for j in range(3)] for i in range(3)]
        allph = [b for k, b in self.dbufs.items() if k[0] == "ph"]
        for t in range(ntiles):
            r0 = tm_row(t * 128)
            pv_, cu, nx = bufs[t % 3]
            P.dma(pv_[:], d["ph"][r0 - 1:r0 + 127, :], reads=allph, writes=[pv_])
            P.dma(cu[:], d["ph"][r0:r0 + 128, :], reads=allph, writes=[cu])
            P.dma(nx[:], d["ph"][r0 + 1:r0 + 129, :], reads=allph, writes=[nx])
            P.op("dve", lambda g: g.tensor_tensor(out=pv_[:], in0=pv_[:], in1=ck[:, 0, :], op=ALU.mult),
                 reads=[pv_, ck], writes=[pv_])
            P.op("pool", lambda g: g.tensor_tensor(out=cu[:], in0=cu[:], in1=ck[:, 1, :], op=ALU.mult),
                 reads=[cu, ck], writes=[cu])
            P.op("dve", lambda g: g.tensor_tensor(out=nx[:], in0=nx[:], in1=ck[:, 2, :], op=ALU.mult),
                 reads=[nx, ck], writes=[nx])
            P.op("pool", lambda g: g.tensor_tensor(out=cu[:], in0=cu[:], in1=pv_[:], op=ALU.add),
                 reads=[cu, pv_], writes=[cu])
            P.op("dve", lambda g: g.tensor_tensor(out=nx[:], in0=nx[:], in1=cb[:], op=ALU.add),
                 reads=[nx, cb], writes=[nx])
            P.op("pool", lambda g: g.tensor_tensor(out=cu[:], in0=cu[:], in1=nx[:], op=ALU.add),
                 reads=[cu, nx], writes=[cu])
            P.dma(d["hcs"][t * 128:(t + 1) * 128, :], cu[:], reads=[cu], writes=[self.db("hcs", t)])
        P.release()

    def hy_load_tabs(self, n):
        P = self.P
        d = self.d
        pre = "hyL" if n == NLAT else "hyC"
        S1 = 2 * n // 64
        T1 = n // 64
        F1n = S1 // 2 + 1
        W1 = P.sbuf([S1, 64, 2, F1n], BF16, "W1")
        P.dma(W1[:].rearrange("p a b c -> p (a b c)"), d[pre + "_W1"], writes=[W1], q="pool")
        W3 = P.sbuf([F1n, 64, 2, T1], BF16, "W3")
        P.dma(W3[:].rearrange("p a b c -> p (a b c)"), d[pre + "_W3"], writes=[W3], q="pool")
        Dm = P.sbuf([128, 3, 128], BF16, "Dm")
        P.dma(Dm[:], d["hy_D"].rearrange("a p c -> p a c"), writes=[Dm], q="pool")
        return dict(W1=W1, W3=W3, Dm=Dm, S1=S1, T1=T1, n=n, F1n=F1n)

    def hy_stage1(self, tb, src_ap, nz, src_reads, cast):
        P = self.P
        d = self.d
        S1 = tb["F1n"]
        P.mark()
        U = P.sbuf([nz, 64, 512], BF16, "U")
        P.dma(U[:], src_ap.rearrange("(a s) c -> a s c", s=64), reads=src_reads, writes=[U],
              q=("pool" if cast else "sp"))
        bo = [P.sbuf([S1, 2, 512], BF16, "bo%d" % i) for i in range(4)]
        bdv = d["Bd"].rearrange("(r s) f c -> s f r c", r=2)
        for s2 in range(64):
            o = bo[s2 % 4]
            for ri in range(2):
                ps = self.ps[(2 * s2 + ri) % 4]
                P.mm(ps[0:S1, :], tb["W1"][0:nz, s2, ri, :], U[:, s2, :], reads=[tb["W1"], U], writes=[ps])
                if ri == 0:
                    P.op("act", lambda g: g.activation(out=o[:, ri, :], in_=ps[0:S1, :], func=AF.Copy),
                         reads=[ps], writes=[o])
                else:
                    P.op("dve", lambda g: g.tensor_copy(out=o[:, ri, :], in_=ps[0:S1, :]), reads=[ps], writes=[o])
            P.dma(bdv[s2, 0:S1], o[:], reads=[o], writes=[self.db("Bd", s2)])
        P.release()

    def hy_stage2(self, tb, cb, cb2=None):
        P = self.P
        d = self.d
        S1 = tb["F1n"]
        allbd = [self.db("Bd", s2) for s2 in range(64)]
        FG = 5
        bins = [P.sbuf([128, FG, 512], BF16, "bin%d" % i) for i in range(3)]
        prev = [None]
        for fg in range(S1 // FG):
            b = bins[fg % 3]
            P.dma(b[:], d["Bd"][:, fg * FG:(fg + 1) * FG, :], reads=allbd, writes=[b])
            for j in range(FG):
                f1 = fg * FG + j
                p1 = self.ps[(f1 % 3) * 2]
                p2 = self.ps[(f1 % 3) * 2 + 1]
                P.mm(p1[:], tb["Dm"][:, 0, :], b[:, j, :], reads=[tb["Dm"], b], writes=[p1])
                P.mm(p2[:], tb["Dm"][:, 1, :], b[:, j, :], reads=[tb["Dm"], b], writes=[p2])
                cb(f1, p1, p2)
                if cb2 is not None and prev[0] is not None:
                    cb2(prev[0])
                prev[0] = f1
        if cb2 is not None and prev[0] is not None:
            cb2(prev[0])

    def hy_filters(self, l, n, kab_name):
        P = self.P
        d = self.d
        pre = "hyL" if n == NLAT else "hyC"
        M = 2 * n
        P.mark()
        tb = self.hy_load_tabs(n)
        hyT = P.sbuf([64, 4], F32, "hyT")
        P.dma(hyT[:], d["hyT"][l], writes=[hyT])
        sc = P.sbuf([64, 4], F32, "hsc")
        for j in range(2):
            P.op("dve", lambda g: g.tensor_scalar(out=sc[:, 2 * j:2 * j + 1], in0=hyT[:, j:j + 1],
                                                  scalar1=1.0 / TWO_PI, scalar2=None, op0=ALU.mult),
                 reads=[hyT], writes=[sc])
            P.op("dve", lambda g: g.tensor_tensor(out=sc[:, 2 * j + 1:2 * j + 2], in0=hyT[:, 2 + j:3 + j],
                                                  in1=sc[:, 2 * j:2 * j + 1], op=ALU.mult),
                 reads=[hyT, sc], writes=[sc])
            P.op("dve", lambda g: g.tensor_scalar(out=sc[:, 2 * j + 1:2 * j + 2], in0=sc[:, 2 * j + 1:2 * j + 2],
                                                  scalar1=64.0, scalar2=None, op0=ALU.add),
                 reads=[sc], writes=[sc])
        w1 = P.sbuf([33, 64], F32, "hw1")
        P.dma(w1[:], d["hyena_w1"][l], writes=[w1])
        w2 = P.sbuf([64, 64], F32, "hw2")
        P.dma(w2[:], d["hyena_w2"][l], writes=[w2])
        w3 = P.sbuf([64, 2048], F32, "hw3")
        P.dma(w3[:], d["hyena_w3"][l], writes=[w3])
        ones_f = P.sbuf([128, 128], F32, "ones_f")
        P.op("pool", lambda g: g.memset(ones_f[:], 1.0), writes=[ones_f])
        G2T = P.sbuf([64, M], F32, "G2T")
        rn = [P.sbuf([128, 512], F32, "rn%d" % o) for o in range(2)]
        P.mark()
        fk = [P.sbuf([33, 512], F32, "fk%d" % i) for i in range(2)]
        vt = [P.sbuf([64, 512], F32, "vt%d" % i) for i in range(2)]
        vi = [P.sbuf([64, 512], I32, "vi%d" % i) for i in range(2)]
        vf = [P.sbuf([64, 512], F32, "vf%d" % i) for i in range(2)]
        g1 = [P.sbuf([64, 512], F32, "g1%d" % i) for i in range(2)]
        cnt = [0]

        def sin_reduce(ps, j, out_ap, out_t):
            i = cnt[0] % 2
            cnt[0] += 1
            P.op("dve", lambda g: g.tensor_scalar(out=vt[i][:], in0=ps[0:64, :], scalar1=sc[:, 2 * j:2 * j + 1],
                                                  scalar2=sc[:, 2 * j + 1:2 * j + 2], op0=ALU.mult, op1=ALU.add),
                 reads=[ps, sc], writes=[vt[i]])
            P.op("dve", lambda g: g.tensor_copy(out=vi[i][:], in_=vt[i][:]), reads=[vt[i]], writes=[vi[i]])
            P.op("pool", lambda g: g.tensor_copy(out=vf[i][:], in_=vi[i][:]), reads=[vi[i]], writes=[vf[i]])
            P.op("pool", lambda g: g.tensor_tensor(out=vt[i][:], in0=vt[i][:], in1=vf[i][:], op=ALU.subtract),
                 reads=[vt[i], vf[i]], writes=[vt[i]])
            P.op("act", lambda g: g.activation(out=out_ap, in_=vt[i][:], func=AF.Sin, scale=TWO_PI),
                 reads=[vt[i]], writes=[out_t])

        for cbk in range(M // 512):
            f = fk[cbk % 2]
            P.dma(f[:], d[pre + "_fK"][:, cbk * 512:(cbk + 1) * 512], writes=[f])
            ps = self.ps[cbk % 2]
            P.mm(ps[0:64, :], w1[:], f[:], reads=[w1, f], writes=[ps])
            gg = g1[cbk % 2]
            sin_reduce(ps, 0, gg[:], gg)
            ps2 = self.ps[2 + cbk % 2]
            P.mm(ps2[0:64, :], w2[:], gg[:], reads=[w2, gg], writes=[ps2])
            sin_reduce(ps2, 1, G2T[:, cbk * 512:(cbk + 1) * 512], G2T)
        P.release()
        P.mark()
        wk = [P.sbuf([128, 512], F32, "wk%d" % i) for i in range(3)]
        kbt = [P.sbuf([128, 512], F32, "kbt%d" % i) for i in range(4)]
        ab = [P.sbuf([128, 512], F32, "ab%d" % i) for i in range(4)]
        kbo = [P.sbuf([128, 512], BF16, "kbo%d" % i) for i in range(4)]
        nlt = M // 128
        c = 0
        for lt in range(nlt):
            dirn = 0 if lt < n // 128 else 1
            w = wk[lt % 3]
            P.dma(w[:], d[pre + "_wK"][lt * 128:(lt + 1) * 128, :], writes=[w])
            for o in range(2):
                ps = self.ps[c % 4]
                kt_ = kbt[c % 4]
                a = ab[c % 4]
                ko = kbo[c % 4]
                c += 1
                P.mm(ps[:], G2T[:, lt * 128:(lt + 1) * 128], w3[:, o * 1024 + dirn * 512:o * 1024 + dirn * 512 + 512],
                     reads=[G2T, w3], writes=[ps])
                P.op("dve", lambda g: g.tensor_tensor(out=kt_[:], in0=ps[:], in1=w[:], op=ALU.mult),
                     reads=[ps, w], writes=[kt_])
                P.op("act", lambda g: g.activation(out=a[:], in_=kt_[:], func=AF.Abs), reads=[kt_], writes=[a])
                P.mm(self.ps[6 + o][:], ones_f[:], a[:], start=(lt == 0), stop=(lt == nlt - 1),
                     reads=[ones_f, a], writes=[self.ps[6 + o]])
                if lt == n // 128:
                    P.op("pool", lambda g: g.memset(kt_[0:1, :], 0.0), reads=[kt_], writes=[kt_])
                P.op("pool", lambda g: g.tensor_copy(out=ko[:], in_=kt_[:]), reads=[kt_], writes=[ko])
                P.dma(d["kbuf"][o, lt * 128:(lt + 1) * 128, :], ko[:], reads=[ko], writes=[self.db("kbuf", (o, lt))])
        for o in range(2):
            P.op("dve", lambda g: g.tensor_scalar(out=rn[o][:], in0=self.ps[6 + o][:], scalar1=float(M), scalar2=None,
                                                  op0=ALU.mult), reads=[self.ps[6 + o]], writes=[rn[o]])
            P.op("dve", lambda g: g.reciprocal(out=rn[o][:], in_=rn[o][:]), reads=[rn[o]], writes=[rn[o]])
        P.release()
        for o in range(2):
            allkb = [self.db("kbuf", (o, lt)) for lt in range(nlt)]
            self.hy_stage1(tb, d["kbuf"][o, 0:M, :], tb["S1"], allkb, False)
            P.mark()
            kab = [P.sbuf([128, 2, 512], BF16, "kab%d" % i) for i in range(4)]

            def cbf(f1, p1, p2):
                k = kab[f1 % 4]
                r = rn[o]
                P.op("dve", lambda g: g.tensor_tensor(out=k[0:64, 0, :], in0=p1[0:64, :], in1=r[0:64, :], op=ALU.mult),
                     reads=[p1, r], writes=[k])
                P.op("dve", lambda g: g.tensor_tensor(out=k[64:128, 0, :], in0=p2[64:128, :], in1=r[64:128, :],
                                                      op=ALU.mult), reads=[p2, r], writes=[k])
                P.op("dve", lambda g: g.scalar_tensor_tensor(out=k[0:64, 1, :], in0=p2[0:64, :], scalar=-1.0,
                                                             in1=r[0:64, :], op0=ALU.mult, op1=ALU.mult),
                     reads=[p2, r], writes=[k])
                P.op("dve", lambda g: g.tensor_tensor(out=k[64:128, 1, :], in0=p1[64:128, :], in1=r[64:128, :],
                                                      op=ALU.mult), reads=[p1, r], writes=[k])
                P.dma(d[kab_name][o, f1].rearrange("a p c -> p a c"), k[:], reads=[k],
                      writes=[self.db(kab_name, (o, f1))])

            self.hy_stage2(tb, cbf)
            P.release()
        P.release()

    def hy_conv(self, l, tb, o, kab_name, src_ap, src_reads, gate_ap, gate_reads, dst_ap, dst_name):
        P = self.P
        d = self.d
        n, S1, T1 = tb["n"], tb["F1n"], tb["T1"]
        self.hy_stage1(tb, src_ap, tb["S1"] // 2, src_reads, True)
        P.mark()
        kab = [P.sbuf([128, 2, 512], BF16, "kab%d" % i) for i in range(4)]
        ta = [P.sbuf([128, 512], F32, "ta%d" % i) for i in range(4)]
        tb2 = [P.sbuf([128, 512], F32, "tb%d" % i) for i in range(4)]
        yh = [P.sbuf([128, 512], BF16, "yh%d" % i) for i in range(4)]
        do = [P.sbuf([128, 512], BF16, "do%d" % i) for i in range(4)]
        allk = [self.db(kab_name, (o, f1)) for f1 in range(S1)]

        def cbf(f1, p1, p2):
            i = f1 % 4
            k = kab[i]
            P.dma(k[:], d[kab_name][o, f1].rearrange("a p c -> p a c"), reads=allk, writes=[k])
            P.op("dve", lambda g: g.tensor_tensor(out=ta[i][:], in0=p1[:], in1=k[:, 0, :], op=ALU.mult),
                 reads=[p1, k], writes=[ta[i]])
            P.op("dve", lambda g: g.tensor_tensor(out=tb2[i][:], in0=p2[:], in1=k[:, 1, :], op=ALU.mult),
                 reads=[p2, k], writes=[tb2[i]])
            P.op("pool", lambda g: g.tensor_tensor(out=yh[i][:], in0=ta[i][:], in1=tb2[i][:], op=ALU.add),
                 reads=[ta[i], tb2[i]], writes=[yh[i]])

        def cbf2(f1):
            i = f1 % 4
            pd_ = self.ps[6 + f1 % 2]
            P.mm(pd_[:], tb["Dm"][:, 2, :], yh[i][:], reads=[tb["Dm"], yh[i]], writes=[pd_])
            P.op("act", lambda g: g.activation(out=do[i][:], in_=pd_[:], func=AF.Copy), reads=[pd_], writes=[do[i]])
            P.dma(d["Dd"][:, f1, :], do[i][:], reads=[do[i]], writes=[self.db("Dd", f1)])

        self.hy_stage2(tb, cbf, cbf2)
        P.release()
        P.mark()
        bias = P.sbuf([64, 512], F32, "hbias")
        P.dma(bias[:], d["hyena_bias"][l, o:o + 1, :].to_broadcast([64, 512]), writes=[bias])
        din = [P.sbuf([S1, 2, 512], BF16, "din%d" % i) for i in range(4)]
        gs = [P.sbuf([T1, 512], F32, "gs%d" % i) for i in range(4)]
        us = [P.sbuf([T1, 512], F32, "us%d" % i) for i in range(4)]
        zo = [P.sbuf([T1, 512], F32, "zo%d" % i) for i in range(4)]
        alld = [self.db("Dd", f1) for f1 in range(S1)]
        ddv = d["Dd"].rearrange("(r t) f c -> t f r c", r=2)
        gv = gate_ap.rearrange("(a s) c -> s a c", s=64)
        uv = src_ap.rearrange("(a s) c -> s a c", s=64)
        dv = dst_ap.rearrange("(a s) c -> s a c", s=64)
        for t2 in range(64):
            i = t2 % 4
            P.dma(din[i][:], ddv[t2, 0:S1], reads=alld, writes=[din[i]])
            P.dma(gs[i][:], gv[t2], reads=gate_reads, writes=[gs[i]])
            P.dma(us[i][:], uv[t2], reads=src_reads, writes=[us[i]])
            py = self.ps[6 + i % 2]
            P.mm(py[0:T1, :], tb["W3"][:, t2, 0, :], din[i][:, 0, :], start=True, stop=False,
                 reads=[tb["W3"], din[i]], writes=[py])
            P.mm(py[0:T1, :], tb["W3"][:, t2, 1, :], din[i][:, 1, :], start=False, stop=True,
                 reads=[tb["W3"], din[i]], writes=[py])
            P.op("pool", lambda g: g.tensor_tensor(out=us[i][:], in0=us[i][:], in1=bias[0:T1, :], op=ALU.mult),
                 reads=[us[i], bias], writes=[us[i]])
            P.op("dve", lambda g: g.tensor_tensor(out=us[i][:], in0=us[i][:], in1=py[0:T1, :], op=ALU.add),
                 reads=[us[i], py], writes=[us[i]])
            P.op("pool", lambda g: g.tensor_tensor(out=zo[i][:], in0=us[i][:], in1=gs[i][:], op=ALU.mult),
                 reads=[us[i], gs[i]], writes=[zo[i]])
            P.dma(dv[t2], zo[i][:], reads=[zo[i]], writes=[self.db(dst_name, ("t2", t2, n))])
        P.release()

    def phase_hyena(self, l, with_ctx):
        P = self.P
        d = self.d
        self.hy_shortconv(l, NTILE if with_ctx else 32)
        self.hy_filters(l, NLAT, "KAB_L")
        if with_ctx:
            self.hy_filters(l, NCTX, "KAB_C")
        segs = [(NLAT, 0, "KAB_L")] + ([(NCTX, NLAT, "KAB_C")] if with_ctx else [])
        for (n, r0, kn) in segs:
            P.mark()
            tb = self.hy_load_tabs(n)
            hcs = [b for k, b in self.dbufs.items() if k[0] == "hcs"]
            self.hy_conv(l, tb, 0, kn, d["hcs"][r0:r0 + n, 0:512], hcs, d["hcs"][r0:r0 + n, 512:1024], hcs,
                         d["zt1"][r0:r0 + n, :], "zt1")
            z1 = [b for k, b in self.dbufs.items() if k[0] == "zt1"]
            self.hy_conv(l, tb, 1, kn, d["zt1"][r0:r0 + n, :], z1, d["hcs"][r0:r0 + n, 1024:1536], hcs,
                         d["zt2"][r0:r0 + n, :], "zt2")
            P.release()
        P.mark()
        zin = [P.sbuf([128, 512], F32, "zin%d" % i) for i in range(3)]
        zT = [P.sbuf([128, 4, 128], BF16, "zT%d" % i) for i in range(3)]
        z2 = [b for k, b in self.dbufs.items() if k[0] == "zt2"]
        yv = d["yT"][2].rearrange("(kc p) t -> p kc t", p=128)
        for t in range(NTILE if with_ctx else 32):
            zi = zin[t % 3]
            P.dma(zi[:], d["zt2"][t * 128:(t + 1) * 128, :], reads=z2, writes=[zi])
            ps = self.ps[t % 2]
            for kc in range(4):
                P.op("pe", lambda g: g.transpose(ps[:, kc * 128:(kc + 1) * 128], zi[:, kc * 128:(kc + 1) * 128],
                                                 self.ident_f[:]), reads=[zi, self.ident_f], writes=[ps])
            P.op("act", lambda g: g.activation(out=zT[t % 3][:].rearrange("p a b -> p (a b)"), in_=ps[:], func=AF.Copy),
                 reads=[ps], writes=[zT[t % 3]])
            P.dma(yv[:, :, t * 128:(t + 1) * 128], zT[t % 3][:], reads=[zT[t % 3]], writes=[self.db("yT", (2, t))])
        P.release()

    KB.declare_hyena = declare_hyena
    KB.hy_shortconv = hy_shortconv
    KB.hy_load_tabs = hy_load_tabs
    KB.hy_stage1 = hy_stage1
    KB.hy_stage2 = hy_stage2
    KB.hy_filters = hy_filters
    KB.hy_conv = hy_conv
    KB.phase_hyena = phase_hyena


_hyena_methods()


def rwkv_tables():
    idx = np.arange(128)
    out = np.zeros((2, 6, 128, 128), np.float32)
    for dd in range(2):
        incl = (idx[:, None] <= idx[None, :]) if dd == 0 else (idx[:, None] >= idx[None, :])
        incl = incl.astype(np.float32)
        ref = 63 if dd == 0 else 64
        out[dd, 0] = incl
        out[dd, 1] = incl - incl[:, ref:ref + 1]
        out[dd, 2] = 1.0 - incl
        out[dd, 3] = incl - np.eye(128, dtype=np.float32)
        out[dd, 4] = incl
        out[dd, 5] = out[dd, 3].T
    return out


def _rwkv_methods():
    def declare_rwkv(self):
        L = 2
        self.inp("rwkv_mu", [L, 2, 1792])
        self.inp("rwkv_kvec", [L, 2, 512])
        self.inp("rwkv_lnp", [L, 3, 512])
        self.inp("rwkv_wupA", [L, 65, 1024])
        self.inp("rwkv_aupA", [L, 65, 1024])
        self.inp("rwkv_g_up", [L, 128, 512])
        self.inp("rw_tri", [2, 6, 128, 128])
        self.scr("yf", [TT, 512])

    def phase_rwkv(self, l, with_ctx):
        P = self.P
        d = self.d
        idb = self.ident_b
        idf = self.ident_f
        P.mark()

        def dve(fn, r, w):
            return P.op("dve", fn, reads=r, writes=w)

        def act(fn, r, w):
            return P.op("act", fn, reads=r, writes=w)

        def pool(fn, r, w):
            return P.op("pool", fn, reads=r, writes=w)

        def T32(name, shape=(128, 512)):
            return P.sbuf(list(shape), F32, name)

        def T16(name, shape=(128, 512)):
            return P.sbuf(list(shape), BF16, name)

        mu = T32("mu", (128, 3, 1792))
        for j in range(2):
            P.dma(mu[:, 1 + j, :], d["rwkv_mu"][l, j:j + 1, :].to_broadcast([128, 1792]), writes=[mu])
        dve(lambda g: g.tensor_tensor(out=mu[:, 0, :], in0=mu[:, 1, :], in1=mu[:, 2, :], op=ALU.add), [mu], [mu])
        dve(lambda g: g.tensor_scalar(out=mu[:, 0, :], in0=mu[:, 0, :], scalar1=-1.0, scalar2=1.0, op0=ALU.mult,
                                      op1=ALU.add), [mu], [mu])
        kv = T32("kv", (128, 3, 512))
        for j in range(2):
            P.dma(kv[:, j, :], d["rwkv_kvec"][l, j:j + 1, :].to_broadcast([128, 512]), writes=[kv])
        dve(lambda g: g.tensor_scalar(out=kv[:, 2, :], in0=kv[:, 1, :], scalar1=-1.0, scalar2=1.0, op0=ALU.mult,
                                      op1=ALU.add), [kv], [kv])
        lnp = T32("lnp", (128, 3, 512))
        P.dma(lnp[:].rearrange("p a c -> p (a c)"),
              d["rwkv_lnp"][l:l + 1].rearrange("o a c -> o (a c)").to_broadcast([128, 1536]), writes=[lnp])
        wupA = T16("wupA", (65, 2, 512))
        P.dma(wupA[:].rearrange("p a c -> p (a c)"), d["rwkv_wupA"][l], writes=[wupA], q="pool")
        aupA = T16("aupA", (65, 2, 512))
        P.dma(aupA[:].rearrange("p a c -> p (a c)"), d["rwkv_aupA"][l], writes=[aupA], q="pool")
        gup = T16("gup", (128, 512))
        P.dma(gup[:], d["rwkv_g_up"][l], writes=[gup], q="pool")
        tri = T32("tri", (128, 2, 6, 128))
        P.dma(tri[:], d["rw_tri"].rearrange("a b p c -> p a b c"), writes=[tri])
        onec = T32("onec", (128, 1))
        pool(lambda g: g.memset(onec[:], 1.0), [], [onec])
        TWA = T16("TWA", (65, 128))
        ALA = T16("ALA", (65, 128))
        pool(lambda g: g.memset(TWA[:], 1.0), [], [TWA])
        pool(lambda g: g.memset(ALA[:], 1.0), [], [ALA])
        cur = [T32("cur%d" % i, (128, 1792)) for i in range(3)]
        prv = T32("prv", (128, 1792))
        nxt = T32("nxt", (128, 1792))
        kk = T32("kk")
        sq = T32("sq")
        s8 = T32("s8", (128, 8))
        r8 = T32("r8", (128, 8))
        tw = T16("tw", (128, 64))
        al = T16("al", (128, 64))
        sgl = T16("sgl", (128, 128))
        lw = T32("lw")
        av = T32("av")
        tt_ = T32("tt")
        bb = T32("bb")
        eW, eWi, eLu, eD, elw, eWx, eLux = [T32(nm) for nm in ("eW", "eWi", "eLu", "eD", "elw", "eWx", "eLux")]
        rt, zt, bt, kt = [T16(nm) for nm in ("rt", "zt", "bt", "kt")]
        RTf, ZTf, BTf, KTf = [T16(nm, (64, 8, 128)) for nm in ("RTf", "ZTf", "BTf", "KTf")]
        X0 = [T16("X0%d" % i, (128, 8, 128)) for i in range(2)]
        XT0 = [T16("XT0%d" % i, (128, 8, 128)) for i in range(2)]
        TT0 = [T16("TT0%d" % i, (128, 8, 128)) for i in range(2)]
        AzkT = [T16("AzkT%d" % i, (128, 8, 128)) for i in range(2)]
        ArbT = [T16("ArbT%d" % i, (128, 8, 128)) for i in range(2)]
        ArkT = [T16("ArkT%d" % i, (128, 8, 128)) for i in range(2)]
        vbf = [T16("vbf%d" % i) for i in range(2)]
        bp = [T16("bp%d" % i) for i in range(2)]
        kp = [T16("kp%d" % i) for i in range(2)]
        ru = [T16("ru%d" % i) for i in range(2)]
        zu = [T16("zu%d" % i) for i in range(2)]
        kd = [T32("kd%d" % i) for i in range(2)]
        kd0 = [T32("kd0%d" % i) for i in range(2)]
        sglT = [T16("sglT%d" % i, (128, 128)) for i in range(2)]
        WC = [T32("WC%d" % i, (64, 8)) for i in range(2)]
        yfl = [T32("yfl%d" % i) for i in range(2)]
        Xs = [T16("Xs%d" % i, (128, 8, 128)) for i in range(2)]
        XTs = [T16("XTs%d" % i, (128, 8, 128)) for i in range(2)]
        TTs = [T16("TTs%d" % i, (128, 8, 128)) for i in range(2)]
        Zp, Gm, U0 = [T16(nm) for nm in ("Zp", "Gm", "U0")]
        Y0 = T32("Y0")
        RpT = T16("RpT", (64, 8, 128))
        Mm = T32("Mm", (64, 8, 64))
        NTt = T32("NTt", (64, 8, 64))
        STf = T32("STf", (64, 8, 64))
        STb = T16("STb", (64, 8, 64))
        Yt = T32("Yt")
        m8 = T32("m8", (128, 8))
        v8 = T32("v8", (128, 8))
        b8 = T32("b8", (128, 8))
        yc = T32("yc")
        sq2 = T32("sq2")
        ob = T16("ob", (128, 4, 128))
        allpb = [b for k, b in self.dbufs.items() if k[0] == "pb"]
        ps = self.ps

        def view8(ap):
            return ap.rearrange("p (h m) -> p h m", m=64)

        def b8c(t8):
            return t8[:].unsqueeze(2).to_broadcast([128, 8, 64])

        for pss in range(2):
            dd = pss
            order = [32, 33] + list(range(32)) if dd == 0 else [33, 32] + list(range(31, -1, -1))
            pool(lambda g: g.memset(STf[:], 0.0), [], [STf])
            pool(lambda g: g.memset(STb[:], 0.0), [], [STb])

            def loads(tj, tn):
                rr = tm_row(tn * 128)
                P.dma(cur[tj % 3][:], d["pb"][rr:rr + 128, :], reads=allpb, writes=[cur[tj % 3]])
                P.dma(prv[:], d["pb"][rr - 1:rr + 127, :], reads=allpb, writes=[prv])
                P.dma(nxt[:], d["pb"][rr + 1:rr + 129, :], reads=allpb, writes=[nxt])

            def stageA(ti):
                t = order[ti]
                pr = ti % 2
                need_y = with_ctx or t < 32
                cu, pv_, nx = cur[ti % 3], prv, nxt
                if pss == 1 and need_y:
                    P.dma(yfl[pr][:], d["yf"][t * 128:(t + 1) * 128, :], reads=[self.db("yf", t)], writes=[yfl[pr]])
                pool(lambda g: g.tensor_tensor(out=nx[:], in0=nx[:], in1=mu[:, 2, :], op=ALU.mult), [nx, mu], [nx])
                dve(lambda g: g.tensor_tensor(out=cu[:], in0=cu[:], in1=mu[:, 0, :], op=ALU.mult), [cu, mu], [cu])
                dve(lambda g: g.tensor_tensor(out=pv_[:], in0=pv_[:], in1=mu[:, 1, :], op=ALU.mult), [pv_, mu], [pv_])
                yield
                dve(lambda g: g.tensor_tensor(out=cu[:], in0=cu[:], in1=pv_[:], op=ALU.add), [cu, pv_], [cu])
                dve(lambda g: g.tensor_tensor(out=cu[:], in0=cu[:], in1=nx[:], op=ALU.add), [cu, nx], [cu])
                if ti + 1 < len(order):
                    loads(ti + 1, order[ti + 1])
                r_ap, k_ap, v_ap = cu[:, 0:512], cu[:, 512:1024], cu[:, 1024:1536]
                dve(lambda g: g.tensor_tensor(out=kk[:], in0=k_ap, in1=kv[:, 0, :], op=ALU.mult), [cu, kv], [kk])
                pool(lambda g: g.tensor_tensor(out=sq[:], in0=kk[:], in1=kk[:], op=ALU.mult), [kk], [sq])
                dve(lambda g: g.tensor_reduce(out=s8[:], in_=view8(sq[:]), axis=AX.X, op=ALU.add), [sq], [s8])
                act(lambda g: g.activation(out=r8[:], in_=s8[:], func=AF.Sqrt, bias=1e-12), [s8], [r8])
                dve(lambda g: g.reciprocal(out=r8[:], in_=r8[:]), [r8], [r8])
                dve(lambda g: g.tensor_tensor(out=view8(kk[:]), in0=view8(kk[:]), in1=b8c(r8), op=ALU.mult),
                    [kk, r8], [kk])
                act(lambda g: g.activation(out=vbf[pr][:], in_=v_ap, func=AF.Copy), [cu], [vbf[pr]])
                act(lambda g: g.activation(out=tw[:], in_=cu[:, 1536:1600], func=AF.Tanh), [cu], [tw])
                act(lambda g: g.activation(out=al[:], in_=cu[:, 1600:1664], func=AF.Copy), [cu], [al])
                yield
                pb0 = self.psb(0)
                P.op("pe", lambda g: g.transpose(pb0[0:64, 0:128], tw[:], idb[:]), reads=[tw, idb], writes=[ps[0]])
                P.op("pe", lambda g: g.transpose(pb0[0:64, 128:256], al[:], idb[:]), reads=[al, idb], writes=[ps[0]])
                dve(lambda g: g.tensor_copy(out=TWA[0:64, :], in_=pb0[0:64, 0:128]), [ps[0]], [TWA])
                dve(lambda g: g.tensor_copy(out=ALA[0:64, :], in_=pb0[0:64, 128:256]), [ps[0]], [ALA])
                if pss == 1:
                    act(lambda g: g.activation(out=sgl[:], in_=cu[:, 1664:1792], func=AF.Sigmoid), [cu], [sgl])
                    pb1 = self.psb(1)
                    P.op("pe", lambda g: g.transpose(pb1[:, 0:128], sgl[:], idb[:]), reads=[sgl, idb], writes=[ps[1]])
                    dve(lambda g: g.tensor_copy(out=sglT[pr][:], in_=pb1[:, 0:128]), [ps[1]], [sglT[pr]])
                    P.mm(ps[2][:], ALA[:], aupA[:, 0, :], reads=[ALA, aupA], writes=[ps[2]])
                    act(lambda g: g.activation(out=av[:], in_=ps[2][:], func=AF.Sigmoid), [ps[2]], [av])
                    dve(lambda g: g.tensor_tensor(out=tt_[:], in0=av[:], in1=kv[:, 1, :], op=ALU.mult), [av, kv], [tt_])
                    pool(lambda g: g.tensor_tensor(out=tt_[:], in0=tt_[:], in1=kv[:, 2, :], op=ALU.add), [tt_, kv], [tt_])
                    dve(lambda g: g.tensor_tensor(out=kd0[pr][:], in0=k_ap, in1=tt_[:], op=ALU.mult), [cu, tt_], [kd0[pr]])
                yield
                P.mm(ps[2][:], TWA[:], wupA[:, dd, :], reads=[TWA, wupA], writes=[ps[2]])
                act(lambda g: g.activation(out=lw[:], in_=ps[2][:], func=AF.Sigmoid), [ps[2]], [lw])
                dve(lambda g: g.tensor_scalar(out=lw[:], in0=lw[:], scalar1=-0.6065306597126334, scalar2=None,
                                              op0=ALU.mult), [lw], [lw])
                P.mm(ps[3][:], ALA[:], aupA[:, dd, :], reads=[ALA, aupA], writes=[ps[3]])
                act(lambda g: g.activation(out=av[:], in_=ps[3][:], func=AF.Sigmoid), [ps[3]], [av])
                dve(lambda g: g.tensor_tensor(out=tt_[:], in0=av[:], in1=kv[:, 1, :], op=ALU.mult), [av, kv], [tt_])
                pool(lambda g: g.tensor_tensor(out=tt_[:], in0=tt_[:], in1=kv[:, 2, :], op=ALU.add), [tt_, kv], [tt_])
                dve(lambda g: g.tensor_tensor(out=kd[pr][:], in0=k_ap, in1=tt_[:], op=ALU.mult), [cu, tt_], [kd[pr]])
                pool(lambda g: g.tensor_tensor(out=bb[:], in0=kk[:], in1=av[:], op=ALU.mult), [kk, av], [bb])
                yield
                P.mm(ps[0][:], tri[:, dd, 0, :], lw[:], reads=[tri, lw], writes=[ps[0]])
                P.mm(ps[1][:], tri[:, dd, 1, :], lw[:], reads=[tri, lw], writes=[ps[1]])
                P.mm(ps[2][:], tri[:, dd, 2, :], lw[:], reads=[tri, lw], writes=[ps[2]])
                for h in range(8):
                    P.mm(ps[3][0:64, h:h + 1], lw[:, h * 64:(h + 1) * 64], onec[:], reads=[lw, onec], writes=[ps[3]])
                act(lambda g: g.activation(out=WC[pr][:], in_=ps[3][0:64, 0:8], func=AF.Exp), [ps[3]], [WC[pr]])
                act(lambda g: g.activation(out=eLu[:], in_=ps[0][:], func=AF.Exp), [ps[0]], [eLu])
                act(lambda g: g.activation(out=eW[:], in_=ps[1][:], func=AF.Exp), [ps[1]], [eW])
                act(lambda g: g.activation(out=eWi[:], in_=ps[1][:], func=AF.Exp, scale=-1.0), [ps[1]], [eWi])
                act(lambda g: g.activation(out=eD[:], in_=ps[2][:], func=AF.Exp), [ps[2]], [eD])
                act(lambda g: g.activation(out=elw[:], in_=lw[:], func=AF.Exp, scale=-1.0), [lw], [elw])
                yield
                dve(lambda g: g.tensor_tensor(out=eWx[:], in0=eW[:], in1=elw[:], op=ALU.mult), [eW, elw], [eWx])
                pool(lambda g: g.tensor_tensor(out=eLux[:], in0=eLu[:], in1=elw[:], op=ALU.mult), [eLu, elw], [eLux])
                dve(lambda g: g.tensor_tensor(out=rt[:], in0=r_ap, in1=eW[:], op=ALU.mult), [cu, eW], [rt])
                dve(lambda g: g.scalar_tensor_tensor(out=zt[:], in0=kk[:], scalar=-1.0, in1=eWx[:], op0=ALU.mult,
                                                     op1=ALU.mult), [kk, eWx], [zt])
                pool(lambda g: g.tensor_tensor(out=bt[:], in0=bb[:], in1=eWi[:], op=ALU.mult), [bb, eWi], [bt])
                dve(lambda g: g.tensor_tensor(out=kt[:], in0=kd[pr][:], in1=eWi[:], op=ALU.mult), [kd[pr], eWi], [kt])
                yield
                pool(lambda g: g.tensor_tensor(out=bp[pr][:], in0=bb[:], in1=eD[:], op=ALU.mult), [bb, eD], [bp[pr]])
                dve(lambda g: g.tensor_tensor(out=kp[pr][:], in0=kd[pr][:], in1=eD[:], op=ALU.mult), [kd[pr], eD], [kp[pr]])
                pool(lambda g: g.tensor_tensor(out=ru[pr][:], in0=r_ap, in1=eLu[:], op=ALU.mult), [cu, eLu], [ru[pr]])
                dve(lambda g: g.scalar_tensor_tensor(out=zu[pr][:], in0=kk[:], scalar=-1.0, in1=eLux[:], op0=ALU.mult,
                                                     op1=ALU.mult), [kk, eLux], [zu[pr]])
                for qi, (src, dstf) in enumerate(((rt, RTf), (zt, ZTf), (bt, BTf), (kt, KTf))):
                    pbx = self.psb(qi % 2)
                    for h in range(8):
                        P.op("pe", lambda g: g.transpose(pbx[0:64, h * 128:(h + 1) * 128], src[:, h * 64:(h + 1) * 64],
                                                         idb[:]), reads=[src, idb], writes=[ps[qi % 2]])
                    if qi % 2 == 0:
                        act(lambda g: g.activation(out=dstf[:].rearrange("p h t -> p (h t)"), in_=pbx[0:64, :],
                                                   func=AF.Copy), [ps[qi % 2]], [dstf])
                    else:
                        dve(lambda g: g.tensor_copy(out=dstf[:].rearrange("p h t -> p (h t)"), in_=pbx[0:64, :]),
                            [ps[qi % 2]], [dstf])
                    if qi == 1:
                        yield
                yield
                nb = [0]

                def amat(Lf, Rf, mi, dst):
                    for hg in range(2):
                        pa = ps[nb[0] % 4]
                        nb[0] += 1
                        for j in range(4):
                            h = hg * 4 + j
                            P.mm(pa[:, j * 128:(j + 1) * 128], Lf[:, h, :], Rf[:, h, :], reads=[Lf, Rf], writes=[pa])
                        dve(lambda g: g.tensor_tensor(out=dst[:, hg * 4:(hg + 1) * 4, :],
                                                      in0=pa[:].rearrange("p (h t) -> p h t", h=4),
                                                      in1=tri[:, dd, mi, :].unsqueeze(1).to_broadcast([128, 4, 128]),
                                                      op=ALU.mult), [pa, tri], [dst])

                amat(ZTf, BTf, 5, X0[pr])
                amat(BTf, ZTf, 3, XT0[pr])
                yield
                amat(KTf, ZTf, 3, AzkT[pr])
                amat(BTf, RTf, 4, ArbT[pr])
                yield
                amat(KTf, RTf, 4, ArkT[pr])
                pool(lambda g: g.tensor_tensor(out=TT0[pr][:], in0=XT0[pr][:],
                                               in1=idb[:].unsqueeze(1).to_broadcast([128, 8, 128]), op=ALU.add),
                     [XT0[pr], idb], [TT0[pr]])

            def stageB(ti):
                t = order[ti]
                pr = ti % 2
                need_y = with_ctx or t < 32
                cu = cur[ti % 3]
                r_ap, v_ap = cu[:, 0:512], cu[:, 1024:1536]
                Xc, XTc, TTc = X0[pr], XT0[pr], TT0[pr]
                cx = 0
                for it in range(6):
                    Xn, XTn, TTn = Xs[cx], XTs[cx], TTs[cx]
                    for hg in range(2):
                        p2 = ps[4 + hg]
                        for j in range(4):
                            h = hg * 4 + j
                            P.mm(p2[:, j * 128:(j + 1) * 128], XTc[:, h, :], Xc[:, h, :], reads=[XTc, Xc], writes=[p2])
                        act(lambda g: g.activation(out=Xn[:, hg * 4:(hg + 1) * 4, :].rearrange("p h t -> p (h t)"),
                                                   in_=p2[:], func=AF.Copy), [p2], [Xn])
                        if it < 5:
                            p3 = ps[6 + hg]
                            for j in range(4):
                                h = hg * 4 + j
                                P.mm(p3[:, j * 128:(j + 1) * 128], Xc[:, h, :], XTc[:, h, :], reads=[Xc, XTc],
                                     writes=[p3])
                            dve(lambda g: g.tensor_copy(out=XTn[:, hg * 4:(hg + 1) * 4, :].rearrange("p h t -> p (h t)"),
                                                        in_=p3[:]), [p3], [XTn])
                    yield
                    for hg in range(2):
                        p4 = ps[4 + hg]
                        for j in range(4):
                            h = hg * 4 + j
                            P.mm(p4[:, j * 128:(j + 1) * 128], Xn[:, h, :], TTc[:, h, :], start=True, stop=False,
                                 reads=[Xn, TTc], writes=[p4])
                            P.mm(p4[:, j * 128:(j + 1) * 128], idb[:], TTc[:, h, :], start=False, stop=True,
                                 reads=[idb, TTc], writes=[p4])
                        act(lambda g: g.activation(out=TTn[:, hg * 4:(hg + 1) * 4, :].rearrange("p h t -> p (h t)"),
                                                   in_=p4[:], func=AF.Copy), [p4], [TTn])
                    Xc, XTc, TTc = Xn, XTn, TTn
                    cx = 1 - cx
                    yield
                TT = TTc
                for h in range(8):
                    P.mm(ps[4][:, h * 64:(h + 1) * 64], TT[:, h, :], zu[pr][:, h * 64:(h + 1) * 64], reads=[TT, zu[pr]],
                         writes=[ps[4]])
                act(lambda g: g.activation(out=Zp[:], in_=ps[4][:], func=AF.Copy), [ps[4]], [Zp])
                for h in range(8):
                    P.mm(ps[5][:, h * 64:(h + 1) * 64], AzkT[pr][:, h, :], vbf[pr][:, h * 64:(h + 1) * 64],
                         reads=[AzkT[pr], vbf[pr]], writes=[ps[5]])
                dve(lambda g: g.tensor_copy(out=Gm[:], in_=ps[5][:]), [ps[5]], [Gm])
                for h in range(8):
                    P.mm(ps[6][:, h * 64:(h + 1) * 64], TT[:, h, :], Gm[:, h * 64:(h + 1) * 64], reads=[TT, Gm], writes=[ps[6]])
                act(lambda g: g.activation(out=U0[:], in_=ps[6][:], func=AF.Copy), [ps[6]], [U0])
                yield
                for h in range(8):
                    P.mm(ps[7][0:64, h * 64:(h + 1) * 64], Zp[:, h * 64:(h + 1) * 64], bp[pr][:, h * 64:(h + 1) * 64],
                         reads=[Zp, bp[pr]], writes=[ps[7]])
                dve(lambda g: g.tensor_tensor(out=Mm[:], in0=idf[0:64, 0:64].unsqueeze(1).to_broadcast([64, 8, 64]),
                                              in1=WC[pr][:].unsqueeze(2).to_broadcast([64, 8, 64]), op=ALU.mult),
                    [idf, WC[pr]], [Mm])
                dve(lambda g: g.tensor_tensor(out=Mm[:], in0=Mm[:], in1=ps[7][0:64, :].rearrange("p (h m) -> p h m", m=64),
                                              op=ALU.add), [Mm, ps[7]], [Mm])
                for h in range(8):
                    hs = slice(h * 64, (h + 1) * 64)
                    P.mm(ps[4][0:64, hs], bp[pr][:, hs], U0[:, hs], start=True, stop=False, reads=[bp[pr], U0], writes=[ps[4]])
                    P.mm(ps[4][0:64, hs], kp[pr][:, hs], vbf[pr][:, hs], start=False, stop=True, reads=[kp[pr], vbf[pr]],
                         writes=[ps[4]])
                act(lambda g: g.activation(out=NTt[:].rearrange("p h m -> p (h m)"), in_=ps[4][0:64, :], func=AF.Copy),
                    [ps[4]], [NTt])
                yield
                if need_y:
                    for h in range(8):
                        hs = slice(h * 64, (h + 1) * 64)
                        P.mm(ps[5][:, hs], ArbT[pr][:, h, :], U0[:, hs], start=True, stop=False, reads=[ArbT[pr], U0],
                             writes=[ps[5]])
                        P.mm(ps[5][:, hs], ArkT[pr][:, h, :], vbf[pr][:, hs], start=False, stop=True,
                             reads=[ArkT[pr], vbf[pr]], writes=[ps[5]])
                    act(lambda g: g.activation(out=Y0[:], in_=ps[5][:], func=AF.Copy), [ps[5]], [Y0])
                    for hg in range(2):
                        prr = ps[6 + hg]
                        for j in range(4):
                            h = hg * 4 + j
                            hs = slice(h * 64, (h + 1) * 64)
                            P.mm(prr[0:64, j * 128:(j + 1) * 128], ru[pr][:, hs], idb[:], start=True, stop=False,
                                 reads=[ru[pr], idb], writes=[prr])
                            P.mm(prr[0:64, j * 128:(j + 1) * 128], Zp[:, hs], ArbT[pr][:, h, :], start=False, stop=True,
                                 reads=[Zp, ArbT[pr]], writes=[prr])
                        dve(lambda g: g.tensor_copy(out=RpT[:, hg * 4:(hg + 1) * 4, :].rearrange("p h t -> p (h t)"),
                                                    in_=prr[0:64, :]), [prr], [RpT])
                    yield
                    for h in range(8):
                        P.mm(ps[4][:, h * 64:(h + 1) * 64], RpT[:, h, :], STb[:, h, :], reads=[RpT, STb], writes=[ps[4]])
                    dve(lambda g: g.tensor_tensor(out=Yt[:], in0=ps[4][:], in1=Y0[:], op=ALU.add), [ps[4], Y0], [Yt])
                for h in range(8):
                    P.mm(ps[5][0:64, h * 64:(h + 1) * 64], Mm[:, h, :], STf[:, h, :], reads=[Mm, STf], writes=[ps[5]])
                dve(lambda g: g.tensor_tensor(out=STf[:], in0=ps[5][0:64, :].rearrange("p (h m) -> p h m", m=64),
                                              in1=NTt[:], op=ALU.add), [ps[5], NTt], [STf])
                act(lambda g: g.activation(out=STb[:], in_=STf[:], func=AF.Copy), [STf], [STb])
                yield
                if not need_y:
                    return
                if pss == 0:
                    P.dma(d["yf"][t * 128:(t + 1) * 128, :], Yt[:], reads=[Yt], writes=[self.db("yf", t)])
                    return
                dve(lambda g: g.tensor_tensor(out=Yt[:], in0=Yt[:], in1=yfl[pr][:], op=ALU.add), [Yt, yfl[pr]], [Yt])
                dve(lambda g: g.tensor_reduce(out=m8[:], in_=view8(Yt[:]), axis=AX.X, op=ALU.add), [Yt], [m8])
                dve(lambda g: g.tensor_scalar(out=m8[:], in0=m8[:], scalar1=1.0 / 64, scalar2=None, op0=ALU.mult), [m8], [m8])
                dve(lambda g: g.tensor_tensor(out=view8(yc[:]), in0=view8(Yt[:]), in1=b8c(m8), op=ALU.subtract),
                    [Yt, m8], [yc])
                pool(lambda g: g.tensor_tensor(out=sq2[:], in0=yc[:], in1=yc[:], op=ALU.mult), [yc], [sq2])
                dve(lambda g: g.tensor_reduce(out=v8[:], in_=view8(sq2[:]), axis=AX.X, op=ALU.add), [sq2], [v8])
                act(lambda g: g.activation(out=v8[:], in_=v8[:], func=AF.Sqrt, scale=1.0 / 64, bias=64e-5), [v8], [v8])
                dve(lambda g: g.reciprocal(out=v8[:], in_=v8[:]), [v8], [v8])
                yield
                dve(lambda g: g.tensor_tensor(out=view8(yc[:]), in0=view8(yc[:]), in1=b8c(v8), op=ALU.mult), [yc, v8], [yc])
                pool(lambda g: g.tensor_tensor(out=yc[:], in0=yc[:], in1=lnp[:, 0, :], op=ALU.mult), [yc, lnp], [yc])
                pool(lambda g: g.tensor_tensor(out=yc[:], in0=yc[:], in1=lnp[:, 1, :], op=ALU.add), [yc, lnp], [yc])
                dve(lambda g: g.tensor_tensor(out=kd0[pr][:], in0=kd0[pr][:], in1=kd[pr][:], op=ALU.add),
                    [kd0[pr], kd[pr]], [kd0[pr]])
                dve(lambda g: g.tensor_tensor(out=kd0[pr][:], in0=kd0[pr][:], in1=r_ap, op=ALU.mult), [kd0[pr], cu], [kd0[pr]])
                dve(lambda g: g.scalar_tensor_tensor(out=sq2[:], in0=kd0[pr][:], scalar=0.5, in1=lnp[:, 2, :], op0=ALU.mult,
                                                     op1=ALU.mult), [kd0[pr], lnp], [sq2])
                dve(lambda g: g.tensor_reduce(out=b8[:], in_=view8(sq2[:]), axis=AX.X, op=ALU.add), [sq2], [b8])
                dve(lambda g: g.tensor_tensor(out=view8(sq2[:]), in0=view8(v_ap), in1=b8c(b8), op=ALU.mult), [cu, b8], [sq2])
                pool(lambda g: g.tensor_tensor(out=yc[:], in0=yc[:], in1=sq2[:], op=ALU.add), [yc, sq2], [yc])
                yield
                P.mm(ps[6][:], sglT[pr][:], gup[:], reads=[sglT[pr], gup], writes=[ps[6]])
                dve(lambda g: g.tensor_tensor(out=yc[:], in0=yc[:], in1=ps[6][:], op=ALU.mult), [yc, ps[6]], [yc])
                for kc in range(4):
                    P.op("pe", lambda g: g.transpose(ps[7][:, kc * 128:(kc + 1) * 128], yc[:, kc * 128:(kc + 1) * 128],
                                                     idf[:]), reads=[yc, idf], writes=[ps[7]])
                act(lambda g: g.activation(out=ob[:].rearrange("p a b -> p (a b)"), in_=ps[7][:], func=AF.Copy),
                    [ps[7]], [ob])
                P.dma(d["yT"][1].rearrange("(kc p) t -> p kc t", p=128)[:, :, t * 128:(t + 1) * 128], ob[:], reads=[ob],
                      writes=[self.db("yT", (1, t))])

            def run2(ga, gb):
                live = [g for g in (ga, gb) if g is not None]
                while live:
                    for g in list(live):
                        try:
                            next(g)
                        except StopIteration:
                            live.remove(g)

            loads(0, order[0])
            run2(stageA(0), None)
            for ti in range(len(order)):
                nxa = stageA(ti + 1) if ti + 1 < len(order) else None
                run2(stageB(ti), nxa)
        P.release()

    KB.declare_rwkv = declare_rwkv
    KB.phase_rwkv = phase_rwkv


_rwkv_methods()


def build_program():
    nc = bass.Bass("TRN2", target_bir_lowering=False)
    kb = KB(nc)
    kb.declare_common()
    kb.declare_pin()
    kb.declare_attn()
    kb.declare_merge()
    kb.declare_hyena()
    kb.declare_rwkv()
    kb.outp("out", [NLAT, DM])
    kb.alloc_persist()
    d = kb.d
    for l in range(2):
        with_ctx = (l == 0)
        kb.phase_mod(l)
        src = (d["xall"], "xall") if l == 0 else (d["xs"], "xs")
        kb.phase_ffn(l, 0, src, (d["xs"], "xs"), NTILE)
        kb.phase_pin(l, (d["xs"], "xs"))
        kb.phase_mla(l, with_ctx)
        kb.phase_swa(l, with_ctx)
        kb.phase_hyena(l, with_ctx)
        kb.phase_rwkv(l, with_ctx)
        kb.phase_merge(l, with_ctx)
        if l == 0:
            kb.phase_ffn(l, 2, (d["xs"], "xs"), (d["xs"], "xs"), NTILE)
        else:
            kb.phase_ffn(l, 2, (d["xs"], "xs"), (d["out"], "out"), 32)
    kb.P.barrier()
    return nc, kb


def kernel(**inputs):
    from concourse.bass_utils import run_bass_kernel_spmd
    nc, kb = build_program()
    sh = host_shared(inputs)
    names = [k for k in kb.d if k in sh]
    maps = []
    for b in range(8):
        pc = host_core(inputs, b)
        m = {k: sh[k] for k in names}
        m.update(pc)
        maps.append(m)
    res = run_bass_kernel_spmd(nc, maps, core_ids=list(range(8)))
    out = np.stack([np.asarray(res.results[b]["out"], dtype=np.float32) for b in range(8)], axis=0)
    return out
```

```python
import numpy as np
import concourse.bass as bass
import concourse.mybir as mybir

F32 = mybir.dt.float32
BF16 = mybir.dt.bfloat16
AF = mybir.ActivationFunctionType
ALU = mybir.AluOpType
AX = mybir.AxisListType

NDSEM = 36
NHW = 28
SEM_LIMIT = 30000
SB_BASE = 16640
SB_TOP = 229376
LAZY_D = 2
SAME_ENG_GAP = 3


class Buf:
    __slots__ = ("w", "r", "name")

    def __init__(self, name=""):
        self.w = None
        self.r = {}
        self.name = name


class Tile:
    def __init__(self, t, buf):
        self.t = t
        self.b = buf

    def __getitem__(self, k):
        return self.t[k]


def _bufs(xs):
    return [x.b if isinstance(x, Tile) else x for x in xs]


class Prog:
    def __init__(self, nc):
        self.nc = nc
        self.eng = {"pe": nc.tensor, "dve": nc.vector, "act": nc.scalar,
                    "pool": nc.gpsimd, "sp": nc.sync}
        self.cnt = {e: 0 for e in self.eng}
        self.known = {e: {} for e in self.eng}
        self.esem = {}
        self.egen = {e: 0 for e in self.eng}
        self.ebase = {e: 0 for e in self.eng}
        self.dsem = []
        self.duse = []
        self.dgen = []
        self.semtab = {}
        self.dnext = 0
        self.dnext_sw = 0
        self.pending = []
        self.nwait = 0
        self.ninstr = 0
        self.sb_off = SB_BASE
        self.sb_mark = []
        self.uid = 0
        nc = self.nc
        for e in self.eng:
            self.esem[e] = nc.alloc_semaphore("es_%s_0" % e)
            self.semtab[("e", e, 0)] = self.esem[e]
        for i in range(NDSEM):
            self.dsem.append(nc.alloc_semaphore("ds%d_0" % i))
            self.semtab[("d", i, 0)] = self.dsem[i]
            self.duse.append(0)
            self.dgen.append(0)

    def _rot_e(self, e):
        if self.cnt[e] - self.ebase[e] >= SEM_LIMIT:
            self.egen[e] += 1
            self.ebase[e] = self.cnt[e]
            self.esem[e] = self.nc.alloc_semaphore("es_%s_%d" % (e, self.egen[e]))
            self.semtab[("e", e, self.egen[e])] = self.esem[e]

    def _rot_d(self, i):
        if 16 * self.duse[i] >= SEM_LIMIT:
            self.dgen[i] += 1
            self.duse[i] = 0
            self.dsem[i] = self.nc.alloc_semaphore("ds%d_%d" % (i, self.dgen[i]))
            self.semtab[("d", i, self.dgen[i])] = self.dsem[i]

    def sbuf(self, shape, dtype, name=None):
        self.uid += 1
        name = (name or "t") + "_%d" % self.uid
        nbytes = int(np.prod(shape[1:])) * mybir.dt.size(dtype)
        off = (self.sb_off + 31) // 32 * 32
        t = self.nc.alloc_sbuf_tensor_at(name, list(shape), dtype, offset=off)
        self.sb_off = off + nbytes
        assert self.sb_off <= SB_TOP, ("SBUF overflow", name, self.sb_off)
        return Tile(t, Buf(name))

    def mark(self):
        self.sb_mark.append(self.sb_off)

    def release(self):
        self.barrier()
        self.sb_off = self.sb_mark.pop()

    def _wait(self, e, kind, id_, gen, val):
        self.known[e][(kind, id_)] = (gen, val)
        self.eng[e].wait_ge(self.semtab[(kind, id_, gen)], val)
        self.nwait += 1

    def _need(self, e, toks):
        kn = self.known[e]
        req = {}
        for t in toks:
            if t is None:
                continue
            kind, id_, gen, val, absidx = t
            if kind == "e" and id_ == e:
                if e == "pe":
                    continue
                if self.cnt[e] + 1 - absidx >= SAME_ENG_GAP:
                    continue
            k = (kind, id_)
            if kn.get(k, (-1, 0)) >= (gen, val):
                continue
            if req.get(k, (-1, 0)) < (gen, val):
                req[k] = (gen, val)
        for (kind, id_), (gen, val) in req.items():
            self._wait(e, kind, id_, gen, val)

    @staticmethod
    def _deps(reads, writes):
        toks = []
        for b in reads:
            toks.append(b.w)
        for b in writes:
            toks.append(b.w)
            toks.extend(b.r.values())
        return toks

    @staticmethod
    def _commit(tok, reads, writes):
        k = (tok[0], tok[1])
        for b in reads:
            b.r[k] = tok
        for b in writes:
            b.w = tok
            b.r = {}

    def op(self, e, fn, reads=(), writes=()):
        reads = _bufs(reads)
        writes = _bufs(writes)
        self._hazard_flush(reads, writes)
        self._rot_e(e)
        self._need(e, self._deps(reads, writes))
        ins = fn(self.eng[e])
        self.cnt[e] += 1
        ins.then_inc(self.esem[e], 1)
        tok = ("e", e, self.egen[e], self.cnt[e] - self.ebase[e], self.cnt[e])
        self._commit(tok, reads, writes)
        self.ninstr += 1
        return ins

    def _flush(self, upto=None):
        n = len(self.pending) if upto is None else upto
        todo, self.pending = self.pending[:n], self.pending[n:]
        for (out, in_, reads, writes, q, kw, _) in todo:
            self._dma_now(out, in_, reads, writes, q, kw)

    def _hazard_flush(self, reads, writes):
        if not self.pending:
            return
        rs = set(map(id, reads))
        ws = set(map(id, writes))
        last = -1
        for i, (_, _, pr, pw, _, _, _) in enumerate(self.pending):
            hit = False
            for b in pr:
                if id(b) in ws:
                    hit = True
            for b in pw:
                if id(b) in ws or id(b) in rs:
                    hit = True
            if hit:
                last = i
        if last >= 0:
            self._flush(last + 1)

    def dma(self, out, in_, reads=(), writes=(), q="sp", **kw):
        reads = _bufs(reads)
        writes = _bufs(writes)
        self._hazard_flush(reads, writes)
        is_store = type(out.tensor).__name__.startswith("DRam") and not type(in_.tensor).__name__.startswith("DRam")
        if is_store and q == "sp" and LAZY_D > 0:
            self.pending.append([out, in_, reads, writes, q, kw, 0])
            return None
        r = self._dma_now(out, in_, reads, writes, q, kw)
        if q == "sp" and self.pending:
            k = 0
            for p in self.pending:
                p[6] += 1
            while k < len(self.pending) and self.pending[k][6] >= LAZY_D:
                k += 1
            if k:
                self._flush(k)
        return r

    def _dma_now(self, out, in_, reads, writes, q, kw):
        if q == "pool":
            i = NHW + self.dnext_sw
            self.dnext_sw = (self.dnext_sw + 1) % (NDSEM - NHW)
        else:
            i = self.dnext
            self.dnext = (self.dnext + 1) % NHW
        toks = self._deps(reads, writes)
        if self.duse[i]:
            toks.append(("d", i, self.dgen[i], 16 * self.duse[i], 0))
        self._need(q, toks)
        self._rot_d(i)
        self.duse[i] += 1
        ins = self.eng[q].dma_start(out=out, in_=in_, **kw)
        ins.then_inc(self.dsem[i], 16)
        tok = ("d", i, self.dgen[i], 16 * self.duse[i], 0)
        self._commit(tok, reads, writes)
        self.ninstr += 1
        return ins

    def barrier(self, engines=None):
        self._flush()
        toks = []
        for e in self.eng:
            if self.cnt[e] > self.ebase[e]:
                toks.append(("e", e, self.egen[e], self.cnt[e] - self.ebase[e]))
            elif self.egen[e] > 0:
                toks.append(("e", e, self.egen[e] - 1, SEM_LIMIT))
        for i, u in enumerate(self.duse):
            if u:
                toks.append(("d", i, self.dgen[i], 16 * u))
        for e in (engines or self.eng):
            kn = self.known[e]
            for kind, id_, gen, val in toks:
                if kind == "e" and id_ == e and e in ("sp", "pe"):
                    continue
                if kn.get((kind, id_), (-1, 0)) >= (gen, val):
                    continue
                self._wait(e, kind, id_, gen, val)

    def mm(self, out, lhsT, rhs, start=True, stop=True, reads=(), writes=()):
        return self.op("pe", lambda g: g.matmul(out, lhsT, rhs, start=start, stop=stop),
                       reads=reads, writes=writes)


DM = 1024
NLAT = 4096
NCTX = 256
TT = NLAT + NCTX
NTILE = TT // 128
FF = 2816
NFC = FF // 128
EPS = 1e-6
I32 = mybir.dt.int32


class KB:
    def __init__(self, nc, ext_in=(), ext_out=()):
        self.nc = nc
        self.P = Prog(nc)
        self.ext_in = set(ext_in)
        self.ext_out = set(ext_out)
        self.d = {}
        self.dbufs = {}
        self.ps = [Tile(nc.alloc_psum_tensor("ps%d" % i, [128, 512], F32), Buf("ps%d" % i))
                   for i in range(8)]

    def inp(self, name, shape, dtype=F32):
        self.d[name] = self.nc.dram_tensor(name, list(shape), dtype, kind="ExternalInput").ap()
        return self.d[name]

    def outp(self, name, shape, dtype=F32):
        self.d[name] = self.nc.dram_tensor(name, list(shape), dtype, kind="ExternalOutput").ap()
        return self.d[name]

    def scr(self, name, shape, dtype=F32):
        kind = "Internal"
        if name in self.ext_in:
            kind = "ExternalInput"
        elif name in self.ext_out:
            kind = "ExternalOutput"
        self.d[name] = self.nc.dram_tensor(name, list(shape), dtype, kind=kind).ap()
        return self.d[name]

    def db(self, name, idx=0):
        k = (name, idx)
        if k not in self.dbufs:
            self.dbufs[k] = Buf("%s_%s" % (name, idx))
        return self.dbufs[k]

    def psb(self, i):
        return self.ps[i][:].bitcast(BF16)

    def rstd_from_ss(self, ss, n, eps, out):
        P = self.P
        ss_t, ss_ap = ss
        o_t, o_ap = out
        P.op("act", lambda g: g.activation(out=o_ap, in_=ss_ap, func=AF.Sqrt, scale=1.0 / n, bias=eps),
             reads=[ss_t], writes=[o_t])
        P.op("dve", lambda g: g.reciprocal(out=o_ap, in_=o_ap), reads=[o_t], writes=[o_t])

    def declare_common(self):
        L = 2
        self.inp("xall", [TT, DM])
        self.inp("cT", [128, 16])
        self.inp("ident", [128, 128])
        self.inp("w_mod", [L, DM, 9 * DM])
        self.inp("b_mod", [L, 9 * DM])
        self.inp("norm_g", [L, 6, DM])
        self.inp("norm_gT", [L, 128, 48])
        self.inp("ffn_w13", [L, 2, DM, 2 * FF])
        self.inp("ffn_w2", [L, 2, FF, DM])
        self.scr("gtrow", [L, 3, 2, DM])
        self.scr("xs", [TT, DM])

    def alloc_persist(self):
        P = self.P
        self.ident_f = P.sbuf([128, 128], F32, "identf")
        self.ident_b = P.sbuf([128, 128], BF16, "identb")
        P.dma(self.ident_f[:], self.d["ident"], writes=[self.ident_f])
        P.dma(self.ident_b[:], self.d["ident"], writes=[self.ident_b], q="pool")
        self.mcol = P.sbuf([128, 72, 2], F32, "mcol")
        self.AB = P.sbuf([128, 3, 2, 8, 2], F32, "AB")

    def phase_mod(self, l):
        P = self.P
        d = self.d
        P.mark()
        cT = P.sbuf([128, 16], F32, "cT")
        P.dma(cT[:], d["cT"], writes=[cT])
        sc = P.sbuf([128, 16], F32, "sc")
        P.op("act", lambda g: g.activation(out=sc[:], in_=cT[:], func=AF.Silu), reads=[cT], writes=[sc])
        mrow = P.sbuf([2, 9 * DM], F32, "mrow")
        brow = P.sbuf([2, 9 * DM], F32, "brow")
        for s in range(2):
            P.dma(brow[s:s + 1, :], d["b_mod"][l:l + 1, :], writes=[brow])
        wt = [P.sbuf([128, 8, 512], F32, "wmod%d" % i) for i in range(2)]
        wsrc = d["w_mod"][l].rearrange("(k p) n -> p k n", p=128)
        for jb in range(18):
            w = wt[jb % 2]
            P.dma(w[:], wsrc[:, :, jb * 512:(jb + 1) * 512], writes=[w])
            ps = self.ps[jb % 2]
            for k in range(8):
                P.mm(ps[0:2, :], sc[:, 2 * k:2 * k + 2], w[:, k, :], start=(k == 0), stop=(k == 7),
                     reads=[sc, w], writes=[ps])
            P.op("dve", lambda g: g.tensor_tensor(out=mrow[:, jb * 512:(jb + 1) * 512], in0=ps[0:2, :],
                                                  in1=brow[:, jb * 512:(jb + 1) * 512], op=ALU.add),
                 reads=[ps, brow], writes=[mrow])
        psc = self.ps[2]
        for c in range(72):
            P.mm(psc[:, 2 * c:2 * c + 2], mrow[0:2, c * 128:(c + 1) * 128], self.ident_f[0:2, 0:2],
                 reads=[mrow, self.ident_f], writes=[psc])
        mcol = self.mcol
        P.op("dve", lambda g: g.tensor_copy(out=mcol[:].rearrange("p c s -> p (c s)"), in_=psc[:, 0:144]),
             reads=[psc], writes=[mcol])
        gcol = P.sbuf([128, 6, 8], F32, "gcol")
        P.dma(gcol[:].rearrange("p n k -> p (n k)"), d["norm_gT"][l], writes=[gcol])
        mc4 = mcol[:].rearrange("p (j k) s -> p j k s", k=8)
        tmp = P.sbuf([128, 8, 2], F32, "abtmp")
        for sub in range(3):
            jsh, jsc, npre = 3 * sub, 3 * sub + 1, 2 * sub
            P.op("dve", lambda g: g.tensor_scalar(out=tmp[:], in0=mc4[:, jsc, :, :], scalar1=1.0, scalar2=None,
                                                  op0=ALU.add), reads=[mcol], writes=[tmp])
            P.op("dve", lambda g: g.tensor_tensor(out=self.AB[:, sub, 0, :, :], in0=tmp[:],
                                                  in1=gcol[:, npre, :].unsqueeze(2).to_broadcast([128, 8, 2]),
                                                  op=ALU.mult), reads=[tmp, gcol], writes=[self.AB])
            P.op("dve", lambda g: g.tensor_copy(out=self.AB[:, sub, 1, :, :], in_=mc4[:, jsh, :, :]),
                 reads=[mcol], writes=[self.AB])
        grow = [P.sbuf([2, DM], F32, "grow%d" % i) for i in range(3)]
        gto = [P.sbuf([2, DM], F32, "gto%d" % i) for i in range(3)]
        for sub in range(3):
            jg, npost = 3 * sub + 2, 2 * sub + 1
            fac = 1.0 if sub == 1 else 0.5
            for s in range(2):
                P.dma(grow[sub][s:s + 1, :], d["norm_g"][l, npost:npost + 1, :], writes=[grow[sub]])
            P.op("dve", lambda g: g.scalar_tensor_tensor(out=gto[sub][:], in0=mrow[:, jg * DM:(jg + 1) * DM],
                                                         scalar=fac, in1=grow[sub][:], op0=ALU.mult,
                                                         op1=ALU.mult),
                 reads=[mrow, grow[sub]], writes=[gto[sub]])
            P.dma(d["gtrow"][l, sub], gto[sub][:], reads=[gto[sub]], writes=[self.db("gtrow", (l, sub))])
        P.release()

    def norm_T(self, xt_t, x_ap, sub, s, xnT_t, xnT_ap, pst, wk):
        self.norm_A(xt_t, x_ap, wk)
        self.norm_B(sub, s, xnT_t, xnT_ap, pst, wk)

    def norm_A(self, xt_t, x_ap, wk):
        P = self.P
        junk, ss, rs, xs = wk
        P.op("act", lambda g: g.activation(out=junk[:], in_=x_ap, func=AF.Square, accum_out=ss[:]),
             reads=[xt_t], writes=[junk, ss])
        self.rstd_from_ss((ss, ss[:]), DM, EPS, (rs, rs[:]))
        P.op("dve", lambda g: g.tensor_scalar(out=xs[:], in0=x_ap, scalar1=rs[:, 0:1], scalar2=None,
                                              op0=ALU.mult), reads=[xt_t, rs], writes=[xs])

    def norm_B(self, sub, s, xnT_t, xnT_ap, pst, wk):
        P = self.P
        junk, ss, rs, xs = wk
        pb = self.psb(pst)
        for k in range(8):
            P.op("pe", lambda g: g.transpose(pb[:, k * 128:(k + 1) * 128], xs[:, k * 128:(k + 1) * 128],
                                             self.ident_b[:]),
                 reads=[xs, self.ident_b], writes=[self.ps[pst]])
        for k in range(8):
            A = self.AB[:, sub, 0, k, s:s + 1]
            B = self.AB[:, sub, 1, k, s:s + 1]
            if k % 2 == 0:
                P.op("dve", lambda g: g.tensor_scalar(out=xnT_ap[:, k, :], in0=pb[:, k * 128:(k + 1) * 128],
                                                      scalar1=A, scalar2=B, op0=ALU.mult, op1=ALU.add),
                     reads=[self.ps[pst], self.AB], writes=[xnT_t])
            else:
                P.op("act", lambda g: g.activation(out=xnT_ap[:, k, :], in_=pb[:, k * 128:(k + 1) * 128],
                                                   func=AF.Identity, scale=A, bias=B),
                     reads=[self.ps[pst], self.AB], writes=[xnT_t])

    def norm_res_out(self, pso, xt_t, x_ap, gt, wk2, dst_ap, dst_buf):
        P = self.P
        junk, s2, r2, tmp = wk2
        for h in range(2):
            P.op("act", lambda g: g.activation(out=junk[:, 0:512], in_=self.ps[pso[h]][:], func=AF.Square,
                                               accum_out=s2[:, h:h + 1]),
                 reads=[self.ps[pso[h]]], writes=[junk, s2])
        P.op("dve", lambda g: g.tensor_tensor(out=s2[:, 2:3], in0=s2[:, 0:1], in1=s2[:, 1:2], op=ALU.add),
             reads=[s2], writes=[s2])
        self.rstd_from_ss((s2, s2[:, 2:3]), DM, EPS, (r2, r2[:]))
        for h in range(2):
            P.op("dve", lambda g: g.scalar_tensor_tensor(out=tmp[:, h * 512:(h + 1) * 512],
                                                         in0=self.ps[pso[h]][:], scalar=r2[:, 0:1],
                                                         in1=gt[:, h * 512:(h + 1) * 512],
                                                         op0=ALU.mult, op1=ALU.mult),
                 reads=[self.ps[pso[h]], r2, gt], writes=[tmp])
        P.op("pool", lambda g: g.tensor_tensor(out=x_ap, in0=x_ap, in1=tmp[:], op=ALU.add),
             reads=[tmp, xt_t], writes=[xt_t])
        P.dma(dst_ap, x_ap, reads=[xt_t], writes=[dst_buf])

    def phase_ffn(self, l, sub, src, dst, ntiles):
        P = self.P
        d = self.d
        wi = 0 if sub == 0 else 1
        P.mark()
        w13 = P.sbuf([128, 8, 2 * FF], BF16, "w13")
        w13b = [Buf("w13_%d" % k) for k in range(8)]
        for k in range(8):
            P.dma(w13[:, k, :], d["ffn_w13"][l, wi, k * 128:(k + 1) * 128, :], writes=[w13b[k]], q="pool")
        w2 = P.sbuf([128, NFC, DM], BF16, "w2")
        w2b = [Buf("w2_%d" % j) for j in range(NFC)]
        for j in range(NFC):
            P.dma(w2[:, j, :], d["ffn_w2"][l, wi, j * 128:(j + 1) * 128, :], writes=[w2b[j]], q="pool")
        gt = [P.sbuf([128, DM], F32, "gt%d" % s) for s in range(2)]
        for s in range(2):
            P.dma(gt[s][:], d["gtrow"][l, sub, s:s + 1, :].to_broadcast([128, DM]),
                  reads=[self.db("gtrow", (l, sub))], writes=[gt[s]])
        xbuf = [P.sbuf([128, 2, DM], F32, "xbuf%d" % i) for i in range(2)]
        xbb = [[Buf("xb%d_%d" % (i, j)) for j in range(2)] for i in range(2)]
        xnT = [P.sbuf([128, 8, 256], BF16, "xnT%d" % i) for i in range(2)]
        hT = P.sbuf([128, NFC, 256], BF16, "hT")
        hTb = [Buf("hT%d" % j) for j in range(NFC)]
        junk = P.sbuf([128, DM], BF16, "junk")
        junk2 = P.sbuf([128, 512], BF16, "junk2")
        s2 = [P.sbuf([128, 3], F32, "s2%d" % i) for i in range(2)]
        r2 = [P.sbuf([128, 1], F32, "r2%d" % i) for i in range(2)]
        tmp = [P.sbuf([128, DM], F32, "tmp%d" % i) for i in range(2)]
        sa = [P.sbuf([128, 256], F32, "sa%d" % i) for i in range(2)]
        ngroups = ntiles // 2
        src_ap, src_name = src
        dst_ap, dst_name = dst

        def load(gi):
            for i in range(2):
                t = 2 * gi + i
                xt = Tile(xbuf[gi % 2].t, xbb[gi % 2][i])
                P.dma(xbuf[gi % 2][:, i, :], src_ap[t * 128:(t + 1) * 128, :],
                      reads=[self.db(src_name, t)], writes=[xt])

        xs4 = [P.sbuf([128, DM], BF16, "xs4_%d" % i) for i in range(4)]
        ss4 = [P.sbuf([128, 1], F32, "ss4_%d" % i) for i in range(4)]
        rs4 = [P.sbuf([128, 1], F32, "rs4_%d" % i) for i in range(4)]

        def wkof(gi, i):
            k = (gi % 2) * 2 + i
            return (junk, ss4[k], rs4[k], xs4[k])

        def normA(gi):
            for i in range(2):
                xt = Tile(xbuf[gi % 2].t, xbb[gi % 2][i])
                self.norm_A(xt, xbuf[gi % 2][:, i, :], wkof(gi, i))

        def normB(gi):
            s_ = 1 if 2 * gi >= 32 else 0
            for i in range(2):
                self.norm_B(sub, s_, xnT[gi % 2], xnT[gi % 2][:, :, i * 128:(i + 1) * 128], 0 if i == 0 else 7,
                            wkof(gi, i))

        load(0)
        normA(0)
        normB(0)
        for gi in range(ngroups):
            if gi + 1 < ngroups:
                load(gi + 1)
            xb = xbuf[gi % 2]
            xn = xnT[gi % 2]
            s = 1 if 2 * gi >= 32 else 0
            for j in range(NFC):
                pa = self.ps[1 + (j % 2) * 2]
                pbk = self.ps[2 + (j % 2) * 2]
                for k in range(8):
                    P.mm(pa[:, 0:256], w13[:, k, j * 128:(j + 1) * 128], xn[:, k, :], start=(k == 0),
                         stop=(k == 7), reads=[w13b[k], xn], writes=[pa])
                for k in range(8):
                    P.mm(pbk[:, 0:256], w13[:, k, FF + j * 128:FF + (j + 1) * 128], xn[:, k, :], start=(k == 0),
                         stop=(k == 7), reads=[w13b[k], xn], writes=[pbk])
                sj = sa[j % 2]
                P.op("act", lambda g: g.activation(out=sj[:], in_=pa[:, 0:256], func=AF.Silu),
                     reads=[pa], writes=[sj])
                P.op("dve", lambda g: g.tensor_tensor(out=hT[:, j, :], in0=sj[:], in1=pbk[:, 0:256], op=ALU.mult),
                     reads=[sj, pbk], writes=[hTb[j]])
            if gi + 1 < ngroups:
                normA(gi + 1)
            for i in range(2):
                t = 2 * gi + i
                xt = Tile(xb.t, xbb[gi % 2][i])
                for h in range(2):
                    po = self.ps[5 + h]
                    for j in range(NFC):
                        P.mm(po[:], hT[:, j, i * 128:(i + 1) * 128], w2[:, j, h * 512:(h + 1) * 512],
                             start=(j == 0), stop=(j == NFC - 1), reads=[hTb[j], w2b[j]], writes=[po])
                self.norm_res_out([5, 6], xt, xb[:, i, :], gt[s], (junk2, s2[i], r2[i], tmp[i]),
                                  dst_ap[t * 128:(t + 1) * 128, :], self.db(dst_name, t))
            if gi + 1 < ngroups:
                normB(gi + 1)
        P.release()


def host_shared(inp):
    f32 = np.float32
    sh = {}
    sh["ident"] = np.eye(128, dtype=f32)
    for k in ("w_mod", "b_mod", "norm_g", "ffn_w13", "ffn_w2"):
        sh[k] = np.ascontiguousarray(inp[k], dtype=f32)
    ng = np.asarray(inp["norm_g"], dtype=f32)
    sh["norm_gT"] = np.ascontiguousarray(ng.reshape(2, 6, 8, 128).transpose(0, 3, 1, 2).reshape(2, 128, 48))
    sh["w_ext"] = build_w_ext(inp["w_in"])
    cm, sm = rope_tables(32)
    sh["rope_m"] = np.ascontiguousarray(np.stack([cm, sm], 0))
    cs, ss_ = rope_tables(64)
    sh["rope_s"] = np.ascontiguousarray(np.stack([np.concatenate([cs, cs], 0), np.concatenate([ss_, ss_], 0)], 0))
    nq = np.asarray(inp["mla_norm_q"], f32)
    nkv = np.asarray(inp["mla_norm_kv"], f32)
    sh["mla_nT"] = np.ascontiguousarray(np.stack([nq[:, 0:128], nq[:, 128:256], nkv], axis=2))
    wuq = np.asarray(inp["mla_w_uq"], f32).reshape(2, 256, 8, 96)
    pm, _ = rope_partner(32)
    sw = np.concatenate([wuq[..., 0:64], wuq[..., 64 + pm]], axis=-1)
    sh["mla_wq2"] = np.ascontiguousarray(np.stack([wuq, sw], axis=3).reshape(2, 256, 8 * 2 * 96))
    wukv = np.asarray(inp["mla_w_ukv"], f32).reshape(2, 128, 8, 128)
    sh["mla_wk"] = np.ascontiguousarray(wukv[..., 0:64].reshape(2, 128, 512))
    sh["mla_wv"] = np.ascontiguousarray(wukv[..., 64:128].reshape(2, 128, 512))
    sh["swa_sink"] = np.ascontiguousarray(inp["swa_sink"], dtype=f32)
    sh["w_branch"] = np.ascontiguousarray(inp["w_branch"], dtype=f32)
    for k in ("hyena_conv", "hyena_conv_b", "hyena_w1", "hyena_w2", "hyena_w3", "hyena_bias"):
        sh[k] = np.ascontiguousarray(inp[k], dtype=f32)
    sh["rwkv_mu"] = np.ascontiguousarray(inp["rwkv_mu"], dtype=f32)
    sh["rwkv_kvec"] = np.ascontiguousarray(inp["rwkv_kvec"], dtype=f32)
    sh["rwkv_lnp"] = np.ascontiguousarray(np.stack([inp["rwkv_ln_g"], inp["rwkv_ln_b"], inp["rwkv_r_k"]], axis=1), dtype=f32)
    wup = np.asarray(inp["rwkv_w_up"], f32)
    w0 = np.asarray(inp["rwkv_w0"], f32)
    sh["rwkv_wupA"] = np.ascontiguousarray(np.concatenate([wup.transpose(0, 2, 1, 3).reshape(2, 64, 1024),
                                                           w0.reshape(2, 1, 1024)], axis=1))
    aup = np.asarray(inp["rwkv_a_up"], f32)
    a0 = np.asarray(inp["rwkv_a0"], f32)
    sh["rwkv_aupA"] = np.ascontiguousarray(np.concatenate([aup.transpose(0, 2, 1, 3).reshape(2, 64, 1024),
                                                           a0.reshape(2, 1, 1024)], axis=1))
    sh["rwkv_g_up"] = np.ascontiguousarray(inp["rwkv_g_up"], dtype=f32)
    sh["rw_tri"] = rwkv_tables()
    hf = np.asarray(inp["hyena_freq"], f32)
    sh["hyT"] = np.ascontiguousarray(np.stack([hf[:, 0], hf[:, 1], np.asarray(inp["hyena_b1"], f32),
                                               np.asarray(inp["hyena_b2"], f32)], axis=2))
    tl = hy_tables(NLAT)
    tc = hy_tables(NCTX)
    sh["hy_D"] = np.ascontiguousarray(np.stack([tl["D2"], tl["D2sw"], tl["E"]], 0))
    sh["hyL_W1"] = np.ascontiguousarray(tl["W1"].reshape(128, -1))
    sh["hyL_W3"] = np.ascontiguousarray(tl["W3"].reshape(65, -1))
    sv = np.arange(512)
    angc = 2.0 * np.pi * ((sv[:, None] * sv[None, :]) % 512) / 512.0
    dft = np.stack([np.cos(angc), -np.sin(angc)], 0).astype(f32)
    sh["hyC_DFT"] = np.ascontiguousarray(dft.reshape(2, 4, 128, 512).transpose(0, 2, 1, 3))
    sh["hyL_fK"], sh["hyL_wK"] = hy_feats(NLAT)
    sh["hyC_fK"], sh["hyC_wK"] = hy_feats(NCTX)
    sh["w_out"] = np.ascontiguousarray(inp["w_out"], dtype=f32)
    bgt = np.asarray(inp["b_gate"], f32).reshape(2, 4, 8, 128).transpose(0, 3, 1, 2).reshape(2, 128, 32)
    sh["b_gateT"] = np.ascontiguousarray(bgt)
    kk = np.arange(128)[:, None]
    qq = np.arange(128)[None, :]
    sh["swa_mask"] = np.ascontiguousarray(np.stack([(qq <= kk), (kk <= qq)], 0).astype(f32))
    return sh


def host_core(inp, b):
    f32 = np.float32
    pc = {}
    pc["xall"] = np.ascontiguousarray(np.concatenate([inp["x"][b], inp["ctx"][b]], axis=0), dtype=f32)
    cv = np.stack([np.asarray(inp["c"][b], f32), np.asarray(inp["c_ctx"], f32)], axis=0)
    pc["cT"] = np.ascontiguousarray(cv.reshape(2, 8, 128).transpose(2, 1, 0).reshape(128, 16))
    return pc


G0, PA0, PB0, PH0, PD0 = 0, 4096, 4512, 6304, 7840
FM_COLS = 1728
TM_COLS = 3456
WX_FM0 = 0
WX_TM0 = FM_COLS
WX_G0 = FM_COLS + TM_COLS
WX_COLS = WX_G0 + 4096
PAD_ROWS = TT + 3


def tm_row(t):
    return 1 + t if t < NLAT else 2 + t


def rope_partner(R):
    H = R // 2
    q = H // 2
    part = np.zeros(R, np.int64)
    sign = np.zeros(R, np.float32)
    for dd in range(R):
        base = (dd // H) * H
        o = dd % H
        if o < q:
            part[dd] = base + o + q
            sign[dd] = -1.0
        else:
            part[dd] = base + o - q
            sign[dd] = 1.0
    return part, sign


def rope_tables(R):
    H = R // 2
    q = H // 2
    t = np.arange(NLAT)
    row = (t // 64).astype(np.float32)
    col = (t % 64).astype(np.float32)
    inv = (10000.0 ** (-np.arange(0, H, 2, dtype=np.float32) / H)).astype(np.float32)
    _, sign = rope_partner(R)
    cos = np.zeros((R, NLAT), np.float32)
    sin = np.zeros((R, NLAT), np.float32)
    for dd in range(R):
        pos = row if dd < H else col
        ang = (pos * inv[(dd % H) % q]).astype(np.float32)
        cos[dd] = np.cos(ang)
        sin[dd] = sign[dd] * np.sin(ang)
    return cos, sin


def build_w_ext(w_in):
    pm, _ = rope_partner(32)
    ps_, _ = rope_partner(64)
    cols = []
    cols += list(range(PA0, PA0 + 384))
    kr0 = PA0 + 384
    cols += [kr0 + i for i in range(32)]
    cols += [kr0 + int(pm[i]) for i in range(32)]
    q0 = PD0
    cols += [q0 + i for i in range(512)]
    cols += [q0 + (i // 64) * 64 + int(ps_[i % 64]) for i in range(512)]
    k0 = PD0 + 512
    cols += [k0 + i for i in range(128)]
    cols += [k0 + (i // 64) * 64 + int(ps_[i % 64]) for i in range(128)]
    assert len(cols) == FM_COLS
    cols += list(range(PB0, PB0 + 1792))
    cols += list(range(PH0, PH0 + 1536))
    cols += list(range(PD0 + 640, PD0 + 768))
    assert len(cols) == FM_COLS + TM_COLS
    cols += list(range(0, 4096))
    return np.ascontiguousarray(np.asarray(w_in, np.float32)[:, :, np.asarray(cols)])


def _pin_methods():
    def declare_pin(self):
        L = 2
        self.inp("w_ext", [L, DM, WX_COLS])
        self.inp("rope_m", [2, 32, NLAT])
        self.inp("rope_s", [2, 128, NLAT])
        self.scr("uT", [8, 128, TT], BF16)
        self.scr("cqkvT", [3, 128, TT], BF16)
        self.scr("krT", [32, TT], BF16)
        self.scr("sqT", [4, 128, TT], BF16)
        self.scr("skT", [128, TT], BF16)
        self.scr("pb", [PAD_ROWS, 1792])
        self.scr("ph", [PAD_ROWS, 1536])
        self.scr("pv", [TT, 128])

    def phase_pin(self, l, src):
        P = self.P
        d = self.d
        src_ap, src_name = src
        P.mark()
        NW = FM_COLS + TM_COLS
        w = P.sbuf([128, 8, NW], BF16, "wpin")
        wb = [Buf("wpin%d" % k) for k in range(8)]
        for k in range(8):
            P.dma(w[:, k, :], d["w_ext"][l, k * 128:(k + 1) * 128, 0:NW], writes=[wb[k]], q="pool")
        z = P.sbuf([1, 1792], F32, "zrow")
        P.op("pool", lambda g: g.memset(z[:], 0.0), writes=[z])
        for r in (0, NLAT + 1, TT + 2):
            P.dma(d["pb"][r:r + 1, :], z[:], reads=[z], writes=[self.db("pb", "pad%d" % r)])
            P.dma(d["ph"][r:r + 1, :], z[:, 0:1536], reads=[z], writes=[self.db("ph", "pad%d" % r)])
        xbuf = [P.sbuf([128, 4, DM], F32, "xbuf%d" % i) for i in range(2)]
        xbb = [[Buf("xb%d_%d" % (i, j)) for j in range(4)] for i in range(2)]
        uT = [P.sbuf([128, 8, 512], BF16, "uT%d" % i) for i in range(2)]
        junk = P.sbuf([128, DM], BF16, "junk")
        ss = [P.sbuf([128, 1], F32, "ss%d" % i) for i in range(8)]
        rs = [P.sbuf([128, 1], F32, "rs%d" % i) for i in range(8)]
        xs = [P.sbuf([128, DM], BF16, "xs%d" % i) for i in range(8)]
        tabm = [P.sbuf([32, 2, 512], F32, "tabm%d" % i) for i in range(2)]
        tabs = [P.sbuf([128, 2, 512], F32, "tabs%d" % i) for i in range(2)]
        t1a = P.sbuf([128, 512], F32, "t1_0")
        t2a = P.sbuf([128, 512], F32, "t2_0")
        t1 = [t1a, t1a]
        t2 = [t2a, t2a]
        fo = [P.sbuf([128, 512], BF16, "fo%d" % i) for i in range(3)]
        tmo = [P.sbuf([128, TM_COLS], F32, "tmo%d" % i) for i in range(2)]
        groups = [list(range(4 * g, 4 * g + 4)) for g in range(8)] + [[32, 33]]

        def load(gi):
            for i, t in enumerate(groups[gi]):
                xt = Tile(xbuf[gi % 2].t, xbb[gi % 2][i])
                P.dma(xbuf[gi % 2][:, i, :], src_ap[t * 128:(t + 1) * 128, :],
                      reads=[self.db(src_name, t)], writes=[xt])
            if gi < 8:
                P.dma(tabm[gi % 2][:], d["rope_m"][:, :, gi * 512:(gi + 1) * 512].rearrange("c p t -> p c t"),
                      writes=[tabm[gi % 2]])
                P.dma(tabs[gi % 2][:], d["rope_s"][:, :, gi * 512:(gi + 1) * 512].rearrange("c p t -> p c t"),
                      writes=[tabs[gi % 2]])

        def wkof(gi, i):
            k = (gi % 2) * 4 + i
            return (junk, ss[k], rs[k], xs[k])

        def normA(gi):
            for i, t in enumerate(groups[gi]):
                xt = Tile(xbuf[gi % 2].t, xbb[gi % 2][i])
                self.norm_A(xt, xbuf[gi % 2][:, i, :], wkof(gi, i))

        def normB(gi):
            s_ = 0 if gi < 8 else 1
            for i, t in enumerate(groups[gi]):
                self.norm_B(1, s_, uT[gi % 2], uT[gi % 2][:, :, i * 128:(i + 1) * 128], 0 if i % 2 == 0 else 7,
                            wkof(gi, i))

        load(0)
        normA(0)
        normB(0)
        nfo = 0
        nev = 0
        for gi, tl in enumerate(groups):
            if gi + 1 < len(groups):
                load(gi + 1)
            n = 128 * len(tl)
            t0 = tl[0] * 128
            lat = gi < 8
            s = 0 if lat else 1
            xb = xbuf[gi % 2]
            u = uT[gi % 2]
            P.dma(d["uT"][:, :, t0:t0 + n].rearrange("k p t -> p k t"), u[:, :, 0:n], reads=[u],
                  writes=[self.db("uT", gi)])

            def fm_mm(ps, c0, m):
                for k in range(8):
                    P.mm(ps[0:m, 0:n], w[:, k, c0:c0 + m], u[:, k, 0:n], start=(k == 0), stop=(k == 7),
                         reads=[wb[k], u], writes=[ps])

            for c in range(3):
                ps = self.ps[1 + (c % 2) * 2]
                fm_mm(ps, c * 128, 128)
                o = fo[nfo % 3]
                nfo += 1
                P.op("act", lambda g: g.activation(out=o[:, 0:n], in_=ps[:, 0:n], func=AF.Copy),
                     reads=[ps], writes=[o])
                P.dma(d["cqkvT"][c, :, t0:t0 + n], o[:, 0:n], reads=[o], writes=[self.db("cqkvT", (c, gi))])
            roped = [(384, 416, 32, tabm, d["krT"][:, t0:t0 + n], ("krT", gi))]
            for c in range(4):
                roped.append((448 + c * 128, 960 + c * 128, 128, tabs, d["sqT"][c, :, t0:t0 + n], ("sqT", (c, gi))))
            roped.append((1472, 1600, 128, tabs, d["skT"][:, t0:t0 + n], ("skT", gi)))
            for ri, (cx, csw, m, tab, dst, dk) in enumerate(roped):
                psx = self.ps[1 + (ri % 2) * 2]
                fm_mm(psx, cx, m)
                o = fo[nfo % 3]
                nfo += 1
                if lat:
                    pss = self.ps[2 + (ri % 2) * 2]
                    fm_mm(pss, csw, m)
                    tb = tab[gi % 2]
                    a1 = t1[ri % 2]
                    a2 = t2[ri % 2]
                    P.op("dve", lambda g: g.tensor_tensor(out=a1[0:m, :], in0=psx[0:m, :], in1=tb[0:m, 0, :],
                                                          op=ALU.mult), reads=[psx, tb], writes=[a1])
                    P.op("dve", lambda g: g.tensor_tensor(out=a2[0:m, :], in0=pss[0:m, :], in1=tb[0:m, 1, :],
                                                          op=ALU.mult), reads=[pss, tb], writes=[a2])
                    P.op("pool", lambda g: g.tensor_tensor(out=o[0:m, :], in0=a1[0:m, :], in1=a2[0:m, :],
                                                           op=ALU.add), reads=[a1, a2], writes=[o])
                else:
                    P.op("act", lambda g: g.activation(out=o[0:m, 0:n], in_=psx[0:m, 0:n], func=AF.Copy),
                         reads=[psx], writes=[o])
                P.dma(dst, o[0:m, 0:n], reads=[o], writes=[self.db(*dk)])
            if gi + 1 < len(groups):
                normA(gi + 1)
            for i, t in enumerate(tl):
                st = tmo[i % 2]
                for cb in range(7):
                    c0 = cb * 512
                    cw = min(512, TM_COLS - c0)
                    ps = self.ps[5 + (cb % 2)]
                    for k in range(8):
                        P.mm(ps[:, 0:cw], u[:, k, i * 128:(i + 1) * 128], w[:, k, FM_COLS + c0:FM_COLS + c0 + cw],
                             start=(k == 0), stop=(k == 7), reads=[u, wb[k]], writes=[ps])
                    if nev % 2 == 0:
                        P.op("act", lambda g: g.activation(out=st[:, c0:c0 + cw], in_=ps[:, 0:cw], func=AF.Copy),
                             reads=[ps], writes=[st])
                    else:
                        P.op("dve", lambda g: g.tensor_copy(out=st[:, c0:c0 + cw], in_=ps[:, 0:cw]),
                             reads=[ps], writes=[st])
                    nev += 1
                r0 = tm_row(t * 128)
                P.dma(d["pb"][r0:r0 + 128, :], st[:, 0:1792], reads=[st], writes=[self.db("pb", t)])
                P.dma(d["ph"][r0:r0 + 128, :], st[:, 1792:3328], reads=[st], writes=[self.db("ph", t)])
                P.dma(d["pv"][t * 128:(t + 1) * 128, :], st[:, 3328:3456], reads=[st], writes=[self.db("pv", t)])
            if gi + 1 < len(groups):
                normB(gi + 1)
        P.release()

    KB.declare_pin = declare_pin
    KB.phase_pin = phase_pin


_pin_methods()


def _attn_methods():
    def declare_attn(self):
        L = 2
        self.inp("mla_nT", [L, 128, 3])
        self.inp("mla_wq2", [L, 256, 8 * 2 * 96])
        self.inp("mla_wk", [L, 128, 512])
        self.inp("mla_wv", [L, 128, 512])
        self.inp("swa_sink", [L, 8])
        self.inp("swa_mask", [2, 128, 128])
        self.scr("yT", [4, 512, TT], BF16)

    def phase_mla(self, l, with_ctx):
        P = self.P
        d = self.d
        P.mark()
        scale = 96.0 ** -0.5
        wq = P.sbuf([128, 2, 8, 2, 96], BF16, "wq")
        for c in range(2):
            P.dma(wq[:, c].rearrange("p h s m -> p (h s m)"), d["mla_wq2"][l, c * 128:(c + 1) * 128, :],
                  writes=[wq], q="pool")
        wk = P.sbuf([128, 8, 64], BF16, "wk")
        P.dma(wk[:].rearrange("p h m -> p (h m)"), d["mla_wk"][l], writes=[wk], q="pool")
        wv = P.sbuf([128, 512], BF16, "wv")
        P.dma(wv[:], d["mla_wv"][l], writes=[wv], q="pool")
        nT = P.sbuf([128, 3], F32, "nT")
        P.dma(nT[:], d["mla_nT"][l], writes=[nT])
        ones_f = P.sbuf([128, 128], F32, "ones_f")
        P.op("pool", lambda g: g.memset(ones_f[:], 1.0), writes=[ones_f])
        cqn = P.sbuf([128, 2, TT], BF16, "cqn")
        ckvn = P.sbuf([128, TT], BF16, "ckvn")
        vaug = P.sbuf([128, NTILE, 8, 128], BF16, "vaug")
        P.op("pool", lambda g: g.memset(vaug[:, :, :, 64:128], 1.0), writes=[vaug])
        groups = [(g * 512, 512) for g in range(8)] + [(NLAT, 256)]
        P.mark()
        xin = [P.sbuf([128, 3, 512], BF16, "xin%d" % i) for i in range(2)]
        sq = [P.sbuf([128, 3, 512], F32, "sq%d" % i) for i in range(2)]
        rsb = [P.sbuf([128, 2, 512], F32, "rsb%d" % i) for i in range(2)]
        for gi, (t0, n) in enumerate(groups):
            xi = xin[gi % 2]
            P.dma(xi[:, :, 0:n], d["cqkvT"][:, :, t0:t0 + n].rearrange("c p t -> p c t"),
                  reads=[self.db("cqkvT", (c, gi)) for c in range(3)], writes=[xi])
            sqi = sq[gi % 2]
            P.op("pool", lambda g: g.tensor_tensor(out=sqi[:, :, 0:n], in0=xi[:, :, 0:n], in1=xi[:, :, 0:n],
                                                   op=ALU.mult), reads=[xi], writes=[sqi])
            psq = self.ps[6]
            psk = self.ps[7]
            for c in range(2):
                P.mm(psq[:, 0:n], ones_f[:], sqi[:, c, 0:n], start=(c == 0), stop=(c == 1),
                     reads=[ones_f, sqi], writes=[psq])
            P.mm(psk[:, 0:n], ones_f[:], sqi[:, 2, 0:n], reads=[ones_f, sqi], writes=[psk])
            r = rsb[gi % 2]
            P.op("act", lambda g: g.activation(out=r[:, 0, 0:n], in_=psq[:, 0:n], func=AF.Sqrt, scale=1.0 / 256,
                                               bias=EPS), reads=[psq], writes=[r])
            P.op("act", lambda g: g.activation(out=r[:, 1, 0:n], in_=psk[:, 0:n], func=AF.Sqrt, scale=1.0 / 128,
                                               bias=EPS), reads=[psk], writes=[r])
            P.op("dve", lambda g: g.reciprocal(out=r[:, :, 0:n], in_=r[:, :, 0:n]), reads=[r], writes=[r])
            for c in range(2):
                P.op("dve", lambda g: g.scalar_tensor_tensor(out=cqn[:, c, t0:t0 + n], in0=xi[:, c, 0:n],
                                                             scalar=nT[:, c:c + 1], in1=r[:, 0, 0:n],
                                                             op0=ALU.mult, op1=ALU.mult),
                     reads=[xi, nT, r], writes=[cqn])
            P.op("dve", lambda g: g.scalar_tensor_tensor(out=ckvn[:, t0:t0 + n], in0=xi[:, 2, 0:n],
                                                         scalar=nT[:, 2:3], in1=r[:, 1, 0:n],
                                                         op0=ALU.mult, op1=ALU.mult),
                 reads=[xi, nT, r], writes=[ckvn])
        P.release()
        for t in range(NTILE):
            ps = self.ps[5 + t % 2]
            P.mm(ps[:], ckvn[:, t * 128:(t + 1) * 128], wv[:], reads=[ckvn, wv], writes=[ps])
            eng = "act" if t % 2 == 0 else "dve"
            if eng == "act":
                P.op("act", lambda g: g.activation(out=vaug[:, t, :, 0:64],
                                                   in_=ps[:].rearrange("p (h m) -> p h m", m=64), func=AF.Copy),
                     reads=[ps], writes=[vaug])
            else:
                P.op("dve", lambda g: g.tensor_copy(out=vaug[:, t, :, 0:64],
                                                    in_=ps[:].rearrange("p (h m) -> p h m", m=64)),
                     reads=[ps], writes=[vaug])
        NQ = TT if with_ctx else NLAT
        KT = [P.sbuf([96, TT], BF16, "KT%d" % i) for i in range(2)]
        QT = [P.sbuf([96, TT], BF16, "QT%d" % i) for i in range(2)]
        tab = [P.sbuf([96, 2, 512], F32, "tab%d" % i) for i in range(2)]
        a1 = [P.sbuf([96, 512], F32, "a1_%d" % i) for i in range(2)]
        a2 = [P.sbuf([96, 512], F32, "a2_%d" % i) for i in range(2)]
        PT = [P.sbuf([128, 512], BF16, "PT%d" % i) for i in range(4)]
        rec = [P.sbuf([64, 512], F32, "rec%d" % i) for i in range(2)]
        yo = [P.sbuf([64, 512], BF16, "yo%d" % i) for i in range(2)]
        npt = 0
        nqg = 0
        for h in range(8):
            kt = KT[h % 2]
            qt = QT[h % 2]
            P.dma(kt[64:96, :], d["krT"], reads=[self.db("krT", gi) for gi in range(9)], writes=[kt])
            for gi, (t0, n) in enumerate(groups):
                lat = gi < 8
                if gi >= 8 and not with_ctx:
                    pass
                pk = self.ps[5]
                P.mm(pk[0:64, 0:n], wk[:, h, :], ckvn[:, t0:t0 + n], reads=[wk, ckvn], writes=[pk])
                P.op("act", lambda g: g.activation(out=kt[0:64, t0:t0 + n], in_=pk[0:64, 0:n], func=AF.Copy),
                     reads=[pk], writes=[kt])
                if gi >= 8 and not with_ctx:
                    continue
                p1 = self.ps[6]
                for c in range(2):
                    P.mm(p1[0:96, 0:n], wq[:, c, h, 0, :], cqn[:, c, t0:t0 + n], start=(c == 0), stop=(c == 1),
                         reads=[wq, cqn], writes=[p1])
                if lat:
                    p2 = self.ps[7]
                    for c in range(2):
                        P.mm(p2[0:96, 0:n], wq[:, c, h, 1, :], cqn[:, c, t0:t0 + n], start=(c == 0),
                             stop=(c == 1), reads=[wq, cqn], writes=[p2])
                    tb = tab[gi % 2]
                    P.dma(tb[64:96, :, :], d["rope_m"][:, :, t0:t0 + n].rearrange("c p t -> p c t"), writes=[tb])
                    P.op("act", lambda g: g.activation(out=qt[0:64, t0:t0 + n], in_=p1[0:64, 0:n], func=AF.Copy),
                         reads=[p1], writes=[qt])
                    b1 = a1[gi % 2]
                    b2 = a2[gi % 2]
                    P.op("dve", lambda g: g.tensor_tensor(out=b1[64:96, :], in0=p1[64:96, :], in1=tb[64:96, 0, :],
                                                          op=ALU.mult), reads=[p1, tb], writes=[b1])
                    P.op("dve", lambda g: g.tensor_tensor(out=b2[64:96, :], in0=p2[64:96, :], in1=tb[64:96, 1, :],
                                                          op=ALU.mult), reads=[p2, tb], writes=[b2])
                    P.op("pool", lambda g: g.tensor_tensor(out=qt[64:96, t0:t0 + n], in0=b1[64:96, :],
                                                           in1=b2[64:96, :], op=ALU.add),
                         reads=[b1, b2], writes=[qt])
                else:
                    P.op("act", lambda g: g.activation(out=qt[0:96, t0:t0 + n], in_=p1[0:96, 0:n], func=AF.Copy),
                         reads=[p1], writes=[qt])
            qgroups = [(g * 512, 512, list(range(NTILE))) for g in range(8)]
            if with_ctx:
                qgroups.append((NLAT, 256, [32, 33]))
            for (q0, n, kbs) in qgroups:
                po = self.ps[3 + nqg % 2]
                nqg += 1
                pend = []

                def pv(item, first, last):
                    kb, pt = item
                    P.mm(po[:, 0:n], vaug[:, kb, h, :], pt[:, 0:n], start=first, stop=last,
                         reads=[vaug, pt], writes=[po])

                for idx, kb in enumerate(kbs):
                    pss = self.ps[npt % 3]
                    pt = PT[npt % 4]
                    npt += 1
                    P.mm(pss[:, 0:n], kt[:, kb * 128:(kb + 1) * 128], qt[:, q0:q0 + n], reads=[kt, qt],
                         writes=[pss])
                    P.op("act", lambda g: g.activation(out=pt[:, 0:n], in_=pss[:, 0:n], func=AF.Exp, scale=scale),
                         reads=[pss], writes=[pt])
                    pend.append((kb, pt))
                    if len(pend) > 2:
                        pv(pend.pop(0), idx == 2, False)
                while pend:
                    first = (len(kbs) - len(pend) == 0)
                    pv(pend.pop(0), first, len(pend) == 0)
                rc = rec[nqg % 2]
                y = yo[nqg % 2]
                P.op("dve", lambda g: g.reciprocal(out=rc[:, 0:n], in_=po[64:128, 0:n]), reads=[po], writes=[rc])
                P.op("dve", lambda g: g.tensor_tensor(out=y[:, 0:n], in0=po[0:64, 0:n], in1=rc[:, 0:n],
                                                      op=ALU.mult), reads=[po, rc], writes=[y])
                P.dma(d["yT"][0, h * 64:(h + 1) * 64, q0:q0 + n], y[:, 0:n], reads=[y],
                      writes=[self.db("yT", (0, h, q0))])
        P.release()

    def phase_swa(self, l, with_ctx):
        P = self.P
        d = self.d
        P.mark()
        scale = 64.0 ** -0.5
        es = P.sbuf([128, 8], F32, "es")
        P.dma(es[:], d["swa_sink"][l:l + 1, :].to_broadcast([128, 8]), writes=[es])
        P.op("act", lambda g: g.activation(out=es[:], in_=es[:], func=AF.Exp), reads=[es], writes=[es])
        msk = P.sbuf([128, 2, 128], BF16, "msk")
        P.dma(msk[:], d["swa_mask"].rearrange("c p t -> p c t"), writes=[msk], q="pool")
        Kk = P.sbuf([64, TT], BF16, "Kk")
        Qk = P.sbuf([64, 4, TT], BF16, "Qk")
        va = P.sbuf([128, NTILE, 128], BF16, "va")
        P.op("pool", lambda g: g.memset(va[:, :, 64:128], 1.0), writes=[va])
        yd = P.sbuf([64, 4, TT], BF16, "yd")
        PT = [P.sbuf([128, 4, 128], BF16, "PT%d" % i) for i in range(6)]
        den = [P.sbuf([64, 4, 128], F32, "den%d" % i) for i in range(2)]
        npt = 0
        nblk = 0
        allsq = [self.db("sqT", (c, gi)) for c in range(4) for gi in range(9)]
        allsk = [self.db("skT", gi) for gi in range(9)]
        allpv = [self.db("pv", t) for t in range(NTILE)]
        for kh in range(2):
            P.dma(Kk[:], d["skT"][kh * 64:(kh + 1) * 64, :], reads=allsk, writes=[Kk])
            for g_ in range(4):
                hh = kh * 4 + g_
                P.dma(Qk[:, g_, :], d["sqT"][hh // 2, (hh % 2) * 64:(hh % 2) * 64 + 64, :], reads=allsq, writes=[Qk])
            P.dma(va[:, :, 0:64], d["pv"].rearrange("(t p) c -> p t c", p=128)[:, :, kh * 64:(kh + 1) * 64],
                  reads=allpv, writes=[va], q="pool")
            nq = NTILE if with_ctx else 32
            for i in range(nq):
                if i < 32:
                    kbs = []
                    if i > 0:
                        kbs.append((i - 1, 0))
                    kbs.append((i, None))
                    if i < 31:
                        kbs.append((i + 1, 1))
                    kbs += [(32, None), (33, None)]
                else:
                    kbs = [(32, None), (33, None)]
                po = self.ps[3 + nblk % 2]
                dn = den[nblk % 2]
                nblk += 1
                sbanks = [0, 1, 2, 5, 6, 7]
                items = []
                for idx, (kb, mk) in enumerate(kbs):
                    pss = self.ps[sbanks[npt % 6]]
                    pt = PT[npt % 6]
                    npt += 1
                    for g_ in range(4):
                        P.mm(pss[:, g_ * 128:(g_ + 1) * 128], Kk[:, kb * 128:(kb + 1) * 128],
                             Qk[:, g_, i * 128:(i + 1) * 128], reads=[Kk, Qk], writes=[pss])
                    items.append((kb, mk, pss, pt))
                for idx, (kb, mk, pss, pt) in enumerate(items):
                    P.op("act", lambda g: g.activation(out=pt[:].rearrange("p g t -> p (g t)"), in_=pss[:],
                                                       func=AF.Exp, scale=scale), reads=[pss], writes=[pt])
                    if mk is not None:
                        P.op("dve", lambda g: g.tensor_tensor(out=pt[:], in0=pt[:],
                                                              in1=msk[:, mk, :].unsqueeze(1).to_broadcast([128, 4, 128]),
                                                              op=ALU.mult), reads=[pt, msk], writes=[pt])
                for idx, (kb, mk, pss, pt) in enumerate(items):
                    P.mm(po[:], va[:, kb, :], pt[:].rearrange("p g t -> p (g t)"), start=(idx == 0),
                         stop=(idx == len(kbs) - 1), reads=[va, pt], writes=[po])
                P.op("dve", lambda g: g.tensor_tensor(out=dn[:], in0=po[64:128, :].rearrange("p (g t) -> p g t", g=4),
                                                      in1=es[64:128, kh * 4:(kh + 1) * 4].unsqueeze(2).to_broadcast([64, 4, 128]),
                                                      op=ALU.add), reads=[po, es], writes=[dn])
                P.op("dve", lambda g: g.reciprocal(out=dn[:], in_=dn[:]), reads=[dn], writes=[dn])
                P.op("dve", lambda g: g.tensor_tensor(out=yd[:, :, i * 128:(i + 1) * 128],
                                                      in0=po[0:64, :].rearrange("p (g t) -> p g t", g=4), in1=dn[:],
                                                      op=ALU.mult), reads=[po, dn], writes=[yd])
            for g_ in range(4):
                hh = kh * 4 + g_
                P.dma(d["yT"][3, hh * 64:(hh + 1) * 64, 0:nq * 128], yd[:, g_, 0:nq * 128], reads=[yd],
                      writes=[self.db("yT", (3, hh))])
        P.release()

    KB.declare_attn = declare_attn
    KB.phase_mla = phase_mla
    KB.phase_swa = phase_swa


_attn_methods()


def _merge_methods():
    def declare_merge(self):
        L = 2
        self.inp("w_branch", [L, 4, 512, DM])
        self.inp("w_out", [L, DM, DM])
        self.inp("b_gateT", [L, 128, 32])

    def phase_merge(self, l, with_ctx, xname="xs"):
        P = self.P
        d = self.d
        P.mark()
        wg = P.sbuf([128, 8, 4096], BF16, "wg")
        wgb = [Buf("wg%d" % k) for k in range(8)]
        for k in range(8):
            P.dma(wg[:, k, :], d["w_ext"][l, k * 128:(k + 1) * 128, WX_G0:WX_G0 + 4096], writes=[wgb[k]], q="pool")
        wbr = P.sbuf([128, 4, 4, DM], BF16, "wbr")
        for br in range(4):
            P.dma(wbr[:, br], d["w_branch"][l, br].rearrange("(kc p) n -> p kc n", p=128), writes=[wbr], q="pool")
        wo = P.sbuf([128, 8, DM], BF16, "wo")
        P.dma(wo[:], d["w_out"][l].rearrange("(k p) n -> p k n", p=128), writes=[wo], q="pool")
        bg = P.sbuf([128, 4, 8], F32, "bg")
        P.dma(bg[:].rearrange("p b o -> p (b o)"), d["b_gateT"][l], writes=[bg])
        gt = [P.sbuf([128, DM], F32, "gt%d" % s) for s in range(2)]
        for s in range(2):
            P.dma(gt[s][:], d["gtrow"][l, 1, s:s + 1, :].to_broadcast([128, DM]),
                  reads=[self.db("gtrow", (l, 1))], writes=[gt[s]])
        groups = [(g * 512, 512) for g in range(8)] + ([(NLAT, 256)] if with_ctx else [])
        uT = [P.sbuf([128, 8, 512], BF16, "uT%d" % i) for i in range(2)]
        yg = [P.sbuf([128, 4, 4, 512], BF16, "yg%d" % i) for i in range(2)]
        mg = P.sbuf([128, 8, 512], BF16, "mg")
        mgb = [Buf("mg%d" % k) for k in range(8)]
        sg = [P.sbuf([128, 512], F32, "sg%d" % i) for i in range(2)]
        tm_ = [P.sbuf([128, 512], F32, "tm%d" % i) for i in range(2)]
        acc = [P.sbuf([128, 512], F32, "acc%d" % i) for i in range(2)]
        xbuf = [P.sbuf([128, DM], F32, "xb%d" % i) for i in range(2)]
        junk = P.sbuf([128, DM], BF16, "junk")
        s2 = [P.sbuf([128, 3], F32, "s2%d" % i) for i in range(2)]
        r2 = [P.sbuf([128, 1], F32, "r2%d" % i) for i in range(2)]
        tmp1 = P.sbuf([128, DM], F32, "tmp")
        tmp = [tmp1, tmp1]
        ally = [b for k, b in self.dbufs.items() if k[0] == "yT"]
        allu = [b for k, b in self.dbufs.items() if k[0] == "uT"]

        def load(gi):
            t0, n = groups[gi]
            P.dma(uT[gi % 2][:, :, 0:n], d["uT"][:, :, t0:t0 + n].rearrange("k p t -> p k t"), reads=allu,
                  writes=[uT[gi % 2]])
            for br in range(4):
                P.dma(yg[gi % 2][:, br, :, 0:n],
                      d["yT"][br].rearrange("(kc p) t -> p kc t", p=128)[:, :, t0:t0 + n], reads=ally,
                      writes=[yg[gi % 2]])

        load(0)
        nx = 0
        for gi, (t0, n) in enumerate(groups):
            if gi + 1 < len(groups):
                load(gi + 1)
            u = uT[gi % 2]
            y = yg[gi % 2]
            s = 0 if gi < 8 else 1
            for oc in range(8):
                ac = acc[oc % 2]
                for br in range(4):
                    psg = self.ps[1 + (br % 2) * 2]
                    psy = self.ps[2 + (br % 2) * 2]
                    c0 = br * 1024 + oc * 128
                    for k in range(8):
                        P.mm(psg[:, 0:n], wg[:, k, c0:c0 + 128], u[:, k, 0:n], start=(k == 0), stop=(k == 7),
                             reads=[wgb[k], u], writes=[psg])
                    for kc in range(4):
                        P.mm(psy[:, 0:n], wbr[:, br, kc, oc * 128:(oc + 1) * 128], y[:, br, kc, 0:n],
                             start=(kc == 0), stop=(kc == 3), reads=[wbr, y], writes=[psy])
                    sgt = sg[br % 2]
                    P.op("act", lambda g: g.activation(out=sgt[:, 0:n], in_=psg[:, 0:n], func=AF.Sigmoid,
                                                       bias=bg[:, br, oc:oc + 1]), reads=[psg, bg], writes=[sgt])
                    if br == 0:
                        P.op("dve", lambda g: g.tensor_tensor(out=ac[:, 0:n], in0=sgt[:, 0:n], in1=psy[:, 0:n],
                                                              op=ALU.mult), reads=[sgt, psy], writes=[ac])
                    else:
                        tt = tm_[br % 2]
                        P.op("dve", lambda g: g.tensor_tensor(out=tt[:, 0:n], in0=sgt[:, 0:n], in1=psy[:, 0:n],
                                                              op=ALU.mult), reads=[sgt, psy], writes=[tt])
                        if br < 3:
                            P.op("pool", lambda g: g.tensor_tensor(out=ac[:, 0:n], in0=ac[:, 0:n], in1=tt[:, 0:n],
                                                                   op=ALU.add), reads=[ac, tt], writes=[ac])
                        else:
                            P.op("pool", lambda g: g.tensor_tensor(out=mg[:, oc, 0:n], in0=ac[:, 0:n],
                                                                   in1=tt[:, 0:n], op=ALU.add),
                                 reads=[ac, tt], writes=[mgb[oc]])
            for i in range(n // 128):
                t = t0 // 128 + i
                xb = xbuf[nx % 2]
                P.dma(xb[:], d[xname][t * 128:(t + 1) * 128, :], reads=[self.db(xname, t)], writes=[xb])
                for h in range(2):
                    po = self.ps[5 + h]
                    for k in range(8):
                        P.mm(po[:], mg[:, k, i * 128:(i + 1) * 128], wo[:, k, h * 512:(h + 1) * 512],
                             start=(k == 0), stop=(k == 7), reads=[mgb[k], wo], writes=[po])
                self.norm_res_out([5, 6], xb, xb[:], gt[s], (junk, s2[nx % 2], r2[nx % 2], tmp[nx % 2]),
                                  d[xname][t * 128:(t + 1) * 128, :], self.db(xname, t))
                nx += 1
        P.release()

    KB.declare_merge = declare_merge
    KB.phase_merge = phase_merge


_merge_methods()


def hy_tables(n):
    M = 2 * n
    S1 = M // 64
    T1 = n // 64
    s1 = np.arange(S1)[:, None, None]
    s2 = np.arange(64)[None, :, None]
    f1 = np.arange(S1)[None, None, :]
    ang = 2.0 * np.pi * ((f1 * (64 * s1 + s2)) % M) / M
    F1n = S1 // 2 + 1
    W1 = np.stack([np.cos(ang), -np.sin(ang)], axis=2).astype(np.float32)[..., :F1n]
    s2v = np.arange(64)[:, None]
    f2v = np.arange(64)[None, :]
    th = 2.0 * np.pi * ((s2v * f2v) % 64) / 64
    c, s = np.cos(th), np.sin(th)
    D2 = np.block([[c, -s], [s, c]]).astype(np.float32)
    D2sw = np.concatenate([D2[:, 64:], D2[:, :64]], axis=1)
    E = np.block([[c, s], [-s, c]]).astype(np.float32)
    f1v = np.arange(S1)[:, None, None]
    t2v = np.arange(64)[None, :, None]
    t1v = np.arange(T1)[None, None, :]
    psi = 2.0 * np.pi * ((f1v * (64 * t1v + t2v)) % M) / M
    W3 = np.stack([np.cos(psi), -np.sin(psi)], axis=2).astype(np.float32)
    cw = np.full(F1n, 2.0, np.float32)
    cw[0] = 1.0
    cw[-1] = 1.0
    W3 = np.ascontiguousarray(W3[:F1n] * cw[:, None, None, None])
    return dict(W1=W1, D2=D2, D2sw=D2sw, E=E, W3=W3, S1=S1, T1=T1, M=M, F1n=F1n)


def hy_feats(n):
    M = 2 * n
    f32 = np.float32
    t = np.linspace(0.0, 1.0, n, dtype=f32)
    bands = np.linspace(1e-4, 15.0, 16, dtype=f32)
    ang = (f32(2.0 * np.pi / n) * np.arange(n, dtype=f32)[:, None] * bands[None, :]).astype(f32)
    feats = np.concatenate([t[:, None], np.cos(ang), -np.sin(ang)], axis=-1).astype(f32)
    deltas = np.abs(np.linspace(np.log(1e-2) / 1.5, np.log(1e-2) / 0.3, 512, dtype=f32)).astype(f32)
    win = np.exp(-t[:, None] * deltas[None, :]).astype(f32)
    idx = np.zeros(M, np.int64)
    idx[:n] = np.arange(n)
    idx[n + 1:] = n - np.arange(1, n)
    fK = feats[idx].copy()
    wK = win[idx].copy()
    fK[n] = feats[0]
    wK[n] = win[0]
    return np.ascontiguousarray(fK.T), np.ascontiguousarray(wK)


def _hyena_methods():
    TWO_PI = 2.0 * np.pi

    def declare_hyena(self):
        L = 2
        self.inp("hyena_conv", [L, 3, 1536])
        self.inp("hyena_conv_b", [L, 1536])
        self.inp("hyena_w1", [L, 33, 64])
        self.inp("hyena_w2", [L, 64, 64])
        self.inp("hyena_w3", [L, 64, 2048])
        self.inp("hyT", [L, 64, 4])
        self.inp("hyena_bias", [L, 2, 512])
        self.inp("hy_D", [3, 128, 128])
        self.inp("hyL_W1", [128, 64 * 2 * 65])
        self.inp("hyL_W3", [65, 64 * 2 * 64])
        self.inp("hyL_fK", [33, 8192])
        self.inp("hyL_wK", [8192, 512])
        self.inp("hyC_DFT", [2, 128, 4, 512])
        self.inp("hyC_fK", [33, 512])
        self.inp("hyC_wK", [512, 512])
        self.scr("hcs", [TT, 1536])
        self.scr("kbuf", [2, 8192, 512], BF16)
        self.scr("Bd", [128, 128, 512], BF16)
        self.scr("Dd", [128, 128, 512], BF16)
        self.scr("KAB_L", [2, 128, 2, 128, 512], BF16)
        self.scr("KC", [2, 2, 4, 128, 512], BF16)
        self.scr("zt1", [TT, 512])
        self.scr("zt2", [TT, 512])

    def hy_shortconv(self, l, ntiles):
        P = self.P
        d = self.d
        P.mark()
        ck = P.sbuf([128, 3, 1536], F32, "ck")
        P.dma(ck[:].rearrange("p a c -> p (a c)"),
              d["hyena_conv"][l:l + 1].rearrange("o a c -> o (a c)").to_broadcast([128, 4608]), writes=[ck])
        cb = P.sbuf([128, 1536], F32, "cb")
        P.dma(cb[:], d["hyena_conv_b"][l:l + 1, :].to_broadcast([128, 1536]), writes=[cb])
        bufs = [[P.sbuf([128, 1536], F32, "sc%d_%d" % (i, j)) for j in range(3)] for i in range(3)]
        allph = [b for k, b in self.dbufs.items() if k[0] == "ph"]
        for t in range(ntiles):
            r0 = tm_row(t * 128)
            pv_, cu, nx = bufs[t % 3]
            P.dma(pv_[:], d["ph"][r0 - 1:r0 + 127, :], reads=allph, writes=[pv_])
            P.dma(cu[:], d["ph"][r0:r0 + 128, :], reads=allph, writes=[cu])
            P.dma(nx[:], d["ph"][r0 + 1:r0 + 129, :], reads=allph, writes=[nx])
            P.op("dve", lambda g: g.tensor_tensor(out=pv_[:], in0=pv_[:], in1=ck[:, 0, :], op=ALU.mult),
                 reads=[pv_, ck], writes=[pv_])
            P.op("pool", lambda g: g.tensor_tensor(out=cu[:], in0=cu[:], in1=ck[:, 1, :], op=ALU.mult),
                 reads=[cu, ck], writes=[cu])
            P.op("dve", lambda g: g.tensor_tensor(out=nx[:], in0=nx[:], in1=ck[:, 2, :], op=ALU.mult),
                 reads=[nx, ck], writes=[nx])
            P.op("pool", lambda g: g.tensor_tensor(out=cu[:], in0=cu[:], in1=pv_[:], op=ALU.add),
                 reads=[cu, pv_], writes=[cu])
            P.op("dve", lambda g: g.tensor_tensor(out=nx[:], in0=nx[:], in1=cb[:], op=ALU.add),
                 reads=[nx, cb], writes=[nx])
            P.op("pool", lambda g: g.tensor_tensor(out=cu[:], in0=cu[:], in1=nx[:], op=ALU.add),
                 reads=[cu, nx], writes=[cu])
            P.dma(d["hcs"][t * 128:(t + 1) * 128, :], cu[:], reads=[cu], writes=[self.db("hcs", t)])
        P.release()

    def hy_load_tabs(self, n):
        P = self.P
        d = self.d
        pre = "hyL" if n == NLAT else "hyC"
        S1 = 2 * n // 64
        T1 = n // 64
        F1n = S1 // 2 + 1
        W1 = P.sbuf([S1, 64, 2, F1n], BF16, "W1")
        P.dma(W1[:].rearrange("p a b c -> p (a b c)"), d[pre + "_W1"], writes=[W1], q="pool")
        W3 = P.sbuf([F1n, 64, 2, T1], BF16, "W3")
        P.dma(W3[:].rearrange("p a b c -> p (a b c)"), d[pre + "_W3"], writes=[W3], q="pool")
        Dm = P.sbuf([128, 3, 128], BF16, "Dm")
        P.dma(Dm[:], d["hy_D"].rearrange("a p c -> p a c"), writes=[Dm], q="pool")
        return dict(W1=W1, W3=W3, Dm=Dm, S1=S1, T1=T1, n=n, F1n=F1n)

    def hy_stage1(self, tb, src_ap, nz, src_reads, cast):
        P = self.P
        d = self.d
        S1 = tb["F1n"]
        P.mark()
        U = P.sbuf([nz, 64, 512], BF16, "U")
        P.dma(U[:], src_ap.rearrange("(a s) c -> a s c", s=64), reads=src_reads, writes=[U],
              q=("pool" if cast else "sp"))
        bo = [P.sbuf([S1, 2, 512], BF16, "bo%d" % i) for i in range(4)]
        bdv = d["Bd"].rearrange("(r s) f c -> s f r c", r=2)
        for s2 in range(64):
            o = bo[s2 % 4]
            for ri in range(2):
                ps = self.ps[(2 * s2 + ri) % 4]
                P.mm(ps[0:S1, :], tb["W1"][0:nz, s2, ri, :], U[:, s2, :], reads=[tb["W1"], U], writes=[ps])
                if ri == 0:
                    P.op("act", lambda g: g.activation(out=o[:, ri, :], in_=ps[0:S1, :], func=AF.Copy),
                         reads=[ps], writes=[o])
                else:
                    P.op("dve", lambda g: g.tensor_copy(out=o[:, ri, :], in_=ps[0:S1, :]), reads=[ps], writes=[o])
            P.dma(bdv[s2, 0:S1], o[:], reads=[o], writes=[self.db("Bd", s2)])
        P.release()

    def hy_stage2(self, tb, cb, cb2=None):
        P = self.P
        d = self.d
        S1 = tb["F1n"]
        allbd = [self.db("Bd", s2) for s2 in range(64)]
        FG = 5
        bins = [P.sbuf([128, FG, 512], BF16, "bin%d" % i) for i in range(3)]
        prev = [None]
        for fg in range(S1 // FG):
            b = bins[fg % 3]
            P.dma(b[:], d["Bd"][:, fg * FG:(fg + 1) * FG, :], reads=allbd, writes=[b])
            for j in range(FG):
                f1 = fg * FG + j
                p1 = self.ps[(f1 % 3) * 2]
                p2 = self.ps[(f1 % 3) * 2 + 1]
                P.mm(p1[:], tb["Dm"][:, 0, :], b[:, j, :], reads=[tb["Dm"], b], writes=[p1])
                P.mm(p2[:], tb["Dm"][:, 1, :], b[:, j, :], reads=[tb["Dm"], b], writes=[p2])
                cb(f1, p1, p2)
                if cb2 is not None and prev[0] is not None:
                    cb2(prev[0])
                prev[0] = f1
        if cb2 is not None and prev[0] is not None:
            cb2(prev[0])

    def hy_filters(self, l, n, kab_name):
        P = self.P
        d = self.d
        pre = "hyL" if n == NLAT else "hyC"
        M = 2 * n
        P.mark()
        tb = self.hy_load_tabs(n) if n == NLAT else None
        hyT = P.sbuf([64, 4], F32, "hyT")
        P.dma(hyT[:], d["hyT"][l], writes=[hyT])
        sc = P.sbuf([64, 4], F32, "hsc")
        for j in range(2):
            P.op("dve", lambda g: g.tensor_scalar(out=sc[:, 2 * j:2 * j + 1], in0=hyT[:, j:j + 1],
                                                  scalar1=1.0 / TWO_PI, scalar2=None, op0=ALU.mult),
                 reads=[hyT], writes=[sc])
            P.op("dve", lambda g: g.tensor_tensor(out=sc[:, 2 * j + 1:2 * j + 2], in0=hyT[:, 2 + j:3 + j],
                                                  in1=sc[:, 2 * j:2 * j + 1], op=ALU.mult),
                 reads=[hyT, sc], writes=[sc])
            P.op("dve", lambda g: g.tensor_scalar(out=sc[:, 2 * j + 1:2 * j + 2], in0=sc[:, 2 * j + 1:2 * j + 2],
                                                  scalar1=64.0, scalar2=None, op0=ALU.add),
                 reads=[sc], writes=[sc])
        w1 = P.sbuf([33, 64], F32, "hw1")
        P.dma(w1[:], d["hyena_w1"][l], writes=[w1])
        w2 = P.sbuf([64, 64], F32, "hw2")
        P.dma(w2[:], d["hyena_w2"][l], writes=[w2])
        w3 = P.sbuf([64, 2048], F32, "hw3")
        P.dma(w3[:], d["hyena_w3"][l], writes=[w3])
        ones_f = P.sbuf([128, 128], F32, "ones_f")
        P.op("pool", lambda g: g.memset(ones_f[:], 1.0), writes=[ones_f])
        G2T = P.sbuf([64, M], F32, "G2T")
        rn = [P.sbuf([128, 512], F32, "rn%d" % o) for o in range(2)]
        P.mark()
        fk = [P.sbuf([33, 512], F32, "fk%d" % i) for i in range(2)]
        vt = [P.sbuf([64, 512], F32, "vt%d" % i) for i in range(2)]
        vi = [P.sbuf([64, 512], I32, "vi%d" % i) for i in range(2)]
        vf = [P.sbuf([64, 512], F32, "vf%d" % i) for i in range(2)]
        g1 = [P.sbuf([64, 512], F32, "g1%d" % i) for i in range(2)]
        cnt = [0]

        def sin_reduce(ps, j, out_ap, out_t):
            i = cnt[0] % 2
            cnt[0] += 1
            P.op("dve", lambda g: g.tensor_scalar(out=vt[i][:], in0=ps[0:64, :], scalar1=sc[:, 2 * j:2 * j + 1],
                                                  scalar2=sc[:, 2 * j + 1:2 * j + 2], op0=ALU.mult, op1=ALU.add),
                 reads=[ps, sc], writes=[vt[i]])
            P.op("dve", lambda g: g.tensor_copy(out=vi[i][:], in_=vt[i][:]), reads=[vt[i]], writes=[vi[i]])
            P.op("pool", lambda g: g.tensor_copy(out=vf[i][:], in_=vi[i][:]), reads=[vi[i]], writes=[vf[i]])
            P.op("pool", lambda g: g.tensor_tensor(out=vt[i][:], in0=vt[i][:], in1=vf[i][:], op=ALU.subtract),
                 reads=[vt[i], vf[i]], writes=[vt[i]])
            P.op("act", lambda g: g.activation(out=out_ap, in_=vt[i][:], func=AF.Sin, scale=TWO_PI),
                 reads=[vt[i]], writes=[out_t])

        for cbk in range(M // 512):
            f = fk[cbk % 2]
            P.dma(f[:], d[pre + "_fK"][:, cbk * 512:(cbk + 1) * 512], writes=[f])
            ps = self.ps[cbk % 2]
            P.mm(ps[0:64, :], w1[:], f[:], reads=[w1, f], writes=[ps])
            gg = g1[cbk % 2]
            sin_reduce(ps, 0, gg[:], gg)
            ps2 = self.ps[2 + cbk % 2]
            P.mm(ps2[0:64, :], w2[:], gg[:], reads=[w2, gg], writes=[ps2])
            sin_reduce(ps2, 1, G2T[:, cbk * 512:(cbk + 1) * 512], G2T)
        P.release()
        P.mark()
        wk = [P.sbuf([128, 512], F32, "wk%d" % i) for i in range(3)]
        kbt = [P.sbuf([128, 512], F32, "kbt%d" % i) for i in range(4)]
        ab = [P.sbuf([128, 512], F32, "ab%d" % i) for i in range(4)]
        kbo = [P.sbuf([128, 512], BF16, "kbo%d" % i) for i in range(4)]
        nlt = M // 128
        c = 0
        for lt in range(nlt):
            dirn = 0 if lt < n // 128 else 1
            w = wk[lt % 3]
            P.dma(w[:], d[pre + "_wK"][lt * 128:(lt + 1) * 128, :], writes=[w])
            for o in range(2):
                ps = self.ps[c % 4]
                kt_ = kbt[c % 4]
                a = ab[c % 4]
                ko = kbo[c % 4]
                c += 1
                P.mm(ps[:], G2T[:, lt * 128:(lt + 1) * 128], w3[:, o * 1024 + dirn * 512:o * 1024 + dirn * 512 + 512],
                     reads=[G2T, w3], writes=[ps])
                P.op("dve", lambda g: g.tensor_tensor(out=kt_[:], in0=ps[:], in1=w[:], op=ALU.mult),
                     reads=[ps, w], writes=[kt_])
                P.op("act", lambda g: g.activation(out=a[:], in_=kt_[:], func=AF.Abs), reads=[kt_], writes=[a])
                P.mm(self.ps[6 + o][:], ones_f[:], a[:], start=(lt == 0), stop=(lt == nlt - 1),
                     reads=[ones_f, a], writes=[self.ps[6 + o]])
                if lt == n // 128:
                    P.op("pool", lambda g: g.memset(kt_[0:1, :], 0.0), reads=[kt_], writes=[kt_])
                P.op("act", lambda g: g.activation(out=ko[:], in_=kt_[:], func=AF.Copy), reads=[kt_], writes=[ko])
                P.dma(d["kbuf"][o, lt * 128:(lt + 1) * 128, :], ko[:], reads=[ko], writes=[self.db("kbuf", (o, lt))])
        for o in range(2):
            P.op("dve", lambda g: g.tensor_scalar(out=rn[o][:], in0=self.ps[6 + o][:], scalar1=float(M), scalar2=None,
                                                  op0=ALU.mult), reads=[self.ps[6 + o]], writes=[rn[o]])
            P.op("dve", lambda g: g.reciprocal(out=rn[o][:], in_=rn[o][:]), reads=[rn[o]], writes=[rn[o]])
        P.release()
        if n != NLAT:
            self.hy_ctx_khat(rn)
            P.release()
            return
        for o in range(2):
            allkb = [self.db("kbuf", (o, lt)) for lt in range(nlt)]
            self.hy_stage1(tb, d["kbuf"][o, 0:M, :], tb["S1"], allkb, False)
            P.mark()
            kab = [P.sbuf([128, 2, 512], BF16, "kab%d" % i) for i in range(4)]

            def cbf(f1, p1, p2):
                k = kab[f1 % 4]
                r = rn[o]
                P.op("dve", lambda g: g.tensor_tensor(out=k[0:64, 0, :], in0=p1[0:64, :], in1=r[0:64, :], op=ALU.mult),
                     reads=[p1, r], writes=[k])
                P.op("dve", lambda g: g.tensor_tensor(out=k[64:128, 0, :], in0=p2[64:128, :], in1=r[64:128, :],
                                                      op=ALU.mult), reads=[p2, r], writes=[k])
                P.op("dve", lambda g: g.scalar_tensor_tensor(out=k[0:64, 1, :], in0=p2[0:64, :], scalar=-1.0,
                                                             in1=r[0:64, :], op0=ALU.mult, op1=ALU.mult),
                     reads=[p2, r], writes=[k])
                P.op("dve", lambda g: g.tensor_tensor(out=k[64:128, 1, :], in0=p1[64:128, :], in1=r[64:128, :],
                                                      op=ALU.mult), reads=[p1, r], writes=[k])
                P.dma(d[kab_name][o, f1].rearrange("a p c -> p a c"), k[:], reads=[k],
                      writes=[self.db(kab_name, (o, f1))])

            self.hy_stage2(tb, cbf)
            P.release()
        P.release()

    def hy_conv(self, l, tb, o, kab_name, src_ap, src_reads, gate_ap, gate_reads, dst_ap, dst_name):
        P = self.P
        d = self.d
        n, S1, T1 = tb["n"], tb["F1n"], tb["T1"]
        self.hy_stage1(tb, src_ap, tb["S1"] // 2, src_reads, True)
        P.mark()
        kab = [P.sbuf([128, 2, 512], BF16, "kab%d" % i) for i in range(4)]
        ta = [P.sbuf([128, 512], F32, "ta%d" % i) for i in range(4)]
        tb2 = [P.sbuf([128, 512], F32, "tb%d" % i) for i in range(4)]
        yh = [P.sbuf([128, 512], BF16, "yh%d" % i) for i in range(4)]
        do = [P.sbuf([128, 512], BF16, "do%d" % i) for i in range(4)]
        allk = [self.db(kab_name, (o, f1)) for f1 in range(S1)]

        def cbf(f1, p1, p2):
            i = f1 % 4
            k = kab[i]
            P.dma(k[:], d[kab_name][o, f1].rearrange("a p c -> p a c"), reads=allk, writes=[k])
            P.op("dve", lambda g: g.tensor_tensor(out=ta[i][:], in0=p1[:], in1=k[:, 0, :], op=ALU.mult),
                 reads=[p1, k], writes=[ta[i]])
            P.op("dve", lambda g: g.tensor_tensor(out=tb2[i][:], in0=p2[:], in1=k[:, 1, :], op=ALU.mult),
                 reads=[p2, k], writes=[tb2[i]])
            P.op("pool", lambda g: g.tensor_tensor(out=yh[i][:], in0=ta[i][:], in1=tb2[i][:], op=ALU.add),
                 reads=[ta[i], tb2[i]], writes=[yh[i]])

        def cbf2(f1):
            i = f1 % 4
            pd_ = self.ps[6 + f1 % 2]
            P.mm(pd_[:], tb["Dm"][:, 2, :], yh[i][:], reads=[tb["Dm"], yh[i]], writes=[pd_])
            P.op("act", lambda g: g.activation(out=do[i][:], in_=pd_[:], func=AF.Copy), reads=[pd_], writes=[do[i]])
            P.dma(d["Dd"][:, f1, :], do[i][:], reads=[do[i]], writes=[self.db("Dd", f1)])

        self.hy_stage2(tb, cbf, cbf2)
        P.release()
        P.mark()
        bias = P.sbuf([64, 512], F32, "hbias")
        P.dma(bias[:], d["hyena_bias"][l, o:o + 1, :].to_broadcast([64, 512]), writes=[bias])
        din = [P.sbuf([S1, 2, 512], BF16, "din%d" % i) for i in range(4)]
        gs = [P.sbuf([T1, 512], F32, "gs%d" % i) for i in range(4)]
        us = [P.sbuf([T1, 512], F32, "us%d" % i) for i in range(4)]
        zo = [P.sbuf([T1, 512], F32, "zo%d" % i) for i in range(4)]
        alld = [self.db("Dd", f1) for f1 in range(S1)]
        ddv = d["Dd"].rearrange("(r t) f c -> t f r c", r=2)
        gv = gate_ap.rearrange("(a s) c -> s a c", s=64)
        uv = src_ap.rearrange("(a s) c -> s a c", s=64)
        dv = dst_ap.rearrange("(a s) c -> s a c", s=64)
        for t2 in range(64):
            i = t2 % 4
            P.dma(din[i][:], ddv[t2, 0:S1], reads=alld, writes=[din[i]])
            P.dma(gs[i][:], gv[t2], reads=gate_reads, writes=[gs[i]])
            P.dma(us[i][:], uv[t2], reads=src_reads, writes=[us[i]])
            py = self.ps[6 + i % 2]
            P.mm(py[0:T1, :], tb["W3"][:, t2, 0, :], din[i][:, 0, :], start=True, stop=False,
                 reads=[tb["W3"], din[i]], writes=[py])
            P.mm(py[0:T1, :], tb["W3"][:, t2, 1, :], din[i][:, 1, :], start=False, stop=True,
                 reads=[tb["W3"], din[i]], writes=[py])
            P.op("pool", lambda g: g.tensor_tensor(out=us[i][:], in0=us[i][:], in1=bias[0:T1, :], op=ALU.mult),
                 reads=[us[i], bias], writes=[us[i]])
            P.op("dve", lambda g: g.tensor_tensor(out=us[i][:], in0=us[i][:], in1=py[0:T1, :], op=ALU.add),
                 reads=[us[i], py], writes=[us[i]])
            P.op("pool", lambda g: g.tensor_tensor(out=zo[i][:], in0=us[i][:], in1=gs[i][:], op=ALU.mult),
                 reads=[us[i], gs[i]], writes=[zo[i]])
            P.dma(dv[t2], zo[i][:], reads=[zo[i]], writes=[self.db(dst_name, ("t2", t2, n))])
        P.release()

    def hy_ctx_khat(self, rn):
        P = self.P
        d = self.d
        P.mark()
        dft = P.sbuf([128, 2, 4, 512], BF16, "dftc")
        for c_ in range(2):
            P.dma(dft[:, c_].rearrange("p s f -> p (s f)"), d["hyC_DFT"][c_].rearrange("p s f -> p (s f)"),
                  writes=[dft], q="pool")
        for o in range(2):
            kb = P.sbuf([128, 4, 512], BF16, "kbc%d" % o)
            P.dma(kb[:], d["kbuf"][o, 0:512, :].rearrange("(s p) c -> p s c", p=128),
                  reads=[self.db("kbuf", (o, lt)) for lt in range(4)], writes=[kb])
            ko = [P.sbuf([128, 512], BF16, "kco%d_%d" % (o, i)) for i in range(4)]
            c = 0
            for ri in range(2):
                for ft in range(4):
                    ps = self.ps[c % 4]
                    k_ = ko[c % 4]
                    c += 1
                    for lt in range(4):
                        P.mm(ps[:], dft[:, ri, lt, ft * 128:(ft + 1) * 128], kb[:, lt, :], start=(lt == 0), stop=(lt == 3),
                             reads=[dft, kb], writes=[ps])
                    P.op("dve", lambda g: g.tensor_tensor(out=k_[:], in0=ps[:], in1=rn[o][:], op=ALU.mult),
                         reads=[ps, rn[o]], writes=[k_])
                    P.dma(d["KC"][o, ri, ft], k_[:], reads=[k_], writes=[self.db("KC", (o, ri, ft))])
        P.release()

    def hy_ctx_conv(self, l, o, src_ap, src_reads, gate_ap, gate_reads, dst_ap, dst_name):
        P = self.P
        d = self.d
        P.mark()
        dft = P.sbuf([128, 2, 4, 512], BF16, "dftc")
        for c_ in range(2):
            P.dma(dft[:, c_].rearrange("p s f -> p (s f)"), d["hyC_DFT"][c_].rearrange("p s f -> p (s f)"),
                  writes=[dft], q="pool")
        u = P.sbuf([128, 2, 512], BF16, "uc")
        P.dma(u[:], src_ap.rearrange("(s p) c -> p s c", p=128), reads=src_reads, writes=[u], q="pool")
        u32 = P.sbuf([128, 2, 512], F32, "uc32")
        P.dma(u32[:], src_ap.rearrange("(s p) c -> p s c", p=128), reads=src_reads, writes=[u32])
        gt_ = P.sbuf([128, 2, 512], F32, "gc32")
        P.dma(gt_[:], gate_ap.rearrange("(s p) c -> p s c", p=128), reads=gate_reads, writes=[gt_])
        bias = P.sbuf([128, 512], F32, "hbias")
        P.dma(bias[:], d["hyena_bias"][l, o:o + 1, :].to_broadcast([128, 512]), writes=[bias])
        kc = P.sbuf([128, 2, 4, 512], BF16, "kc")
        P.dma(kc[:], d["KC"][o].rearrange("r f p c -> p r f c"),
              reads=[self.db("KC", (o, ri, ft)) for ri in range(2) for ft in range(4)], writes=[kc])
        Y = P.sbuf([128, 2, 4, 512], BF16, "Yc")
        ta = [P.sbuf([128, 512], F32, "cta%d" % i) for i in range(4)]
        for ft in range(4):
            pre = self.ps[(ft % 2) * 2]
            pim = self.ps[(ft % 2) * 2 + 1]
            for st in range(2):
                P.mm(pre[:], dft[:, 0, st, ft * 128:(ft + 1) * 128], u[:, st, :], start=(st == 0), stop=(st == 1),
                     reads=[dft, u], writes=[pre])
            for st in range(2):
                P.mm(pim[:], dft[:, 1, st, ft * 128:(ft + 1) * 128], u[:, st, :], start=(st == 0), stop=(st == 1),
                     reads=[dft, u], writes=[pim])
            P.op("dve", lambda g: g.tensor_tensor(out=ta[0][:], in0=pre[:], in1=kc[:, 0, ft, :], op=ALU.mult),
                 reads=[pre, kc], writes=[ta[0]])
            P.op("dve", lambda g: g.tensor_tensor(out=ta[1][:], in0=pim[:], in1=kc[:, 1, ft, :], op=ALU.mult),
                 reads=[pim, kc], writes=[ta[1]])
            P.op("dve", lambda g: g.tensor_tensor(out=ta[2][:], in0=pre[:], in1=kc[:, 1, ft, :], op=ALU.mult),
                 reads=[pre, kc], writes=[ta[2]])
            P.op("dve", lambda g: g.tensor_tensor(out=ta[3][:], in0=pim[:], in1=kc[:, 0, ft, :], op=ALU.mult),
                 reads=[pim, kc], writes=[ta[3]])
            P.op("pool", lambda g: g.tensor_tensor(out=Y[:, 0, ft, :], in0=ta[0][:], in1=ta[1][:], op=ALU.subtract),
                 reads=[ta[0], ta[1]], writes=[Y])
            P.op("pool", lambda g: g.tensor_tensor(out=Y[:, 1, ft, :], in0=ta[2][:], in1=ta[3][:], op=ALU.add),
                 reads=[ta[2], ta[3]], writes=[Y])
        zo = [P.sbuf([128, 512], F32, "czo%d" % i) for i in range(2)]
        for tt in range(2):
            py = self.ps[4 + tt]
            n_mm = 0
            for ri in range(2):
                for ft in range(4):
                    P.mm(py[:], dft[:, ri, ft, tt * 128:(tt + 1) * 128], Y[:, ri, ft, :], start=(n_mm == 0),
                         stop=(n_mm == 7), reads=[dft, Y], writes=[py])
                    n_mm += 1
            P.op("pool", lambda g: g.tensor_tensor(out=zo[tt][:], in0=u32[:, tt, :], in1=bias[:], op=ALU.mult),
                 reads=[u32, bias], writes=[zo[tt]])
            P.op("dve", lambda g: g.tensor_tensor(out=zo[tt][:], in0=zo[tt][:], in1=py[:], op=ALU.add),
                 reads=[zo[tt], py], writes=[zo[tt]])
            P.op("pool", lambda g: g.tensor_tensor(out=zo[tt][:], in0=zo[tt][:], in1=gt_[:, tt, :], op=ALU.mult),
                 reads=[zo[tt], gt_], writes=[zo[tt]])
            P.dma(dst_ap[tt * 128:(tt + 1) * 128, :], zo[tt][:], reads=[zo[tt]],
                  writes=[self.db(dst_name, ("c", tt))])
        P.release()

    def phase_hyena(self, l, with_ctx):
        P = self.P
        d = self.d
        self.hy_shortconv(l, NTILE if with_ctx else 32)
        self.hy_filters(l, NLAT, "KAB_L")
        if with_ctx:
            self.hy_filters(l, NCTX, "KC")
        if with_ctx:
            r0 = NLAT
            hcs = [b for k, b in self.dbufs.items() if k[0] == "hcs"]
            self.hy_ctx_conv(l, 0, d["hcs"][r0:r0 + NCTX, 0:512], hcs, d["hcs"][r0:r0 + NCTX, 512:1024], hcs,
                             d["zt1"][r0:r0 + NCTX, :], "zt1")
            z1 = [b for k, b in self.dbufs.items() if k[0] == "zt1"]
            self.hy_ctx_conv(l, 1, d["zt1"][r0:r0 + NCTX, :], z1, d["hcs"][r0:r0 + NCTX, 1024:1536], hcs,
                             d["zt2"][r0:r0 + NCTX, :], "zt2")
        segs = [(NLAT, 0, "KAB_L")]
        for (n, r0, kn) in segs:
            P.mark()
            tb = self.hy_load_tabs(n)
            hcs = [b for k, b in self.dbufs.items() if k[0] == "hcs"]
            self.hy_conv(l, tb, 0, kn, d["hcs"][r0:r0 + n, 0:512], hcs, d["hcs"][r0:r0 + n, 512:1024], hcs,
                         d["zt1"][r0:r0 + n, :], "zt1")
            z1 = [b for k, b in self.dbufs.items() if k[0] == "zt1"]
            self.hy_conv(l, tb, 1, kn, d["zt1"][r0:r0 + n, :], z1, d["hcs"][r0:r0 + n, 1024:1536], hcs,
                         d["zt2"][r0:r0 + n, :], "zt2")
            P.release()
        P.mark()
        zin = [P.sbuf([128, 512], F32, "zin%d" % i) for i in range(3)]
        zT = [P.sbuf([128, 4, 128], BF16, "zT%d" % i) for i in range(3)]
        z2 = [b for k, b in self.dbufs.items() if k[0] == "zt2"]
        yv = d["yT"][2].rearrange("(kc p) t -> p kc t", p=128)
        for t in range(NTILE if with_ctx else 32):
            zi = zin[t % 3]
            P.dma(zi[:], d["zt2"][t * 128:(t + 1) * 128, :], reads=z2, writes=[zi])
            ps = self.ps[t % 2]
            for kc in range(4):
                P.op("pe", lambda g: g.transpose(ps[:, kc * 128:(kc + 1) * 128], zi[:, kc * 128:(kc + 1) * 128],
                                                 self.ident_f[:]), reads=[zi, self.ident_f], writes=[ps])
            P.op("act", lambda g: g.activation(out=zT[t % 3][:].rearrange("p a b -> p (a b)"), in_=ps[:], func=AF.Copy),
                 reads=[ps], writes=[zT[t % 3]])
            P.dma(yv[:, :, t * 128:(t + 1) * 128], zT[t % 3][:], reads=[zT[t % 3]], writes=[self.db("yT", (2, t))])
        P.release()

    KB.declare_hyena = declare_hyena
    KB.hy_shortconv = hy_shortconv
    KB.hy_load_tabs = hy_load_tabs
    KB.hy_stage1 = hy_stage1
    KB.hy_stage2 = hy_stage2
    KB.hy_filters = hy_filters
    KB.hy_conv = hy_conv
    KB.hy_ctx_khat = hy_ctx_khat
    KB.hy_ctx_conv = hy_ctx_conv
    KB.phase_hyena = phase_hyena


_hyena_methods()


def rwkv_tables():
    idx = np.arange(128)
    out = np.zeros((2, 6, 128, 128), np.float32)
    for dd in range(2):
        incl = (idx[:, None] <= idx[None, :]) if dd == 0 else (idx[:, None] >= idx[None, :])
        incl = incl.astype(np.float32)
        ref = 63 if dd == 0 else 64
        out[dd, 0] = incl
        out[dd, 1] = incl - incl[:, ref:ref + 1]
        out[dd, 2] = 1.0 - incl
        out[dd, 3] = incl - np.eye(128, dtype=np.float32)
        out[dd, 4] = incl
        out[dd, 5] = out[dd, 3].T
    return out


def _rwkv_methods():
    def declare_rwkv(self):
        L = 2
        self.inp("rwkv_mu", [L, 2, 1792])
        self.inp("rwkv_kvec", [L, 2, 512])
        self.inp("rwkv_lnp", [L, 3, 512])
        self.inp("rwkv_wupA", [L, 65, 1024])
        self.inp("rwkv_aupA", [L, 65, 1024])
        self.inp("rwkv_g_up", [L, 128, 512])
        self.inp("rw_tri", [2, 6, 128, 128])
        self.scr("yf", [TT, 512])

    def phase_rwkv(self, l, with_ctx):
        P = self.P
        d = self.d
        idb = self.ident_b
        idf = self.ident_f
        P.mark()

        def dve(fn, r, w):
            return P.op("dve", fn, reads=r, writes=w)

        def act(fn, r, w):
            return P.op("act", fn, reads=r, writes=w)

        def pool(fn, r, w):
            return P.op("pool", fn, reads=r, writes=w)

        def T32(name, shape=(128, 512)):
            return P.sbuf(list(shape), F32, name)

        def T16(name, shape=(128, 512)):
            return P.sbuf(list(shape), BF16, name)

        mu = T32("mu", (128, 3, 1792))
        for j in range(2):
            P.dma(mu[:, 1 + j, :], d["rwkv_mu"][l, j:j + 1, :].to_broadcast([128, 1792]), writes=[mu])
        dve(lambda g: g.tensor_tensor(out=mu[:, 0, :], in0=mu[:, 1, :], in1=mu[:, 2, :], op=ALU.add), [mu], [mu])
        dve(lambda g: g.tensor_scalar(out=mu[:, 0, :], in0=mu[:, 0, :], scalar1=-1.0, scalar2=1.0, op0=ALU.mult,
                                      op1=ALU.add), [mu], [mu])
        kv = T32("kv", (128, 3, 512))
        for j in range(2):
            P.dma(kv[:, j, :], d["rwkv_kvec"][l, j:j + 1, :].to_broadcast([128, 512]), writes=[kv])
        dve(lambda g: g.tensor_scalar(out=kv[:, 2, :], in0=kv[:, 1, :], scalar1=-1.0, scalar2=1.0, op0=ALU.mult,
                                      op1=ALU.add), [kv], [kv])
        lnp = T32("lnp", (128, 3, 512))
        P.dma(lnp[:].rearrange("p a c -> p (a c)"),
              d["rwkv_lnp"][l:l + 1].rearrange("o a c -> o (a c)").to_broadcast([128, 1536]), writes=[lnp])
        wupA = T16("wupA", (65, 2, 512))
        P.dma(wupA[:].rearrange("p a c -> p (a c)"), d["rwkv_wupA"][l], writes=[wupA], q="pool")
        aupA = T16("aupA", (65, 2, 512))
        P.dma(aupA[:].rearrange("p a c -> p (a c)"), d["rwkv_aupA"][l], writes=[aupA], q="pool")
        gup = T16("gup", (128, 512))
        P.dma(gup[:], d["rwkv_g_up"][l], writes=[gup], q="pool")
        tri = T32("tri", (128, 2, 6, 128))
        P.dma(tri[:], d["rw_tri"].rearrange("a b p c -> p a b c"), writes=[tri])
        onec = T32("onec", (128, 1))
        pool(lambda g: g.memset(onec[:], 1.0), [], [onec])
        TWA = T16("TWA", (65, 128))
        ALA = T16("ALA", (65, 128))
        pool(lambda g: g.memset(TWA[:], 1.0), [], [TWA])
        pool(lambda g: g.memset(ALA[:], 1.0), [], [ALA])
        cur = [T32("cur%d" % i, (128, 1792)) for i in range(3)]
        prv = T32("prv", (128, 1792))
        nxt = T32("nxt", (128, 1792))
        kk = T32("kk")
        sq = T32("sq")
        s8 = T32("s8", (128, 8))
        r8 = T32("r8", (128, 8))
        tw = T16("tw", (128, 64))
        al = T16("al", (128, 64))
        sgl = T16("sgl", (128, 128))
        lw = T32("lw")
        av = T32("av")
        tt_ = T32("tt")
        bb = T32("bb")
        eW, eWi, eLu, eD, elw, eWx, eLux = [T32(nm) for nm in ("eW", "eWi", "eLu", "eD", "elw", "eWx", "eLux")]
        rt, zt, bt, kt = [T16(nm) for nm in ("rt", "zt", "bt", "kt")]
        RTf, ZTf, BTf, KTf = [T16(nm, (64, 8, 128)) for nm in ("RTf", "ZTf", "BTf", "KTf")]
        X0 = [T16("X0%d" % i, (128, 8, 128)) for i in range(2)]
        XT0 = [T16("XT0%d" % i, (128, 8, 128)) for i in range(2)]
        TT0 = [T16("TT0%d" % i, (128, 8, 128)) for i in range(2)]
        AzkT = [T16("AzkT%d" % i, (128, 8, 128)) for i in range(2)]
        ArbT = [T16("ArbT%d" % i, (128, 8, 128)) for i in range(2)]
        ArkT = [T16("ArkT%d" % i, (128, 8, 128)) for i in range(2)]
        vbf = [T16("vbf%d" % i) for i in range(2)]
        bp = [T16("bp%d" % i) for i in range(2)]
        kp = [T16("kp%d" % i) for i in range(2)]
        ru = [T16("ru%d" % i) for i in range(2)]
        zu = [T16("zu%d" % i) for i in range(2)]
        kd = [T32("kd%d" % i) for i in range(2)]
        kd0 = [T32("kd0%d" % i) for i in range(2)]
        sglT = [T16("sglT%d" % i, (128, 128)) for i in range(2)]
        WC = [T32("WC%d" % i, (64, 8)) for i in range(2)]
        yfl = [T32("yfl%d" % i) for i in range(2)]
        Xs = [T16("Xs%d" % i, (128, 8, 128)) for i in range(2)]
        XTs = [T16("XTs%d" % i, (128, 8, 128)) for i in range(2)]
        TTs = [T16("TTs%d" % i, (128, 8, 128)) for i in range(2)]
        Zp, Gm, U0 = [T16(nm) for nm in ("Zp", "Gm", "U0")]
        Y0 = T32("Y0")
        RpT = T16("RpT", (64, 8, 128))
        Mm = T32("Mm", (64, 8, 64))
        NTt = T32("NTt", (64, 8, 64))
        STf = T32("STf", (64, 8, 64))
        STb = T16("STb", (64, 8, 64))
        Yt = T32("Yt")
        m8 = T32("m8", (128, 8))
        v8 = T32("v8", (128, 8))
        b8 = T32("b8", (128, 8))
        yc = T32("yc")
        sq2 = T32("sq2")
        ob = T16("ob", (128, 4, 128))
        allpb = [b for k, b in self.dbufs.items() if k[0] == "pb"]
        ps = self.ps

        def view8(ap):
            return ap.rearrange("p (h m) -> p h m", m=64)

        def b8c(t8):
            return t8[:].unsqueeze(2).to_broadcast([128, 8, 64])

        for pss in range(2):
            dd = pss
            order = [32, 33] + list(range(32)) if dd == 0 else [33, 32] + list(range(31, -1, -1))
            pool(lambda g: g.memset(STf[:], 0.0), [], [STf])
            pool(lambda g: g.memset(STb[:], 0.0), [], [STb])

            def loads(tj, tn):
                rr = tm_row(tn * 128)
                P.dma(cur[tj % 3][:], d["pb"][rr:rr + 128, :], reads=allpb, writes=[cur[tj % 3]])
                P.dma(prv[:], d["pb"][rr - 1:rr + 127, :], reads=allpb, writes=[prv])
                P.dma(nxt[:], d["pb"][rr + 1:rr + 129, :], reads=allpb, writes=[nxt])

            def stageA(ti):
                t = order[ti]
                pr = ti % 2
                need_y = with_ctx or t < 32
                cu, pv_, nx = cur[ti % 3], prv, nxt
                if pss == 1 and need_y:
                    P.dma(yfl[pr][:], d["yf"][t * 128:(t + 1) * 128, :], reads=[self.db("yf", t)], writes=[yfl[pr]])
                pool(lambda g: g.tensor_tensor(out=nx[:], in0=nx[:], in1=mu[:, 2, :], op=ALU.mult), [nx, mu], [nx])
                dve(lambda g: g.tensor_tensor(out=cu[:], in0=cu[:], in1=mu[:, 0, :], op=ALU.mult), [cu, mu], [cu])
                dve(lambda g: g.tensor_tensor(out=pv_[:], in0=pv_[:], in1=mu[:, 1, :], op=ALU.mult), [pv_, mu], [pv_])
                yield
                dve(lambda g: g.tensor_tensor(out=cu[:], in0=cu[:], in1=pv_[:], op=ALU.add), [cu, pv_], [cu])
                dve(lambda g: g.tensor_tensor(out=cu[:], in0=cu[:], in1=nx[:], op=ALU.add), [cu, nx], [cu])
                if ti + 1 < len(order):
                    loads(ti + 1, order[ti + 1])
                r_ap, k_ap, v_ap = cu[:, 0:512], cu[:, 512:1024], cu[:, 1024:1536]
                dve(lambda g: g.tensor_tensor(out=kk[:], in0=k_ap, in1=kv[:, 0, :], op=ALU.mult), [cu, kv], [kk])
                pool(lambda g: g.tensor_tensor(out=sq[:], in0=kk[:], in1=kk[:], op=ALU.mult), [kk], [sq])
                dve(lambda g: g.tensor_reduce(out=s8[:], in_=view8(sq[:]), axis=AX.X, op=ALU.add), [sq], [s8])
                act(lambda g: g.activation(out=r8[:], in_=s8[:], func=AF.Sqrt, bias=1e-12), [s8], [r8])
                dve(lambda g: g.reciprocal(out=r8[:], in_=r8[:]), [r8], [r8])
                dve(lambda g: g.tensor_tensor(out=view8(kk[:]), in0=view8(kk[:]), in1=b8c(r8), op=ALU.mult),
                    [kk, r8], [kk])
                act(lambda g: g.activation(out=vbf[pr][:], in_=v_ap, func=AF.Copy), [cu], [vbf[pr]])
                act(lambda g: g.activation(out=tw[:], in_=cu[:, 1536:1600], func=AF.Tanh), [cu], [tw])
                act(lambda g: g.activation(out=al[:], in_=cu[:, 1600:1664], func=AF.Copy), [cu], [al])
                yield
                pb0 = self.psb(0)
                P.op("pe", lambda g: g.transpose(pb0[0:64, 0:128], tw[:], idb[:]), reads=[tw, idb], writes=[ps[0]])
                P.op("pe", lambda g: g.transpose(pb0[0:64, 128:256], al[:], idb[:]), reads=[al, idb], writes=[ps[0]])
                dve(lambda g: g.tensor_copy(out=TWA[0:64, :], in_=pb0[0:64, 0:128]), [ps[0]], [TWA])
                dve(lambda g: g.tensor_copy(out=ALA[0:64, :], in_=pb0[0:64, 128:256]), [ps[0]], [ALA])
                if pss == 1:
                    act(lambda g: g.activation(out=sgl[:], in_=cu[:, 1664:1792], func=AF.Sigmoid), [cu], [sgl])
                    pb1 = self.psb(1)
                    P.op("pe", lambda g: g.transpose(pb1[:, 0:128], sgl[:], idb[:]), reads=[sgl, idb], writes=[ps[1]])
                    dve(lambda g: g.tensor_copy(out=sglT[pr][:], in_=pb1[:, 0:128]), [ps[1]], [sglT[pr]])
                    P.mm(ps[2][:], ALA[:], aupA[:, 0, :], reads=[ALA, aupA], writes=[ps[2]])
                    act(lambda g: g.activation(out=av[:], in_=ps[2][:], func=AF.Sigmoid), [ps[2]], [av])
                    dve(lambda g: g.tensor_tensor(out=tt_[:], in0=av[:], in1=kv[:, 1, :], op=ALU.mult), [av, kv], [tt_])
                    pool(lambda g: g.tensor_tensor(out=tt_[:], in0=tt_[:], in1=kv[:, 2, :], op=ALU.add), [tt_, kv], [tt_])
                    dve(lambda g: g.tensor_tensor(out=kd0[pr][:], in0=k_ap, in1=tt_[:], op=ALU.mult), [cu, tt_], [kd0[pr]])
                yield
                P.mm(ps[2][:], TWA[:], wupA[:, dd, :], reads=[TWA, wupA], writes=[ps[2]])
                act(lambda g: g.activation(out=lw[:], in_=ps[2][:], func=AF.Sigmoid), [ps[2]], [lw])
                dve(lambda g: g.tensor_scalar(out=lw[:], in0=lw[:], scalar1=-0.6065306597126334, scalar2=None,
                                              op0=ALU.mult), [lw], [lw])
                P.mm(ps[3][:], ALA[:], aupA[:, dd, :], reads=[ALA, aupA], writes=[ps[3]])
                act(lambda g: g.activation(out=av[:], in_=ps[3][:], func=AF.Sigmoid), [ps[3]], [av])
                dve(lambda g: g.tensor_tensor(out=tt_[:], in0=av[:], in1=kv[:, 1, :], op=ALU.mult), [av, kv], [tt_])
                pool(lambda g: g.tensor_tensor(out=tt_[:], in0=tt_[:], in1=kv[:, 2, :], op=ALU.add), [tt_, kv], [tt_])
                dve(lambda g: g.tensor_tensor(out=kd[pr][:], in0=k_ap, in1=tt_[:], op=ALU.mult), [cu, tt_], [kd[pr]])
                pool(lambda g: g.tensor_tensor(out=bb[:], in0=kk[:], in1=av[:], op=ALU.mult), [kk, av], [bb])
                yield
                P.mm(ps[0][:], tri[:, dd, 0, :], lw[:], reads=[tri, lw], writes=[ps[0]])
                P.mm(ps[1][:], tri[:, dd, 1, :], lw[:], reads=[tri, lw], writes=[ps[1]])
                P.mm(ps[2][:], tri[:, dd, 2, :], lw[:], reads=[tri, lw], writes=[ps[2]])
                for h in range(8):
                    P.mm(ps[3][0:64, h:h + 1], lw[:, h * 64:(h + 1) * 64], onec[:], reads=[lw, onec], writes=[ps[3]])
                act(lambda g: g.activation(out=WC[pr][:], in_=ps[3][0:64, 0:8], func=AF.Exp), [ps[3]], [WC[pr]])
                act(lambda g: g.activation(out=eLu[:], in_=ps[0][:], func=AF.Exp), [ps[0]], [eLu])
                act(lambda g: g.activation(out=eW[:], in_=ps[1][:], func=AF.Exp), [ps[1]], [eW])
                act(lambda g: g.activation(out=eWi[:], in_=ps[1][:], func=AF.Exp, scale=-1.0), [ps[1]], [eWi])
                act(lambda g: g.activation(out=eD[:], in_=ps[2][:], func=AF.Exp), [ps[2]], [eD])
                act(lambda g: g.activation(out=elw[:], in_=lw[:], func=AF.Exp, scale=-1.0), [lw], [elw])
                yield
                dve(lambda g: g.tensor_tensor(out=eWx[:], in0=eW[:], in1=elw[:], op=ALU.mult), [eW, elw], [eWx])
                pool(lambda g: g.tensor_tensor(out=eLux[:], in0=eLu[:], in1=elw[:], op=ALU.mult), [eLu, elw], [eLux])
                dve(lambda g: g.tensor_tensor(out=rt[:], in0=r_ap, in1=eW[:], op=ALU.mult), [cu, eW], [rt])
                dve(lambda g: g.scalar_tensor_tensor(out=zt[:], in0=kk[:], scalar=-1.0, in1=eWx[:], op0=ALU.mult,
                                                     op1=ALU.mult), [kk, eWx], [zt])
                pool(lambda g: g.tensor_tensor(out=bt[:], in0=bb[:], in1=eWi[:], op=ALU.mult), [bb, eWi], [bt])
                dve(lambda g: g.tensor_tensor(out=kt[:], in0=kd[pr][:], in1=eWi[:], op=ALU.mult), [kd[pr], eWi], [kt])
                yield
                pool(lambda g: g.tensor_tensor(out=bp[pr][:], in0=bb[:], in1=eD[:], op=ALU.mult), [bb, eD], [bp[pr]])
                dve(lambda g: g.tensor_tensor(out=kp[pr][:], in0=kd[pr][:], in1=eD[:], op=ALU.mult), [kd[pr], eD], [kp[pr]])
                pool(lambda g: g.tensor_tensor(out=ru[pr][:], in0=r_ap, in1=eLu[:], op=ALU.mult), [cu, eLu], [ru[pr]])
                dve(lambda g: g.scalar_tensor_tensor(out=zu[pr][:], in0=kk[:], scalar=-1.0, in1=eLux[:], op0=ALU.mult,
                                                     op1=ALU.mult), [kk, eLux], [zu[pr]])
                for qi, (src, dstf) in enumerate(((rt, RTf), (zt, ZTf), (bt, BTf), (kt, KTf))):
                    pbx = self.psb(qi % 2)
                    for h in range(8):
                        P.op("pe", lambda g: g.transpose(pbx[0:64, h * 128:(h + 1) * 128], src[:, h * 64:(h + 1) * 64],
                                                         idb[:]), reads=[src, idb], writes=[ps[qi % 2]])
                    if qi % 2 == 0:
                        act(lambda g: g.activation(out=dstf[:].rearrange("p h t -> p (h t)"), in_=pbx[0:64, :],
                                                   func=AF.Copy), [ps[qi % 2]], [dstf])
                    else:
                        dve(lambda g: g.tensor_copy(out=dstf[:].rearrange("p h t -> p (h t)"), in_=pbx[0:64, :]),
                            [ps[qi % 2]], [dstf])
                    if qi == 1:
                        yield
                yield
                nb = [0]

                def amat(Lf, Rf, mi, dst):
                    for hg in range(2):
                        pa = ps[nb[0] % 4]
                        nb[0] += 1
                        for j in range(4):
                            h = hg * 4 + j
                            P.mm(pa[:, j * 128:(j + 1) * 128], Lf[:, h, :], Rf[:, h, :], reads=[Lf, Rf], writes=[pa])
                        dve(lambda g: g.tensor_tensor(out=dst[:, hg * 4:(hg + 1) * 4, :],
                                                      in0=pa[:].rearrange("p (h t) -> p h t", h=4),
                                                      in1=tri[:, dd, mi, :].unsqueeze(1).to_broadcast([128, 4, 128]),
                                                      op=ALU.mult), [pa, tri], [dst])

                amat(ZTf, BTf, 5, X0[pr])
                amat(BTf, ZTf, 3, XT0[pr])
                yield
                amat(KTf, ZTf, 3, AzkT[pr])
                amat(BTf, RTf, 4, ArbT[pr])
                yield
                amat(KTf, RTf, 4, ArkT[pr])
                pool(lambda g: g.tensor_tensor(out=TT0[pr][:], in0=XT0[pr][:],
                                               in1=idb[:].unsqueeze(1).to_broadcast([128, 8, 128]), op=ALU.add),
                     [XT0[pr], idb], [TT0[pr]])

            def stageB(ti):
                t = order[ti]
                pr = ti % 2
                need_y = with_ctx or t < 32
                cu = cur[ti % 3]
                r_ap, v_ap = cu[:, 0:512], cu[:, 1024:1536]
                Xc, XTc, TTc = X0[pr], XT0[pr], TT0[pr]
                cx = 0
                for it in range(6):
                    Xn, XTn, TTn = Xs[cx], XTs[cx], TTs[cx]
                    for hg in range(2):
                        p2 = ps[4 + hg]
                        for j in range(4):
                            h = hg * 4 + j
                            P.mm(p2[:, j * 128:(j + 1) * 128], XTc[:, h, :], Xc[:, h, :], reads=[XTc, Xc], writes=[p2])
                        act(lambda g: g.activation(out=Xn[:, hg * 4:(hg + 1) * 4, :].rearrange("p h t -> p (h t)"),
                                                   in_=p2[:], func=AF.Copy), [p2], [Xn])
                        if it < 5:
                            p3 = ps[6 + hg]
                            for j in range(4):
                                h = hg * 4 + j
                                P.mm(p3[:, j * 128:(j + 1) * 128], Xc[:, h, :], XTc[:, h, :], reads=[Xc, XTc],
                                     writes=[p3])
                            dve(lambda g: g.tensor_copy(out=XTn[:, hg * 4:(hg + 1) * 4, :].rearrange("p h t -> p (h t)"),
                                                        in_=p3[:]), [p3], [XTn])
                    yield
                    for hg in range(2):
                        p4 = ps[4 + hg]
                        for j in range(4):
                            h = hg * 4 + j
                            P.mm(p4[:, j * 128:(j + 1) * 128], Xn[:, h, :], TTc[:, h, :], start=True, stop=False,
                                 reads=[Xn, TTc], writes=[p4])
                            P.mm(p4[:, j * 128:(j + 1) * 128], idb[:], TTc[:, h, :], start=False, stop=True,
                                 reads=[idb, TTc], writes=[p4])
                        act(lambda g: g.activation(out=TTn[:, hg * 4:(hg + 1) * 4, :].rearrange("p h t -> p (h t)"),
                                                   in_=p4[:], func=AF.Copy), [p4], [TTn])
                    Xc, XTc, TTc = Xn, XTn, TTn
                    cx = 1 - cx
                    yield
                TT = TTc
                for h in range(8):
                    P.mm(ps[4][:, h * 64:(h + 1) * 64], TT[:, h, :], zu[pr][:, h * 64:(h + 1) * 64], reads=[TT, zu[pr]],
                         writes=[ps[4]])
                act(lambda g: g.activation(out=Zp[:], in_=ps[4][:], func=AF.Copy), [ps[4]], [Zp])
                for h in range(8):
                    P.mm(ps[5][:, h * 64:(h + 1) * 64], AzkT[pr][:, h, :], vbf[pr][:, h * 64:(h + 1) * 64],
                         reads=[AzkT[pr], vbf[pr]], writes=[ps[5]])
                dve(lambda g: g.tensor_copy(out=Gm[:], in_=ps[5][:]), [ps[5]], [Gm])
                for h in range(8):
                    P.mm(ps[6][:, h * 64:(h + 1) * 64], TT[:, h, :], Gm[:, h * 64:(h + 1) * 64], reads=[TT, Gm], writes=[ps[6]])
                act(lambda g: g.activation(out=U0[:], in_=ps[6][:], func=AF.Copy), [ps[6]], [U0])
                yield
                for h in range(8):
                    P.mm(ps[7][0:64, h * 64:(h + 1) * 64], Zp[:, h * 64:(h + 1) * 64], bp[pr][:, h * 64:(h + 1) * 64],
                         reads=[Zp, bp[pr]], writes=[ps[7]])
                dve(lambda g: g.tensor_tensor(out=Mm[:], in0=idf[0:64, 0:64].unsqueeze(1).to_broadcast([64, 8, 64]),
                                              in1=WC[pr][:].unsqueeze(2).to_broadcast([64, 8, 64]), op=ALU.mult),
                    [idf, WC[pr]], [Mm])
                dve(lambda g: g.tensor_tensor(out=Mm[:], in0=Mm[:], in1=ps[7][0:64, :].rearrange("p (h m) -> p h m", m=64),
                                              op=ALU.add), [Mm, ps[7]], [Mm])
                for h in range(8):
                    hs = slice(h * 64, (h + 1) * 64)
                    P.mm(ps[4][0:64, hs], bp[pr][:, hs], U0[:, hs], start=True, stop=False, reads=[bp[pr], U0], writes=[ps[4]])
                    P.mm(ps[4][0:64, hs], kp[pr][:, hs], vbf[pr][:, hs], start=False, stop=True, reads=[kp[pr], vbf[pr]],
                         writes=[ps[4]])
                act(lambda g: g.activation(out=NTt[:].rearrange("p h m -> p (h m)"), in_=ps[4][0:64, :], func=AF.Copy),
                    [ps[4]], [NTt])
                yield
                if need_y:
                    for h in range(8):
                        hs = slice(h * 64, (h + 1) * 64)
                        P.mm(ps[5][:, hs], ArbT[pr][:, h, :], U0[:, hs], start=True, stop=False, reads=[ArbT[pr], U0],
                             writes=[ps[5]])
                        P.mm(ps[5][:, hs], ArkT[pr][:, h, :], vbf[pr][:, hs], start=False, stop=True,
                             reads=[ArkT[pr], vbf[pr]], writes=[ps[5]])
                    act(lambda g: g.activation(out=Y0[:], in_=ps[5][:], func=AF.Copy), [ps[5]], [Y0])
                    for hg in range(2):
                        prr = ps[6 + hg]
                        for j in range(4):
                            h = hg * 4 + j
                            hs = slice(h * 64, (h + 1) * 64)
                            P.mm(prr[0:64, j * 128:(j + 1) * 128], ru[pr][:, hs], idb[:], start=True, stop=False,
                                 reads=[ru[pr], idb], writes=[prr])
                            P.mm(prr[0:64, j * 128:(j + 1) * 128], Zp[:, hs], ArbT[pr][:, h, :], start=False, stop=True,
                                 reads=[Zp, ArbT[pr]], writes=[prr])
                        dve(lambda g: g.tensor_copy(out=RpT[:, hg * 4:(hg + 1) * 4, :].rearrange("p h t -> p (h t)"),
                                                    in_=prr[0:64, :]), [prr], [RpT])
                    yield
                    for h in range(8):
                        P.mm(ps[4][:, h * 64:(h + 1) * 64], RpT[:, h, :], STb[:, h, :], reads=[RpT, STb], writes=[ps[4]])
                    dve(lambda g: g.tensor_tensor(out=Yt[:], in0=ps[4][:], in1=Y0[:], op=ALU.add), [ps[4], Y0], [Yt])
                for h in range(8):
                    P.mm(ps[5][0:64, h * 64:(h + 1) * 64], Mm[:, h, :], STf[:, h, :], reads=[Mm, STf], writes=[ps[5]])
                dve(lambda g: g.tensor_tensor(out=STf[:], in0=ps[5][0:64, :].rearrange("p (h m) -> p h m", m=64),
                                              in1=NTt[:], op=ALU.add), [ps[5], NTt], [STf])
                act(lambda g: g.activation(out=STb[:], in_=STf[:], func=AF.Copy), [STf], [STb])
                yield
                if not need_y:
                    return
                if pss == 0:
                    P.dma(d["yf"][t * 128:(t + 1) * 128, :], Yt[:], reads=[Yt], writes=[self.db("yf", t)])
                    return
                dve(lambda g: g.tensor_tensor(out=Yt[:], in0=Yt[:], in1=yfl[pr][:], op=ALU.add), [Yt, yfl[pr]], [Yt])
                dve(lambda g: g.tensor_reduce(out=m8[:], in_=view8(Yt[:]), axis=AX.X, op=ALU.add), [Yt], [m8])
                dve(lambda g: g.tensor_scalar(out=m8[:], in0=m8[:], scalar1=1.0 / 64, scalar2=None, op0=ALU.mult), [m8], [m8])
                dve(lambda g: g.tensor_tensor(out=view8(yc[:]), in0=view8(Yt[:]), in1=b8c(m8), op=ALU.subtract),
                    [Yt, m8], [yc])
                pool(lambda g: g.tensor_tensor(out=sq2[:], in0=yc[:], in1=yc[:], op=ALU.mult), [yc], [sq2])
                dve(lambda g: g.tensor_reduce(out=v8[:], in_=view8(sq2[:]), axis=AX.X, op=ALU.add), [sq2], [v8])
                act(lambda g: g.activation(out=v8[:], in_=v8[:], func=AF.Sqrt, scale=1.0 / 64, bias=64e-5), [v8], [v8])
                dve(lambda g: g.reciprocal(out=v8[:], in_=v8[:]), [v8], [v8])
                yield
                dve(lambda g: g.tensor_tensor(out=view8(yc[:]), in0=view8(yc[:]), in1=b8c(v8), op=ALU.mult), [yc, v8], [yc])
                pool(lambda g: g.tensor_tensor(out=yc[:], in0=yc[:], in1=lnp[:, 0, :], op=ALU.mult), [yc, lnp], [yc])
                pool(lambda g: g.tensor_tensor(out=yc[:], in0=yc[:], in1=lnp[:, 1, :], op=ALU.add), [yc, lnp], [yc])
                dve(lambda g: g.tensor_tensor(out=kd0[pr][:], in0=kd0[pr][:], in1=kd[pr][:], op=ALU.add),
                    [kd0[pr], kd[pr]], [kd0[pr]])
                dve(lambda g: g.tensor_tensor(out=kd0[pr][:], in0=kd0[pr][:], in1=r_ap, op=ALU.mult), [kd0[pr], cu], [kd0[pr]])
                dve(lambda g: g.scalar_tensor_tensor(out=sq2[:], in0=kd0[pr][:], scalar=0.5, in1=lnp[:, 2, :], op0=ALU.mult,
                                                     op1=ALU.mult), [kd0[pr], lnp], [sq2])
                dve(lambda g: g.tensor_reduce(out=b8[:], in_=view8(sq2[:]), axis=AX.X, op=ALU.add), [sq2], [b8])
                dve(lambda g: g.tensor_tensor(out=view8(sq2[:]), in0=view8(v_ap), in1=b8c(b8), op=ALU.mult), [cu, b8], [sq2])
                pool(lambda g: g.tensor_tensor(out=yc[:], in0=yc[:], in1=sq2[:], op=ALU.add), [yc, sq2], [yc])
                yield
                P.mm(ps[6][:], sglT[pr][:], gup[:], reads=[sglT[pr], gup], writes=[ps[6]])
                dve(lambda g: g.tensor_tensor(out=yc[:], in0=yc[:], in1=ps[6][:], op=ALU.mult), [yc, ps[6]], [yc])
                for kc in range(4):
                    P.op("pe", lambda g: g.transpose(ps[7][:, kc * 128:(kc + 1) * 128], yc[:, kc * 128:(kc + 1) * 128],
                                                     idf[:]), reads=[yc, idf], writes=[ps[7]])
                act(lambda g: g.activation(out=ob[:].rearrange("p a b -> p (a b)"), in_=ps[7][:], func=AF.Copy),
                    [ps[7]], [ob])
                P.dma(d["yT"][1].rearrange("(kc p) t -> p kc t", p=128)[:, :, t * 128:(t + 1) * 128], ob[:], reads=[ob],
                      writes=[self.db("yT", (1, t))])

            def run2(ga, gb):
                live = [g for g in (ga, gb) if g is not None]
                while live:
                    for g in list(live):
                        try:
                            next(g)
                        except StopIteration:
                            live.remove(g)

            loads(0, order[0])
            run2(stageA(0), None)
            for ti in range(len(order)):
                nxa = stageA(ti + 1) if ti + 1 < len(order) else None
                run2(stageB(ti), nxa)
        P.release()

    KB.declare_rwkv = declare_rwkv
    KB.phase_rwkv = phase_rwkv


_rwkv_methods()


def build_program():
    nc = bass.Bass("TRN2", target_bir_lowering=False)
    kb = KB(nc)
    kb.declare_common()
    kb.declare_pin()
    kb.declare_attn()
    kb.declare_merge()
    kb.declare_hyena()
    kb.declare_rwkv()
    kb.outp("out", [NLAT, DM])
    kb.alloc_persist()
    d = kb.d
    for l in range(2):
        with_ctx = (l == 0)
        kb.phase_mod(l)
        src = (d["xall"], "xall") if l == 0 else (d["xs"], "xs")
        kb.phase_ffn(l, 0, src, (d["xs"], "xs"), NTILE)
        kb.phase_pin(l, (d["xs"], "xs"))
        kb.phase_mla(l, with_ctx)
        kb.phase_swa(l, with_ctx)
        kb.phase_hyena(l, with_ctx)
        kb.phase_rwkv(l, with_ctx)
        kb.phase_merge(l, with_ctx)
        if l == 0:
            kb.phase_ffn(l, 2, (d["xs"], "xs"), (d["xs"], "xs"), NTILE)
        else:
            kb.phase_ffn(l, 2, (d["xs"], "xs"), (d["out"], "out"), 32)
    kb.P.barrier()
    return nc, kb


def kernel(**inputs):
    from concourse.bass_utils import run_bass_kernel_spmd
    nc, kb = build_program()
    sh = host_shared(inputs)
    names = [k for k in kb.d if k in sh]
    maps = []
    for b in range(8):
        pc = host_core(inputs, b)
        m = {k: sh[k] for k in names}
        m.update(pc)
        maps.append(m)
    res = run_bass_kernel_spmd(nc, maps, core_ids=list(range(8)))
    out = np.stack([np.asarray(res.results[b]["out"], dtype=np.float32) for b in range(8)], axis=0)
    return out
```

```python
import numpy as np
import concourse.bass as bass
import concourse.mybir as mybir

F32 = mybir.dt.float32
BF16 = mybir.dt.bfloat16
AF = mybir.ActivationFunctionType
ALU = mybir.AluOpType
AX = mybir.AxisListType

NDSEM = 36
NHW = 28
SEM_LIMIT = 30000
SB_BASE = 16640
SB_TOP = 229376
LAZY_D = 2
SAME_ENG_GAP = 3


class Buf:
    __slots__ = ("w", "r", "name")

    def __init__(self, name=""):
        self.w = None
        self.r = {}
        self.name = name


class Tile:
    def __init__(self, t, buf):
        self.t = t
        self.b = buf

    def __getitem__(self, k):
        return self.t[k]


def _bufs(xs):
    return [x.b if isinstance(x, Tile) else x for x in xs]


class Prog:
    def __init__(self, nc):
        self.nc = nc
        self.eng = {"pe": nc.tensor, "dve": nc.vector, "act": nc.scalar,
                    "pool": nc.gpsimd, "sp": nc.sync}
        self.cnt = {e: 0 for e in self.eng}
        self.known = {e: {} for e in self.eng}
        self.esem = {}
        self.egen = {e: 0 for e in self.eng}
        self.ebase = {e: 0 for e in self.eng}
        self.dsem = []
        self.duse = []
        self.dgen = []
        self.semtab = {}
        self.dnext = 0
        self.dnext_sw = 0
        self.pending = []
        self.nwait = 0
        self.ninstr = 0
        self.sb_off = SB_BASE
        self.sb_mark = []
        self.uid = 0
        nc = self.nc
        for e in self.eng:
            self.esem[e] = nc.alloc_semaphore("es_%s_0" % e)
            self.semtab[("e", e, 0)] = self.esem[e]
        for i in range(NDSEM):
            self.dsem.append(nc.alloc_semaphore("ds%d_0" % i))
            self.semtab[("d", i, 0)] = self.dsem[i]
            self.duse.append(0)
            self.dgen.append(0)

    def _rot_e(self, e):
        if self.cnt[e] - self.ebase[e] >= SEM_LIMIT:
            self.egen[e] += 1
            self.ebase[e] = self.cnt[e]
            self.esem[e] = self.nc.alloc_semaphore("es_%s_%d" % (e, self.egen[e]))
            self.semtab[("e", e, self.egen[e])] = self.esem[e]

    def _rot_d(self, i):
        if 16 * self.duse[i] >= SEM_LIMIT:
            self.dgen[i] += 1
            self.duse[i] = 0
            self.dsem[i] = self.nc.alloc_semaphore("ds%d_%d" % (i, self.dgen[i]))
            self.semtab[("d", i, self.dgen[i])] = self.dsem[i]

    def sbuf(self, shape, dtype, name=None):
        self.uid += 1
        name = (name or "t") + "_%d" % self.uid
        nbytes = int(np.prod(shape[1:])) * mybir.dt.size(dtype)
        off = (self.sb_off + 31) // 32 * 32
        t = self.nc.alloc_sbuf_tensor_at(name, list(shape), dtype, offset=off)
        self.sb_off = off + nbytes
        assert self.sb_off <= SB_TOP, ("SBUF overflow", name, self.sb_off)
        return Tile(t, Buf(name))

    def mark(self):
        self.sb_mark.append(self.sb_off)

    def release(self):
        self.barrier()
        self.sb_off = self.sb_mark.pop()

    def _wait(self, e, kind, id_, gen, val):
        self.known[e][(kind, id_)] = (gen, val)
        self.eng[e].wait_ge(self.semtab[(kind, id_, gen)], val)
        self.nwait += 1

    def _need(self, e, toks):
        kn = self.known[e]
        req = {}
        for t in toks:
            if t is None:
                continue
            kind, id_, gen, val, absidx = t
            if kind == "e" and id_ == e:
                if e == "pe":
                    continue
                if self.cnt[e] + 1 - absidx >= SAME_ENG_GAP:
                    continue
            k = (kind, id_)
            if kn.get(k, (-1, 0)) >= (gen, val):
                continue
            if req.get(k, (-1, 0)) < (gen, val):
                req[k] = (gen, val)
        for (kind, id_), (gen, val) in req.items():
            self._wait(e, kind, id_, gen, val)

    @staticmethod
    def _deps(reads, writes):
        toks = []
        for b in reads:
            toks.append(b.w)
        for b in writes:
            toks.append(b.w)
            toks.extend(b.r.values())
        return toks

    @staticmethod
    def _commit(tok, reads, writes):
        k = (tok[0], tok[1])
        for b in reads:
            b.r[k] = tok
        for b in writes:
            b.w = tok
            b.r = {}

    def op(self, e, fn, reads=(), writes=()):
        reads = _bufs(reads)
        writes = _bufs(writes)
        self._hazard_flush(reads, writes)
        self._rot_e(e)
        self._need(e, self._deps(reads, writes))
        ins = fn(self.eng[e])
        self.cnt[e] += 1
        ins.then_inc(self.esem[e], 1)
        tok = ("e", e, self.egen[e], self.cnt[e] - self.ebase[e], self.cnt[e])
        self._commit(tok, reads, writes)
        self.ninstr += 1
        return ins

    def _flush(self, upto=None):
        n = len(self.pending) if upto is None else upto
        todo, self.pending = self.pending[:n], self.pending[n:]
        for (out, in_, reads, writes, q, kw, _) in todo:
            self._dma_now(out, in_, reads, writes, q, kw)

    def _hazard_flush(self, reads, writes):
        if not self.pending:
            return
        rs = set(map(id, reads))
        ws = set(map(id, writes))
        last = -1
        for i, (_, _, pr, pw, _, _, _) in enumerate(self.pending):
            hit = False
            for b in pr:
                if id(b) in ws:
                    hit = True
            for b in pw:
                if id(b) in ws or id(b) in rs:
                    hit = True
            if hit:
                last = i
        if last >= 0:
            self._flush(last + 1)

    def dma(self, out, in_, reads=(), writes=(), q="sp", **kw):
        reads = _bufs(reads)
        writes = _bufs(writes)
        self._hazard_flush(reads, writes)
        is_store = type(out.tensor).__name__.startswith("DRam") and not type(in_.tensor).__name__.startswith("DRam")
        if is_store and q == "sp" and LAZY_D > 0:
            self.pending.append([out, in_, reads, writes, q, kw, 0])
            return None
        r = self._dma_now(out, in_, reads, writes, q, kw)
        if q == "sp" and self.pending:
            k = 0
            for p in self.pending:
                p[6] += 1
            while k < len(self.pending) and self.pending[k][6] >= LAZY_D:
                k += 1
            if k:
                self._flush(k)
        return r

    def _dma_now(self, out, in_, reads, writes, q, kw):
        if q == "pool":
            i = NHW + self.dnext_sw
            self.dnext_sw = (self.dnext_sw + 1) % (NDSEM - NHW)
        else:
            i = self.dnext
            self.dnext = (self.dnext + 1) % NHW
        toks = self._deps(reads, writes)
        if self.duse[i]:
            toks.append(("d", i, self.dgen[i], 16 * self.duse[i], 0))
        self._need(q, toks)
        self._rot_d(i)
        self.duse[i] += 1
        ins = self.eng[q].dma_start(out=out, in_=in_, **kw)
        ins.then_inc(self.dsem[i], 16)
        tok = ("d", i, self.dgen[i], 16 * self.duse[i], 0)
        self._commit(tok, reads, writes)
        self.ninstr += 1
        return ins

    def barrier(self, engines=None):
        self._flush()
        toks = []
        for e in self.eng:
            if self.cnt[e] > self.ebase[e]:
                toks.append(("e", e, self.egen[e], self.cnt[e] - self.ebase[e]))
            elif self.egen[e] > 0:
                toks.append(("e", e, self.egen[e] - 1, SEM_LIMIT))
        for i, u in enumerate(self.duse):
            if u:
                toks.append(("d", i, self.dgen[i], 16 * u))
        for e in (engines or self.eng):
            kn = self.known[e]
            for kind, id_, gen, val in toks:
                if kind == "e" and id_ == e and e in ("sp", "pe"):
                    continue
                if kn.get((kind, id_), (-1, 0)) >= (gen, val):
                    continue
                self._wait(e, kind, id_, gen, val)

    def mm(self, out, lhsT, rhs, start=True, stop=True, reads=(), writes=()):
        return self.op("pe", lambda g: g.matmul(out, lhsT, rhs, start=start, stop=stop),
                       reads=reads, writes=writes)


DM = 1024
NLAT = 4096
NCTX = 256
TT = NLAT + NCTX
NTILE = TT // 128
FF = 2816
NFC = FF // 128
EPS = 1e-6
I32 = mybir.dt.int32


class KB:
    def __init__(self, nc, ext_in=(), ext_out=()):
        self.nc = nc
        self.P = Prog(nc)
        self.ext_in = set(ext_in)
        self.ext_out = set(ext_out)
        self.d = {}
        self.dbufs = {}
        self.ps = [Tile(nc.alloc_psum_tensor("ps%d" % i, [128, 512], F32), Buf("ps%d" % i))
                   for i in range(8)]

    def inp(self, name, shape, dtype=F32):
        self.d[name] = self.nc.dram_tensor(name, list(shape), dtype, kind="ExternalInput").ap()
        return self.d[name]

    def outp(self, name, shape, dtype=F32):
        self.d[name] = self.nc.dram_tensor(name, list(shape), dtype, kind="ExternalOutput").ap()
        return self.d[name]

    def scr(self, name, shape, dtype=F32):
        kind = "Internal"
        if name in self.ext_in:
            kind = "ExternalInput"
        elif name in self.ext_out:
            kind = "ExternalOutput"
        self.d[name] = self.nc.dram_tensor(name, list(shape), dtype, kind=kind).ap()
        return self.d[name]

    def db(self, name, idx=0):
        k = (name, idx)
        if k not in self.dbufs:
            self.dbufs[k] = Buf("%s_%s" % (name, idx))
        return self.dbufs[k]

    def psb(self, i):
        return self.ps[i][:].bitcast(BF16)

    def rstd_from_ss(self, ss, n, eps, out):
        P = self.P
        ss_t, ss_ap = ss
        o_t, o_ap = out
        P.op("act", lambda g: g.activation(out=o_ap, in_=ss_ap, func=AF.Sqrt, scale=1.0 / n, bias=eps),
             reads=[ss_t], writes=[o_t])
        P.op("dve", lambda g: g.reciprocal(out=o_ap, in_=o_ap), reads=[o_t], writes=[o_t])

    def declare_common(self):
        L = 2
        self.inp("xall", [TT, DM])
        self.inp("cT", [128, 16])
        self.inp("ident", [128, 128])
        self.inp("w_mod", [L, DM, 9 * DM])
        self.inp("b_mod", [L, 9 * DM])
        self.inp("norm_g", [L, 6, DM])
        self.inp("norm_gT", [L, 128, 48])
        self.inp("ffn_w13", [L, 2, DM, 2 * FF])
        self.inp("ffn_w2", [L, 2, FF, DM])
        self.scr("gtrow", [L, 3, 2, DM])
        self.scr("xs", [TT, DM])

    def alloc_persist(self):
        P = self.P
        self.ident_f = P.sbuf([128, 128], F32, "identf")
        self.ident_b = P.sbuf([128, 128], BF16, "identb")
        P.dma(self.ident_f[:], self.d["ident"], writes=[self.ident_f])
        P.dma(self.ident_b[:], self.d["ident"], writes=[self.ident_b], q="pool")
        self.mcol = P.sbuf([128, 72, 2], F32, "mcol")
        self.AB = P.sbuf([128, 3, 2, 8, 2], F32, "AB")

    def phase_mod(self, l):
        P = self.P
        d = self.d
        P.mark()
        cT = P.sbuf([128, 16], F32, "cT")
        P.dma(cT[:], d["cT"], writes=[cT])
        sc = P.sbuf([128, 16], F32, "sc")
        P.op("act", lambda g: g.activation(out=sc[:], in_=cT[:], func=AF.Silu), reads=[cT], writes=[sc])
        mrow = P.sbuf([2, 9 * DM], F32, "mrow")
        brow = P.sbuf([2, 9 * DM], F32, "brow")
        for s in range(2):
            P.dma(brow[s:s + 1, :], d["b_mod"][l:l + 1, :], writes=[brow])
        wt = [P.sbuf([128, 8, 512], F32, "wmod%d" % i) for i in range(2)]
        wsrc = d["w_mod"][l].rearrange("(k p) n -> p k n", p=128)
        for jb in range(18):
            w = wt[jb % 2]
            P.dma(w[:], wsrc[:, :, jb * 512:(jb + 1) * 512], writes=[w])
            ps = self.ps[jb % 2]
            for k in range(8):
                P.mm(ps[0:2, :], sc[:, 2 * k:2 * k + 2], w[:, k, :], start=(k == 0), stop=(k == 7),
                     reads=[sc, w], writes=[ps])
            P.op("dve", lambda g: g.tensor_tensor(out=mrow[:, jb * 512:(jb + 1) * 512], in0=ps[0:2, :],
                                                  in1=brow[:, jb * 512:(jb + 1) * 512], op=ALU.add),
                 reads=[ps, brow], writes=[mrow])
        psc = self.ps[2]
        for c in range(72):
            P.mm(psc[:, 2 * c:2 * c + 2], mrow[0:2, c * 128:(c + 1) * 128], self.ident_f[0:2, 0:2],
                 reads=[mrow, self.ident_f], writes=[psc])
        mcol = self.mcol
        P.op("dve", lambda g: g.tensor_copy(out=mcol[:].rearrange("p c s -> p (c s)"), in_=psc[:, 0:144]),
             reads=[psc], writes=[mcol])
        gcol = P.sbuf([128, 6, 8], F32, "gcol")
        P.dma(gcol[:].rearrange("p n k -> p (n k)"), d["norm_gT"][l], writes=[gcol])
        mc4 = mcol[:].rearrange("p (j k) s -> p j k s", k=8)
        tmp = P.sbuf([128, 8, 2], F32, "abtmp")
        for sub in range(3):
            jsh, jsc, npre = 3 * sub, 3 * sub + 1, 2 * sub
            P.op("dve", lambda g: g.tensor_scalar(out=tmp[:], in0=mc4[:, jsc, :, :], scalar1=1.0, scalar2=None,
                                                  op0=ALU.add), reads=[mcol], writes=[tmp])
            P.op("dve", lambda g: g.tensor_tensor(out=self.AB[:, sub, 0, :, :], in0=tmp[:],
                                                  in1=gcol[:, npre, :].unsqueeze(2).to_broadcast([128, 8, 2]),
                                                  op=ALU.mult), reads=[tmp, gcol], writes=[self.AB])
            P.op("dve", lambda g: g.tensor_copy(out=self.AB[:, sub, 1, :, :], in_=mc4[:, jsh, :, :]),
                 reads=[mcol], writes=[self.AB])
        grow = [P.sbuf([2, DM], F32, "grow%d" % i) for i in range(3)]
        gto = [P.sbuf([2, DM], F32, "gto%d" % i) for i in range(3)]
        for sub in range(3):
            jg, npost = 3 * sub + 2, 2 * sub + 1
            fac = 1.0 if sub == 1 else 0.5
            for s in range(2):
                P.dma(grow[sub][s:s + 1, :], d["norm_g"][l, npost:npost + 1, :], writes=[grow[sub]])
            P.op("dve", lambda g: g.scalar_tensor_tensor(out=gto[sub][:], in0=mrow[:, jg * DM:(jg + 1) * DM],
                                                         scalar=fac, in1=grow[sub][:], op0=ALU.mult,
                                                         op1=ALU.mult),
                 reads=[mrow, grow[sub]], writes=[gto[sub]])
            P.dma(d["gtrow"][l, sub], gto[sub][:], reads=[gto[sub]], writes=[self.db("gtrow", (l, sub))])
        P.release()

    def norm_T(self, xt_t, x_ap, sub, s, xnT_t, xnT_ap, pst, wk):
        self.norm_A(xt_t, x_ap, wk)
        self.norm_B(sub, s, xnT_t, xnT_ap, pst, wk)

    def norm_A(self, xt_t, x_ap, wk):
        P = self.P
        junk, ss, rs, xs = wk
        P.op("act", lambda g: g.activation(out=junk[:], in_=x_ap, func=AF.Square, accum_out=ss[:]),
             reads=[xt_t], writes=[junk, ss])
        self.rstd_from_ss((ss, ss[:]), DM, EPS, (rs, rs[:]))
        P.op("dve", lambda g: g.tensor_scalar(out=xs[:], in0=x_ap, scalar1=rs[:, 0:1], scalar2=None,
                                              op0=ALU.mult), reads=[xt_t, rs], writes=[xs])

    def norm_B(self, sub, s, xnT_t, xnT_ap, pst, wk):
        P = self.P
        junk, ss, rs, xs = wk
        pb = self.psb(pst)
        for k in range(8):
            P.op("pe", lambda g: g.transpose(pb[:, k * 128:(k + 1) * 128], xs[:, k * 128:(k + 1) * 128],
                                             self.ident_b[:]),
                 reads=[xs, self.ident_b], writes=[self.ps[pst]])
        for k in range(8):
            A = self.AB[:, sub, 0, k, s:s + 1]
            B = self.AB[:, sub, 1, k, s:s + 1]
            if k % 2 == 0:
                P.op("dve", lambda g: g.tensor_scalar(out=xnT_ap[:, k, :], in0=pb[:, k * 128:(k + 1) * 128],
                                                      scalar1=A, scalar2=B, op0=ALU.mult, op1=ALU.add),
                     reads=[self.ps[pst], self.AB], writes=[xnT_t])
            else:
                P.op("act", lambda g: g.activation(out=xnT_ap[:, k, :], in_=pb[:, k * 128:(k + 1) * 128],
                                                   func=AF.Identity, scale=A, bias=B),
                     reads=[self.ps[pst], self.AB], writes=[xnT_t])

    def norm_res_out(self, pso, xt_t, x_ap, gt, wk2, dst_ap, dst_buf):
        P = self.P
        junk, s2, r2, tmp = wk2
        for h in range(2):
            P.op("act", lambda g: g.activation(out=junk[:, 0:512], in_=self.ps[pso[h]][:], func=AF.Square,
                                               accum_out=s2[:, h:h + 1]),
                 reads=[self.ps[pso[h]]], writes=[junk, s2])
        P.op("dve", lambda g: g.tensor_tensor(out=s2[:, 2:3], in0=s2[:, 0:1], in1=s2[:, 1:2], op=ALU.add),
             reads=[s2], writes=[s2])
        self.rstd_from_ss((s2, s2[:, 2:3]), DM, EPS, (r2, r2[:]))
        for h in range(2):
            P.op("dve", lambda g: g.scalar_tensor_tensor(out=tmp[:, h * 512:(h + 1) * 512],
                                                         in0=self.ps[pso[h]][:], scalar=r2[:, 0:1],
                                                         in1=gt[:, h * 512:(h + 1) * 512],
                                                         op0=ALU.mult, op1=ALU.mult),
                 reads=[self.ps[pso[h]], r2, gt], writes=[tmp])
        P.op("pool", lambda g: g.tensor_tensor(out=x_ap, in0=x_ap, in1=tmp[:], op=ALU.add),
             reads=[tmp, xt_t], writes=[xt_t])
        P.dma(dst_ap, x_ap, reads=[xt_t], writes=[dst_buf])

    def phase_ffn(self, l, sub, src, dst, ntiles):
        P = self.P
        d = self.d
        wi = 0 if sub == 0 else 1
        P.mark()
        w13 = P.sbuf([128, 8, 2 * FF], BF16, "w13")
        w13b = [Buf("w13_%d" % k) for k in range(8)]
        for k in range(8):
            P.dma(w13[:, k, :], d["ffn_w13"][l, wi, k * 128:(k + 1) * 128, :], writes=[w13b[k]], q="pool")
        w2 = P.sbuf([128, NFC, DM], BF16, "w2")
        w2b = [Buf("w2_%d" % j) for j in range(NFC)]
        for j in range(NFC):
            P.dma(w2[:, j, :], d["ffn_w2"][l, wi, j * 128:(j + 1) * 128, :], writes=[w2b[j]], q="pool")
        gt = [P.sbuf([128, DM], F32, "gt%d" % s) for s in range(2)]
        for s in range(2):
            P.dma(gt[s][:], d["gtrow"][l, sub, s:s + 1, :].to_broadcast([128, DM]),
                  reads=[self.db("gtrow", (l, sub))], writes=[gt[s]])
        xbuf = [P.sbuf([128, 2, DM], F32, "xbuf%d" % i) for i in range(2)]
        xbb = [[Buf("xb%d_%d" % (i, j)) for j in range(2)] for i in range(2)]
        xnT = [P.sbuf([128, 8, 256], BF16, "xnT%d" % i) for i in range(2)]
        hT = P.sbuf([128, NFC, 256], BF16, "hT")
        hTb = [Buf("hT%d" % j) for j in range(NFC)]
        junk = P.sbuf([128, DM], BF16, "junk")
        junk2 = P.sbuf([128, 512], BF16, "junk2")
        s2 = [P.sbuf([128, 3], F32, "s2%d" % i) for i in range(2)]
        r2 = [P.sbuf([128, 1], F32, "r2%d" % i) for i in range(2)]
        tmp = [P.sbuf([128, DM], F32, "tmp%d" % i) for i in range(2)]
        sa = [P.sbuf([128, 256], F32, "sa%d" % i) for i in range(2)]
        ngroups = ntiles // 2
        src_ap, src_name = src
        dst_ap, dst_name = dst

        def load(gi):
            for i in range(2):
                t = 2 * gi + i
                xt = Tile(xbuf[gi % 2].t, xbb[gi % 2][i])
                P.dma(xbuf[gi % 2][:, i, :], src_ap[t * 128:(t + 1) * 128, :],
                      reads=[self.db(src_name, t)], writes=[xt])

        xs4 = [P.sbuf([128, DM], BF16, "xs4_%d" % i) for i in range(4)]
        ss4 = [P.sbuf([128, 1], F32, "ss4_%d" % i) for i in range(4)]
        rs4 = [P.sbuf([128, 1], F32, "rs4_%d" % i) for i in range(4)]

        def wkof(gi, i):
            k = (gi % 2) * 2 + i
            return (junk, ss4[k], rs4[k], xs4[k])

        def normA(gi):
            for i in range(2):
                xt = Tile(xbuf[gi % 2].t, xbb[gi % 2][i])
                self.norm_A(xt, xbuf[gi % 2][:, i, :], wkof(gi, i))

        def normB(gi):
            s_ = 1 if 2 * gi >= 32 else 0
            for i in range(2):
                self.norm_B(sub, s_, xnT[gi % 2], xnT[gi % 2][:, :, i * 128:(i + 1) * 128], 0 if i == 0 else 7,
                            wkof(gi, i))

        load(0)
        normA(0)
        normB(0)
        for gi in range(ngroups):
            if gi + 1 < ngroups:
                load(gi + 1)
            xb = xbuf[gi % 2]
            xn = xnT[gi % 2]
            s = 1 if 2 * gi >= 32 else 0
            for j in range(NFC):
                pa = self.ps[1 + (j % 2) * 2]
                pbk = self.ps[2 + (j % 2) * 2]
                for k in range(8):
                    P.mm(pa[:, 0:256], w13[:, k, j * 128:(j + 1) * 128], xn[:, k, :], start=(k == 0),
                         stop=(k == 7), reads=[w13b[k], xn], writes=[pa])
                for k in range(8):
                    P.mm(pbk[:, 0:256], w13[:, k, FF + j * 128:FF + (j + 1) * 128], xn[:, k, :], start=(k == 0),
                         stop=(k == 7), reads=[w13b[k], xn], writes=[pbk])
                sj = sa[j % 2]
                P.op("act", lambda g: g.activation(out=sj[:], in_=pa[:, 0:256], func=AF.Silu),
                     reads=[pa], writes=[sj])
                P.op("dve", lambda g: g.tensor_tensor(out=hT[:, j, :], in0=sj[:], in1=pbk[:, 0:256], op=ALU.mult),
                     reads=[sj, pbk], writes=[hTb[j]])
            if gi + 1 < ngroups:
                normA(gi + 1)
            for i in range(2):
                t = 2 * gi + i
                xt = Tile(xb.t, xbb[gi % 2][i])
                for h in range(2):
                    po = self.ps[5 + h]
                    for j in range(NFC):
                        P.mm(po[:], hT[:, j, i * 128:(i + 1) * 128], w2[:, j, h * 512:(h + 1) * 512],
                             start=(j == 0), stop=(j == NFC - 1), reads=[hTb[j], w2b[j]], writes=[po])
                self.norm_res_out([5, 6], xt, xb[:, i, :], gt[s], (junk2, s2[i], r2[i], tmp[i]),
                                  dst_ap[t * 128:(t + 1) * 128, :], self.db(dst_name, t))
            if gi + 1 < ngroups:
                normB(gi + 1)
        P.release()


def host_shared(inp):
    f32 = np.float32
    sh = {}
    sh["ident"] = np.eye(128, dtype=f32)
    for k in ("w_mod", "b_mod", "norm_g", "ffn_w13", "ffn_w2"):
        sh[k] = np.ascontiguousarray(inp[k], dtype=f32)
    ng = np.asarray(inp["norm_g"], dtype=f32)
    sh["norm_gT"] = np.ascontiguousarray(ng.reshape(2, 6, 8, 128).transpose(0, 3, 1, 2).reshape(2, 128, 48))
    sh["w_ext"] = build_w_ext(inp["w_in"])
    cm, sm = rope_tables(32)
    sh["rope_m"] = np.ascontiguousarray(np.stack([cm, sm], 0))
    cs, ss_ = rope_tables(64)
    sh["rope_s"] = np.ascontiguousarray(np.stack([np.concatenate([cs, cs], 0), np.concatenate([ss_, ss_], 0)], 0))
    nq = np.asarray(inp["mla_norm_q"], f32)
    nkv = np.asarray(inp["mla_norm_kv"], f32)
    sh["mla_nT"] = np.ascontiguousarray(np.stack([nq[:, 0:128], nq[:, 128:256], nkv], axis=2))
    wuq = np.asarray(inp["mla_w_uq"], f32).reshape(2, 256, 8, 96)
    pm, _ = rope_partner(32)
    sw = np.concatenate([wuq[..., 0:64], wuq[..., 64 + pm]], axis=-1)
    sh["mla_wq2"] = np.ascontiguousarray(np.stack([wuq, sw], axis=3).reshape(2, 256, 8 * 2 * 96))
    wukv = np.asarray(inp["mla_w_ukv"], f32).reshape(2, 128, 8, 128)
    sh["mla_wk"] = np.ascontiguousarray(wukv[..., 0:64].reshape(2, 128, 512))
    sh["mla_wv"] = np.ascontiguousarray(wukv[..., 64:128].reshape(2, 128, 512))
    sh["swa_sink"] = np.ascontiguousarray(inp["swa_sink"], dtype=f32)
    sh["w_branch"] = np.ascontiguousarray(inp["w_branch"], dtype=f32)
    for k in ("hyena_conv", "hyena_conv_b", "hyena_w1", "hyena_w2", "hyena_w3", "hyena_bias"):
        sh[k] = np.ascontiguousarray(inp[k], dtype=f32)
    sh["rwkv_mu"] = np.ascontiguousarray(inp["rwkv_mu"], dtype=f32)
    sh["rwkv_kvec"] = np.ascontiguousarray(inp["rwkv_kvec"], dtype=f32)
    sh["rwkv_lnp"] = np.ascontiguousarray(np.stack([inp["rwkv_ln_g"], inp["rwkv_ln_b"], inp["rwkv_r_k"]], axis=1), dtype=f32)
    wup = np.asarray(inp["rwkv_w_up"], f32)
    w0 = np.asarray(inp["rwkv_w0"], f32)
    sh["rwkv_wupA"] = np.ascontiguousarray(np.concatenate([wup.transpose(0, 2, 1, 3).reshape(2, 64, 1024),
                                                           w0.reshape(2, 1, 1024)], axis=1))
    aup = np.asarray(inp["rwkv_a_up"], f32)
    a0 = np.asarray(inp["rwkv_a0"], f32)
    sh["rwkv_aupA"] = np.ascontiguousarray(np.concatenate([aup.transpose(0, 2, 1, 3).reshape(2, 64, 1024),
                                                           a0.reshape(2, 1, 1024)], axis=1))
    sh["rwkv_g_up"] = np.ascontiguousarray(inp["rwkv_g_up"], dtype=f32)
    sh["rw_tri"] = rwkv_tables()
    hf = np.asarray(inp["hyena_freq"], f32)
    sh["hyT"] = np.ascontiguousarray(np.stack([hf[:, 0], hf[:, 1], np.asarray(inp["hyena_b1"], f32),
                                               np.asarray(inp["hyena_b2"], f32)], axis=2))
    tl = hy_tables(NLAT)
    tc = hy_tables(NCTX)
    sh["hy_D"] = np.ascontiguousarray(np.stack([tl["D2"], tl["D2sw"], tl["E"]], 0))
    sh["hyL_W1"] = np.ascontiguousarray(tl["W1"].reshape(128, -1))
    sh["hyL_W3"] = np.ascontiguousarray(tl["W3"].reshape(65, -1))
    sv = np.arange(512)
    angc = 2.0 * np.pi * ((sv[:, None] * sv[None, :]) % 512) / 512.0
    dft = np.stack([np.cos(angc), -np.sin(angc)], 0).astype(f32)
    sh["hyC_DFT"] = np.ascontiguousarray(dft.reshape(2, 4, 128, 512).transpose(0, 2, 1, 3))
    sh["hyL_fK"], sh["hyL_wK"] = hy_feats(NLAT)
    sh["hyC_fK"], sh["hyC_wK"] = hy_feats(NCTX)
    sh["w_out"] = np.ascontiguousarray(inp["w_out"], dtype=f32)
    bgt = np.asarray(inp["b_gate"], f32).reshape(2, 4, 8, 128).transpose(0, 3, 1, 2).reshape(2, 128, 32)
    sh["b_gateT"] = np.ascontiguousarray(bgt)
    kk = np.arange(128)[:, None]
    qq = np.arange(128)[None, :]
    sh["swa_mask"] = np.ascontiguousarray(np.stack([(qq <= kk), (kk <= qq)], 0).astype(f32))
    return sh


def host_core(inp, b):
    f32 = np.float32
    pc = {}
    pc["xall"] = np.ascontiguousarray(np.concatenate([inp["x"][b], inp["ctx"][b]], axis=0), dtype=f32)
    cv = np.stack([np.asarray(inp["c"][b], f32), np.asarray(inp["c_ctx"], f32)], axis=0)
    pc["cT"] = np.ascontiguousarray(cv.reshape(2, 8, 128).transpose(2, 1, 0).reshape(128, 16))
    return pc


G0, PA0, PB0, PH0, PD0 = 0, 4096, 4512, 6304, 7840
FM_COLS = 1728
TM_COLS = 3456
WX_FM0 = 0
WX_TM0 = FM_COLS
WX_G0 = FM_COLS + TM_COLS
WX_COLS = WX_G0 + 4096
PAD_ROWS = TT + 3


def tm_row(t):
    return 1 + t if t < NLAT else 2 + t


def rope_partner(R):
    H = R // 2
    q = H // 2
    part = np.zeros(R, np.int64)
    sign = np.zeros(R, np.float32)
    for dd in range(R):
        base = (dd // H) * H
        o = dd % H
        if o < q:
            part[dd] = base + o + q
            sign[dd] = -1.0
        else:
            part[dd] = base + o - q
            sign[dd] = 1.0
    return part, sign


def rope_tables(R):
    H = R // 2
    q = H // 2
    t = np.arange(NLAT)
    row = (t // 64).astype(np.float32)
    col = (t % 64).astype(np.float32)
    inv = (10000.0 ** (-np.arange(0, H, 2, dtype=np.float32) / H)).astype(np.float32)
    _, sign = rope_partner(R)
    cos = np.zeros((R, NLAT), np.float32)
    sin = np.zeros((R, NLAT), np.float32)
    for dd in range(R):
        pos = row if dd < H else col
        ang = (pos * inv[(dd % H) % q]).astype(np.float32)
        cos[dd] = np.cos(ang)
        sin[dd] = sign[dd] * np.sin(ang)
    return cos, sin


def build_w_ext(w_in):
    pm, _ = rope_partner(32)
    ps_, _ = rope_partner(64)
    cols = []
    cols += list(range(PA0, PA0 + 384))
    kr0 = PA0 + 384
    cols += [kr0 + i for i in range(32)]
    cols += [kr0 + int(pm[i]) for i in range(32)]
    q0 = PD0
    cols += [q0 + i for i in range(512)]
    cols += [q0 + (i // 64) * 64 + int(ps_[i % 64]) for i in range(512)]
    k0 = PD0 + 512
    cols += [k0 + i for i in range(128)]
    cols += [k0 + (i // 64) * 64 + int(ps_[i % 64]) for i in range(128)]
    assert len(cols) == FM_COLS
    cols += list(range(PB0, PB0 + 1792))
    cols += list(range(PH0, PH0 + 1536))
    cols += list(range(PD0 + 640, PD0 + 768))
    assert len(cols) == FM_COLS + TM_COLS
    cols += list(range(0, 4096))
    return np.ascontiguousarray(np.asarray(w_in, np.float32)[:, :, np.asarray(cols)])


def _pin_methods():
    def declare_pin(self):
        L = 2
        self.inp("w_ext", [L, DM, WX_COLS])
        self.inp("rope_m", [2, 32, NLAT])
        self.inp("rope_s", [2, 128, NLAT])
        self.scr("uT", [8, 128, TT], BF16)
        self.scr("cqkvT", [3, 128, TT], BF16)
        self.scr("krT", [32, TT], BF16)
        self.scr("sqT", [4, 128, TT], BF16)
        self.scr("skT", [128, TT], BF16)
        self.scr("pb", [PAD_ROWS, 1792])
        self.scr("ph", [PAD_ROWS, 1536])
        self.scr("pv", [TT, 128])

    def phase_pin(self, l, src):
        P = self.P
        d = self.d
        src_ap, src_name = src
        P.mark()
        NW = FM_COLS + TM_COLS
        w = P.sbuf([128, 8, NW], BF16, "wpin")
        wb = [Buf("wpin%d" % k) for k in range(8)]
        for k in range(8):
            P.dma(w[:, k, :], d["w_ext"][l, k * 128:(k + 1) * 128, 0:NW], writes=[wb[k]], q="pool")
        z = P.sbuf([1, 1792], F32, "zrow")
        P.op("pool", lambda g: g.memset(z[:], 0.0), writes=[z])
        for r in (0, NLAT + 1, TT + 2):
            P.dma(d["pb"][r:r + 1, :], z[:], reads=[z], writes=[self.db("pb", "pad%d" % r)])
            P.dma(d["ph"][r:r + 1, :], z[:, 0:1536], reads=[z], writes=[self.db("ph", "pad%d" % r)])
        xbuf = [P.sbuf([128, 4, DM], F32, "xbuf%d" % i) for i in range(2)]
        xbb = [[Buf("xb%d_%d" % (i, j)) for j in range(4)] for i in range(2)]
        uT = [P.sbuf([128, 8, 512], BF16, "uT%d" % i) for i in range(2)]
        junk = P.sbuf([128, DM], BF16, "junk")
        ss = [P.sbuf([128, 1], F32, "ss%d" % i) for i in range(8)]
        rs = [P.sbuf([128, 1], F32, "rs%d" % i) for i in range(8)]
        xs = [P.sbuf([128, DM], BF16, "xs%d" % i) for i in range(8)]
        tabm = [P.sbuf([32, 2, 512], F32, "tabm%d" % i) for i in range(2)]
        tabs = [P.sbuf([128, 2, 512], F32, "tabs%d" % i) for i in range(2)]
        t1a = P.sbuf([128, 512], F32, "t1_0")
        t2a = P.sbuf([128, 512], F32, "t2_0")
        t1 = [t1a, t1a]
        t2 = [t2a, t2a]
        fo = [P.sbuf([128, 512], BF16, "fo%d" % i) for i in range(3)]
        tmo = [P.sbuf([128, TM_COLS], F32, "tmo%d" % i) for i in range(2)]
        groups = [list(range(4 * g, 4 * g + 4)) for g in range(8)] + [[32, 33]]

        def load(gi):
            for i, t in enumerate(groups[gi]):
                xt = Tile(xbuf[gi % 2].t, xbb[gi % 2][i])
                P.dma(xbuf[gi % 2][:, i, :], src_ap[t * 128:(t + 1) * 128, :],
                      reads=[self.db(src_name, t)], writes=[xt])
            if gi < 8:
                P.dma(tabm[gi % 2][:], d["rope_m"][:, :, gi * 512:(gi + 1) * 512].rearrange("c p t -> p c t"),
                      writes=[tabm[gi % 2]])
                P.dma(tabs[gi % 2][:], d["rope_s"][:, :, gi * 512:(gi + 1) * 512].rearrange("c p t -> p c t"),
                      writes=[tabs[gi % 2]])

        def wkof(gi, i):
            k = (gi % 2) * 4 + i
            return (junk, ss[k], rs[k], xs[k])

        def normA(gi):
            for i, t in enumerate(groups[gi]):
                xt = Tile(xbuf[gi % 2].t, xbb[gi % 2][i])
                self.norm_A(xt, xbuf[gi % 2][:, i, :], wkof(gi, i))

        def normB(gi):
            s_ = 0 if gi < 8 else 1
            for i, t in enumerate(groups[gi]):
                self.norm_B(1, s_, uT[gi % 2], uT[gi % 2][:, :, i * 128:(i + 1) * 128], 0 if i % 2 == 0 else 7,
                            wkof(gi, i))

        load(0)
        normA(0)
        normB(0)
        nfo = 0
        nev = 0
        for gi, tl in enumerate(groups):
            if gi + 1 < len(groups):
                load(gi + 1)
            n = 128 * len(tl)
            t0 = tl[0] * 128
            lat = gi < 8
            s = 0 if lat else 1
            xb = xbuf[gi % 2]
            u = uT[gi % 2]
            P.dma(d["uT"][:, :, t0:t0 + n].rearrange("k p t -> p k t"), u[:, :, 0:n], reads=[u],
                  writes=[self.db("uT", gi)])

            def fm_mm(ps, c0, m):
                for k in range(8):
                    P.mm(ps[0:m, 0:n], w[:, k, c0:c0 + m], u[:, k, 0:n], start=(k == 0), stop=(k == 7),
                         reads=[wb[k], u], writes=[ps])

            for c in range(3):
                ps = self.ps[1 + (c % 2) * 2]
                fm_mm(ps, c * 128, 128)
                o = fo[nfo % 3]
                nfo += 1
                P.op("act", lambda g: g.activation(out=o[:, 0:n], in_=ps[:, 0:n], func=AF.Copy),
                     reads=[ps], writes=[o])
                P.dma(d["cqkvT"][c, :, t0:t0 + n], o[:, 0:n], reads=[o], writes=[self.db("cqkvT", (c, gi))])
            roped = [(384, 416, 32, tabm, d["krT"][:, t0:t0 + n], ("krT", gi))]
            for c in range(4):
                roped.append((448 + c * 128, 960 + c * 128, 128, tabs, d["sqT"][c, :, t0:t0 + n], ("sqT", (c, gi))))
            roped.append((1472, 1600, 128, tabs, d["skT"][:, t0:t0 + n], ("skT", gi)))
            for ri, (cx, csw, m, tab, dst, dk) in enumerate(roped):
                psx = self.ps[1 + (ri % 2) * 2]
                fm_mm(psx, cx, m)
                o = fo[nfo % 3]
                nfo += 1
                if lat:
                    pss = self.ps[2 + (ri % 2) * 2]
                    fm_mm(pss, csw, m)
                    tb = tab[gi % 2]
                    a1 = t1[ri % 2]
                    a2 = t2[ri % 2]
                    P.op("dve", lambda g: g.tensor_tensor(out=a1[0:m, :], in0=psx[0:m, :], in1=tb[0:m, 0, :],
                                                          op=ALU.mult), reads=[psx, tb], writes=[a1])
                    P.op("dve", lambda g: g.tensor_tensor(out=a2[0:m, :], in0=pss[0:m, :], in1=tb[0:m, 1, :],
                                                          op=ALU.mult), reads=[pss, tb], writes=[a2])
                    P.op("pool", lambda g: g.tensor_tensor(out=o[0:m, :], in0=a1[0:m, :], in1=a2[0:m, :],
                                                           op=ALU.add), reads=[a1, a2], writes=[o])
                else:
                    P.op("act", lambda g: g.activation(out=o[0:m, 0:n], in_=psx[0:m, 0:n], func=AF.Copy),
                         reads=[psx], writes=[o])
                P.dma(dst, o[0:m, 0:n], reads=[o], writes=[self.db(*dk)])
            if gi + 1 < len(groups):
                normA(gi + 1)
            for i, t in enumerate(tl):
                st = tmo[i % 2]
                for cb in range(7):
                    c0 = cb * 512
                    cw = min(512, TM_COLS - c0)
                    ps = self.ps[5 + (cb % 2)]
                    for k in range(8):
                        P.mm(ps[:, 0:cw], u[:, k, i * 128:(i + 1) * 128], w[:, k, FM_COLS + c0:FM_COLS + c0 + cw],
                             start=(k == 0), stop=(k == 7), reads=[u, wb[k]], writes=[ps])
                    if nev % 2 == 0:
                        P.op("act", lambda g: g.activation(out=st[:, c0:c0 + cw], in_=ps[:, 0:cw], func=AF.Copy),
                             reads=[ps], writes=[st])
                    else:
                        P.op("dve", lambda g: g.tensor_copy(out=st[:, c0:c0 + cw], in_=ps[:, 0:cw]),
                             reads=[ps], writes=[st])
                    nev += 1
                r0 = tm_row(t * 128)
                P.dma(d["pb"][r0:r0 + 128, :], st[:, 0:1792], reads=[st], writes=[self.db("pb", t)])
                P.dma(d["ph"][r0:r0 + 128, :], st[:, 1792:3328], reads=[st], writes=[self.db("ph", t)])
                P.dma(d["pv"][t * 128:(t + 1) * 128, :], st[:, 3328:3456], reads=[st], writes=[self.db("pv", t)])
            if gi + 1 < len(groups):
                normB(gi + 1)
        P.release()

    KB.declare_pin = declare_pin
    KB.phase_pin = phase_pin


_pin_methods()


def _attn_methods():
    def declare_attn(self):
        L = 2
        self.inp("mla_nT", [L, 128, 3])
        self.inp("mla_wq2", [L, 256, 8 * 2 * 96])
        self.inp("mla_wk", [L, 128, 512])
        self.inp("mla_wv", [L, 128, 512])
        self.inp("swa_sink", [L, 8])
        self.inp("swa_mask", [2, 128, 128])
        self.scr("yT", [4, 512, TT], BF16)

    def phase_mla(self, l, with_ctx):
        P = self.P
        d = self.d
        P.mark()
        scale = 96.0 ** -0.5
        wq = P.sbuf([128, 2, 8, 2, 96], BF16, "wq")
        for c in range(2):
            P.dma(wq[:, c].rearrange("p h s m -> p (h s m)"), d["mla_wq2"][l, c * 128:(c + 1) * 128, :],
                  writes=[wq], q="pool")
        wk = P.sbuf([128, 8, 64], BF16, "wk")
        P.dma(wk[:].rearrange("p h m -> p (h m)"), d["mla_wk"][l], writes=[wk], q="pool")
        wv = P.sbuf([128, 512], BF16, "wv")
        P.dma(wv[:], d["mla_wv"][l], writes=[wv], q="pool")
        nT = P.sbuf([128, 3], F32, "nT")
        P.dma(nT[:], d["mla_nT"][l], writes=[nT])
        ones_f = P.sbuf([128, 128], F32, "ones_f")
        P.op("pool", lambda g: g.memset(ones_f[:], 1.0), writes=[ones_f])
        cqn = P.sbuf([128, 2, TT], BF16, "cqn")
        ckvn = P.sbuf([128, TT], BF16, "ckvn")
        vaug = P.sbuf([128, NTILE, 8, 128], BF16, "vaug")
        P.op("pool", lambda g: g.memset(vaug[:, :, :, 64:128], 1.0), writes=[vaug])
        groups = [(g * 512, 512) for g in range(8)] + [(NLAT, 256)]
        P.mark()
        xin = [P.sbuf([128, 3, 512], BF16, "xin%d" % i) for i in range(2)]
        sq = [P.sbuf([128, 3, 512], F32, "sq%d" % i) for i in range(2)]
        rsb = [P.sbuf([128, 2, 512], F32, "rsb%d" % i) for i in range(2)]
        for gi, (t0, n) in enumerate(groups):
            xi = xin[gi % 2]
            P.dma(xi[:, :, 0:n], d["cqkvT"][:, :, t0:t0 + n].rearrange("c p t -> p c t"),
                  reads=[self.db("cqkvT", (c, gi)) for c in range(3)], writes=[xi])
            sqi = sq[gi % 2]
            P.op("pool", lambda g: g.tensor_tensor(out=sqi[:, :, 0:n], in0=xi[:, :, 0:n], in1=xi[:, :, 0:n],
                                                   op=ALU.mult), reads=[xi], writes=[sqi])
            psq = self.ps[6]
            psk = self.ps[7]
            for c in range(2):
                P.mm(psq[:, 0:n], ones_f[:], sqi[:, c, 0:n], start=(c == 0), stop=(c == 1),
                     reads=[ones_f, sqi], writes=[psq])
            P.mm(psk[:, 0:n], ones_f[:], sqi[:, 2, 0:n], reads=[ones_f, sqi], writes=[psk])
            r = rsb[gi % 2]
            P.op("act", lambda g: g.activation(out=r[:, 0, 0:n], in_=psq[:, 0:n], func=AF.Sqrt, scale=1.0 / 256,
                                               bias=EPS), reads=[psq], writes=[r])
            P.op("act", lambda g: g.activation(out=r[:, 1, 0:n], in_=psk[:, 0:n], func=AF.Sqrt, scale=1.0 / 128,
                                               bias=EPS), reads=[psk], writes=[r])
            P.op("dve", lambda g: g.reciprocal(out=r[:, :, 0:n], in_=r[:, :, 0:n]), reads=[r], writes=[r])
            for c in range(2):
                P.op("dve", lambda g: g.scalar_tensor_tensor(out=cqn[:, c, t0:t0 + n], in0=xi[:, c, 0:n],
                                                             scalar=nT[:, c:c + 1], in1=r[:, 0, 0:n],
                                                             op0=ALU.mult, op1=ALU.mult),
                     reads=[xi, nT, r], writes=[cqn])
            P.op("dve", lambda g: g.scalar_tensor_tensor(out=ckvn[:, t0:t0 + n], in0=xi[:, 2, 0:n],
                                                         scalar=nT[:, 2:3], in1=r[:, 1, 0:n],
                                                         op0=ALU.mult, op1=ALU.mult),
                 reads=[xi, nT, r], writes=[ckvn])
        P.release()
        for t in range(NTILE):
            ps = self.ps[5 + t % 2]
            P.mm(ps[:], ckvn[:, t * 128:(t + 1) * 128], wv[:], reads=[ckvn, wv], writes=[ps])
            eng = "act" if t % 2 == 0 else "dve"
            if eng == "act":
                P.op("act", lambda g: g.activation(out=vaug[:, t, :, 0:64],
                                                   in_=ps[:].rearrange("p (h m) -> p h m", m=64), func=AF.Copy),
                     reads=[ps], writes=[vaug])
            else:
                P.op("dve", lambda g: g.tensor_copy(out=vaug[:, t, :, 0:64],
                                                    in_=ps[:].rearrange("p (h m) -> p h m", m=64)),
                     reads=[ps], writes=[vaug])
        NQ = TT if with_ctx else NLAT
        KT = [P.sbuf([96, TT], BF16, "KT%d" % i) for i in range(2)]
        QT = [P.sbuf([96, TT], BF16, "QT%d" % i) for i in range(2)]
        tab = [P.sbuf([96, 2, 512], F32, "tab%d" % i) for i in range(2)]
        a1 = [P.sbuf([96, 512], F32, "a1_%d" % i) for i in range(2)]
        a2 = [P.sbuf([96, 512], F32, "a2_%d" % i) for i in range(2)]
        PT = [P.sbuf([128, 512], BF16, "PT%d" % i) for i in range(4)]
        rec = [P.sbuf([64, 512], F32, "rec%d" % i) for i in range(2)]
        yo = [P.sbuf([64, 512], BF16, "yo%d" % i) for i in range(2)]
        npt = 0
        nqg = 0
        for h in range(8):
            kt = KT[h % 2]
            qt = QT[h % 2]
            P.dma(kt[64:96, :], d["krT"], reads=[self.db("krT", gi) for gi in range(9)], writes=[kt])
            for gi, (t0, n) in enumerate(groups):
                lat = gi < 8
                if gi >= 8 and not with_ctx:
                    pass
                pk = self.ps[5]
                P.mm(pk[0:64, 0:n], wk[:, h, :], ckvn[:, t0:t0 + n], reads=[wk, ckvn], writes=[pk])
                P.op("act", lambda g: g.activation(out=kt[0:64, t0:t0 + n], in_=pk[0:64, 0:n], func=AF.Copy),
                     reads=[pk], writes=[kt])
                if gi >= 8 and not with_ctx:
                    continue
                p1 = self.ps[6]
                for c in range(2):
                    P.mm(p1[0:96, 0:n], wq[:, c, h, 0, :], cqn[:, c, t0:t0 + n], start=(c == 0), stop=(c == 1),
                         reads=[wq, cqn], writes=[p1])
                if lat:
                    p2 = self.ps[7]
                    for c in range(2):
                        P.mm(p2[0:96, 0:n], wq[:, c, h, 1, :], cqn[:, c, t0:t0 + n], start=(c == 0),
                             stop=(c == 1), reads=[wq, cqn], writes=[p2])
                    tb = tab[gi % 2]
                    P.dma(tb[64:96, :, :], d["rope_m"][:, :, t0:t0 + n].rearrange("c p t -> p c t"), writes=[tb])
                    P.op("act", lambda g: g.activation(out=qt[0:64, t0:t0 + n], in_=p1[0:64, 0:n], func=AF.Copy),
                         reads=[p1], writes=[qt])
                    b1 = a1[gi % 2]
                    b2 = a2[gi % 2]
                    P.op("dve", lambda g: g.tensor_tensor(out=b1[64:96, :], in0=p1[64:96, :], in1=tb[64:96, 0, :],
                                                          op=ALU.mult), reads=[p1, tb], writes=[b1])
                    P.op("dve", lambda g: g.tensor_tensor(out=b2[64:96, :], in0=p2[64:96, :], in1=tb[64:96, 1, :],
                                                          op=ALU.mult), reads=[p2, tb], writes=[b2])
                    P.op("pool", lambda g: g.tensor_tensor(out=qt[64:96, t0:t0 + n], in0=b1[64:96, :],
                                                           in1=b2[64:96, :], op=ALU.add),
                         reads=[b1, b2], writes=[qt])
                else:
                    P.op("act", lambda g: g.activation(out=qt[0:96, t0:t0 + n], in_=p1[0:96, 0:n], func=AF.Copy),
                         reads=[p1], writes=[qt])
            qgroups = [(g * 512, 512, list(range(NTILE))) for g in range(8)]
            if with_ctx:
                qgroups.append((NLAT, 256, [32, 33]))
            for (q0, n, kbs) in qgroups:
                po = self.ps[3 + nqg % 2]
                nqg += 1
                pend = []

                def pv(item, first, last):
                    kb, pt = item
                    P.mm(po[:, 0:n], vaug[:, kb, h, :], pt[:, 0:n], start=first, stop=last,
                         reads=[vaug, pt], writes=[po])

                for idx, kb in enumerate(kbs):
                    pss = self.ps[npt % 3]
                    pt = PT[npt % 4]
                    npt += 1
                    P.mm(pss[:, 0:n], kt[:, kb * 128:(kb + 1) * 128], qt[:, q0:q0 + n], reads=[kt, qt],
                         writes=[pss])
                    P.op("act", lambda g: g.activation(out=pt[:, 0:n], in_=pss[:, 0:n], func=AF.Exp, scale=scale),
                         reads=[pss], writes=[pt])
                    pend.append((kb, pt))
                    if len(pend) > 2:
                        pv(pend.pop(0), idx == 2, False)
                while pend:
                    first = (len(kbs) - len(pend) == 0)
                    pv(pend.pop(0), first, len(pend) == 0)
                rc = rec[nqg % 2]
                y = yo[nqg % 2]
                P.op("dve", lambda g: g.reciprocal(out=rc[:, 0:n], in_=po[64:128, 0:n]), reads=[po], writes=[rc])
                P.op("dve", lambda g: g.tensor_tensor(out=y[:, 0:n], in0=po[0:64, 0:n], in1=rc[:, 0:n],
                                                      op=ALU.mult), reads=[po, rc], writes=[y])
                P.dma(d["yT"][0, h * 64:(h + 1) * 64, q0:q0 + n], y[:, 0:n], reads=[y],
                      writes=[self.db("yT", (0, h, q0))])
        P.release()

    def phase_swa(self, l, with_ctx):
        P = self.P
        d = self.d
        P.mark()
        scale = 64.0 ** -0.5
        es = P.sbuf([128, 8], F32, "es")
        P.dma(es[:], d["swa_sink"][l:l + 1, :].to_broadcast([128, 8]), writes=[es])
        P.op("act", lambda g: g.activation(out=es[:], in_=es[:], func=AF.Exp), reads=[es], writes=[es])
        msk = P.sbuf([128, 2, 128], BF16, "msk")
        P.dma(msk[:], d["swa_mask"].rearrange("c p t -> p c t"), writes=[msk], q="pool")
        Kk = P.sbuf([64, TT], BF16, "Kk")
        Qk = P.sbuf([64, 4, TT], BF16, "Qk")
        va = P.sbuf([128, NTILE, 128], BF16, "va")
        P.op("pool", lambda g: g.memset(va[:, :, 64:128], 1.0), writes=[va])
        yd = P.sbuf([64, 4, TT], BF16, "yd")
        PT = [P.sbuf([128, 4, 128], BF16, "PT%d" % i) for i in range(6)]
        den = [P.sbuf([64, 4, 128], F32, "den%d" % i) for i in range(2)]
        npt = 0
        nblk = 0
        allsq = [self.db("sqT", (c, gi)) for c in range(4) for gi in range(9)]
        allsk = [self.db("skT", gi) for gi in range(9)]
        allpv = [self.db("pv", t) for t in range(NTILE)]
        for kh in range(2):
            P.dma(Kk[:], d["skT"][kh * 64:(kh + 1) * 64, :], reads=allsk, writes=[Kk])
            for g_ in range(4):
                hh = kh * 4 + g_
                P.dma(Qk[:, g_, :], d["sqT"][hh // 2, (hh % 2) * 64:(hh % 2) * 64 + 64, :], reads=allsq, writes=[Qk])
            P.dma(va[:, :, 0:64], d["pv"].rearrange("(t p) c -> p t c", p=128)[:, :, kh * 64:(kh + 1) * 64],
                  reads=allpv, writes=[va], q="pool")
            nq = NTILE if with_ctx else 32
            for i in range(nq):
                if i < 32:
                    kbs = []
                    if i > 0:
                        kbs.append((i - 1, 0))
                    kbs.append((i, None))
                    if i < 31:
                        kbs.append((i + 1, 1))
                    kbs += [(32, None), (33, None)]
                else:
                    kbs = [(32, None), (33, None)]
                po = self.ps[3 + nblk % 2]
                dn = den[nblk % 2]
                nblk += 1
                sbanks = [0, 1, 2, 5, 6, 7]
                items = []
                for idx, (kb, mk) in enumerate(kbs):
                    pss = self.ps[sbanks[npt % 6]]
                    pt = PT[npt % 6]
                    npt += 1
                    for g_ in range(4):
                        P.mm(pss[:, g_ * 128:(g_ + 1) * 128], Kk[:, kb * 128:(kb + 1) * 128],
                             Qk[:, g_, i * 128:(i + 1) * 128], reads=[Kk, Qk], writes=[pss])
                    items.append((kb, mk, pss, pt))
                for idx, (kb, mk, pss, pt) in enumerate(items):
                    P.op("act", lambda g: g.activation(out=pt[:].rearrange("p g t -> p (g t)"), in_=pss[:],
                                                       func=AF.Exp, scale=scale), reads=[pss], writes=[pt])
                    if mk is not None:
                        P.op("dve", lambda g: g.tensor_tensor(out=pt[:], in0=pt[:],
                                                              in1=msk[:, mk, :].unsqueeze(1).to_broadcast([128, 4, 128]),
                                                              op=ALU.mult), reads=[pt, msk], writes=[pt])
                for idx, (kb, mk, pss, pt) in enumerate(items):
                    P.mm(po[:], va[:, kb, :], pt[:].rearrange("p g t -> p (g t)"), start=(idx == 0),
                         stop=(idx == len(kbs) - 1), reads=[va, pt], writes=[po])
                P.op("dve", lambda g: g.tensor_tensor(out=dn[:], in0=po[64:128, :].rearrange("p (g t) -> p g t", g=4),
                                                      in1=es[64:128, kh * 4:(kh + 1) * 4].unsqueeze(2).to_broadcast([64, 4, 128]),
                                                      op=ALU.add), reads=[po, es], writes=[dn])
                P.op("dve", lambda g: g.reciprocal(out=dn[:], in_=dn[:]), reads=[dn], writes=[dn])
                P.op("dve", lambda g: g.tensor_tensor(out=yd[:, :, i * 128:(i + 1) * 128],
                                                      in0=po[0:64, :].rearrange("p (g t) -> p g t", g=4), in1=dn[:],
                                                      op=ALU.mult), reads=[po, dn], writes=[yd])
            for g_ in range(4):
                hh = kh * 4 + g_
                P.dma(d["yT"][3, hh * 64:(hh + 1) * 64, 0:nq * 128], yd[:, g_, 0:nq * 128], reads=[yd],
                      writes=[self.db("yT", (3, hh))])
        P.release()

    KB.declare_attn = declare_attn
    KB.phase_mla = phase_mla
    KB.phase_swa = phase_swa


_attn_methods()


def _merge_methods():
    def declare_merge(self):
        L = 2
        self.inp("w_branch", [L, 4, 512, DM])
        self.inp("w_out", [L, DM, DM])
        self.inp("b_gateT", [L, 128, 32])

    def phase_merge(self, l, with_ctx, xname="xs"):
        P = self.P
        d = self.d
        P.mark()
        wg = P.sbuf([128, 8, 4096], BF16, "wg")
        wgb = [Buf("wg%d" % k) for k in range(8)]
        for k in range(8):
            P.dma(wg[:, k, :], d["w_ext"][l, k * 128:(k + 1) * 128, WX_G0:WX_G0 + 4096], writes=[wgb[k]], q="pool")
        wbr = P.sbuf([128, 4, 4, DM], BF16, "wbr")
        for br in range(4):
            P.dma(wbr[:, br], d["w_branch"][l, br].rearrange("(kc p) n -> p kc n", p=128), writes=[wbr], q="pool")
        wo = P.sbuf([128, 8, DM], BF16, "wo")
        P.dma(wo[:], d["w_out"][l].rearrange("(k p) n -> p k n", p=128), writes=[wo], q="pool")
        bg = P.sbuf([128, 4, 8], F32, "bg")
        P.dma(bg[:].rearrange("p b o -> p (b o)"), d["b_gateT"][l], writes=[bg])
        gt = [P.sbuf([128, DM], F32, "gt%d" % s) for s in range(2)]
        for s in range(2):
            P.dma(gt[s][:], d["gtrow"][l, 1, s:s + 1, :].to_broadcast([128, DM]),
                  reads=[self.db("gtrow", (l, 1))], writes=[gt[s]])
        groups = [(g * 512, 512) for g in range(8)] + ([(NLAT, 256)] if with_ctx else [])
        uT = [P.sbuf([128, 8, 512], BF16, "uT%d" % i) for i in range(2)]
        yg = [P.sbuf([128, 4, 4, 512], BF16, "yg%d" % i) for i in range(2)]
        mg = P.sbuf([128, 8, 512], BF16, "mg")
        mgb = [Buf("mg%d" % k) for k in range(8)]
        sg = [P.sbuf([128, 512], F32, "sg%d" % i) for i in range(2)]
        tm_ = [P.sbuf([128, 512], F32, "tm%d" % i) for i in range(2)]
        acc = [P.sbuf([128, 512], F32, "acc%d" % i) for i in range(2)]
        xbuf = [P.sbuf([128, DM], F32, "xb%d" % i) for i in range(2)]
        junk = P.sbuf([128, DM], BF16, "junk")
        s2 = [P.sbuf([128, 3], F32, "s2%d" % i) for i in range(2)]
        r2 = [P.sbuf([128, 1], F32, "r2%d" % i) for i in range(2)]
        tmp1 = P.sbuf([128, DM], F32, "tmp")
        tmp = [tmp1, tmp1]
        ally = [b for k, b in self.dbufs.items() if k[0] == "yT"]
        allu = [b for k, b in self.dbufs.items() if k[0] == "uT"]

        def load(gi):
            t0, n = groups[gi]
            P.dma(uT[gi % 2][:, :, 0:n], d["uT"][:, :, t0:t0 + n].rearrange("k p t -> p k t"), reads=allu,
                  writes=[uT[gi % 2]])
            for br in range(4):
                P.dma(yg[gi % 2][:, br, :, 0:n],
                      d["yT"][br].rearrange("(kc p) t -> p kc t", p=128)[:, :, t0:t0 + n], reads=ally,
                      writes=[yg[gi % 2]])

        load(0)
        nx = 0
        for gi, (t0, n) in enumerate(groups):
            if gi + 1 < len(groups):
                load(gi + 1)
            u = uT[gi % 2]
            y = yg[gi % 2]
            s = 0 if gi < 8 else 1
            for oc in range(8):
                ac = acc[oc % 2]
                for br in range(4):
                    psg = self.ps[1 + (br % 2) * 2]
                    psy = self.ps[2 + (br % 2) * 2]
                    c0 = br * 1024 + oc * 128
                    for k in range(8):
                        P.mm(psg[:, 0:n], wg[:, k, c0:c0 + 128], u[:, k, 0:n], start=(k == 0), stop=(k == 7),
                             reads=[wgb[k], u], writes=[psg])
                    for kc in range(4):
                        P.mm(psy[:, 0:n], wbr[:, br, kc, oc * 128:(oc + 1) * 128], y[:, br, kc, 0:n],
                             start=(kc == 0), stop=(kc == 3), reads=[wbr, y], writes=[psy])
                    sgt = sg[br % 2]
                    P.op("act", lambda g: g.activation(out=sgt[:, 0:n], in_=psg[:, 0:n], func=AF.Sigmoid,
                                                       bias=bg[:, br, oc:oc + 1]), reads=[psg, bg], writes=[sgt])
                    if br == 0:
                        P.op("dve", lambda g: g.tensor_tensor(out=ac[:, 0:n], in0=sgt[:, 0:n], in1=psy[:, 0:n],
                                                              op=ALU.mult), reads=[sgt, psy], writes=[ac])
                    else:
                        tt = tm_[br % 2]
                        P.op("dve", lambda g: g.tensor_tensor(out=tt[:, 0:n], in0=sgt[:, 0:n], in1=psy[:, 0:n],
                                                              op=ALU.mult), reads=[sgt, psy], writes=[tt])
                        if br < 3:
                            P.op("pool", lambda g: g.tensor_tensor(out=ac[:, 0:n], in0=ac[:, 0:n], in1=tt[:, 0:n],
                                                                   op=ALU.add), reads=[ac, tt], writes=[ac])
                        else:
                            P.op("pool", lambda g: g.tensor_tensor(out=mg[:, oc, 0:n], in0=ac[:, 0:n],
                                                                   in1=tt[:, 0:n], op=ALU.add),
                                 reads=[ac, tt], writes=[mgb[oc]])
            for i in range(n // 128):
                t = t0 // 128 + i
                xb = xbuf[nx % 2]
                P.dma(xb[:], d[xname][t * 128:(t + 1) * 128, :], reads=[self.db(xname, t)], writes=[xb])
                for h in range(2):
                    po = self.ps[5 + h]
                    for k in range(8):
                        P.mm(po[:], mg[:, k, i * 128:(i + 1) * 128], wo[:, k, h * 512:(h + 1) * 512],
                             start=(k == 0), stop=(k == 7), reads=[mgb[k], wo], writes=[po])
                self.norm_res_out([5, 6], xb, xb[:], gt[s], (junk, s2[nx % 2], r2[nx % 2], tmp[nx % 2]),
                                  d[xname][t * 128:(t + 1) * 128, :], self.db(xname, t))
                nx += 1
        P.release()

    KB.declare_merge = declare_merge
    KB.phase_merge = phase_merge


_merge_methods()


def hy_tables(n):
    M = 2 * n
    S1 = M // 64
    T1 = n // 64
    s1 = np.arange(S1)[:, None, None]
    s2 = np.arange(64)[None, :, None]
    f1 = np.arange(S1)[None, None, :]
    ang = 2.0 * np.pi * ((f1 * (64 * s1 + s2)) % M) / M
    F1n = S1 // 2 + 1
    W1 = np.stack([np.cos(ang), -np.sin(ang)], axis=2).astype(np.float32)[..., :F1n]
    s2v = np.arange(64)[:, None]
    f2v = np.arange(64)[None, :]
    th = 2.0 * np.pi * ((s2v * f2v) % 64) / 64
    c, s = np.cos(th), np.sin(th)
    D2 = np.block([[c, -s], [s, c]]).astype(np.float32)
    D2sw = np.concatenate([D2[:, 64:], D2[:, :64]], axis=1)
    E = np.block([[c, s], [-s, c]]).astype(np.float32)
    f1v = np.arange(S1)[:, None, None]
    t2v = np.arange(64)[None, :, None]
    t1v = np.arange(T1)[None, None, :]
    psi = 2.0 * np.pi * ((f1v * (64 * t1v + t2v)) % M) / M
    W3 = np.stack([np.cos(psi), -np.sin(psi)], axis=2).astype(np.float32)
    cw = np.full(F1n, 2.0, np.float32)
    cw[0] = 1.0
    cw[-1] = 1.0
    W3 = np.ascontiguousarray(W3[:F1n] * cw[:, None, None, None])
    return dict(W1=W1, D2=D2, D2sw=D2sw, E=E, W3=W3, S1=S1, T1=T1, M=M, F1n=F1n)


def hy_feats(n):
    M = 2 * n
    f32 = np.float32
    t = np.linspace(0.0, 1.0, n, dtype=f32)
    bands = np.linspace(1e-4, 15.0, 16, dtype=f32)
    ang = (f32(2.0 * np.pi / n) * np.arange(n, dtype=f32)[:, None] * bands[None, :]).astype(f32)
    feats = np.concatenate([t[:, None], np.cos(ang), -np.sin(ang)], axis=-1).astype(f32)
    deltas = np.abs(np.linspace(np.log(1e-2) / 1.5, np.log(1e-2) / 0.3, 512, dtype=f32)).astype(f32)
    win = np.exp(-t[:, None] * deltas[None, :]).astype(f32)
    idx = np.zeros(M, np.int64)
    idx[:n] = np.arange(n)
    idx[n + 1:] = n - np.arange(1, n)
    fK = feats[idx].copy()
    wK = win[idx].copy()
    fK[n] = feats[0]
    wK[n] = win[0]
    return np.ascontiguousarray(fK.T), np.ascontiguousarray(wK)


def _hyena_methods():
    TWO_PI = 2.0 * np.pi

    def declare_hyena(self):
        L = 2
        self.inp("hyena_conv", [L, 3, 1536])
        self.inp("hyena_conv_b", [L, 1536])
        self.inp("hyena_w1", [L, 33, 64])
        self.inp("hyena_w2", [L, 64, 64])
        self.inp("hyena_w3", [L, 64, 2048])
        self.inp("hyT", [L, 64, 4])
        self.inp("hyena_bias", [L, 2, 512])
        self.inp("hy_D", [3, 128, 128])
        self.inp("hyL_W1", [128, 64 * 2 * 65])
        self.inp("hyL_W3", [65, 64 * 2 * 64])
        self.inp("hyL_fK", [33, 8192])
        self.inp("hyL_wK", [8192, 512])
        self.inp("hyC_DFT", [2, 128, 4, 512])
        self.inp("hyC_fK", [33, 512])
        self.inp("hyC_wK", [512, 512])
        self.scr("hcs", [TT, 1536])
        self.scr("kbuf", [2, 8192, 512], BF16)
        self.scr("Bd", [128, 128, 512], BF16)
        self.scr("Dd", [128, 128, 512], BF16)
        self.scr("KAB_L", [2, 128, 2, 128, 512], BF16)
        self.scr("KC", [2, 2, 4, 128, 512], BF16)
        self.scr("zt1", [TT, 512])
        self.scr("zt2", [TT, 512])

    def hy_shortconv(self, l, ntiles):
        P = self.P
        d = self.d
        P.mark()
        ck = P.sbuf([128, 3, 1536], F32, "ck")
        P.dma(ck[:].rearrange("p a c -> p (a c)"),
              d["hyena_conv"][l:l + 1].rearrange("o a c -> o (a c)").to_broadcast([128, 4608]), writes=[ck])
        cb = P.sbuf([128, 1536], F32, "cb")
        P.dma(cb[:], d["hyena_conv_b"][l:l + 1, :].to_broadcast([128, 1536]), writes=[cb])
        bufs = [[P.sbuf([128, 1536], F32, "sc%d_%d" % (i, j)) for j in range(3)] for i in range(3)]
        allph = [b for k, b in self.dbufs.items() if k[0] == "ph"]
        for t in range(ntiles):
            r0 = tm_row(t * 128)
            pv_, cu, nx = bufs[t % 3]
            P.dma(pv_[:], d["ph"][r0 - 1:r0 + 127, :], reads=allph, writes=[pv_])
            P.dma(cu[:], d["ph"][r0:r0 + 128, :], reads=allph, writes=[cu])
            P.dma(nx[:], d["ph"][r0 + 1:r0 + 129, :], reads=allph, writes=[nx])
            P.op("dve", lambda g: g.tensor_tensor(out=pv_[:], in0=pv_[:], in1=ck[:, 0, :], op=ALU.mult),
                 reads=[pv_, ck], writes=[pv_])
            P.op("pool", lambda g: g.tensor_tensor(out=cu[:], in0=cu[:], in1=ck[:, 1, :], op=ALU.mult),
                 reads=[cu, ck], writes=[cu])
            P.op("dve", lambda g: g.tensor_tensor(out=nx[:], in0=nx[:], in1=ck[:, 2, :], op=ALU.mult),
                 reads=[nx, ck], writes=[nx])
            P.op("dve", lambda g: g.tensor_tensor(out=nx[:], in0=nx[:], in1=cb[:], op=ALU.add),
                 reads=[nx, cb], writes=[nx])
            P.op("dve", lambda g: g.tensor_tensor(out=cu[:], in0=cu[:], in1=pv_[:], op=ALU.add),
                 reads=[cu, pv_], writes=[cu])
            P.op("pool", lambda g: g.tensor_tensor(out=cu[:], in0=cu[:], in1=nx[:], op=ALU.add),
                 reads=[cu, nx], writes=[cu])
            P.dma(d["hcs"][t * 128:(t + 1) * 128, :], cu[:], reads=[cu], writes=[self.db("hcs", t)])
        P.release()

    def hy_load_tabs(self, n):
        P = self.P
        d = self.d
        pre = "hyL" if n == NLAT else "hyC"
        S1 = 2 * n // 64
        T1 = n // 64
        F1n = S1 // 2 + 1
        W1 = P.sbuf([S1, 64, 2, F1n], BF16, "W1")
        P.dma(W1[:].rearrange("p a b c -> p (a b c)"), d[pre + "_W1"], writes=[W1], q="pool")
        W3 = P.sbuf([F1n, 64, 2, T1], BF16, "W3")
        P.dma(W3[:].rearrange("p a b c -> p (a b c)"), d[pre + "_W3"], writes=[W3], q="pool")
        Dm = P.sbuf([128, 3, 128], BF16, "Dm")
        P.dma(Dm[:], d["hy_D"].rearrange("a p c -> p a c"), writes=[Dm], q="pool")
        return dict(W1=W1, W3=W3, Dm=Dm, S1=S1, T1=T1, n=n, F1n=F1n)

    def hy_stage1(self, tb, src_ap, nz, src_reads, cast):
        P = self.P
        d = self.d
        S1 = tb["F1n"]
        P.mark()
        U = P.sbuf([nz, 64, 512], BF16, "U")
        P.dma(U[:], src_ap.rearrange("(a s) c -> a s c", s=64), reads=src_reads, writes=[U],
              q=("pool" if cast else "sp"))
        bo = [P.sbuf([S1, 2, 512], BF16, "bo%d" % i) for i in range(4)]
        bdv = d["Bd"].rearrange("(r s) f c -> s f r c", r=2)
        for s2 in range(64):
            o = bo[s2 % 4]
            for ri in range(2):
                ps = self.ps[(2 * s2 + ri) % 4]
                P.mm(ps[0:S1, :], tb["W1"][0:nz, s2, ri, :], U[:, s2, :], reads=[tb["W1"], U], writes=[ps])
                if ri == 0:
                    P.op("act", lambda g: g.activation(out=o[:, ri, :], in_=ps[0:S1, :], func=AF.Copy),
                         reads=[ps], writes=[o])
                else:
                    P.op("dve", lambda g: g.tensor_copy(out=o[:, ri, :], in_=ps[0:S1, :]), reads=[ps], writes=[o])
            P.dma(bdv[s2, 0:S1], o[:], reads=[o], writes=[self.db("Bd", s2)])
        P.release()

    def hy_stage2(self, tb, cb, cb2=None):
        P = self.P
        d = self.d
        S1 = tb["F1n"]
        allbd = [self.db("Bd", s2) for s2 in range(64)]
        FG = 5
        bins = [P.sbuf([128, FG, 512], BF16, "bin%d" % i) for i in range(3)]
        prev = [None]
        for fg in range(S1 // FG):
            b = bins[fg % 3]
            P.dma(b[:], d["Bd"][:, fg * FG:(fg + 1) * FG, :], reads=allbd, writes=[b])
            for j in range(FG):
                f1 = fg * FG + j
                p1 = self.ps[(f1 % 3) * 2]
                p2 = self.ps[(f1 % 3) * 2 + 1]
                P.mm(p1[:], tb["Dm"][:, 0, :], b[:, j, :], reads=[tb["Dm"], b], writes=[p1])
                P.mm(p2[:], tb["Dm"][:, 1, :], b[:, j, :], reads=[tb["Dm"], b], writes=[p2])
                cb(f1, p1, p2)
                if cb2 is not None and prev[0] is not None:
                    cb2(prev[0])
                prev[0] = f1
        if cb2 is not None and prev[0] is not None:
            cb2(prev[0])

    def hy_filters(self, l, n, kab_name):
        P = self.P
        d = self.d
        pre = "hyL" if n == NLAT else "hyC"
        M = 2 * n
        P.mark()
        tb = self.hy_load_tabs(n) if n == NLAT else None
        hyT = P.sbuf([64, 4], F32, "hyT")
        P.dma(hyT[:], d["hyT"][l], writes=[hyT])
        sc = P.sbuf([64, 4], F32, "hsc")
        for j in range(2):
            P.op("dve", lambda g: g.tensor_scalar(out=sc[:, 2 * j:2 * j + 1], in0=hyT[:, j:j + 1],
                                                  scalar1=1.0 / TWO_PI, scalar2=None, op0=ALU.mult),
                 reads=[hyT], writes=[sc])
            P.op("dve", lambda g: g.tensor_tensor(out=sc[:, 2 * j + 1:2 * j + 2], in0=hyT[:, 2 + j:3 + j],
                                                  in1=sc[:, 2 * j:2 * j + 1], op=ALU.mult),
                 reads=[hyT, sc], writes=[sc])
            P.op("dve", lambda g: g.tensor_scalar(out=sc[:, 2 * j + 1:2 * j + 2], in0=sc[:, 2 * j + 1:2 * j + 2],
                                                  scalar1=64.0, scalar2=None, op0=ALU.add),
                 reads=[sc], writes=[sc])
        w1 = P.sbuf([33, 64], F32, "hw1")
        P.dma(w1[:], d["hyena_w1"][l], writes=[w1])
        w2 = P.sbuf([64, 64], F32, "hw2")
        P.dma(w2[:], d["hyena_w2"][l], writes=[w2])
        w3 = P.sbuf([64, 2048], F32, "hw3")
        P.dma(w3[:], d["hyena_w3"][l], writes=[w3])
        ones_f = P.sbuf([128, 128], F32, "ones_f")
        P.op("pool", lambda g: g.memset(ones_f[:], 1.0), writes=[ones_f])
        G2T = P.sbuf([64, M], F32, "G2T")
        rn = [P.sbuf([128, 512], F32, "rn%d" % o) for o in range(2)]
        P.mark()
        fk = [P.sbuf([33, 512], F32, "fk%d" % i) for i in range(2)]
        vt = [P.sbuf([64, 512], F32, "vt%d" % i) for i in range(2)]
        vi = [P.sbuf([64, 512], I32, "vi%d" % i) for i in range(2)]
        vf = [P.sbuf([64, 512], F32, "vf%d" % i) for i in range(2)]
        g1 = [P.sbuf([64, 512], F32, "g1%d" % i) for i in range(2)]
        cnt = [0]

        def sin_reduce(ps, j, out_ap, out_t):
            i = cnt[0] % 2
            cnt[0] += 1
            P.op("dve", lambda g: g.tensor_scalar(out=vt[i][:], in0=ps[0:64, :], scalar1=sc[:, 2 * j:2 * j + 1],
                                                  scalar2=sc[:, 2 * j + 1:2 * j + 2], op0=ALU.mult, op1=ALU.add),
                 reads=[ps, sc], writes=[vt[i]])
            P.op("dve", lambda g: g.tensor_copy(out=vi[i][:], in_=vt[i][:]), reads=[vt[i]], writes=[vi[i]])
            P.op("pool", lambda g: g.tensor_copy(out=vf[i][:], in_=vi[i][:]), reads=[vi[i]], writes=[vf[i]])
            P.op("pool", lambda g: g.tensor_tensor(out=vt[i][:], in0=vt[i][:], in1=vf[i][:], op=ALU.subtract),
                 reads=[vt[i], vf[i]], writes=[vt[i]])
            P.op("act", lambda g: g.activation(out=out_ap, in_=vt[i][:], func=AF.Sin, scale=TWO_PI),
                 reads=[vt[i]], writes=[out_t])

        for cbk in range(M // 512):
            f = fk[cbk % 2]
            P.dma(f[:], d[pre + "_fK"][:, cbk * 512:(cbk + 1) * 512], writes=[f])
            ps = self.ps[cbk % 2]
            P.mm(ps[0:64, :], w1[:], f[:], reads=[w1, f], writes=[ps])
            gg = g1[cbk % 2]
            sin_reduce(ps, 0, gg[:], gg)
            ps2 = self.ps[2 + cbk % 2]
            P.mm(ps2[0:64, :], w2[:], gg[:], reads=[w2, gg], writes=[ps2])
            sin_reduce(ps2, 1, G2T[:, cbk * 512:(cbk + 1) * 512], G2T)
        P.release()
        P.mark()
        wk = [P.sbuf([128, 512], F32, "wk%d" % i) for i in range(3)]
        kbt = [P.sbuf([128, 512], F32, "kbt%d" % i) for i in range(4)]
        ab = [P.sbuf([128, 512], F32, "ab%d" % i) for i in range(4)]
        kbo = [P.sbuf([128, 512], BF16, "kbo%d" % i) for i in range(4)]
        nlt = M // 128
        c = 0
        for lt in range(nlt):
            dirn = 0 if lt < n // 128 else 1
            w = wk[lt % 3]
            P.dma(w[:], d[pre + "_wK"][lt * 128:(lt + 1) * 128, :], writes=[w])
            for o in range(2):
                ps = self.ps[c % 4]
                kt_ = kbt[c % 4]
                a = ab[c % 4]
                ko = kbo[c % 4]
                c += 1
                P.mm(ps[:], G2T[:, lt * 128:(lt + 1) * 128], w3[:, o * 1024 + dirn * 512:o * 1024 + dirn * 512 + 512],
                     reads=[G2T, w3], writes=[ps])
                P.op("dve", lambda g: g.tensor_tensor(out=kt_[:], in0=ps[:], in1=w[:], op=ALU.mult),
                     reads=[ps, w], writes=[kt_])
                P.op("act", lambda g: g.activation(out=a[:], in_=kt_[:], func=AF.Abs), reads=[kt_], writes=[a])
                P.mm(self.ps[6 + o][:], ones_f[:], a[:], start=(lt == 0), stop=(lt == nlt - 1),
                     reads=[ones_f, a], writes=[self.ps[6 + o]])
                if lt == n // 128:
                    P.op("pool", lambda g: g.memset(kt_[0:1, :], 0.0), reads=[kt_], writes=[kt_])
                P.op("act", lambda g: g.activation(out=ko[:], in_=kt_[:], func=AF.Copy), reads=[kt_], writes=[ko])
                P.dma(d["kbuf"][o, lt * 128:(lt + 1) * 128, :], ko[:], reads=[ko], writes=[self.db("kbuf", (o, lt))])
        for o in range(2):
            P.op("dve", lambda g: g.tensor_scalar(out=rn[o][:], in0=self.ps[6 + o][:], scalar1=float(M), scalar2=None,
                                                  op0=ALU.mult), reads=[self.ps[6 + o]], writes=[rn[o]])
            P.op("dve", lambda g: g.reciprocal(out=rn[o][:], in_=rn[o][:]), reads=[rn[o]], writes=[rn[o]])
        P.release()
        if n != NLAT:
            self.hy_ctx_khat(rn)
            P.release()
            return
        for o in range(2):
            allkb = [self.db("kbuf", (o, lt)) for lt in range(nlt)]
            self.hy_stage1(tb, d["kbuf"][o, 0:M, :], tb["S1"], allkb, False)
            P.mark()
            kab = [P.sbuf([128, 2, 512], BF16, "kab%d" % i) for i in range(4)]

            def cbf(f1, p1, p2):
                k = kab[f1 % 4]
                r = rn[o]
                P.op("dve", lambda g: g.tensor_tensor(out=k[0:64, 0, :], in0=p1[0:64, :], in1=r[0:64, :], op=ALU.mult),
                     reads=[p1, r], writes=[k])
                P.op("dve", lambda g: g.tensor_tensor(out=k[64:128, 0, :], in0=p2[64:128, :], in1=r[64:128, :],
                                                      op=ALU.mult), reads=[p2, r], writes=[k])
                P.op("dve", lambda g: g.scalar_tensor_tensor(out=k[0:64, 1, :], in0=p2[0:64, :], scalar=-1.0,
                                                             in1=r[0:64, :], op0=ALU.mult, op1=ALU.mult),
                     reads=[p2, r], writes=[k])
                P.op("dve", lambda g: g.tensor_tensor(out=k[64:128, 1, :], in0=p1[64:128, :], in1=r[64:128, :],
                                                      op=ALU.mult), reads=[p1, r], writes=[k])
                P.dma(d[kab_name][o, f1].rearrange("a p c -> p a c"), k[:], reads=[k],
                      writes=[self.db(kab_name, (o, f1))])

            self.hy_stage2(tb, cbf)
            P.release()
        P.release()

    def hy_conv(self, l, tb, o, kab_name, src_ap, src_reads, gate_ap, gate_reads, dst_ap, dst_name):
        P = self.P
        d = self.d
        n, S1, T1 = tb["n"], tb["F1n"], tb["T1"]
        self.hy_stage1(tb, src_ap, tb["S1"] // 2, src_reads, True)
        P.mark()
        kab = [P.sbuf([128, 2, 512], BF16, "kab%d" % i) for i in range(4)]
        ta = [P.sbuf([128, 512], F32, "ta%d" % i) for i in range(4)]
        tb2 = [P.sbuf([128, 512], F32, "tb%d" % i) for i in range(4)]
        yh = [P.sbuf([128, 512], BF16, "yh%d" % i) for i in range(4)]
        do = [P.sbuf([128, 512], BF16, "do%d" % i) for i in range(4)]
        allk = [self.db(kab_name, (o, f1)) for f1 in range(S1)]

        def cbf(f1, p1, p2):
            i = f1 % 4
            k = kab[i]
            P.dma(k[:], d[kab_name][o, f1].rearrange("a p c -> p a c"), reads=allk, writes=[k])
            P.op("dve", lambda g: g.tensor_tensor(out=ta[i][:], in0=p1[:], in1=k[:, 0, :], op=ALU.mult),
                 reads=[p1, k], writes=[ta[i]])
            P.op("dve", lambda g: g.tensor_tensor(out=tb2[i][:], in0=p2[:], in1=k[:, 1, :], op=ALU.mult),
                 reads=[p2, k], writes=[tb2[i]])
            P.op("pool", lambda g: g.tensor_tensor(out=yh[i][:], in0=ta[i][:], in1=tb2[i][:], op=ALU.add),
                 reads=[ta[i], tb2[i]], writes=[yh[i]])

        def cbf2(f1):
            i = f1 % 4
            pd_ = self.ps[6 + f1 % 2]
            P.mm(pd_[:], tb["Dm"][:, 2, :], yh[i][:], reads=[tb["Dm"], yh[i]], writes=[pd_])
            P.op("act", lambda g: g.activation(out=do[i][:], in_=pd_[:], func=AF.Copy), reads=[pd_], writes=[do[i]])
            P.dma(d["Dd"][:, f1, :], do[i][:], reads=[do[i]], writes=[self.db("Dd", f1)])

        self.hy_stage2(tb, cbf, cbf2)
        P.release()
        P.mark()
        bias = P.sbuf([64, 512], F32, "hbias")
        P.dma(bias[:], d["hyena_bias"][l, o:o + 1, :].to_broadcast([64, 512]), writes=[bias])
        din = [P.sbuf([S1, 2, 512], BF16, "din%d" % i) for i in range(4)]
        gs = [P.sbuf([T1, 512], F32, "gs%d" % i) for i in range(4)]
        us = [P.sbuf([T1, 512], F32, "us%d" % i) for i in range(4)]
        zo = [P.sbuf([T1, 512], F32, "zo%d" % i) for i in range(4)]
        alld = [self.db("Dd", f1) for f1 in range(S1)]
        ddv = d["Dd"].rearrange("(r t) f c -> t f r c", r=2)
        gv = gate_ap.rearrange("(a s) c -> s a c", s=64)
        uv = src_ap.rearrange("(a s) c -> s a c", s=64)
        dv = dst_ap.rearrange("(a s) c -> s a c", s=64)
        for t2 in range(64):
            i = t2 % 4
            P.dma(din[i][:], ddv[t2, 0:S1], reads=alld, writes=[din[i]])
            P.dma(gs[i][:], gv[t2], reads=gate_reads, writes=[gs[i]])
            P.dma(us[i][:], uv[t2], reads=src_reads, writes=[us[i]])
            py = self.ps[6 + i % 2]
            P.mm(py[0:T1, :], tb["W3"][:, t2, 0, :], din[i][:, 0, :], start=True, stop=False,
                 reads=[tb["W3"], din[i]], writes=[py])
            P.mm(py[0:T1, :], tb["W3"][:, t2, 1, :], din[i][:, 1, :], start=False, stop=True,
                 reads=[tb["W3"], din[i]], writes=[py])
            P.op("pool", lambda g: g.tensor_tensor(out=us[i][:], in0=us[i][:], in1=bias[0:T1, :], op=ALU.mult),
                 reads=[us[i], bias], writes=[us[i]])
            P.op("dve", lambda g: g.tensor_tensor(out=us[i][:], in0=us[i][:], in1=py[0:T1, :], op=ALU.add),
                 reads=[us[i], py], writes=[us[i]])
            P.op("pool", lambda g: g.tensor_tensor(out=zo[i][:], in0=us[i][:], in1=gs[i][:], op=ALU.mult),
                 reads=[us[i], gs[i]], writes=[zo[i]])
            P.dma(dv[t2], zo[i][:], reads=[zo[i]], writes=[self.db(dst_name, ("t2", t2, n))])
        P.release()

    def hy_ctx_khat(self, rn):
        P = self.P
        d = self.d
        P.mark()
        dft = P.sbuf([128, 2, 4, 512], BF16, "dftc")
        for c_ in range(2):
            P.dma(dft[:, c_].rearrange("p s f -> p (s f)"), d["hyC_DFT"][c_].rearrange("p s f -> p (s f)"),
                  writes=[dft], q="pool")
        for o in range(2):
            kb = P.sbuf([128, 4, 512], BF16, "kbc%d" % o)
            P.dma(kb[:], d["kbuf"][o, 0:512, :].rearrange("(s p) c -> p s c", p=128),
                  reads=[self.db("kbuf", (o, lt)) for lt in range(4)], writes=[kb])
            ko = [P.sbuf([128, 512], BF16, "kco%d_%d" % (o, i)) for i in range(4)]
            c = 0
            for ri in range(2):
                for ft in range(4):
                    ps = self.ps[c % 4]
                    k_ = ko[c % 4]
                    c += 1
                    for lt in range(4):
                        P.mm(ps[:], dft[:, ri, lt, ft * 128:(ft + 1) * 128], kb[:, lt, :], start=(lt == 0), stop=(lt == 3),
                             reads=[dft, kb], writes=[ps])
                    P.op("dve", lambda g: g.tensor_tensor(out=k_[:], in0=ps[:], in1=rn[o][:], op=ALU.mult),
                         reads=[ps, rn[o]], writes=[k_])
                    P.dma(d["KC"][o, ri, ft], k_[:], reads=[k_], writes=[self.db("KC", (o, ri, ft))])
        P.release()

    def hy_ctx_conv(self, l, o, src_ap, src_reads, gate_ap, gate_reads, dst_ap, dst_name):
        P = self.P
        d = self.d
        P.mark()
        dft = P.sbuf([128, 2, 4, 512], BF16, "dftc")
        for c_ in range(2):
            P.dma(dft[:, c_].rearrange("p s f -> p (s f)"), d["hyC_DFT"][c_].rearrange("p s f -> p (s f)"),
                  writes=[dft], q="pool")
        u = P.sbuf([128, 2, 512], BF16, "uc")
        P.dma(u[:], src_ap.rearrange("(s p) c -> p s c", p=128), reads=src_reads, writes=[u], q="pool")
        u32 = P.sbuf([128, 2, 512], F32, "uc32")
        P.dma(u32[:], src_ap.rearrange("(s p) c -> p s c", p=128), reads=src_reads, writes=[u32])
        gt_ = P.sbuf([128, 2, 512], F32, "gc32")
        P.dma(gt_[:], gate_ap.rearrange("(s p) c -> p s c", p=128), reads=gate_reads, writes=[gt_])
        bias = P.sbuf([128, 512], F32, "hbias")
        P.dma(bias[:], d["hyena_bias"][l, o:o + 1, :].to_broadcast([128, 512]), writes=[bias])
        kc = P.sbuf([128, 2, 4, 512], BF16, "kc")
        for r_ in range(2):
            P.dma(kc[:, r_], d["KC"][o, r_].rearrange("f p c -> p f c"),
                  reads=[self.db("KC", (o, r_, ft)) for ft in range(4)], writes=[kc])
        Y = P.sbuf([128, 2, 4, 512], BF16, "Yc")
        ta = [P.sbuf([128, 512], F32, "cta%d" % i) for i in range(4)]
        for ft in range(4):
            pre = self.ps[(ft % 2) * 2]
            pim = self.ps[(ft % 2) * 2 + 1]
            for st in range(2):
                P.mm(pre[:], dft[:, 0, st, ft * 128:(ft + 1) * 128], u[:, st, :], start=(st == 0), stop=(st == 1),
                     reads=[dft, u], writes=[pre])
            for st in range(2):
                P.mm(pim[:], dft[:, 1, st, ft * 128:(ft + 1) * 128], u[:, st, :], start=(st == 0), stop=(st == 1),
                     reads=[dft, u], writes=[pim])
            P.op("dve", lambda g: g.tensor_tensor(out=ta[0][:], in0=pre[:], in1=kc[:, 0, ft, :], op=ALU.mult),
                 reads=[pre, kc], writes=[ta[0]])
            P.op("dve", lambda g: g.tensor_tensor(out=ta[1][:], in0=pim[:], in1=kc[:, 1, ft, :], op=ALU.mult),
                 reads=[pim, kc], writes=[ta[1]])
            P.op("dve", lambda g: g.tensor_tensor(out=ta[2][:], in0=pre[:], in1=kc[:, 1, ft, :], op=ALU.mult),
                 reads=[pre, kc], writes=[ta[2]])
            P.op("dve", lambda g: g.tensor_tensor(out=ta[3][:], in0=pim[:], in1=kc[:, 0, ft, :], op=ALU.mult),
                 reads=[pim, kc], writes=[ta[3]])
            P.op("pool", lambda g: g.tensor_tensor(out=Y[:, 0, ft, :], in0=ta[0][:], in1=ta[1][:], op=ALU.subtract),
                 reads=[ta[0], ta[1]], writes=[Y])
            P.op("pool", lambda g: g.tensor_tensor(out=Y[:, 1, ft, :], in0=ta[2][:], in1=ta[3][:], op=ALU.add),
                 reads=[ta[2], ta[3]], writes=[Y])
        zo = [P.sbuf([128, 512], F32, "czo%d" % i) for i in range(2)]
        for tt in range(2):
            py = self.ps[4 + tt]
            n_mm = 0
            for ri in range(2):
                for ft in range(4):
                    P.mm(py[:], dft[:, ri, ft, tt * 128:(tt + 1) * 128], Y[:, ri, ft, :], start=(n_mm == 0),
                         stop=(n_mm == 7), reads=[dft, Y], writes=[py])
                    n_mm += 1
            P.op("pool", lambda g: g.tensor_tensor(out=zo[tt][:], in0=u32[:, tt, :], in1=bias[:], op=ALU.mult),
                 reads=[u32, bias], writes=[zo[tt]])
            P.op("dve", lambda g: g.tensor_tensor(out=zo[tt][:], in0=zo[tt][:], in1=py[:], op=ALU.add),
                 reads=[zo[tt], py], writes=[zo[tt]])
            P.op("pool", lambda g: g.tensor_tensor(out=zo[tt][:], in0=zo[tt][:], in1=gt_[:, tt, :], op=ALU.mult),
                 reads=[zo[tt], gt_], writes=[zo[tt]])
            P.dma(dst_ap[tt * 128:(tt + 1) * 128, :], zo[tt][:], reads=[zo[tt]],
                  writes=[self.db(dst_name, ("c", tt))])
        P.release()

    def phase_hyena(self, l, with_ctx):
        P = self.P
        d = self.d
        self.hy_shortconv(l, NTILE if with_ctx else 32)
        self.hy_filters(l, NLAT, "KAB_L")
        if with_ctx:
            self.hy_filters(l, NCTX, "KC")
        if with_ctx:
            r0 = NLAT
            hcs = [b for k, b in self.dbufs.items() if k[0] == "hcs"]
            self.hy_ctx_conv(l, 0, d["hcs"][r0:r0 + NCTX, 0:512], hcs, d["hcs"][r0:r0 + NCTX, 512:1024], hcs,
                             d["zt1"][r0:r0 + NCTX, :], "zt1")
            z1 = [b for k, b in self.dbufs.items() if k[0] == "zt1"]
            self.hy_ctx_conv(l, 1, d["zt1"][r0:r0 + NCTX, :], z1, d["hcs"][r0:r0 + NCTX, 1024:1536], hcs,
                             d["zt2"][r0:r0 + NCTX, :], "zt2")
        segs = [(NLAT, 0, "KAB_L")]
        for (n, r0, kn) in segs:
            P.mark()
            tb = self.hy_load_tabs(n)
            hcs = [b for k, b in self.dbufs.items() if k[0] == "hcs"]
            self.hy_conv(l, tb, 0, kn, d["hcs"][r0:r0 + n, 0:512], hcs, d["hcs"][r0:r0 + n, 512:1024], hcs,
                         d["zt1"][r0:r0 + n, :], "zt1")
            z1 = [b for k, b in self.dbufs.items() if k[0] == "zt1"]
            self.hy_conv(l, tb, 1, kn, d["zt1"][r0:r0 + n, :], z1, d["hcs"][r0:r0 + n, 1024:1536], hcs,
                         d["zt2"][r0:r0 + n, :], "zt2")
            P.release()
        P.mark()
        zin = [P.sbuf([128, 512], F32, "zin%d" % i) for i in range(3)]
        zT = [P.sbuf([128, 4, 128], BF16, "zT%d" % i) for i in range(3)]
        z2 = [b for k, b in self.dbufs.items() if k[0] == "zt2"]
        yv = d["yT"][2].rearrange("(kc p) t -> p kc t", p=128)
        for t in range(NTILE if with_ctx else 32):
            zi = zin[t % 3]
            P.dma(zi[:], d["zt2"][t * 128:(t + 1) * 128, :], reads=z2, writes=[zi])
            ps = self.ps[t % 2]
            for kc in range(4):
                P.op("pe", lambda g: g.transpose(ps[:, kc * 128:(kc + 1) * 128], zi[:, kc * 128:(kc + 1) * 128],
                                                 self.ident_f[:]), reads=[zi, self.ident_f], writes=[ps])
            P.op("act", lambda g: g.activation(out=zT[t % 3][:].rearrange("p a b -> p (a b)"), in_=ps[:], func=AF.Copy),
                 reads=[ps], writes=[zT[t % 3]])
            P.dma(yv[:, :, t * 128:(t + 1) * 128], zT[t % 3][:], reads=[zT[t % 3]], writes=[self.db("yT", (2, t))])
        P.release()

    KB.declare_hyena = declare_hyena
    KB.hy_shortconv = hy_shortconv
    KB.hy_load_tabs = hy_load_tabs
    KB.hy_stage1 = hy_stage1
    KB.hy_stage2 = hy_stage2
    KB.hy_filters = hy_filters
    KB.hy_conv = hy_conv
    KB.hy_ctx_khat = hy_ctx_khat
    KB.hy_ctx_conv = hy_ctx_conv
    KB.phase_hyena = phase_hyena


_hyena_methods()


def rwkv_tables():
    idx = np.arange(128)
    out = np.zeros((2, 6, 128, 128), np.float32)
    for dd in range(2):
        incl = (idx[:, None] <= idx[None, :]) if dd == 0 else (idx[:, None] >= idx[None, :])
        incl = incl.astype(np.float32)
        ref = 63 if dd == 0 else 64
        out[dd, 0] = incl
        out[dd, 1] = incl - incl[:, ref:ref + 1]
        out[dd, 2] = 1.0 - incl
        out[dd, 3] = incl - np.eye(128, dtype=np.float32)
        out[dd, 4] = incl
        out[dd, 5] = out[dd, 3].T
    return out


def _rwkv_methods():
    def declare_rwkv(self):
        L = 2
        self.inp("rwkv_mu", [L, 2, 1792])
        self.inp("rwkv_kvec", [L, 2, 512])
        self.inp("rwkv_lnp", [L, 3, 512])
        self.inp("rwkv_wupA", [L, 65, 1024])
        self.inp("rwkv_aupA", [L, 65, 1024])
        self.inp("rwkv_g_up", [L, 128, 512])
        self.inp("rw_tri", [2, 6, 128, 128])
        self.scr("yf", [TT, 512])

    def phase_rwkv(self, l, with_ctx):
        P = self.P
        d = self.d
        idb = self.ident_b
        idf = self.ident_f
        P.mark()

        def dve(fn, r, w):
            return P.op("dve", fn, reads=r, writes=w)

        def act(fn, r, w):
            return P.op("act", fn, reads=r, writes=w)

        def pool(fn, r, w):
            return P.op("pool", fn, reads=r, writes=w)

        def T32(name, shape=(128, 512)):
            return P.sbuf(list(shape), F32, name)

        def T16(name, shape=(128, 512)):
            return P.sbuf(list(shape), BF16, name)

        mu = T32("mu", (128, 3, 1792))
        for j in range(2):
            P.dma(mu[:, 1 + j, :], d["rwkv_mu"][l, j:j + 1, :].to_broadcast([128, 1792]), writes=[mu])
        dve(lambda g: g.tensor_tensor(out=mu[:, 0, :], in0=mu[:, 1, :], in1=mu[:, 2, :], op=ALU.add), [mu], [mu])
        dve(lambda g: g.tensor_scalar(out=mu[:, 0, :], in0=mu[:, 0, :], scalar1=-1.0, scalar2=1.0, op0=ALU.mult,
                                      op1=ALU.add), [mu], [mu])
        kv = T32("kv", (128, 3, 512))
        for j in range(2):
            P.dma(kv[:, j, :], d["rwkv_kvec"][l, j:j + 1, :].to_broadcast([128, 512]), writes=[kv])
        dve(lambda g: g.tensor_scalar(out=kv[:, 2, :], in0=kv[:, 1, :], scalar1=-1.0, scalar2=1.0, op0=ALU.mult,
                                      op1=ALU.add), [kv], [kv])
        lnp = T32("lnp", (128, 3, 512))
        P.dma(lnp[:].rearrange("p a c -> p (a c)"),
              d["rwkv_lnp"][l:l + 1].rearrange("o a c -> o (a c)").to_broadcast([128, 1536]), writes=[lnp])
        wupA = T16("wupA", (65, 2, 512))
        P.dma(wupA[:].rearrange("p a c -> p (a c)"), d["rwkv_wupA"][l], writes=[wupA], q="pool")
        aupA = T16("aupA", (65, 2, 512))
        P.dma(aupA[:].rearrange("p a c -> p (a c)"), d["rwkv_aupA"][l], writes=[aupA], q="pool")
        gup = T16("gup", (128, 512))
        P.dma(gup[:], d["rwkv_g_up"][l], writes=[gup], q="pool")
        tri = T32("tri", (128, 2, 6, 128))
        P.dma(tri[:], d["rw_tri"].rearrange("a b p c -> p a b c"), writes=[tri])
        onec = T32("onec", (128, 1))
        pool(lambda g: g.memset(onec[:], 1.0), [], [onec])
        TWA = T16("TWA", (65, 128))
        ALA = T16("ALA", (65, 128))
        pool(lambda g: g.memset(TWA[:], 1.0), [], [TWA])
        pool(lambda g: g.memset(ALA[:], 1.0), [], [ALA])
        cur = [T32("cur%d" % i, (128, 1792)) for i in range(3)]
        prv = T32("prv", (128, 1792))
        nxt = T32("nxt", (128, 1792))
        kk = T32("kk")
        sq = T32("sq")
        s8 = T32("s8", (128, 8))
        r8 = T32("r8", (128, 8))
        tw = T16("tw", (128, 64))
        al = T16("al", (128, 64))
        sgl = T16("sgl", (128, 128))
        lw = T32("lw")
        av = T32("av")
        tt_ = T32("tt")
        bb = T32("bb")
        eW, eWi, eLu, eD, elw, eWx, eLux = [T32(nm) for nm in ("eW", "eWi", "eLu", "eD", "elw", "eWx", "eLux")]
        rt, zt, bt, kt = [T16(nm) for nm in ("rt", "zt", "bt", "kt")]
        RTf, ZTf, BTf, KTf = [T16(nm, (64, 8, 128)) for nm in ("RTf", "ZTf", "BTf", "KTf")]
        X0 = [T16("X0%d" % i, (128, 8, 128)) for i in range(2)]
        XT0 = [T16("XT0%d" % i, (128, 8, 128)) for i in range(2)]
        TT0 = [T16("TT0%d" % i, (128, 8, 128)) for i in range(2)]
        AzkT = [T16("AzkT%d" % i, (128, 8, 128)) for i in range(2)]
        ArbT = [T16("ArbT%d" % i, (128, 8, 128)) for i in range(2)]
        ArkT = [T16("ArkT%d" % i, (128, 8, 128)) for i in range(2)]
        vbf = [T16("vbf%d" % i) for i in range(2)]
        bp = [T16("bp%d" % i) for i in range(2)]
        kp = [T16("kp%d" % i) for i in range(2)]
        ru = [T16("ru%d" % i) for i in range(2)]
        zu = [T16("zu%d" % i) for i in range(2)]
        kd = [T32("kd%d" % i) for i in range(2)]
        kd0 = [T32("kd0%d" % i) for i in range(2)]
        sglT = [T16("sglT%d" % i, (128, 128)) for i in range(2)]
        WC = [T32("WC%d" % i, (64, 8)) for i in range(2)]
        yfl = [T32("yfl%d" % i) for i in range(2)]
        Xs = [T16("Xs%d" % i, (128, 8, 128)) for i in range(2)]
        XTs = [T16("XTs%d" % i, (128, 8, 128)) for i in range(2)]
        TTs = [T16("TTs%d" % i, (128, 8, 128)) for i in range(2)]
        Zp, Gm, U0 = [T16(nm) for nm in ("Zp", "Gm", "U0")]
        Y0 = T32("Y0")
        RpT = T16("RpT", (64, 8, 128))
        Mm = T32("Mm", (64, 8, 64))
        NTt = T32("NTt", (64, 8, 64))
        STf = T32("STf", (64, 8, 64))
        STb = T16("STb", (64, 8, 64))
        Yt = T32("Yt")
        m8 = T32("m8", (128, 8))
        v8 = T32("v8", (128, 8))
        b8 = T32("b8", (128, 8))
        yc = T32("yc")
        sq2 = T32("sq2")
        ob = T16("ob", (128, 4, 128))
        allpb = [b for k, b in self.dbufs.items() if k[0] == "pb"]
        ps = self.ps

        def view8(ap):
            return ap.rearrange("p (h m) -> p h m", m=64)

        def b8c(t8):
            return t8[:].unsqueeze(2).to_broadcast([128, 8, 64])

        for pss in range(2):
            dd = pss
            order = [32, 33] + list(range(32)) if dd == 0 else [33, 32] + list(range(31, -1, -1))
            pool(lambda g: g.memset(STf[:], 0.0), [], [STf])
            pool(lambda g: g.memset(STb[:], 0.0), [], [STb])

            def loads(tj, tn):
                rr = tm_row(tn * 128)
                P.dma(cur[tj % 3][:], d["pb"][rr:rr + 128, :], reads=allpb, writes=[cur[tj % 3]])
                P.dma(prv[:], d["pb"][rr - 1:rr + 127, :], reads=allpb, writes=[prv])
                P.dma(nxt[:], d["pb"][rr + 1:rr + 129, :], reads=allpb, writes=[nxt])

            def stageA(ti):
                t = order[ti]
                pr = ti % 2
                need_y = with_ctx or t < 32
                cu, pv_, nx = cur[ti % 3], prv, nxt
                if pss == 1 and need_y:
                    P.dma(yfl[pr][:], d["yf"][t * 128:(t + 1) * 128, :], reads=[self.db("yf", t)], writes=[yfl[pr]])
                pool(lambda g: g.tensor_tensor(out=nx[:], in0=nx[:], in1=mu[:, 2, :], op=ALU.mult), [nx, mu], [nx])
                dve(lambda g: g.tensor_tensor(out=cu[:], in0=cu[:], in1=mu[:, 0, :], op=ALU.mult), [cu, mu], [cu])
                dve(lambda g: g.tensor_tensor(out=pv_[:], in0=pv_[:], in1=mu[:, 1, :], op=ALU.mult), [pv_, mu], [pv_])
                yield
                dve(lambda g: g.tensor_tensor(out=cu[:], in0=cu[:], in1=pv_[:], op=ALU.add), [cu, pv_], [cu])
                dve(lambda g: g.tensor_tensor(out=cu[:], in0=cu[:], in1=nx[:], op=ALU.add), [cu, nx], [cu])
                if ti + 1 < len(order):
                    loads(ti + 1, order[ti + 1])
                r_ap, k_ap, v_ap = cu[:, 0:512], cu[:, 512:1024], cu[:, 1024:1536]
                dve(lambda g: g.tensor_tensor(out=kk[:], in0=k_ap, in1=kv[:, 0, :], op=ALU.mult), [cu, kv], [kk])
                pool(lambda g: g.tensor_tensor(out=sq[:], in0=kk[:], in1=kk[:], op=ALU.mult), [kk], [sq])
                dve(lambda g: g.tensor_reduce(out=s8[:], in_=view8(sq[:]), axis=AX.X, op=ALU.add), [sq], [s8])
                act(lambda g: g.activation(out=r8[:], in_=s8[:], func=AF.Sqrt, bias=1e-12), [s8], [r8])
                dve(lambda g: g.reciprocal(out=r8[:], in_=r8[:]), [r8], [r8])
                dve(lambda g: g.tensor_tensor(out=view8(kk[:]), in0=view8(kk[:]), in1=b8c(r8), op=ALU.mult),
                    [kk, r8], [kk])
                act(lambda g: g.activation(out=vbf[pr][:], in_=v_ap, func=AF.Copy), [cu], [vbf[pr]])
                act(lambda g: g.activation(out=tw[:], in_=cu[:, 1536:1600], func=AF.Tanh), [cu], [tw])
                act(lambda g: g.activation(out=al[:], in_=cu[:, 1600:1664], func=AF.Copy), [cu], [al])
                yield
                pb0 = self.psb(0)
                P.op("pe", lambda g: g.transpose(pb0[0:64, 0:128], tw[:], idb[:]), reads=[tw, idb], writes=[ps[0]])
                P.op("pe", lambda g: g.transpose(pb0[0:64, 128:256], al[:], idb[:]), reads=[al, idb], writes=[ps[0]])
                dve(lambda g: g.tensor_copy(out=TWA[0:64, :], in_=pb0[0:64, 0:128]), [ps[0]], [TWA])
                dve(lambda g: g.tensor_copy(out=ALA[0:64, :], in_=pb0[0:64, 128:256]), [ps[0]], [ALA])
                if pss == 1:
                    act(lambda g: g.activation(out=sgl[:], in_=cu[:, 1664:1792], func=AF.Sigmoid), [cu], [sgl])
                    pb1 = self.psb(1)
                    P.op("pe", lambda g: g.transpose(pb1[:, 0:128], sgl[:], idb[:]), reads=[sgl, idb], writes=[ps[1]])
                    dve(lambda g: g.tensor_copy(out=sglT[pr][:], in_=pb1[:, 0:128]), [ps[1]], [sglT[pr]])
                    P.mm(ps[2][:], ALA[:], aupA[:, 0, :], reads=[ALA, aupA], writes=[ps[2]])
                    act(lambda g: g.activation(out=av[:], in_=ps[2][:], func=AF.Sigmoid), [ps[2]], [av])
                    dve(lambda g: g.tensor_tensor(out=tt_[:], in0=av[:], in1=kv[:, 1, :], op=ALU.mult), [av, kv], [tt_])
                    pool(lambda g: g.tensor_tensor(out=tt_[:], in0=tt_[:], in1=kv[:, 2, :], op=ALU.add), [tt_, kv], [tt_])
                    dve(lambda g: g.tensor_tensor(out=kd0[pr][:], in0=k_ap, in1=tt_[:], op=ALU.mult), [cu, tt_], [kd0[pr]])
                yield
                P.mm(ps[2][:], TWA[:], wupA[:, dd, :], reads=[TWA, wupA], writes=[ps[2]])
                act(lambda g: g.activation(out=lw[:], in_=ps[2][:], func=AF.Sigmoid), [ps[2]], [lw])
                dve(lambda g: g.tensor_scalar(out=lw[:], in0=lw[:], scalar1=-0.6065306597126334, scalar2=None,
                                              op0=ALU.mult), [lw], [lw])
                P.mm(ps[3][:], ALA[:], aupA[:, dd, :], reads=[ALA, aupA], writes=[ps[3]])
                act(lambda g: g.activation(out=av[:], in_=ps[3][:], func=AF.Sigmoid), [ps[3]], [av])
                dve(lambda g: g.tensor_tensor(out=tt_[:], in0=av[:], in1=kv[:, 1, :], op=ALU.mult), [av, kv], [tt_])
                pool(lambda g: g.tensor_tensor(out=tt_[:], in0=tt_[:], in1=kv[:, 2, :], op=ALU.add), [tt_, kv], [tt_])
                dve(lambda g: g.tensor_tensor(out=kd[pr][:], in0=k_ap, in1=tt_[:], op=ALU.mult), [cu, tt_], [kd[pr]])
                pool(lambda g: g.tensor_tensor(out=bb[:], in0=kk[:], in1=av[:], op=ALU.mult), [kk, av], [bb])
                yield
                P.mm(ps[0][:], tri[:, dd, 0, :], lw[:], reads=[tri, lw], writes=[ps[0]])
                P.mm(ps[1][:], tri[:, dd, 1, :], lw[:], reads=[tri, lw], writes=[ps[1]])
                P.mm(ps[2][:], tri[:, dd, 2, :], lw[:], reads=[tri, lw], writes=[ps[2]])
                for h in range(8):
                    P.mm(ps[3][0:64, h:h + 1], lw[:, h * 64:(h + 1) * 64], onec[:], reads=[lw, onec], writes=[ps[3]])
                act(lambda g: g.activation(out=WC[pr][:], in_=ps[3][0:64, 0:8], func=AF.Exp), [ps[3]], [WC[pr]])
                act(lambda g: g.activation(out=eLu[:], in_=ps[0][:], func=AF.Exp), [ps[0]], [eLu])
                act(lambda g: g.activation(out=eW[:], in_=ps[1][:], func=AF.Exp), [ps[1]], [eW])
                act(lambda g: g.activation(out=eWi[:], in_=ps[1][:], func=AF.Exp, scale=-1.0), [ps[1]], [eWi])
                act(lambda g: g.activation(out=eD[:], in_=ps[2][:], func=AF.Exp), [ps[2]], [eD])
                act(lambda g: g.activation(out=elw[:], in_=lw[:], func=AF.Exp, scale=-1.0), [lw], [elw])
                yield
                dve(lambda g: g.tensor_tensor(out=eWx[:], in0=eW[:], in1=elw[:], op=ALU.mult), [eW, elw], [eWx])
                pool(lambda g: g.tensor_tensor(out=eLux[:], in0=eLu[:], in1=elw[:], op=ALU.mult), [eLu, elw], [eLux])
                dve(lambda g: g.tensor_tensor(out=rt[:], in0=r_ap, in1=eW[:], op=ALU.mult), [cu, eW], [rt])
                dve(lambda g: g.scalar_tensor_tensor(out=zt[:], in0=kk[:], scalar=-1.0, in1=eWx[:], op0=ALU.mult,
                                                     op1=ALU.mult), [kk, eWx], [zt])
                pool(lambda g: g.tensor_tensor(out=bt[:], in0=bb[:], in1=eWi[:], op=ALU.mult), [bb, eWi], [bt])
                dve(lambda g: g.tensor_tensor(out=kt[:], in0=kd[pr][:], in1=eWi[:], op=ALU.mult), [kd[pr], eWi], [kt])
                yield
                pool(lambda g: g.tensor_tensor(out=bp[pr][:], in0=bb[:], in1=eD[:], op=ALU.mult), [bb, eD], [bp[pr]])
                dve(lambda g: g.tensor_tensor(out=kp[pr][:], in0=kd[pr][:], in1=eD[:], op=ALU.mult), [kd[pr], eD], [kp[pr]])
                pool(lambda g: g.tensor_tensor(out=ru[pr][:], in0=r_ap, in1=eLu[:], op=ALU.mult), [cu, eLu], [ru[pr]])
                dve(lambda g: g.scalar_tensor_tensor(out=zu[pr][:], in0=kk[:], scalar=-1.0, in1=eLux[:], op0=ALU.mult,
                                                     op1=ALU.mult), [kk, eLux], [zu[pr]])
                for qi, (src, dstf) in enumerate(((rt, RTf), (zt, ZTf), (bt, BTf), (kt, KTf))):
                    pbx = self.psb(qi % 2)
                    for h in range(8):
                        P.op("pe", lambda g: g.transpose(pbx[0:64, h * 128:(h + 1) * 128], src[:, h * 64:(h + 1) * 64],
                                                         idb[:]), reads=[src, idb], writes=[ps[qi % 2]])
                    if qi % 2 == 0:
                        act(lambda g: g.activation(out=dstf[:].rearrange("p h t -> p (h t)"), in_=pbx[0:64, :],
                                                   func=AF.Copy), [ps[qi % 2]], [dstf])
                    else:
                        dve(lambda g: g.tensor_copy(out=dstf[:].rearrange("p h t -> p (h t)"), in_=pbx[0:64, :]),
                            [ps[qi % 2]], [dstf])
                    if qi == 1:
                        yield
                yield
                nb = [0]

                def amat(Lf, Rf, mi, dst):
                    for hg in range(2):
                        pa = ps[nb[0] % 4]
                        nb[0] += 1
                        for j in range(4):
                            h = hg * 4 + j
                            P.mm(pa[:, j * 128:(j + 1) * 128], Lf[:, h, :], Rf[:, h, :], reads=[Lf, Rf], writes=[pa])
                        dve(lambda g: g.tensor_tensor(out=dst[:, hg * 4:(hg + 1) * 4, :],
                                                      in0=pa[:].rearrange("p (h t) -> p h t", h=4),
                                                      in1=tri[:, dd, mi, :].unsqueeze(1).to_broadcast([128, 4, 128]),
                                                      op=ALU.mult), [pa, tri], [dst])

                amat(ZTf, BTf, 5, X0[pr])
                amat(BTf, ZTf, 3, XT0[pr])
                yield
                amat(KTf, ZTf, 3, AzkT[pr])
                amat(BTf, RTf, 4, ArbT[pr])
                yield
                amat(KTf, RTf, 4, ArkT[pr])
                pool(lambda g: g.tensor_tensor(out=TT0[pr][:], in0=XT0[pr][:],
                                               in1=idb[:].unsqueeze(1).to_broadcast([128, 8, 128]), op=ALU.add),
                     [XT0[pr], idb], [TT0[pr]])

            def stageB(ti):
                t = order[ti]
                pr = ti % 2
                need_y = with_ctx or t < 32
                cu = cur[ti % 3]
                r_ap, v_ap = cu[:, 0:512], cu[:, 1024:1536]
                Xc, XTc, TTc = X0[pr], XT0[pr], TT0[pr]
                cx = 0
                for it in range(6):
                    Xn, XTn, TTn = Xs[cx], XTs[cx], TTs[cx]
                    for hg in range(2):
                        p2 = ps[4 + hg]
                        for j in range(4):
                            h = hg * 4 + j
                            P.mm(p2[:, j * 128:(j + 1) * 128], XTc[:, h, :], Xc[:, h, :], reads=[XTc, Xc], writes=[p2])
                        act(lambda g: g.activation(out=Xn[:, hg * 4:(hg + 1) * 4, :].rearrange("p h t -> p (h t)"),
                                                   in_=p2[:], func=AF.Copy), [p2], [Xn])
                        if it < 5:
                            p3 = ps[6 + hg]
                            for j in range(4):
                                h = hg * 4 + j
                                P.mm(p3[:, j * 128:(j + 1) * 128], Xc[:, h, :], XTc[:, h, :], reads=[Xc, XTc],
                                     writes=[p3])
                            dve(lambda g: g.tensor_copy(out=XTn[:, hg * 4:(hg + 1) * 4, :].rearrange("p h t -> p (h t)"),
                                                        in_=p3[:]), [p3], [XTn])
                    yield
                    for hg in range(2):
                        p4 = ps[4 + hg]
                        for j in range(4):
                            h = hg * 4 + j
                            P.mm(p4[:, j * 128:(j + 1) * 128], Xn[:, h, :], TTc[:, h, :], start=True, stop=False,
                                 reads=[Xn, TTc], writes=[p4])
                            P.mm(p4[:, j * 128:(j + 1) * 128], idb[:], TTc[:, h, :], start=False, stop=True,
                                 reads=[idb, TTc], writes=[p4])
                        act(lambda g: g.activation(out=TTn[:, hg * 4:(hg + 1) * 4, :].rearrange("p h t -> p (h t)"),
                                                   in_=p4[:], func=AF.Copy), [p4], [TTn])
                    Xc, XTc, TTc = Xn, XTn, TTn
                    cx = 1 - cx
                    yield
                TT = TTc
                for h in range(8):
                    P.mm(ps[4][:, h * 64:(h + 1) * 64], TT[:, h, :], zu[pr][:, h * 64:(h + 1) * 64], reads=[TT, zu[pr]],
                         writes=[ps[4]])
                act(lambda g: g.activation(out=Zp[:], in_=ps[4][:], func=AF.Copy), [ps[4]], [Zp])
                for h in range(8):
                    P.mm(ps[5][:, h * 64:(h + 1) * 64], AzkT[pr][:, h, :], vbf[pr][:, h * 64:(h + 1) * 64],
                         reads=[AzkT[pr], vbf[pr]], writes=[ps[5]])
                dve(lambda g: g.tensor_copy(out=Gm[:], in_=ps[5][:]), [ps[5]], [Gm])
                for h in range(8):
                    P.mm(ps[6][:, h * 64:(h + 1) * 64], TT[:, h, :], Gm[:, h * 64:(h + 1) * 64], reads=[TT, Gm], writes=[ps[6]])
                act(lambda g: g.activation(out=U0[:], in_=ps[6][:], func=AF.Copy), [ps[6]], [U0])
                yield
                for h in range(8):
                    P.mm(ps[7][0:64, h * 64:(h + 1) * 64], Zp[:, h * 64:(h + 1) * 64], bp[pr][:, h * 64:(h + 1) * 64],
                         reads=[Zp, bp[pr]], writes=[ps[7]])
                dve(lambda g: g.tensor_tensor(out=Mm[:], in0=idf[0:64, 0:64].unsqueeze(1).to_broadcast([64, 8, 64]),
                                              in1=WC[pr][:].unsqueeze(2).to_broadcast([64, 8, 64]), op=ALU.mult),
                    [idf, WC[pr]], [Mm])
                dve(lambda g: g.tensor_tensor(out=Mm[:], in0=Mm[:], in1=ps[7][0:64, :].rearrange("p (h m) -> p h m", m=64),
                                              op=ALU.add), [Mm, ps[7]], [Mm])
                for h in range(8):
                    hs = slice(h * 64, (h + 1) * 64)
                    P.mm(ps[4][0:64, hs], bp[pr][:, hs], U0[:, hs], start=True, stop=False, reads=[bp[pr], U0], writes=[ps[4]])
                    P.mm(ps[4][0:64, hs], kp[pr][:, hs], vbf[pr][:, hs], start=False, stop=True, reads=[kp[pr], vbf[pr]],
                         writes=[ps[4]])
                act(lambda g: g.activation(out=NTt[:].rearrange("p h m -> p (h m)"), in_=ps[4][0:64, :], func=AF.Copy),
                    [ps[4]], [NTt])
                yield
                if need_y:
                    for h in range(8):
                        hs = slice(h * 64, (h + 1) * 64)
                        P.mm(ps[5][:, hs], ArbT[pr][:, h, :], U0[:, hs], start=True, stop=False, reads=[ArbT[pr], U0],
                             writes=[ps[5]])
                        P.mm(ps[5][:, hs], ArkT[pr][:, h, :], vbf[pr][:, hs], start=False, stop=True,
                             reads=[ArkT[pr], vbf[pr]], writes=[ps[5]])
                    act(lambda g: g.activation(out=Y0[:], in_=ps[5][:], func=AF.Copy), [ps[5]], [Y0])
                    for hg in range(2):
                        prr = ps[6 + hg]
                        for j in range(4):
                            h = hg * 4 + j
                            hs = slice(h * 64, (h + 1) * 64)
                            P.mm(prr[0:64, j * 128:(j + 1) * 128], ru[pr][:, hs], idb[:], start=True, stop=False,
                                 reads=[ru[pr], idb], writes=[prr])
                            P.mm(prr[0:64, j * 128:(j + 1) * 128], Zp[:, hs], ArbT[pr][:, h, :], start=False, stop=True,
                                 reads=[Zp, ArbT[pr]], writes=[prr])
                        dve(lambda g: g.tensor_copy(out=RpT[:, hg * 4:(hg + 1) * 4, :].rearrange("p h t -> p (h t)"),
                                                    in_=prr[0:64, :]), [prr], [RpT])
                    yield
                    for h in range(8):
                        P.mm(ps[4][:, h * 64:(h + 1) * 64], RpT[:, h, :], STb[:, h, :], reads=[RpT, STb], writes=[ps[4]])
                    dve(lambda g: g.tensor_tensor(out=Yt[:], in0=ps[4][:], in1=Y0[:], op=ALU.add), [ps[4], Y0], [Yt])
                for h in range(8):
                    P.mm(ps[5][0:64, h * 64:(h + 1) * 64], Mm[:, h, :], STf[:, h, :], reads=[Mm, STf], writes=[ps[5]])
                dve(lambda g: g.tensor_tensor(out=STf[:], in0=ps[5][0:64, :].rearrange("p (h m) -> p h m", m=64),
                                              in1=NTt[:], op=ALU.add), [ps[5], NTt], [STf])
                act(lambda g: g.activation(out=STb[:], in_=STf[:], func=AF.Copy), [STf], [STb])
                yield
                if not need_y:
                    return
                if pss == 0:
                    P.dma(d["yf"][t * 128:(t + 1) * 128, :], Yt[:], reads=[Yt], writes=[self.db("yf", t)])
                    return
                dve(lambda g: g.tensor_tensor(out=Yt[:], in0=Yt[:], in1=yfl[pr][:], op=ALU.add), [Yt, yfl[pr]], [Yt])
                dve(lambda g: g.tensor_reduce(out=m8[:], in_=view8(Yt[:]), axis=AX.X, op=ALU.add), [Yt], [m8])
                dve(lambda g: g.tensor_scalar(out=m8[:], in0=m8[:], scalar1=1.0 / 64, scalar2=None, op0=ALU.mult), [m8], [m8])
                dve(lambda g: g.tensor_tensor(out=view8(yc[:]), in0=view8(Yt[:]), in1=b8c(m8), op=ALU.subtract),
                    [Yt, m8], [yc])
                pool(lambda g: g.tensor_tensor(out=sq2[:], in0=yc[:], in1=yc[:], op=ALU.mult), [yc], [sq2])
                dve(lambda g: g.tensor_reduce(out=v8[:], in_=view8(sq2[:]), axis=AX.X, op=ALU.add), [sq2], [v8])
                act(lambda g: g.activation(out=v8[:], in_=v8[:], func=AF.Sqrt, scale=1.0 / 64, bias=64e-5), [v8], [v8])
                dve(lambda g: g.reciprocal(out=v8[:], in_=v8[:]), [v8], [v8])
                yield
                dve(lambda g: g.tensor_tensor(out=view8(yc[:]), in0=view8(yc[:]), in1=b8c(v8), op=ALU.mult), [yc, v8], [yc])
                pool(lambda g: g.tensor_tensor(out=yc[:], in0=yc[:], in1=lnp[:, 0, :], op=ALU.mult), [yc, lnp], [yc])
                pool(lambda g: g.tensor_tensor(out=yc[:], in0=yc[:], in1=lnp[:, 1, :], op=ALU.add), [yc, lnp], [yc])
                dve(lambda g: g.tensor_tensor(out=kd0[pr][:], in0=kd0[pr][:], in1=kd[pr][:], op=ALU.add),
                    [kd0[pr], kd[pr]], [kd0[pr]])
                dve(lambda g: g.tensor_tensor(out=kd0[pr][:], in0=kd0[pr][:], in1=r_ap, op=ALU.mult), [kd0[pr], cu], [kd0[pr]])
                dve(lambda g: g.scalar_tensor_tensor(out=sq2[:], in0=kd0[pr][:], scalar=0.5, in1=lnp[:, 2, :], op0=ALU.mult,
                                                     op1=ALU.mult), [kd0[pr], lnp], [sq2])
                dve(lambda g: g.tensor_reduce(out=b8[:], in_=view8(sq2[:]), axis=AX.X, op=ALU.add), [sq2], [b8])
                dve(lambda g: g.tensor_tensor(out=view8(sq2[:]), in0=view8(v_ap), in1=b8c(b8), op=ALU.mult), [cu, b8], [sq2])
                pool(lambda g: g.tensor_tensor(out=yc[:], in0=yc[:], in1=sq2[:], op=ALU.add), [yc, sq2], [yc])
                yield
                P.mm(ps[6][:], sglT[pr][:], gup[:], reads=[sglT[pr], gup], writes=[ps[6]])
                dve(lambda g: g.tensor_tensor(out=yc[:], in0=yc[:], in1=ps[6][:], op=ALU.mult), [yc, ps[6]], [yc])
                for kc in range(4):
                    P.op("pe", lambda g: g.transpose(ps[7][:, kc * 128:(kc + 1) * 128], yc[:, kc * 128:(kc + 1) * 128],
                                                     idf[:]), reads=[yc, idf], writes=[ps[7]])
                act(lambda g: g.activation(out=ob[:].rearrange("p a b -> p (a b)"), in_=ps[7][:], func=AF.Copy),
                    [ps[7]], [ob])
                P.dma(d["yT"][1].rearrange("(kc p) t -> p kc t", p=128)[:, :, t * 128:(t + 1) * 128], ob[:], reads=[ob],
                      writes=[self.db("yT", (1, t))])

            def run2(ga, gb):
                live = [g for g in (ga, gb) if g is not None]
                while live:
                    for g in list(live):
                        try:
                            next(g)
                        except StopIteration:
                            live.remove(g)

            loads(0, order[0])
            run2(stageA(0), None)
            for ti in range(len(order)):
                nxa = stageA(ti + 1) if ti + 1 < len(order) else None
                run2(stageB(ti), nxa)
        P.release()

    KB.declare_rwkv = declare_rwkv
    KB.phase_rwkv = phase_rwkv


_rwkv_methods()


def build_program():
    nc = bass.Bass("TRN2", target_bir_lowering=False)
    kb = KB(nc)
    kb.declare_common()
    kb.declare_pin()
    kb.declare_attn()
    kb.declare_merge()
    kb.declare_hyena()
    kb.declare_rwkv()
    kb.outp("out", [NLAT, DM])
    kb.alloc_persist()
    d = kb.d
    for l in range(2):
        with_ctx = (l == 0)
        kb.phase_mod(l)
        src = (d["xall"], "xall") if l == 0 else (d["xs"], "xs")
        kb.phase_ffn(l, 0, src, (d["xs"], "xs"), NTILE)
        kb.phase_pin(l, (d["xs"], "xs"))
        kb.phase_mla(l, with_ctx)
        kb.phase_swa(l, with_ctx)
        kb.phase_hyena(l, with_ctx)
        kb.phase_rwkv(l, with_ctx)
        kb.phase_merge(l, with_ctx)
        if l == 0:
            kb.phase_ffn(l, 2, (d["xs"], "xs"), (d["xs"], "xs"), NTILE)
        else:
            kb.phase_ffn(l, 2, (d["xs"], "xs"), (d["out"], "out"), 32)
    kb.P.barrier()
    return nc, kb


def kernel(**inputs):
    from concourse.bass_utils import run_bass_kernel_spmd
    nc, kb = build_program()
    sh = host_shared(inputs)
    names = [k for k in kb.d if k in sh]
    maps = []
    for b in range(8):
        pc = host_core(inputs, b)
        m = {k: sh[k] for k in names}
        m.update(pc)
        maps.append(m)
    res = run_bass_kernel_spmd(nc, maps, core_ids=list(range(8)))
    out = np.stack([np.asarray(res.results[b]["out"], dtype=np.float32) for b in range(8)], axis=0)
    return out
```
